# Optimizing a Trainium2 kernel written in Bass

```python
import math
import jax, jax.numpy as jnp
from jax import lax
import numpy as np

D_MODEL = 2048
BATCH = 2
SEQ = 16384
DEPTH = 1
DEC_BATCH = 32
DEC_SEQ = 16
PAST_LEN = 1024

CHUNK = 64
QBLK = 128
DIFF_WIDTH = D_MODEL // 2
DIFF_H = 8
DIFF_DH = DIFF_WIDTH // DIFF_H // 2
CONV_W = D_MODEL // 4
CONV_K = 3
MEM_WIDTH = D_MODEL // 4
MEM_H = 4
MEM_DH = MEM_WIDTH // MEM_H
N_MEM = 256
N_BUCKETS = 32
MAX_DISTANCE = 128
MIX_WIDTH = DIFF_WIDTH + CONV_W + MEM_WIDTH
ALPHA = (2 * DEPTH) ** 0.25
BETA = (8 * DEPTH) ** -0.25
EPS = 1e-5

PROJ_WIDTHS = (2 * DIFF_WIDTH // 2 * 1, DIFF_WIDTH, DIFF_WIDTH, CONV_W, CONV_W, CONV_W, MEM_WIDTH,
               DIFF_WIDTH, CONV_W, MEM_WIDTH)
PROJ_TOTAL = sum(PROJ_WIDTHS)
PROJ_SPLITS = tuple(int(s) for s in np.cumsum(PROJ_WIDTHS)[:-1])

kernel_name = "hybrid_diffattn_shortconv_mem_stream_step"


def layer_norm(x, g, b):
    xf = x.astype(jnp.float32)
    mu = jnp.mean(xf, axis=-1, keepdims=True)
    var = jnp.mean(jnp.square(xf - mu), axis=-1, keepdims=True)
    return ((xf - mu) * lax.rsqrt(var + EPS) * g.astype(jnp.float32) + b.astype(jnp.float32)).astype(x.dtype)


def rms_norm(x, g):
    xf = x.astype(jnp.float32)
    y = xf * lax.rsqrt(jnp.mean(jnp.square(xf), axis=-1, keepdims=True) + EPS)
    return (y * g.astype(jnp.float32)).astype(x.dtype)


def t5_bucket(rel):
    half = N_BUCKETS // 2
    max_exact = half // 2
    ret = jnp.where(rel > 0, half, 0)
    n = jnp.abs(rel)
    nf = jnp.maximum(n, 1).astype(jnp.float32)
    large = max_exact + (jnp.log(nf / max_exact) / math.log(MAX_DISTANCE / max_exact)
                         * (half - max_exact)).astype(jnp.int32)
    large = jnp.minimum(large, half - 1)
    return ret + jnp.where(n < max_exact, n, large)


def rel_bias(qpos, kpos, table):
    bucket = t5_bucket(kpos[None, :] - qpos[:, None])
    return jnp.transpose(jnp.take(table, bucket, axis=0), (2, 0, 1))


def chunk_mask(qpos, kpos):
    return (kpos[None, :] // CHUNK) <= (qpos[:, None] // CHUNK)


def diff_attention(q, k, v, bias, mask, lam):
    s = jnp.einsum('bqhcd,bkhcd->bchqk', q, k).astype(jnp.float32) * (DIFF_DH ** -0.5)
    s = s + bias.astype(jnp.float32)
    s = jnp.where(mask, s, -jnp.inf)
    p = jax.nn.softmax(s, axis=-1)
    a = p[:, 0] - lam * p[:, 1]
    return jnp.einsum('bhqk,bkhe->bqhe', a.astype(v.dtype), v)


def prompt_diff_attention(q, k, v, lam, table):
    b, s = q.shape[0], q.shape[1]
    nb = s // QBLK
    qb = q.reshape(b, nb, QBLK, DIFF_H, 2, DIFF_DH).swapaxes(0, 1)
    kpos = jnp.arange(s)

    def block(args):
        qi, i = args
        qpos = i * QBLK + jnp.arange(QBLK)
        return diff_attention(qi, k, v, rel_bias(qpos, kpos, table), chunk_mask(qpos, kpos), lam)

    o = lax.map(block, (qb, jnp.arange(nb)))
    return o.swapaxes(0, 1).reshape(b, s, DIFF_H, 2 * DIFF_DH)


def mem_attention(q, mk, mv):
    s = jnp.einsum('bqhd,bkhd->bhqk', q, mk).astype(jnp.float32) * (MEM_DH ** -0.5)
    p = jax.nn.softmax(s, axis=-1)
    o = jnp.einsum('bhqk,bkhd->bqhd', p.astype(mv.dtype), mv)
    return o.reshape(q.shape[0], q.shape[1], MEM_WIDTH)


def causal_conv(u_padded, w, s):
    y = w[0] * u_padded[:, 0:s]
    for j in range(1, CONV_K):
        y = y + w[j] * u_padded[:, j:j + s]
    return y


def in_projection(x, w_in):
    p = jnp.einsum('bsd,de->bse', x, w_in)
    q, k, v, h, bg, cg, mq, g_d, g_c, g_m = jnp.split(p, PROJ_SPLITS, axis=-1)
    b, s = x.shape[0], x.shape[1]
    q = q.reshape(b, s, DIFF_H, 2, DIFF_DH)
    k = k.reshape(b, s, DIFF_H, 2, DIFF_DH)
    v = v.reshape(b, s, DIFF_H, 2 * DIFF_DH)
    mq = mq.reshape(b, s, MEM_H, MEM_DH)
    return q, k, v, h, bg, cg, mq, g_d, g_c, g_m


def diff_lambda(lq1, lk1, lq2, lk2, lam_init):
    f = jnp.float32
    return (jnp.exp(jnp.sum(lq1.astype(f) * lk1.astype(f))) -
            jnp.exp(jnp.sum(lq2.astype(f) * lk2.astype(f))) + lam_init)


def merge_and_norm(x, o_diff, conv_y, o_mem, g_d, g_c, g_m, subln_g, lam_init, w_out, ln_g, ln_b):
    b, s = x.shape[0], x.shape[1]
    o_diff = (rms_norm(o_diff, subln_g) * (1.0 - lam_init)).reshape(b, s, DIFF_WIDTH)
    z = jnp.concatenate([o_diff * jax.nn.silu(g_d),
                         conv_y * jax.nn.silu(g_c),
                         o_mem * jax.nn.silu(g_m)], axis=-1)
    o = jnp.einsum('bse,ed->bsd', z, w_out)
    return layer_norm(ALPHA * x + o, ln_g, ln_b)


def setup_inputs(seed: int = 0) -> dict:
    key = jax.random.key(seed)
    ks = jax.random.split(key, 20)
    f = jnp.float32

    def nrm(k, shape, s):
        return jax.random.normal(k, shape, f) * s

    return {
        "x_prompt": nrm(ks[0], (BATCH, SEQ, D_MODEL), 1.0),
        "x_sample": nrm(ks[1], (DEC_BATCH, DEC_SEQ, D_MODEL), 1.0),
        "cache_diff_k": nrm(ks[2], (DEPTH, DEC_BATCH, PAST_LEN, DIFF_H, 2 * DIFF_DH), 1.0),
        "cache_diff_v": nrm(ks[3], (DEPTH, DEC_BATCH, PAST_LEN, DIFF_H, 2 * DIFF_DH), 1.0),
        "cache_conv": nrm(ks[4], (DEPTH, DEC_BATCH, CONV_K - 1, CONV_W), 1.0),
        "cache_mem_k": nrm(ks[5], (DEPTH, DEC_BATCH, N_MEM, MEM_H, MEM_DH), 1.0),
        "cache_mem_v": nrm(ks[6], (DEPTH, DEC_BATCH, N_MEM, MEM_H, MEM_DH), 1.0),
        "mem_prompt": nrm(ks[7], (BATCH, N_MEM, D_MODEL), 1.0),
        "rel_bias_table": nrm(ks[8], (N_BUCKETS, DIFF_H), 0.5),
        "w_in": nrm(ks[9], (DEPTH, D_MODEL, PROJ_TOTAL), D_MODEL ** -0.5),
        "w_mem_kv": nrm(ks[10], (DEPTH, D_MODEL, 2 * MEM_WIDTH), D_MODEL ** -0.5),
        "conv_w": nrm(ks[11], (DEPTH, CONV_K, CONV_W), 0.5),
        "lambda_q1": nrm(ks[12], (DEPTH, DIFF_DH), 0.1),
        "lambda_k1": nrm(ks[13], (DEPTH, DIFF_DH), 0.1),
        "lambda_q2": nrm(ks[14], (DEPTH, DIFF_DH), 0.1),
        "lambda_k2": nrm(ks[15], (DEPTH, DIFF_DH), 0.1),
        "subln_g": 1.0 + nrm(ks[16], (DEPTH, 2 * DIFF_DH), 0.02),
        "w_out": nrm(ks[17], (DEPTH, MIX_WIDTH, D_MODEL), BETA * MIX_WIDTH ** -0.5),
        "ln_g": 1.0 + nrm(ks[18], (DEPTH, D_MODEL), 0.02),
        "ln_b": nrm(ks[19], (DEPTH, D_MODEL), 0.02),
    }


def reference(x_prompt, x_sample, cache_diff_k, cache_diff_v, cache_conv, cache_mem_k, cache_mem_v,
              mem_prompt, rel_bias_table, w_in, w_mem_kv, conv_w, lambda_q1, lambda_k1, lambda_q2,
              lambda_k2, subln_g, w_out, ln_g, ln_b):
    xp, xs = x_prompt, x_sample
    bp, sp = xp.shape[0], xp.shape[1]
    bs, ss = xs.shape[0], xs.shape[1]
    past = cache_diff_k.shape[2]
    kp_l, vp_l, cp_l, mkp_l, mvp_l, ks_l, vs_l, cs_l = [], [], [], [], [], [], [], []

    for l in range(DEPTH):
        lam_init = 0.8 - 0.6 * math.exp(-0.3 * l)
        lam = diff_lambda(lambda_q1[l], lambda_k1[l], lambda_q2[l], lambda_k2[l], lam_init)

        q, k, v, h, bg, cg, mq, g_d, g_c, g_m = in_projection(xp, w_in[l])
        o_diff = prompt_diff_attention(q, k, v, lam, rel_bias_table)
        u = cg * h
        u_pad = jnp.concatenate([jnp.zeros((bp, CONV_K - 1, CONV_W), u.dtype), u], axis=1)
        conv_y = bg * causal_conv(u_pad, conv_w[l], sp)
        mkv = jnp.einsum('bmd,de->bme', mem_prompt, w_mem_kv[l])
        mk, mv = jnp.split(mkv, 2, axis=-1)
        mk = mk.reshape(bp, N_MEM, MEM_H, MEM_DH)
        mv = mv.reshape(bp, N_MEM, MEM_H, MEM_DH)
        o_mem = mem_attention(mq, mk, mv)
        kp_l.append(k.reshape(bp, sp, DIFF_H, 2 * DIFF_DH))
        vp_l.append(v)
        cp_l.append(u[:, sp - (CONV_K - 1):])
        mkp_l.append(mk)
        mvp_l.append(mv)
        xp = merge_and_norm(xp, o_diff, conv_y, o_mem, g_d, g_c, g_m, subln_g[l], lam_init,
                            w_out[l], ln_g[l], ln_b[l])

        q, k, v, h, bg, cg, mq, g_d, g_c, g_m = in_projection(xs, w_in[l])
        k_all = jnp.concatenate([cache_diff_k[l].reshape(bs, past, DIFF_H, 2, DIFF_DH), k], axis=1)
        v_all = jnp.concatenate([cache_diff_v[l], v], axis=1)
        qpos = past + jnp.arange(ss)
        kpos = jnp.arange(past + ss)
        o_diff = diff_attention(q, k_all, v_all, rel_bias(qpos, kpos, rel_bias_table),
                                chunk_mask(qpos, kpos), lam)
        u = cg * h
        u_pad = jnp.concatenate([cache_conv[l], u], axis=1)
        conv_y = bg * causal_conv(u_pad, conv_w[l], ss)
        o_mem = mem_attention(mq, cache_mem_k[l], cache_mem_v[l])
        ks_l.append(k.reshape(bs, ss, DIFF_H, 2 * DIFF_DH))
        vs_l.append(v)
        cs_l.append(u_pad[:, ss:])
        xs = merge_and_norm(xs, o_diff, conv_y, o_mem, g_d, g_c, g_m, subln_g[l], lam_init,
                            w_out[l], ln_g[l], ln_b[l])

    return (xp, xs, jnp.stack(kp_l), jnp.stack(vp_l), jnp.stack(cp_l), jnp.stack(mkp_l),
            jnp.stack(mvp_l), jnp.stack(ks_l), jnp.stack(vs_l), jnp.stack(cs_l))
```

```python
import math
from contextlib import ExitStack

import numpy as np
import concourse.bass as bass
import concourse.mybir as mybir
from concourse.bass_utils import run_bass_kernel_spmd

F32 = mybir.dt.float32
BF16 = mybir.dt.bfloat16
AF = mybir.ActivationFunctionType
ALU = mybir.AluOpType
AX = mybir.AxisListType

NCORES = 8
D = 2048
TW = 512
NT = 35
NSLOT = 8
NCH = 16
H = 8
MH = 4
NMEM = 256
EPS = 1e-5
ALPHA = 2.0 ** 0.25
LAM_INIT = 0.8 - 0.6 * math.exp(-0.0)
ACCW = 130
RK = 4
RP = 3
NEG = -30000.0
SS = 16
NSTR = 4
PAST = 1024

WG = [("w_in", 0), ("w_in", 512),
      ("w_in", 3072), ("w_in", 3584), ("w_in", 4096),
      ("w_in", 4608),
      ("w_in", 5120), ("w_in", 5632),
      ("w_in", 6144), ("w_in", 6656),
      ("w_out", 0), ("w_out", 512), ("w_out", 1024), ("w_out", 1536)]
G_Q0, G_Q1, G_H, G_B, G_C, G_MQ, G_GD0, G_GD1, G_GC, G_GM, G_O0 = range(11)


class Sem:
    def __init__(self, h):
        self.h = h
        self.v = 0


class Prog:
    ENG = ("pe", "act", "dve", "pool", "sp")

    def __init__(self, nc, stack):
        self.nc = nc
        self.stack = stack
        self.sems = []
        self.q = {e: [] for e in self.ENG}
        self.chain = {e: self.sem("ch_" + e) for e in self.ENG}
        self.barrier_toks = {e: [] for e in self.ENG}
        self.waited = {e: {} for e in self.ENG}
        self.dsems = {}

    def sem(self, name):
        s = Sem(self.stack.enter_context(self.nc.semaphore(name)))
        self.sems.append(s)
        return s

    def op(self, eng, fn, after=(), sig="chain", amt=1):
        waits = [t for t in after if t is not None]
        if self.barrier_toks[eng]:
            waits = list(self.barrier_toks[eng]) + waits
            self.barrier_toks[eng] = []
        w2 = []
        wd = self.waited[eng]
        for (s, v) in waits:
            if wd.get(id(s), 0) >= v:
                continue
            wd[id(s)] = v
            w2.append((s, v))
        if sig == "chain":
            sig = self.chain[eng]
        tok = None
        if sig is not None:
            sig.v += amt
            tok = (sig, sig.v)
        self.q[eng].append((w2, fn, sig, amt))
        return tok

    def dma(self, eng, out, in_, after=(), key=None):
        assert key is not None
        if key not in self.dsems:
            self.dsems[key] = self.sem("d_" + key)
        return self.op(eng, lambda e: e.dma_start(out=out, in_=in_), after, self.dsems[key], 16)

    def emit(self):
        nc = self.nc
        q = self.q

        def run(e, lst):
            for (waits, fn, sig, amt) in lst:
                for (s, v) in waits:
                    e.wait_ge(s.h, v)
                inst = fn(e)
                if sig is not None:
                    inst.then_inc(sig.h, amt)

        with nc.Block() as block:
            if q["pe"]:
                @block.tensor
                def _(e):
                    run(e, q["pe"])
            if q["act"]:
                @block.scalar
                def _(e):
                    run(e, q["act"])
            if q["dve"]:
                @block.vector
                def _(e):
                    run(e, q["dve"])
            if q["pool"]:
                @block.gpsimd
                def _(e):
                    run(e, q["pool"])
            if q["sp"]:
                @block.sync
                def _(e):
                    run(e, q["sp"])
        self.q = {e: [] for e in self.ENG}
        toks = [(s, s.v) for s in self.sems if s.v > 0]
        for e in self.ENG:
            self.barrier_toks[e] = list(toks)


def build(nslot=NSLOT, do_sample=True):
    nc = bass.Bass("TRN2", target_bir_lowering=False)
    top = ExitStack()
    with top:
        P = Prog(nc, top)

        def din(name, shape, dt=F32):
            return nc.dram_tensor(name, list(shape), dt, kind="ExternalInput").ap()

        def dout(name, shape, dt=F32):
            return nc.dram_tensor(name, list(shape), dt, kind="ExternalOutput").ap()

        def dscr(name, shape, dt):
            return nc.dram_tensor(name, list(shape), dt).ap()

        def sb(stack, name, shape, dt):
            return stack.enter_context(nc.sbuf_tensor("s_" + name, list(shape), dt))

        xT = din("xT", [D, NT * TW])
        xown = din("xown", [NSLOT * TW, D])
        vmask = din("vmask", [128, NT * 32])
        w_srcs = {"w_in": din("w_in", [D, 7168]), "w_out": din("w_out", [D, D])}
        w_mem = din("w_mem", [D, 1024])
        memT = din("memT", [D, NMEM])
        ident_d = din("ident", [128, 128])
        tab_d = din("tab", [128, 256])
        lam_d = din("lamv", [128, 256])
        gsub_d = din("gsub", [128, 512])
        lng_d = din("lng", [128, D])
        lnb_d = din("lnb", [128, D])
        cw_d = din("convw", [128, 12])
        bkt_d = din("bkt", [128, 256])
        dmask_d = din("dmask", [128, 128])

        y_o = dout("y", [NSLOT * TW, D])
        koT_o = dout("koT", [H, 128, NSLOT * TW])
        vo_o = dout("vo", [NSLOT * TW, 1024])
        convo_o = dout("convo", [128, 4, 2])
        mkoT_o = dout("mkoT", [MH, 128, NMEM])
        mvo_o = dout("mvo", [NMEM, 512])

        if do_sample:
            xsT_d = din("xsT", [D, NSTR * SS])
            xs_d = din("xs", [NSTR * SS, D])
            ckT_d = din("ckT", [NSTR, H, 128, PAST])
            cv_d = din("cv", [NSTR, PAST, 1024])
            cconv_d = din("cconv", [128, 4, NSTR, 2])
            cmkT_d = din("cmkT", [NSTR, MH, 128, NMEM])
            cmv_d = din("cmv", [NSTR, NMEM, 512])
            ys_o = dout("ys", [NSTR * SS, D])
            ksT_o = dout("ksT", [H, 128, NSTR * SS])
            vs_o = dout("vs", [NSTR * SS, 1024])
            convs_o = dout("convs", [128, NSTR, 4, 2])

        wsc = dscr("wsc", [len(WG) + 4, 128, NCH, 512], BF16)
        kts = dscr("kts", [NT, 128, H, 512], BF16)
        vxs = dscr("vxs", [NT, 128, H, 4 * 129], BF16)

        psS = top.enter_context(nc.psum_tensor("psS", [128, 4, 512], F32))
        psA = top.enter_context(nc.psum_tensor("psA", [128, 3, 512], F32))
        psT = top.enter_context(nc.psum_tensor("psT", [128, 512], F32))

        psTb = psT[:].bitcast(BF16)

        def bank(b):
            return psS[:, b, :] if b < 4 else psA[:, b - 4, :]

        ident = sb(top, "identb", [128, 128], BF16)
        tab = sb(top, "tab", [128, 256], F32)
        cfar = sb(top, "cfar", [128, 8], F32)
        biasT = sb(top, "biasT", [128, H, 2, 2, 128], BF16)
        gs4 = sb(top, "gs4", [128, 512], F32)
        cw = sb(top, "cw", [128, 12], F32)
        nlam = sb(top, "nlam", [128, 1], F32)
        epsb = sb(top, "epsb", [128, 1], F32)
        mkT = sb(top, "mkT", [128, MH, NMEM], BF16)
        mvx = sb(top, "mvx", [128, 2, MH, 129], BF16)

        SA = ExitStack()
        SA.__enter__()
        wkv = sb(SA, "wkv", [128, 4, NCH, 512], BF16)
        bkt = sb(SA, "bkt", [128, 256], F32)
        dmask = sb(SA, "dmask", [128, 128], F32)
        eqm = sb(SA, "eqm", [128, 128], F32)
        bacc = sb(SA, "bacc", [128, 128], F32)
        bhi = sb(SA, "bhi", [128, 128], F32)
        S0 = ExitStack()
        S0.__enter__()
        stg = [sb(S0, "stg%d" % i, [128, NCH, 512], F32) for i in range(2)]
        wbf = [sb(S0, "wbf%d" % i, [128, NCH, 512], BF16) for i in range(2)]
        identf = sb(S0, "identf", [128, 128], F32)
        lamv = sb(S0, "lamvs", [128, 256], F32)
        ltmp = sb(S0, "ltmp", [128, 64], F32)
        lsum = sb(S0, "lsum", [128, 4], F32)
        memb = sb(S0, "memb", [128, NCH, NMEM], BF16)
        mkf = sb(S0, "mkf", [128, MH, NMEM], F32)
        mvf = sb(S0, "mvf", [128, 2, 512], F32)

        P.dma("sp", identf[:], ident_d, key="c0")
        P.dma("sp", tab[:], tab_d, key="c0")
        P.dma("sp", lamv[:], lam_d, key="c0")
        P.dma("sp", gs4[:], gsub_d, key="c0")
        P.dma("sp", cw[:], cw_d, key="c0")
        P.dma("sp", bkt[:], bkt_d, key="c0")
        t_c0 = P.dma("sp", dmask[:], dmask_d, key="c0")
        t_idb = P.op("dve", lambda e: e.tensor_copy(out=ident[:], in_=identf[:]), [t_c0])
        P.op("dve", lambda e: e.memset(epsb[:], EPS), ())
        t_gs2 = P.op("dve", lambda e: e.tensor_scalar(out=gs4[:], in0=gs4[:], scalar1=1.0 - LAM_INIT,
                                                      scalar2=None, op0=ALU.mult), [t_c0])
        tl = t_c0
        for k in range(2):
            tl = P.op("dve", lambda e, k=k: e.tensor_tensor(out=ltmp[:], in0=lamv[:, 128 * k:128 * k + 64],
                                                            in1=lamv[:, 128 * k + 64:128 * k + 128], op=ALU.mult), [tl])
            tl = P.op("dve", lambda e, k=k: e.reduce_sum(out=lsum[:, k:k + 1], in_=ltmp[:], axis=AX.X), [tl])
        tl = P.op("act", lambda e: e.activation(out=lsum[:, 2:4], in_=lsum[:, 0:2], func=AF.Exp), [tl])
        tl = P.op("dve", lambda e: e.tensor_tensor(out=nlam[:], in0=lsum[:, 3:4], in1=lsum[:, 2:3], op=ALU.subtract), [tl])
        t_nlam = P.op("dve", lambda e: e.tensor_scalar(out=nlam[:], in0=nlam[:], scalar1=-LAM_INIT, scalar2=None,
                                                       op0=ALU.add), [tl])
        t_cf = P.op("dve", lambda e: e.tensor_copy(out=cfar[:], in_=tab[:, 120:128]), [t_c0])
        def wsrc(name, col):
            return w_srcs[name].rearrange("(c p) n -> p c n", p=128)[:, :, col:col + 512]

        jobs = [("kv", k, w_srcs["w_in"].rearrange("(c p) n -> p c n", p=128)[:, :, 1024 + 512 * k:1536 + 512 * k])
                for k in range(4)]
        jobs += [("mem", k, w_mem.rearrange("(c p) n -> p c n", p=128)[:, :, 512 * k:512 * k + 512]) for k in range(2)]
        stg_free = [None, None]
        wbf_free = [None, None]
        memw_tok = [None, None]
        for n, (kind, idx, src) in enumerate(jobs):
            bsel = n % 2
            t_ld = P.dma("sp", stg[bsel][:], src, [stg_free[bsel]], key="stg%d" % bsel)
            if kind == "kv":
                dst = wkv[:, idx]
            else:
                dst = wbf[bsel][:]
            if n % 2 == 0:
                t_c = P.op("dve", lambda e, dst=dst, bsel=bsel: e.tensor_copy(out=dst, in_=stg[bsel][:]),
                           [t_ld, wbf_free[bsel]])
            else:
                t_c = P.op("act", lambda e, dst=dst, bsel=bsel: e.activation(out=dst, in_=stg[bsel][:], func=AF.Copy),
                           [t_ld, wbf_free[bsel]])
            stg_free[bsel] = t_c
            if kind == "scr":
                wbf_free[bsel] = P.dma("act", wsc[idx], wbf[bsel][:], [t_c], key="wst%d" % bsel)
            elif kind == "kv":
                P.dma("act", wsc[len(WG) + idx], wkv[:, idx], [t_c], key="wstkv")
            elif kind == "mem":
                memw_tok[idx] = (t_c, bsel)

        memstg = stg[0][:, :, 0:NMEM]
        t_m = P.dma("sp", memstg, memT.rearrange("(c p) n -> p c n", p=128), [stg_free[0]], key="stg0")
        t_mb = P.op("dve", lambda e: e.tensor_copy(out=memb[:], in_=memstg), [t_m])
        bank_free = [None] * 7
        jb = 0
        wmk = wbf[memw_tok[0][1]]
        wmv = wbf[memw_tok[1][1]]
        for mh in range(MH):
            b = jb % 7
            jb += 1
            for c in range(NCH):
                t_pe = P.op("pe", lambda e, b=b, c=c, mh=mh: e.matmul(
                    bank(b)[:, 0:NMEM], lhsT=wmk[:, c, mh * 128:mh * 128 + 128], rhs=memb[:, c, :],
                    start=(c == 0), stop=(c == NCH - 1)),
                    [t_mb, memw_tok[0][0], bank_free[b], t_idb] if c == 0 else (), sig=("chain" if c == NCH - 1 else None))
            t1 = P.op("act", lambda e, b=b, mh=mh: e.activation(out=mkT[:, mh, :], in_=bank(b)[:, 0:NMEM], func=AF.Copy), [t_pe])
            t2 = P.op("dve", lambda e, b=b, mh=mh: e.tensor_copy(out=mkf[:, mh, :], in_=bank(b)[:, 0:NMEM]), [t_pe, t1])
            bank_free[b] = t2
        P.dma("sp", mkoT_o.rearrange("h p n -> p h n"), mkf[:], [t2], key="mko")
        for blk in range(2):
            b = jb % 7
            jb += 1
            for c in range(NCH):
                t_pe = P.op("pe", lambda e, b=b, c=c, blk=blk: e.matmul(
                    bank(b), lhsT=memb[:, c, blk * 128:blk * 128 + 128], rhs=wmv[:, c, :],
                    start=(c == 0), stop=(c == NCH - 1)),
                    [t_mb, memw_tok[1][0], bank_free[b]] if c == 0 else (), sig=("chain" if c == NCH - 1 else None))
            t1 = P.op("act", lambda e, b=b, blk=blk: e.activation(
                out=mvx[:, blk, :, 0:128], in_=bank(b).rearrange("p (h e) -> p h e", e=128), func=AF.Copy), [t_pe])
            t2 = P.op("dve", lambda e, b=b, blk=blk: e.tensor_copy(out=mvf[:, blk, :], in_=bank(b)), [t_pe, t1])
            bank_free[b] = t2
        for blk in range(2):
            t2 = P.op("dve", lambda e, blk=blk: e.memset(mvx[:, blk, :, 128:129], 1.0), [t2])
        P.dma("sp", mvo_o.rearrange("(b p) n -> p b n", p=128), mvf[:], [t2], key="mvo")
        P.emit()
        S0.__exit__(None, None, None)

        xstg = [sb(SA, "xstg%d" % i, [128, 4, TW], F32) for i in range(2)]
        xb = [sb(SA, "xb%d" % i, [128, NCH, TW], BF16) for i in range(2)]
        ktst = [sb(SA, "ktst%d" % i, [128, H, TW], BF16) for i in range(2)]
        vxst = [sb(SA, "vxst%d" % i, [128, H, 4, 129], BF16) for i in range(2)]
        vm = sb(SA, "vm", [128, NT * 32], F32)
        kof = [sb(SA, "kof%d" % i, [128, TW], F32) for i in range(2)]
        vof = [sb(SA, "vof%d" % i, [128, 512], F32) for i in range(2)]

        def bias_gen():
            tp = t_c0
            for h in range(H):
                for kind in range(2):
                    nb = range(32) if kind == 0 else range(1, 16)
                    first = True
                    for b in nb:
                        tp = P.op("dve", lambda e, b=b, kind=kind: e.tensor_single_scalar(
                            out=eqm[:], in_=bkt[:, 128 * kind:128 * kind + 128], scalar=float(b), op=ALU.is_equal), [tp])
                        if first:
                            tp = P.op("dve", lambda e, b=b, h=h: e.tensor_scalar(
                                out=bacc[:], in0=eqm[:], scalar1=tab[:, b * 8 + h:b * 8 + h + 1], scalar2=None,
                                op0=ALU.mult), [tp])
                            first = False
                        else:
                            tp = P.op("dve", lambda e, b=b, h=h: e.scalar_tensor_tensor(
                                out=bacc[:], in0=eqm[:], scalar=tab[:, b * 8 + h:b * 8 + h + 1], in1=bacc[:],
                                op0=ALU.mult, op1=ALU.add), [tp])
                        yield
                    tp = P.op("dve", lambda e, h=h: e.tensor_scalar(
                        out=bacc[:], in0=bacc[:], scalar1=tab[:, 120 + h:121 + h], scalar2=None, op0=ALU.subtract), [tp])
                    if kind == 0:
                        tp = P.op("dve", lambda e: e.tensor_tensor(out=bacc[:], in0=bacc[:], in1=dmask[:], op=ALU.add), [tp])
                    tp = P.op("dve", lambda e, h=h, kind=kind: e.tensor_copy(out=biasT[:, h, kind, 0, :], in_=bacc[:]), [tp])
                    tp = P.op("dve", lambda e, h=h, kind=kind: e.tensor_copy(out=bhi[:], in_=biasT[:, h, kind, 0, :]), [tp])
                    tp = P.op("dve", lambda e: e.tensor_tensor(out=bacc[:], in0=bacc[:], in1=bhi[:], op=ALU.subtract), [tp])
                    tp = P.op("dve", lambda e, h=h, kind=kind: e.tensor_copy(out=biasT[:, h, kind, 1, :], in_=bacc[:]), [tp])
                    yield
        bgen = bias_gen()
        wq_f = [sb(SA, "wqf%d" % i, [128, NCH, 128], F32) for i in range(2)]
        wq_b = [sb(SA, "wqb%d" % i, [128, NCH, 128], BF16) for i in range(2)]

        def wcast_gen():
            f_free = [None, None]
            b_free = [None, None]
            n = 0
            for gi in range(len(WG)):
                name, col = WG[gi]
                srcv = w_srcs[name].rearrange("(c p) n -> p c n", p=128)
                for qq in range(4):
                    k = n % 2
                    n += 1
                    t_ld = P.dma("sp", wq_f[k][:], srcv[:, :, col + qq * 128:col + qq * 128 + 128], [f_free[k]],
                                 key="wqf%d" % k)
                    t_c = P.op("act", lambda e, k=k: e.activation(out=wq_b[k][:], in_=wq_f[k][:], func=AF.Copy),
                               [t_ld, b_free[k]])
                    f_free[k] = t_c
                    b_free[k] = P.dma("act", wsc[gi, :, :, qq * 128:qq * 128 + 128], wq_b[k][:], [t_c], key="wqs%d" % k)
                    yield
        wgen = wcast_gen()
        t_vm = P.dma("sp", vm[:], vmask, key="vm")
        xTr = xT.rearrange("(c p) t -> p c t", p=128)
        xstg_free = [None] * 2
        xb_free = [None, None]
        kt_free = [None, None]
        vx_free = [None, None]
        kof_free = [None, None]
        vof_free = [None, None]
        nko = 0
        nvo = 0
        bank_free = [None] * 7
        jb = 0
        npiece = 0
        ntile = NT if nslot == NSLOT else 4 * (nslot - 1) + 4
        def load_x(T):
            nonlocal npiece
            xs_ = T % 2
            t_x = []
            for pc in range(4):
                st = npiece % 2
                npiece += 1
                t_ld = P.dma("sp", xstg[st][:], xTr[:, 4 * pc:4 * pc + 4, T * TW:(T + 1) * TW], [xstg_free[st]],
                             key="xstg%d" % st)
                t_c = P.op("dve", lambda e, st=st, pc=pc, xs_=xs_: e.tensor_copy(
                    out=xb[xs_][:, 4 * pc:4 * pc + 4, :], in_=xstg[st][:]), [t_ld, xb_free[xs_]])
                xstg_free[st] = t_c
                t_x.append(t_c)
            return t_x

        t_x_next = load_x(0)
        for T in range(ntile):
            xs_ = T % 2
            own = (T % 4 == 3) and (T // 4) < nslot
            slot = T // 4
            t_x = t_x_next
            if T + 1 < ntile:
                t_x_next = load_x(T + 1)
            t_kev = []
            for h in range(H):
                b = jb % 7
                jb += 1
                for c in range(NCH):
                    t_pe = P.op("pe", lambda e, b=b, c=c, h=h, xs_=xs_: e.matmul(
                        bank(b), lhsT=wkv[:, h // 4, c, (h % 4) * 128:(h % 4) * 128 + 128], rhs=xb[xs_][:, c, :],
                        start=(c == 0), stop=(c == NCH - 1)),
                        (t_x + [bank_free[b]]) if c == 0 else (), sig=("chain" if c == NCH - 1 else None))
                t1 = P.op("act", lambda e, b=b, h=h, xs_=xs_: e.activation(
                    out=ktst[xs_][:, h, :], in_=bank(b), func=AF.Copy), [t_pe, kt_free[xs_]])
                t_kev.append(t1)
                if own:
                    ks_ = nko % 2
                    nko += 1
                    t2 = P.op("dve", lambda e, b=b, ks_=ks_: e.tensor_copy(out=kof[ks_][:], in_=bank(b)),
                              [t_pe, t1, kof_free[ks_]])
                    bank_free[b] = t2
                    kof_free[ks_] = P.dma("act", koT_o[h, :, slot * TW:(slot + 1) * TW], kof[ks_][:], [t2],
                                          key="kof%d" % ks_)
                else:
                    bank_free[b] = t1
            kt_free[xs_] = P.dma("act", kts[T], ktst[xs_][:], t_kev, key="kst%d" % xs_)
            t_vev = []
            t_vm2 = P.op("pool", lambda e, xs_=xs_, T=T: e.tensor_copy(
                out=vxst[xs_][:, :, :, 128], in_=vm[:, T * 32:(T + 1) * 32].rearrange("p (h s) -> p h s", s=4)),
                [t_vm, vx_free[xs_]])
            t_vev.append(t_vm2)
            for s in range(4):
                for g in range(2):
                    b = jb % 7
                    jb += 1
                    for c in range(NCH):
                        t_pe = P.op("pe", lambda e, b=b, c=c, s=s, g=g, xs_=xs_: e.matmul(
                            bank(b), lhsT=xb[xs_][:, c, s * 128:s * 128 + 128], rhs=wkv[:, 2 + g, c, :],
                            start=(c == 0), stop=(c == NCH - 1)),
                            (t_x + [bank_free[b]]) if c == 0 else (), sig=("chain" if c == NCH - 1 else None))
                    t1 = P.op("dve", lambda e, b=b, s=s, g=g, xs_=xs_: e.tensor_copy(
                        out=vxst[xs_][:, 4 * g:4 * g + 4, s, 0:128], in_=bank(b).rearrange("p (h e) -> p h e", e=128)),
                        [t_pe, vx_free[xs_]])
                    t_vev.append(t1)
                    if own:
                        vs_ = nvo % 2
                        nvo += 1
                        t2 = P.op("act", lambda e, b=b, vs_=vs_: e.activation(
                            out=vof[vs_][:], in_=bank(b), func=AF.Copy), [t_pe, t1, vof_free[vs_]])
                        bank_free[b] = t2
                        vof_free[vs_] = P.dma("act", vo_o[slot * TW + s * 128:slot * TW + s * 128 + 128,
                                                         g * 512:g * 512 + 512], vof[vs_][:], [t2], key="vof%d" % vs_)
                    else:
                        bank_free[b] = t1
            for _ in range(30):
                if next(bgen, "done") == "done":
                    break
            for _ in range(2):
                if next(wgen, "done") == "done":
                    break
            xb_free[xs_] = t_pe
            vx_free[xs_] = P.dma("pool", vxs[T], vxst[xs_][:].rearrange("p h s e -> p h (s e)"), t_vev, key="vst%d" % xs_)
        for _ in bgen:
            pass
        for _ in wgen:
            pass
        P.emit()
        SA.__exit__(None, None, None)

        if do_sample:
            SS_ = ExitStack()
            SS_.__enter__()
            NTK = NSTR * SS
            wS = sb(SS_, "wS", [128, NCH, 512], BF16)
            xsf = sb(SS_, "xsf", [128, NCH, NTK], F32)
            xsb = sb(SS_, "xsb", [128, NCH, NTK], BF16)
            qTs = sb(SS_, "qTs", [128, H, NTK], BF16)
            kTs = sb(SS_, "kTs", [128, H, NTK], BF16)
            kTf = sb(SS_, "kTf", [128, H, NTK], F32)
            hTs = sb(SS_, "hTs", [128, 4, NTK], F32)
            CTs = sb(SS_, "CTs", [128, 4, NTK], F32)
            BTs = sb(SS_, "BTs", [128, 4, NTK], F32)
            gcTs = sb(SS_, "gcTs", [128, 4, NTK], F32)
            mqTs = sb(SS_, "mqTs", [128, MH, NTK], BF16)
            upad = sb(SS_, "upad", [128, 4, NSTR, SS + 2], F32)
            cvt = sb(SS_, "cvt", [128, NSTR, SS], F32)
            cvt2 = sb(SS_, "cvt2", [128, NSTR, SS], F32)
            zTcs = sb(SS_, "zTcs", [128, 4, NTK], BF16)
            convsb = sb(SS_, "convsb", [128, NSTR, 4, 2], F32)
            vfs = [sb(SS_, "vfs%d" % i, [SS, 512], F32) for i in range(2)]
            vnew = sb(SS_, "vnew", [SS, NSTR, H, 129], BF16)
            sgds = sb(SS_, "sgds", [SS, NSTR, 1024], F32)
            sgms = sb(SS_, "sgms", [SS, NSTR, 512], F32)
            ztoks = sb(SS_, "ztoks", [SS, NSTR, 1536], BF16)
            ckf = [sb(SS_, "ckf%d" % i, [128, 1024], F32) for i in range(2)]
            ckbr = [sb(SS_, "ckbr%d" % i, [128, 4, PAST], BF16) for i in range(3)]
            cvf = [sb(SS_, "cvf%d" % i, [128, 1024], F32) for i in range(2)]
            vxcr = [sb(SS_, "vxcr%d" % i, [128, 8, 4, 129], BF16) for i in range(3)]
            cmkf = sb(SS_, "cmkf", [128, MH, NMEM], F32)
            cmkb = sb(SS_, "cmkb", [128, MH, NMEM], BF16)
            cmvf = sb(SS_, "cmvf", [128, 2, 512], F32)
            cmvx = sb(SS_, "cmvx", [128, 2, MH, 129], BF16)
            pTs = sb(SS_, "pTs", [128, 2, 9, SS], BF16)
            pTm = sb(SS_, "pTm", [128, 2, SS], BF16)
            accS = sb(SS_, "accS", [SS, 2, ACCW], F32)
            oS2 = [sb(SS_, "oS2%d" % i, [SS, 128], F32) for i in range(2)]
            rstdS2 = [sb(SS_, "rstdS2%d" % i, [SS, 2], F32) for i in range(2)]
            acc_freeM = [None]
            tS = sb(SS_, "tS", [SS, 128], F32)
            sqS = sb(SS_, "sqS", [SS, 128], F32)
            recS = sb(SS_, "recS", [SS, 2], F32)
            nr2S = sb(SS_, "nr2S", [SS, 1], F32)
            ssqS = sb(SS_, "ssqS", [SS, 1], F32)
            rstdS = sb(SS_, "rstdS", [SS, 1], F32)
            zTs = sb(SS_, "zTs", [128, NCH, NTK], BF16)
            ysbS = sb(SS_, "ysbS", [NTK, D], F32)
            xrS = [sb(SS_, "xrS%d" % i, [NTK, 512], F32) for i in range(2)]
            lnS = [sb(SS_, "lnS%d" % i, [NTK, 512], F32) for i in range(2)]
            bnS = sb(SS_, "bnS", [NTK, 4, 6], F32)
            mvS = sb(SS_, "mvS", [NTK, 2], F32)
            lnrS = sb(SS_, "lnrS", [NTK, 1], F32)

            t_xs = P.dma("sp", xsf[:], xsT_d.rearrange("(c p) n -> p c n", p=128), key="xsf")
            t_xs = P.op("dve", lambda e: e.tensor_copy(out=xsb[:], in_=xsf[:]), [t_xs])
            t_cc = P.dma("sp", upad[:, :, :, 0:2], cconv_d, key="cconv")
            bank_free = [None] * 7
            jbs = [0]
            wS_free = [None]

            def loadS(gi):
                return P.dma("sp", wS[:], wsc[gi], [wS_free[0]], key="wS")

            def jobS(mm, parts, n_out, evac, deps):
                b = jbs[0] % 7
                jbs[0] += 1
                for c in range(NCH):
                    t_pe = P.op("pe", lambda e, b=b, c=c: mm(e, bank(b)[0:parts, 0:n_out], c, c == 0, c == NCH - 1),
                                (list(deps) + [bank_free[b]]) if c == 0 else (),
                                sig=("chain" if c == NCH - 1 else None))
                t = evac(bank(b)[0:parts, 0:n_out], t_pe)
                bank_free[b] = t
                return t_pe, t

            KV0 = len(WG)
            t_fm = []
            fm_groups = [(G_Q0, "q", 0), (G_Q1, "q", 4), (KV0, "k", 0), (KV0 + 1, "k", 4), (G_H, "h", 0), (G_C, "c", 0),
                         (G_B, "b", 0), (G_GC, "gc", 0), (G_MQ, "mq", 0)]
            for (gi, kind, h0) in fm_groups:
                t_w = loadS(gi)
                for sub in range(4):
                    def mm(e, out, c, st_, sp_, sub=sub):
                        return e.matmul(out, lhsT=wS[:, c, sub * 128:sub * 128 + 128], rhs=xsb[:, c, :], start=st_, stop=sp_)
                    if kind == "q":
                        def ev(bk, tpe, hh=h0 + sub):
                            return P.op("act", lambda e: e.mul(out=qTs[:, hh, :], in_=bk, mul=0.125), [tpe])
                    elif kind == "k":
                        def ev(bk, tpe, hh=h0 + sub):
                            t1 = P.op("act", lambda e: e.activation(out=kTs[:, hh, :], in_=bk, func=AF.Copy), [tpe])
                            return P.op("dve", lambda e: e.tensor_copy(out=kTf[:, hh, :], in_=bk), [tpe, t1])
                    elif kind == "mq":
                        def ev(bk, tpe, sub=sub):
                            return P.op("act", lambda e: e.mul(out=mqTs[:, sub, :], in_=bk, mul=128.0 ** -0.5), [tpe])
                    elif kind == "gc":
                        def ev(bk, tpe, sub=sub):
                            return P.op("act", lambda e: e.activation(out=gcTs[:, sub, :], in_=bk, func=AF.Silu), [tpe])
                    else:
                        tgt = {"h": hTs, "c": CTs, "b": BTs}[kind]

                        def ev(bk, tpe, sub=sub, tgt=tgt):
                            return P.op("dve", lambda e: e.tensor_copy(out=tgt[:, sub, :], in_=bk), [tpe])
                    last_pe, t_e = jobS(mm, 128, NTK, ev, [t_xs, t_w])
                    t_fm.append(t_e)
                wS_free[0] = last_pe
            P.dma("sp", ksT_o.rearrange("h p n -> p h n"), kTf[:], t_fm, key="ksT")
            t_tm = []
            nvf = 0
            vf_free = [None, None]
            for (gi, kind, g) in [(KV0 + 2, "v", 0), (KV0 + 3, "v", 1), (G_GD0, "gd", 0), (G_GD1, "gd", 1), (G_GM, "gm", 0)]:
                t_w = loadS(gi)
                for st in range(NSTR):
                    def mm(e, out, c, st_, sp_, st=st):
                        return e.matmul(out, lhsT=xsb[:, c, st * SS:(st + 1) * SS], rhs=wS[:, c, :], start=st_, stop=sp_)
                    if kind == "v":
                        vsel = nvf % 2
                        nvf += 1

                        def ev(bk, tpe, st=st, g=g, vsel=vsel):
                            t1 = P.op("act", lambda e: e.activation(
                                out=vnew[:, st, 4 * g:4 * g + 4, 0:128], in_=bk.rearrange("p (h e) -> p h e", e=128),
                                func=AF.Copy), [tpe])
                            t2 = P.op("dve", lambda e: e.tensor_copy(out=vfs[vsel][:], in_=bk), [tpe, t1, vf_free[vsel]])
                            vf_free[vsel] = P.dma("sp", vs_o[st * SS:(st + 1) * SS, g * 512:g * 512 + 512], vfs[vsel][:], [t2],
                                                  key="vfs%d" % vsel)
                            return t2
                    elif kind == "gd":
                        def ev(bk, tpe, st=st, g=g):
                            t1 = P.op("act", lambda e: e.activation(out=sgds[:, st, g * 512:g * 512 + 512], in_=bk,
                                                                    func=AF.Silu), [tpe])
                            return P.op("dve", lambda e: e.tensor_tensor(out=sgds[:, st, g * 512:g * 512 + 512],
                                                                         in0=sgds[:, st, g * 512:g * 512 + 512],
                                                                         in1=gs4[0:SS, :], op=ALU.mult), [t1])
                    else:
                        def ev(bk, tpe, st=st):
                            return P.op("act", lambda e: e.activation(out=sgms[:, st, :], in_=bk, func=AF.Silu), [tpe])
                    last_pe, t_e = jobS(mm, SS, 512, ev, [t_xs, t_w])
                    t_tm.append(t_e)
                wS_free[0] = last_pe
            t_on = P.op("dve", lambda e: e.memset(vnew[:, :, :, 128:129].rearrange("p s h e -> p (s h e)"), 1.0), t_tm)
            t_tm.append(t_on)
            t_wo = loadS(G_O0)
            t = None
            t_cvs = []
            for cc in range(4):
                t = P.op("pool", lambda e, cc=cc: e.tensor_tensor(
                    out=upad[:, cc, :, 2:SS + 2], in0=CTs[:, cc, :].rearrange("p (s t) -> p s t", t=SS),
                    in1=hTs[:, cc, :].rearrange("p (s t) -> p s t", t=SS), op=ALU.mult), t_fm + [t_cc, t])
                t_u = P.op("pool", lambda e, cc=cc: e.tensor_copy(out=convsb[:, :, cc, :], in_=upad[:, cc, :, SS:SS + 2]), [t])
                t_cvs.append(t_u)
                t = P.op("pool", lambda e, cc=cc: e.tensor_scalar(out=cvt[:], in0=upad[:, cc, :, 0:SS],
                                                                  scalar1=cw[:, cc * 3:cc * 3 + 1], scalar2=None,
                                                                  op0=ALU.mult), [t])
                for jx in (1, 2):
                    t = P.op("pool", lambda e, cc=cc, jx=jx: e.tensor_scalar(
                        out=cvt2[:], in0=upad[:, cc, :, jx:jx + SS], scalar1=cw[:, cc * 3 + jx:cc * 3 + jx + 1],
                        scalar2=None, op0=ALU.mult), [t])
                    t = P.op("pool", lambda e: e.tensor_tensor(out=cvt[:], in0=cvt[:], in1=cvt2[:], op=ALU.add), [t])
                t = P.op("pool", lambda e, cc=cc: e.tensor_tensor(
                    out=cvt[:], in0=cvt[:], in1=BTs[:, cc, :].rearrange("p (s t) -> p s t", t=SS), op=ALU.mult), [t])
                t = P.op("pool", lambda e, cc=cc: e.tensor_tensor(
                    out=zTcs[:, cc, :].rearrange("p (s t) -> p s t", t=SS), in0=cvt[:],
                    in1=gcTs[:, cc, :].rearrange("p (s t) -> p s t", t=SS), op=ALU.mult), [t])
                t_cvs.append(t)
            P.dma("pool", convs_o, convsb[:], t_cvs, key="convs")

            t_epS = [None]
            NU = NSTR * 2
            RU = 3
            ck_free = [None] * RU
            vx_free = [None] * RU
            cm_free = [None]
            nck = [0]
            ncv = [0]
            ckf_free = [None, None]
            cvf_free = [None, None]
            sb_free = {}
            acc_freeS = [None, None]
            ucache = {}
            t_ones = [P.op("dve", lambda e, r=r: e.memset(vxcr[r][:, :, :, 128:129].rearrange("p b h e -> p (b h e)"), 1.0), ())
                      for r in range(RU)]

            def load_unit(u):
                st, hh = divmod(u, 2)
                r = u % RU
                t_ck = []
                for h4 in range(4):
                    k_ = nck[0] % 2
                    nck[0] += 1
                    t_ld = P.dma("sp", ckf[k_][:], ckT_d[st, hh * 4 + h4], [ckf_free[k_]], key="ckf%d" % k_)
                    t_c = P.op("act", lambda e, k_=k_, h4=h4, r=r: e.activation(out=ckbr[r][:, h4, :], in_=ckf[k_][:],
                                                                             func=AF.Copy), [t_ld, ck_free[r]])
                    ckf_free[k_] = t_c
                    t_ck.append(t_c)
                t_cv_ = [t_ones[r]]
                for blk in range(8):
                    k_ = ncv[0] % 2
                    ncv[0] += 1
                    t_ld = P.dma("sp", cvf[k_][:, 0:512], cv_d[st, blk * 128:(blk + 1) * 128, hh * 512:(hh + 1) * 512],
                                 [cvf_free[k_]], key="cvf%d" % k_)
                    t_c = P.op("dve", lambda e, k_=k_, blk=blk, r=r: e.tensor_copy(
                        out=vxcr[r][:, blk, :, 0:128], in_=cvf[k_][:, 0:512].rearrange("p (h e) -> p h e", e=128)),
                        [t_ld, vx_free[r]])
                    cvf_free[k_] = t_c
                    t_cv_.append(t_c)
                ucache[u] = (t_ck, t_cv_)

            for u in range(min(2, NU)):
                load_unit(u)
            pend_final = [None]
            tp_prev = [None]
            for u in range(NU):
                st, hh = divmod(u, 2)
                r = u % RU
                if u + 2 < NU:
                    load_unit(u + 2)
                t_ck, t_cv_ = ucache[u]
                if hh == 0:
                    t_l1 = P.dma("sp", cmkf[:], cmkT_d[st].rearrange("h p n -> p h n"), [cm_free[0]], key="cmk")
                    t_l2 = P.dma("sp", cmvf[:], cmv_d[st].rearrange("(b p) n -> p b n", p=128), [cm_free[0]], key="cmv")
                    t_m1 = P.op("pool", lambda e: e.tensor_copy(out=cmkb[:], in_=cmkf[:]), [t_l1, cm_free[0]])
                    t_m2 = P.op("pool", lambda e: e.tensor_copy(
                        out=cmvx[:, :, :, 0:128], in_=cmvf[:].rearrange("p b (h e) -> p b h e", e=128)), [t_l2, cm_free[0]])
                    t_m3 = P.op("pool", lambda e: e.memset(cmvx[:, :, :, 128:129].rearrange("p b h e -> p (b h e)"), 1.0), [t_m2])
                    t_mem = [t_m1, t_m2, t_m3]
                qs = slice(st * SS, (st + 1) * SS)
                for h4 in range(4):
                    h = hh * 4 + h4
                    cnt = u * 4 + h4
                    sbk = cnt % 2
                    abk = cnt % 2
                    Sv = psS[:, sbk, 0:2 * 9 * SS].rearrange("p (m k q) -> p m k q", m=2, k=9)
                    for m in range(2):
                        for kb in range(8):
                            tq = P.op("pe", lambda e, m=m, kb=kb, h4=h4, h=h, Sv=Sv, qs=qs, r=r: e.matmul(
                                Sv[:, m, kb, :], lhsT=ckbr[r][64 * m:64 * m + 64, h4, kb * 128:kb * 128 + 128],
                                rhs=qTs[64 * m:64 * m + 64, h, qs], start=(m == 0 and kb == 0), stop=False,
                                skip_group_check=True),
                                (t_ck + t_fm + t_tm + [sb_free.get(sbk)]) if (m == 0 and kb == 0) else (), sig=None)
                        tq = P.op("pe", lambda e, m=m, h=h, Sv=Sv, qs=qs: e.matmul(
                            Sv[0:SS, m, 8, :], lhsT=kTs[64 * m:64 * m + 64, h, qs], rhs=qTs[64 * m:64 * m + 64, h, qs],
                            start=False, stop=False, skip_group_check=True), (), sig=None)
                        for part in range(2):
                            tq = P.op("pe", lambda e, m=m, h=h, part=part, Sv=Sv: e.matmul(
                                Sv[:, m, 7, :], lhsT=ident[:], rhs=biasT[:, h, 1, part, 0:SS], start=False, stop=False,
                                skip_group_check=True), (), sig=None)
                            tq = P.op("pe", lambda e, m=m, h=h, part=part, Sv=Sv: e.matmul(
                                Sv[0:SS, m, 8, :], lhsT=ident[0:SS, 0:SS], rhs=biasT[0:SS, h, 0, part, 0:SS], start=False,
                                stop=True, skip_group_check=True), (), sig=("chain" if (m == 1 and part == 1) else None))
                    te1 = P.op("act", lambda e, h=h, Sv=Sv: e.activation(
                        out=pTs[:, :, 0:8, :], in_=Sv[:, :, 0:8, :], func=AF.Exp, bias=cfar[:, h:h + 1], scale=1.0),
                        [tq, tp_prev[0]])
                    te2 = P.op("act", lambda e, h=h, Sv=Sv: e.activation(
                        out=pTs[0:SS, :, 8, :], in_=Sv[0:SS, :, 8, :], func=AF.Exp, bias=cfar[0:SS, h:h + 1], scale=1.0), [tq])
                    sb_free[sbk] = te2
                    for m in range(2):
                        for kb in range(8):
                            tp_ = P.op("pe", lambda e, m=m, kb=kb, h4=h4, r=r, abk=abk: e.matmul(
                                psA[0:SS, abk, m * ACCW:m * ACCW + 129], lhsT=pTs[:, m, kb, :], rhs=vxcr[r][:, kb, h4, :],
                                start=(m == 0 and kb == 0), stop=False, skip_group_check=True),
                                ([te1, te2, acc_freeS[abk]] + t_cv_) if (m == 0 and kb == 0) else (), sig=None)
                        tp_ = P.op("pe", lambda e, m=m, h=h, st=st, abk=abk: e.matmul(
                            psA[0:SS, abk, m * ACCW:m * ACCW + 129], lhsT=pTs[0:SS, m, 8, :], rhs=vnew[:, st, h, :],
                            start=False, stop=True, skip_group_check=True), (), sig=("chain" if m == 1 else None))
                    tp_prev[0] = tp_
                    if pend_final[0] is not None:
                        pend_final[0]()
                        pend_final[0] = None
                    t = P.op("dve", lambda e, abk=abk: e.tensor_copy(
                        out=accS[:], in_=psA[0:SS, abk, 0:2 * ACCW].rearrange("p (a w) -> p a w", w=ACCW)), [tp_, t_epS[0]])
                    acc_freeS[abk] = t
                    t = P.op("dve", lambda e: e.reciprocal(out=recS[:], in_=accS[:, :, 128]), [t])
                    t = P.op("dve", lambda e: e.tensor_scalar(out=nr2S[:], in0=recS[:, 1:2], scalar1=nlam[0:SS, 0:1],
                                                              scalar2=None, op0=ALU.mult), [t, t_nlam])
                    t = P.op("dve", lambda e: e.tensor_scalar(out=tS[:], in0=accS[:, 1, 0:128], scalar1=nr2S[:, 0:1],
                                                              scalar2=None, op0=ALU.mult), [t])
                    oSel = oS2[cnt % 2]
                    rSel = rstdS2[cnt % 2]
                    t = P.op("dve", lambda e, oSel=oSel: e.scalar_tensor_tensor(out=oSel[:], in0=accS[:, 0, 0:128], scalar=recS[:, 0:1],
                                                                                in1=tS[:], op0=ALU.mult, op1=ALU.add), [t])
                    t = P.op("dve", lambda e, oSel=oSel: e.tensor_tensor(out=sqS[:], in0=oSel[:], in1=oSel[:], op=ALU.mult), [t])
                    t = P.op("dve", lambda e, rSel=rSel: e.reduce_sum(out=rSel[:, 0:1], in_=sqS[:], axis=AX.X), [t])
                    t_epS[0] = t
                    t = P.op("act", lambda e, rSel=rSel: e.activation(out=rSel[:, 1:2], in_=rSel[:, 0:1], func=AF.Ln,
                                                                      bias=epsb[0:SS, 0:1], scale=1.0 / 128.0), [t])
                    t = P.op("act", lambda e, rSel=rSel: e.activation(out=rSel[:, 1:2], in_=rSel[:, 1:2], func=AF.Exp, scale=-0.5), [t])

                    def fin(st=st, h=h, oSel=oSel, rSel=rSel, t=t):
                        t_epS[0] = P.op("dve", lambda e: e.scalar_tensor_tensor(
                            out=ztoks[:, st, h * 128:h * 128 + 128], in0=oSel[:], scalar=rSel[:, 1:2],
                            in1=sgds[:, st, h * 128:h * 128 + 128], op0=ALU.mult, op1=ALU.mult), [t, t_epS[0]])
                    pend_final[0] = fin
                ck_free[r] = tq
                vx_free[r] = tp_
                if hh == 0:
                    continue
                if pend_final[0] is not None:
                    pend_final[0]()
                    pend_final[0] = None
                for mh in range(MH):
                    Sm = psS[:, 2, 0:2 * SS].rearrange("p (k q) -> p k q", k=2)
                    for blk in range(2):
                        tq = P.op("pe", lambda e, mh=mh, blk=blk, Sm=Sm, qs=qs: e.matmul(
                            Sm[:, blk, :], lhsT=cmkb[:, mh, blk * 128:blk * 128 + 128], rhs=mqTs[:, mh, qs],
                            start=(blk == 0), stop=(blk == 1), skip_group_check=True),
                            (t_mem + [sb_free.get(2)]) if blk == 0 else (), sig=("chain" if blk == 1 else None))
                    te = P.op("act", lambda e, Sm=Sm: e.activation(out=pTm[:], in_=Sm, func=AF.Exp), [tq, tp_prev[0]])
                    sb_free[2] = te
                    for blk in range(2):
                        tp_ = P.op("pe", lambda e, mh=mh, blk=blk: e.matmul(
                            psA[0:SS, 2, 0:129], lhsT=pTm[:, blk, :], rhs=cmvx[:, blk, mh, :],
                            start=(blk == 0), stop=(blk == 1), skip_group_check=True),
                            [te, acc_freeM[0]] if blk == 0 else (), sig=("chain" if blk == 1 else None))
                    tp_prev[0] = tp_
                    t = P.op("dve", lambda e: e.tensor_copy(out=accS[:, 0, :], in_=psA[0:SS, 2, 0:ACCW]), [tp_, t_epS[0]])
                    acc_freeM[0] = t
                    t = P.op("dve", lambda e: e.reciprocal(out=recS[:, 0:1], in_=accS[:, 0, 128:129]), [t])
                    t = P.op("dve", lambda e, st=st, mh=mh: e.scalar_tensor_tensor(
                        out=ztoks[:, st, 1024 + mh * 128:1024 + mh * 128 + 128], in0=accS[:, 0, 0:128], scalar=recS[:, 0:1],
                        in1=sgms[:, st, mh * 128:mh * 128 + 128], op0=ALU.mult, op1=ALU.mult), [t])
                    t_epS[0] = t
                cm_free[0] = tp_
            t_zs = []
            tpf = [None]
            for st in range(NSTR):
                for k in range(12):
                    tt = P.op("pe", lambda e, st=st, k=k: e.transpose(
                        psTb[:, k * SS:(k + 1) * SS], ztoks[:, st, k * 128:(k + 1) * 128], ident[0:SS, 0:SS]),
                        [t_epS[0], tpf[0]] if k == 0 else (), sig=("chain" if k == 11 else None))
                t1 = P.op("dve", lambda e, st=st: e.tensor_copy(
                    out=zTs[:, 0:8, st * SS:(st + 1) * SS], in_=psTb[:, 0:8 * SS].rearrange("p (k t) -> p k t", t=SS)), [tt])
                t2 = P.op("dve", lambda e, st=st: e.tensor_copy(
                    out=zTs[:, 12:16, st * SS:(st + 1) * SS], in_=psTb[:, 8 * SS:12 * SS].rearrange("p (k t) -> p k t", t=SS)), [tt, t1])
                tpf[0] = t2
                t_zs += [t1, t2]
            t_zs.append(P.op("pool", lambda e: e.tensor_copy(out=zTs[:, 8:12, :], in_=zTcs[:]), t_cvs))
            t_ys = []
            xr_free = [None, None]
            for g in range(4):
                t_w = t_wo if g == 0 else loadS(G_O0 + g)
                t_xr = P.dma("sp", xrS[g % 2][:], xs_d[:, g * 512:(g + 1) * 512], [xr_free[g % 2]], key="xrS%d" % (g % 2))

                def mm(e, out, c, st_, sp_):
                    return e.matmul(out, lhsT=zTs[:, c, :], rhs=wS[:, c, :], start=st_, stop=sp_)

                def ev(bk, tpe, g=g, t_xr=t_xr):
                    return P.op("dve", lambda e: e.scalar_tensor_tensor(
                        out=ysbS[:, g * 512:(g + 1) * 512], in0=xrS[g % 2][:], scalar=ALPHA, in1=bk, op0=ALU.mult,
                        op1=ALU.add), [tpe, t_xr])
                last_pe, t_e = jobS(mm, NTK, 512, ev, t_zs + [t_w])
                xr_free[g % 2] = t_e
                wS_free[0] = last_pe
                t_ys.append(t_e)
            t = None
            for g in range(4):
                t = P.op("dve", lambda e, g=g: e.bn_stats(out=bnS[:, g, :], in_=ysbS[:, g * 512:(g + 1) * 512]), t_ys + [t])
            t = P.op("dve", lambda e: e.bn_aggr(out=mvS[:], in_=bnS[:].rearrange("p a b -> p (a b)")), [t])
            t = P.op("act", lambda e: e.activation(out=lnrS[:], in_=mvS[:, 1:2], func=AF.Ln, bias=epsb[0:NTK, 0:1], scale=1.0), [t])
            t = P.op("act", lambda e: e.activation(out=lnrS[:], in_=lnrS[:], func=AF.Exp, scale=-0.5), [t])
            t = P.op("dve", lambda e: e.tensor_scalar(out=ysbS[:], in0=ysbS[:], scalar1=mvS[:, 0:1], scalar2=lnrS[:, 0:1],
                                                      op0=ALU.subtract, op1=ALU.mult), [t])
            ln_free = [None, None]
            for g in range(4):
                tg = P.dma("sp", lnS[0][:], lng_d[0:NTK, g * 512:(g + 1) * 512], [ln_free[0]], key="lnS0")
                tb = P.dma("sp", lnS[1][:], lnb_d[0:NTK, g * 512:(g + 1) * 512], [ln_free[1]], key="lnS1")
                t = P.op("dve", lambda e, g=g: e.tensor_tensor(out=ysbS[:, g * 512:(g + 1) * 512],
                                                               in0=ysbS[:, g * 512:(g + 1) * 512], in1=lnS[0][:], op=ALU.mult), [t, tg])
                ln_free[0] = t
                t = P.op("dve", lambda e, g=g: e.tensor_tensor(out=ysbS[:, g * 512:(g + 1) * 512],
                                                               in0=ysbS[:, g * 512:(g + 1) * 512], in1=lnS[1][:], op=ALU.add), [t, tb])
                ln_free[1] = t
            P.dma("sp", ys_o, ysbS[:], [t], key="ysS")
            P.emit()
            SS_.__exit__(None, None, None)

        SB = ExitStack()
        SB.__enter__()
        wring = [sb(SB, "wring%d" % i, [128, NCH, 512], BF16) for i in range(2)]
        xz = sb(SB, "xz", [128, NCH, TW + 2], BF16)
        stgB = sb(SB, "stgB", [128, 2, 2 * (TW + 2)], F32)
        qT = sb(SB, "qT", [128, H, TW], BF16)
        mqT = sb(SB, "mqT", [128, MH, TW], BF16)
        big = sb(SB, "big", [128, 8208], F32)
        sgd = sb(SB, "sgd", [128, 4, 1024], F32)
        sgm = sb(SB, "sgm", [128, 4, 512], F32)
        ztok = sb(SB, "ztok", [128, 4, 1536], BF16)
        zTc = sb(SB, "zTc", [128, 4, TW], BF16)
        kring = sb(SB, "kring", [128, RK, TW], BF16)
        vring = sb(SB, "vring", [128, RK, 4, 129], BF16)
        pring = sb(SB, "pring", [128, RP, 2, TW], BF16)
        accs = sb(SB, "accs", [128, 8, ACCW], F32)
        otmp = sb(SB, "otmp", [128, 4, 128], F32)
        ttmp = sb(SB, "ttmp", [128, 128], F32)
        sqtmp = sb(SB, "sqtmp", [128, 128], F32)
        rec = sb(SB, "rec", [128, 8], F32)
        nr2 = sb(SB, "nr2", [128, 4], F32)
        ssq = sb(SB, "ssq", [128, 4], F32)
        rstd = sb(SB, "rstd", [128, 4], F32)
        halo = sb(SB, "halo", [128, 8, 2], F32)
        ctmp = sb(SB, "ctmp", [128, TW], F32)
        ctmp2 = sb(SB, "ctmp2", [128, TW], F32)
        utmp = sb(SB, "utmp", [128, TW + 2], F32)
        xres = [sb(SB, "xres%d" % i, [128, 512], F32) for i in range(2)]
        bnst = sb(SB, "bnst", [128, 4, 6], F32)
        bnst4 = sb(SB, "bnst4", [128, 4, 4, 6], F32)
        mv4 = sb(SB, "mv4", [128, 4, 2], F32)
        lnr4 = sb(SB, "lnr4", [128, 4], F32)
        mv = sb(SB, "mv", [128, 2], F32)
        lnr = sb(SB, "lnr", [128, 1], F32)
        convu = sb(SB, "convu", [128, 4, 2], F32)

        hT = big[:, 0:4 * 514].rearrange("p (c t) -> p c t", t=514)
        CT = big[:, 2056:2 * 2056].rearrange("p (c t) -> p c t", t=514)
        BT = big[:, 4112:4112 + 2048].rearrange("p (c t) -> p c t", t=512)
        sgcT = big[:, 6160:6160 + 2048].rearrange("p (c t) -> p c t", t=512)
        ysb = big[:, 0:8192].rearrange("p (s n) -> p s n", n=2048)
        lngs = sb(SB, "lngs", [128, D], F32)
        lngb = lngs[:]
        t_lng = P.dma("sp", lngs[:], lng_d, key="lng")
        zT = sgd[:].rearrange("p s n -> p (s n)").bitcast(BF16).rearrange("p (c t) -> p c t", t=TW)

        lnb_sb = sb(SB, "lnb_sb", [128, D], F32)
        t_lnb = P.dma("sp", lnb_sb[:], lnb_d, key="lnb")

        sem_qk = P.sem("qk")
        sem_exp = P.sem("exp")
        sem_pv = P.sem("pv")

        bank_free = [None] * 7
        halo_free = [None]
        jbc = [0]
        wr_free = [None, None]
        wr_cnt = [0]

        def load_w(gi, after=()):
            r = wr_cnt[0] % 2
            wr_cnt[0] += 1
            t = P.dma("sp", wring[r][:], wsc[gi], [wr_free[r]] + list(after), key="w%d" % r)
            return r, t

        def job(mm, n_out, evac, deps):
            b = jbc[0] % 7
            jbc[0] += 1
            for c in range(NCH):
                t_pe = P.op("pe", lambda e, b=b, c=c: mm(e, bank(b)[:, 0:n_out], c, c == 0, c == NCH - 1),
                            (list(deps) + [bank_free[b]]) if c == 0 else (),
                            sig=("chain" if c == NCH - 1 else None))
            t = evac(bank(b)[:, 0:n_out], t_pe)
            bank_free[b] = t
            return t_pe, t

        xTr2 = xT.rearrange("(c p) t -> p c t", p=128)
        stgv = [stgB[:, k, :].rearrange("p (c t) -> p c t", t=TW + 2) for k in range(2)]
        prev_slot_done = []
        xz_free = []
        lng_used = []
        nblk = [0]
        ncidx = [0]
        kv_free = [None] * RK
        pv_tok = {}
        exp_tok = {}
        acc_free = [None]
        y_st_tok = [None]

        stg_free2 = [None, None]

        def load_xz(T):
            t_x = []
            for pc in range(8):
                k = pc % 2
                t_ld = P.dma("sp", stgv[k], xTr2[:, 2 * pc:2 * pc + 2, T * TW - 2:(T + 1) * TW],
                             [stg_free2[k]], key="xz%d" % k)
                t_c = P.op("dve", lambda e, k=k, pc=pc: e.tensor_copy(out=xz[:, 2 * pc:2 * pc + 2, :], in_=stgv[k]),
                           [t_ld] + xz_free)
                stg_free2[k] = t_c
                t_x.append(t_c)
            return t_x

        t_x_next = load_xz(3)
        for i in range(nslot):
            T = 4 * i + 3
            t_x = t_x_next
            nxt = load_w(G_Q0)
            order = [G_Q0, G_Q1, G_MQ, G_GD0, G_GD1, G_GM, G_H, G_C, G_B, G_GC]
            t_conv_in = {}
            t_q = []
            t_gate = []
            for oi, gi in enumerate(order):
                r, t_w = nxt
                last_pe = None
                if gi in (G_Q0, G_Q1, G_H, G_B, G_C, G_MQ, G_GC):
                    for sub in range(4):
                        def mm(e, out, c, st, sp_, r=r, sub=sub):
                            return e.matmul(out, lhsT=wring[r][:, c, sub * 128:sub * 128 + 128], rhs=xz[:, c, 2:TW + 2],
                                            start=st, stop=sp_)
                        if gi in (G_Q0, G_Q1):
                            hh = (gi - G_Q0) * 4 + sub

                            def ev(bk, tpe, hh=hh):
                                return P.op("act", lambda e: e.mul(out=qT[:, hh, :], in_=bk, mul=0.125), [tpe])
                        elif gi == G_MQ:
                            def ev(bk, tpe, sub=sub):
                                return P.op("act", lambda e: e.mul(out=mqT[:, sub, :], in_=bk, mul=128.0 ** -0.5), [tpe])
                        elif gi == G_H:
                            def ev(bk, tpe, sub=sub):
                                return P.op("dve", lambda e: e.tensor_copy(out=hT[:, sub, 2:TW + 2], in_=bk),
                                            [tpe] + prev_slot_done)
                        elif gi == G_C:
                            def ev(bk, tpe, sub=sub):
                                return P.op("dve", lambda e: e.tensor_copy(out=CT[:, sub, 2:TW + 2], in_=bk),
                                            [tpe] + prev_slot_done)
                        elif gi == G_B:
                            def ev(bk, tpe, sub=sub):
                                return P.op("dve", lambda e: e.tensor_copy(out=BT[:, sub, :], in_=bk),
                                            [tpe] + prev_slot_done)
                        else:
                            def ev(bk, tpe, sub=sub):
                                return P.op("act", lambda e: e.activation(out=sgcT[:, sub, :], in_=bk, func=AF.Silu),
                                            [tpe] + prev_slot_done)
                        last_pe, t_e = job(mm, TW, ev, t_x + [t_w])
                        if gi in (G_Q0, G_Q1, G_MQ):
                            t_q.append(t_e)
                        else:
                            t_conv_in[(gi, sub)] = t_e
                        if gi in (G_H, G_C):
                            hidx = (0 if gi == G_H else 4) + sub
                            for c in range(NCH):
                                last_pe = P.op("pe", lambda e, c=c, r=r, sub=sub, hidx=hidx: e.matmul(
                                    psT[:, 2 * hidx:2 * hidx + 2], lhsT=wring[r][:, c, sub * 128:sub * 128 + 128],
                                    rhs=xz[:, c, 0:2], start=(c == 0), stop=(c == NCH - 1)),
                                    [halo_free[0]] if c == 0 else (), sig=("chain" if c == NCH - 1 else None))
                            tgt = hT if gi == G_H else CT
                            halo_free[0] = P.op("dve", lambda e, tgt=tgt, sub=sub, hidx=hidx: e.tensor_copy(
                                out=tgt[:, sub, 0:2], in_=psT[:, 2 * hidx:2 * hidx + 2]), [last_pe] + prev_slot_done)
                            t_conv_in[(gi, sub, "halo")] = halo_free[0]
                else:
                    for s in range(4):
                        def mm(e, out, c, st, sp_, r=r, s=s):
                            return e.matmul(out, lhsT=xz[:, c, 2 + s * 128:2 + s * 128 + 128], rhs=wring[r][:, c, :],
                                            start=st, stop=sp_)
                        if gi in (G_GD0, G_GD1):
                            g = gi - G_GD0

                            def ev(bk, tpe, s=s, g=g):
                                t1 = P.op("act", lambda e: e.activation(out=sgd[:, s, g * 512:g * 512 + 512], in_=bk,
                                                                        func=AF.Silu), [tpe])
                                t2 = P.op("pool", lambda e: e.tensor_tensor(out=sgd[:, s, g * 512:g * 512 + 512],
                                                                            in0=sgd[:, s, g * 512:g * 512 + 512],
                                                                            in1=gs4[:], op=ALU.mult), [t1])
                                t_gate.append(t2)
                                return t1
                        else:
                            def ev(bk, tpe, s=s):
                                return P.op("act", lambda e: e.activation(out=sgm[:, s, :], in_=bk, func=AF.Silu),
                                            [tpe])
                        last_pe, t_e = job(mm, 512, ev, t_x + [t_w])
                        t_gate.append(t_e)
                wr_free[r] = last_pe
                if oi + 1 < len(order):
                    nxt = load_w(order[oi + 1])
            xz_free = [last_pe]
            if i + 1 < nslot:
                t_x_next = load_xz(4 * (i + 1) + 3)
            w_o = [load_w(G_O0), load_w(G_O0 + 1)]
            t_cv = []
            for cc in range(4):
                dep = [t_conv_in[(G_H, cc)], t_conv_in[(G_C, cc)], t_conv_in[(G_H, cc, "halo")],
                       t_conv_in[(G_C, cc, "halo")], t_conv_in[(G_B, cc)], t_conv_in[(G_GC, cc)]]
                t = P.op("pool", lambda e, cc=cc: e.tensor_tensor(out=utmp[:], in0=CT[:, cc, :], in1=hT[:, cc, :],
                                                                  op=ALU.mult), dep + t_cv[-1:])
                if i == nslot - 1:
                    t_u = P.op("pool", lambda e, cc=cc: e.tensor_copy(out=convu[:, cc, :], in_=utmp[:, TW:TW + 2]), [t])
                    t_cv.append(t_u)
                t = P.op("pool", lambda e, cc=cc: e.tensor_scalar(out=ctmp[:], in0=utmp[:, 0:TW],
                                                                  scalar1=cw[:, cc * 3:cc * 3 + 1], scalar2=None,
                                                                  op0=ALU.mult), [t])
                for jx in (1, 2):
                    t = P.op("pool", lambda e, cc=cc, jx=jx: e.tensor_scalar(
                        out=ctmp2[:], in0=utmp[:, jx:jx + TW], scalar1=cw[:, cc * 3 + jx:cc * 3 + jx + 1], scalar2=None,
                        op0=ALU.mult), [t])
                    t = P.op("pool", lambda e: e.tensor_tensor(out=ctmp[:], in0=ctmp[:], in1=ctmp2[:], op=ALU.add), [t])
                t = P.op("pool", lambda e, cc=cc: e.tensor_tensor(out=ctmp[:], in0=ctmp[:], in1=BT[:, cc, :], op=ALU.mult), [t])
                t = P.op("pool", lambda e, cc=cc: e.tensor_tensor(out=zTc[:, cc, :], in0=ctmp[:], in1=sgcT[:, cc, :],
                                                                  op=ALU.mult), [t])
                t_cv.append(t)
            if i == nslot - 1:
                t_cv.append(P.dma("pool", convo_o, convu[:], t_cv, key="convo"))

            t_q = t_q + t_gate + list(t_conv_in.values())
            blocks = []
            for h in range(H):
                for kc in range(T + 1):
                    for kbl in range(4):
                        blocks.append((h, kc, kbl))
            nb_total = len(blocks)
            t_ep = []
            kvslot = {}

            def issue_kv(h, kc):
                cidx = ncidx[0]
                ncidx[0] += 1
                sl = cidx % RK
                t1 = P.dma("sp", kring[:, sl, :], kts[kc, :, h, :], [kv_free[sl]], key="kv%d" % sl)
                t2 = P.dma("sp", vring[:, sl].rearrange("p s e -> p (s e)"), vxs[kc, :, h, :], [kv_free[sl]],
                           key="kv%d" % sl)
                kvslot[(h, kc)] = (sl, t2)

            chunk_list = [(h, kc) for h in range(H) for kc in range(T + 1)]
            kv_issued = [0]

            def ensure_kv(upto):
                while kv_issued[0] < min(upto, len(chunk_list)):
                    issue_kv(*chunk_list[kv_issued[0]])
                    kv_issued[0] += 1

            ensure_kv(RK - 1)

            def do_qk(bi):
                h, kc, kbl = blocks[bi]
                n = nblk[0] + bi
                buf = n % 2
                sl, t_kv = kvslot[(h, kc)]
                diag = (kc == T)
                a = kbl * 128 if diag else 0
                adds = []
                if diag:
                    adds.append((kbl, 0))
                    if kbl + 1 < 4:
                        adds.append((kbl + 1, 1))
                elif kc == T - 1 and kbl == 3:
                    adds.append((0, 1))
                deps = [t_kv] + t_q + (exp_tok.get(n - 2) and [exp_tok[n - 2]] or [])
                for m in range(2):
                    last = (m == 1 and not adds)
                    tq = P.op("pe", lambda e, m=m, buf=buf, sl=sl, kbl=kbl, a=a, h=h: e.matmul(
                        psS[:, buf * 2 + m, a:TW], lhsT=kring[64 * m:64 * m + 64, sl, kbl * 128:kbl * 128 + 128],
                        rhs=qT[64 * m:64 * m + 64, h, a:TW], start=True, stop=True, skip_group_check=True),
                        deps if m == 0 else (), sig=(sem_qk if last else None))
                for ai, (s_, kind) in enumerate(adds):
                    for m in range(2):
                        for part in range(2):
                            last = (ai == len(adds) - 1 and m == 1 and part == 1)
                            tq = P.op("pe", lambda e, m=m, buf=buf, s_=s_, kind=kind, part=part, h=h: e.matmul(
                                psS[:, buf * 2 + m, s_ * 128:s_ * 128 + 128], lhsT=ident[:],
                                rhs=biasT[:, h, kind, part, :], start=False, stop=True, skip_group_check=True),
                                (), sig=(sem_qk if last else None))
                return tq, a

            qk_info = {}
            deferred = []
            cur_bi = [0]

            def do_exp(bi):
                h, kc, kbl = blocks[bi]
                n = nblk[0] + bi
                tq, a = qk_info[bi]
                deps = [tq]
                if n - RP in pv_tok:
                    deps.append(pv_tok[n - RP])
                exp_tok[n] = P.op("act", lambda e, n=n, a=a, h=h: e.activation(
                    out=pring[:, n % RP, :, a:TW], in_=psS[:, (n % 2) * 2:(n % 2) * 2 + 2, a:TW], func=AF.Exp,
                    bias=cfar[:, h:h + 1], scale=1.0), deps, sig=sem_exp)

            def do_pv(bi):
                h, kc, kbl = blocks[bi]
                n = nblk[0] + bi
                sl, t_kv = kvslot[(h, kc)]
                diag = (kc == T)
                first = (kc == 0 and kbl == 0)
                lastb = (kc == T and kbl == 3)
                s0 = kbl if diag else 0
                ops = [(s, m) for s in range(s0, 4) for m in range(2)]
                for oi_, (s, m) in enumerate(ops):
                    a_ = s * 2 + m
                    deps = []
                    if oi_ == 0:
                        deps = [exp_tok[n]]
                        if first:
                            deps.append(acc_free[0])
                    tp_ = P.op("pe", lambda e, s=s, m=m, a_=a_, n=n, sl=sl, kbl=kbl, first=first, lastb=lastb: e.matmul(
                        psA[:, a_ // 3, (a_ % 3) * ACCW:(a_ % 3) * ACCW + 129],
                        lhsT=pring[:, n % RP, m, s * 128:s * 128 + 128], rhs=vring[:, sl, kbl, :],
                        start=(first and a_ % 3 == 0), stop=lastb, skip_group_check=True),
                        deps, sig=(sem_pv if oi_ == len(ops) - 1 else None))
                pv_tok[n] = tp_
                if kbl == 3:
                    kv_free[sl] = tp_
                    ensure_kv(kv_issued[0] + 1)
                if lastb:
                    epilogue(h, tp_)

            def epilogue(h, t_last):
                deps = [t_last] + t_gate
                tcs = []
                for bk_ in range(3):
                    na = 3 if bk_ < 2 else 2
                    tcs.append(P.op("dve", lambda e, bk_=bk_, na=na: e.tensor_copy(
                        out=accs[:, bk_ * 3:bk_ * 3 + na, :],
                        in_=psA[:, bk_, 0:na * ACCW].rearrange("p (a w) -> p a w", w=ACCW)), deps + t_ep[-1:]))
                acc_free[0] = tcs[-1]
                t = P.op("dve", lambda e: e.reciprocal(out=rec[:], in_=accs[:, :, 128]), tcs)
                t = P.op("dve", lambda e: e.tensor_scalar(out=nr2[:], in0=rec[:].rearrange("p (s m) -> p s m", m=2)[:, :, 1],
                                                          scalar1=nlam[:, 0:1], scalar2=None, op0=ALU.mult), [t, t_nlam])
                for s in range(4):
                    t = P.op("dve", lambda e, s=s: e.tensor_scalar(out=ttmp[:], in0=accs[:, 2 * s + 1, 0:128],
                                                                   scalar1=nr2[:, s:s + 1], scalar2=None, op0=ALU.mult), [t])
                    t = P.op("dve", lambda e, s=s: e.scalar_tensor_tensor(
                        out=otmp[:, s, :], in0=accs[:, 2 * s, 0:128], scalar=rec[:, 2 * s:2 * s + 1], in1=ttmp[:],
                        op0=ALU.mult, op1=ALU.add), [t])
                    t = P.op("dve", lambda e, s=s: e.tensor_tensor(out=sqtmp[:], in0=otmp[:, s, :], in1=otmp[:, s, :],
                                                                   op=ALU.mult), [t])
                    t = P.op("dve", lambda e, s=s: e.reduce_sum(out=ssq[:, s:s + 1], in_=sqtmp[:], axis=AX.X), [t])
                t_ssq = t

                def part2(h=h, t_ssq=t_ssq):
                    t = P.op("act", lambda e: e.activation(out=rstd[:], in_=ssq[:], func=AF.Ln, bias=epsb[:, 0:1],
                                                           scale=1.0 / 128.0), [t_ssq])
                    t = P.op("act", lambda e: e.activation(out=rstd[:], in_=rstd[:], func=AF.Exp, scale=-0.5), [t])
                    for s in range(4):
                        t = P.op("dve", lambda e, s=s, h=h: e.scalar_tensor_tensor(
                            out=ztok[:, s, h * 128:h * 128 + 128], in0=otmp[:, s, :], scalar=rstd[:, s:s + 1],
                            in1=sgd[:, s, h * 128:h * 128 + 128], op0=ALU.mult, op1=ALU.mult), [t])
                    t_ep.append(t)
                deferred.append([cur_bi[0] + 6, part2])

            for mh in range(MH):
                for blk in range(2):
                    n = nblk[0]
                    nblk[0] += 1
                    buf = n % 2
                    deps = t_q + ([exp_tok[n - 2]] if (n - 2) in exp_tok else [])
                    tq = P.op("pe", lambda e, buf=buf, mh=mh, blk=blk: e.matmul(
                        psS[:, buf * 2, :], lhsT=mkT[:, mh, blk * 128:blk * 128 + 128], rhs=mqT[:, mh, :],
                        start=True, stop=True, skip_group_check=True), deps, sig=sem_qk)
                    deps = [tq]
                    if n - RP in pv_tok:
                        deps.append(pv_tok[n - RP])
                    exp_tok[n] = P.op("act", lambda e, n=n, buf=buf: e.activation(
                        out=pring[:, n % RP, 0, :], in_=psS[:, buf * 2, :], func=AF.Exp), deps, sig=sem_exp)
                    for s in range(4):
                        deps = []
                        if s == 0:
                            deps = [exp_tok[n]]
                            if blk == 0:
                                deps.append(acc_free[0])
                        tp_ = P.op("pe", lambda e, s=s, n=n, mh=mh, blk=blk: e.matmul(
                            psA[:, s // 3, (s % 3) * ACCW:(s % 3) * ACCW + 129],
                            lhsT=pring[:, n % RP, 0, s * 128:s * 128 + 128], rhs=mvx[:, blk, mh, :],
                            start=(blk == 0 and s % 3 == 0), stop=(blk == 1), skip_group_check=True),
                            deps, sig=(sem_pv if s == 3 else None))
                    pv_tok[n] = tp_
                tcs = [P.op("dve", lambda e: e.tensor_copy(
                    out=accs[:, 0:3, :], in_=psA[:, 0, 0:3 * ACCW].rearrange("p (a w) -> p a w", w=ACCW)),
                    [tp_] + t_gate + t_ep[-1:]),
                    P.op("dve", lambda e: e.tensor_copy(
                        out=accs[:, 3:4, :], in_=psA[:, 1, 0:ACCW].rearrange("p (a w) -> p a w", w=ACCW)), [tp_])]
                acc_free[0] = tcs[-1]
                t = P.op("dve", lambda e: e.reciprocal(out=rec[:, 0:4], in_=accs[:, 0:4, 128]), tcs)
                for s in range(4):
                    t = P.op("dve", lambda e, s=s, mh=mh: e.scalar_tensor_tensor(
                        out=ztok[:, s, 1024 + mh * 128:1024 + mh * 128 + 128], in0=accs[:, s, 0:128],
                        scalar=rec[:, s:s + 1], in1=sgm[:, s, mh * 128:mh * 128 + 128], op0=ALU.mult, op1=ALU.mult), [t])
                t_ep.append(t)

            for b0 in range(min(2, nb_total)):
                qk_info[b0] = do_qk(b0)
                do_exp(b0)
            for bi in range(nb_total):
                cur_bi[0] = bi
                if bi + 2 < nb_total:
                    qk_info[bi + 2] = do_qk(bi + 2)
                    do_exp(bi + 2)
                while deferred and deferred[0][0] <= bi:
                    deferred.pop(0)[1]()
                do_pv(bi)
            while deferred:
                deferred.pop(0)[1]()
            nblk[0] += nb_total

            t_zt = []
            t_tp_free = [None]
            for s in range(4):
                for grp in range(3):
                    for k4 in range(4):
                        cidx_ = grp * 4 + k4
                        tt = P.op("pe", lambda e, s=s, cidx_=cidx_, k4=k4: e.transpose(
                            psTb[:, k4 * 128:k4 * 128 + 128], ztok[:, s, cidx_ * 128:cidx_ * 128 + 128], ident[:]),
                            (t_ep[-1:] + [t_tp_free[0]]) if k4 == 0 else (), sig=("chain" if k4 == 3 else None))
                    dst0 = grp * 4 if grp < 2 else 12
                    tcp = P.op("dve", lambda e, s=s, dst0=dst0: e.tensor_copy(
                        out=zT[:, dst0:dst0 + 4, s * 128:s * 128 + 128],
                        in_=psTb[:, 0:512].rearrange("p (k t) -> p k t", t=128)), [tt])
                    t_tp_free[0] = tcp
                    t_zt.append(tcp)
            t_zt.append(P.op("pool", lambda e: e.tensor_copy(out=zT[:, 8:12, 0:TW], in_=zTc[:]), t_cv + t_ep[-1:]))

            t_y = {}
            xr_free = [None, None]
            xcnt = 0
            for g in range(4):
                if g < 2:
                    r, t_w = w_o[g]
                else:
                    r, t_w = load_w(G_O0 + g)
                for s in range(4):
                    xsel = xcnt % 2
                    xcnt += 1
                    t_xr = P.dma("sp", xres[xsel][:], xown[i * TW + s * 128:i * TW + s * 128 + 128, g * 512:g * 512 + 512],
                                 [xr_free[xsel]], key="xr%d" % xsel)

                    def mm(e, out, c, st, sp_, r=r, s=s):
                        return e.matmul(out, lhsT=zT[:, c, s * 128:s * 128 + 128], rhs=wring[r][:, c, :], start=st, stop=sp_)

                    def ev(bk, tpe, s=s, g=g, xsel=xsel, t_xr=t_xr):
                        return P.op("dve", lambda e: e.scalar_tensor_tensor(
                            out=ysb[:, s, g * 512:g * 512 + 512], in0=xres[xsel][:], scalar=ALPHA, in1=bk,
                            op0=ALU.mult, op1=ALU.add), [tpe, t_xr, y_st_tok[0]] + t_cv)
                    last_pe, t_e = job(mm, 512, ev, t_zt + [t_w])
                    xr_free[xsel] = t_e
                    t_y[(s, g)] = t_e
                wr_free[r] = last_pe
            t_done = []
            t = None
            for s in range(4):
                for g in range(4):
                    t = P.op("dve", lambda e, s=s, g=g: e.bn_stats(out=bnst4[:, s, g, :], in_=ysb[:, s, g * 512:g * 512 + 512]),
                             [t_y[(s, g)], t])
                t = P.op("dve", lambda e, s=s: e.bn_aggr(out=mv4[:, s, :], in_=bnst4[:, s].rearrange("p a b -> p (a b)")), [t])
            t_r = P.op("act", lambda e: e.activation(out=lnr4[:], in_=mv4[:, :, 1], func=AF.Ln, bias=epsb[:, 0:1], scale=1.0), [t])
            t_r = P.op("act", lambda e: e.activation(out=lnr4[:], in_=lnr4[:], func=AF.Exp, scale=-0.5), [t_r])
            for s in range(4):
                t = P.op("dve", lambda e, s=s: e.scalar_tensor_tensor(out=ysb[:, s, :], in0=ysb[:, s, :], scalar=mv4[:, s, 0:1],
                                                                      in1=lngb, op0=ALU.subtract, op1=ALU.mult), [t, t_lng])
                t = P.op("dve", lambda e, s=s: e.scalar_tensor_tensor(out=ysb[:, s, :], in0=ysb[:, s, :], scalar=lnr4[:, s:s + 1],
                                                                      in1=lnb_sb[:], op0=ALU.mult, op1=ALU.add), [t, t_lnb, t_r])
                t_o = P.dma("pool", y_o[i * TW + s * 128:i * TW + s * 128 + 128, :], ysb[:, s, :], [t], key="yout")
                t_done.append(t_o)
            y_st_tok[0] = t_done[-1]
            prev_slot_done = t_done + [t]
        P.emit()
        P.op("dve", lambda e: e.memset(lnr[:], 0.0), ())
        P.emit()
        SB.__exit__(None, None, None)
    return nc


_CACHE = {}


def _host_consts():
    half, max_exact = 16, 8

    def bucket(rel):
        ret = np.where(rel > 0, half, 0)
        n = np.abs(rel)
        nf = np.maximum(n, 1).astype(np.float32)
        large = max_exact + (np.log(nf / max_exact) / math.log(128 / max_exact) * (half - max_exact)).astype(np.int32)
        large = np.minimum(large, half - 1)
        return ret + np.where(n < max_exact, n, large)

    k = np.arange(128)[:, None]
    q = np.arange(128)[None, :]
    bd = bucket(k - q)
    bp = bucket(k - q - 128)
    bkt = np.concatenate([bd, bp], axis=1).astype(np.float32)
    dmask = np.where((k // 64) <= (q // 64), 0.0, NEG).astype(np.float32)
    return bkt, dmask


def kernel(x_prompt, x_sample, cache_diff_k, cache_diff_v, cache_conv, cache_mem_k, cache_mem_v,
           mem_prompt, rel_bias_table, w_in, w_mem_kv, conv_w, lambda_q1, lambda_k1, lambda_q2,
           lambda_k2, subln_g, w_out, ln_g, ln_b, _nslot=NSLOT, _sample=True):
    f = np.float32
    x_prompt = np.asarray(x_prompt, f)
    B, S, _ = x_prompt.shape
    key = (_nslot, _sample)
    if key not in _CACHE:
        _CACHE[key] = build(_nslot, _sample)
    nc = _CACHE[key]

    bkt, dmask = _host_consts()
    rep = lambda v, n=128: np.ascontiguousarray(np.broadcast_to(np.asarray(v, f).reshape(1, -1), (n, np.asarray(v).size)))
    shared = {
        "w_in": np.ascontiguousarray(np.asarray(w_in, f)[0]),
        "w_out": np.ascontiguousarray(np.asarray(w_out, f)[0]),
        "w_mem": np.ascontiguousarray(np.asarray(w_mem_kv, f)[0]),
        "ident": np.eye(128, dtype=f),
        "tab": rep(np.asarray(rel_bias_table, f).reshape(-1)),
        "lamv": rep(np.concatenate([np.asarray(a, f).reshape(-1) for a in
                                    (lambda_q1, lambda_k1, lambda_q2, lambda_k2)])),
        "gsub": rep(np.tile(np.asarray(subln_g, f).reshape(-1), 4)),
        "lng": rep(np.asarray(ln_g, f).reshape(-1)),
        "lnb": rep(np.asarray(ln_b, f).reshape(-1)),
        "convw": np.ascontiguousarray(np.asarray(conv_w, f)[0].reshape(3, 4, 128).transpose(2, 1, 0).reshape(128, 12)),
        "bkt": bkt,
        "dmask": dmask,
    }
    in_maps = []
    xTb = [np.ascontiguousarray(x_prompt[b].T) for b in range(B)]
    memTb = [np.ascontiguousarray(np.asarray(mem_prompt, f)[b].T) for b in range(B)]
    for c in range(NCORES):
        b, j = c // 4, c % 4
        xT = np.zeros((D, NT * TW), f)
        xT[:, (3 - j) * TW:(3 - j) * TW + S] = xTb[b]
        xown = np.concatenate([x_prompt[b, (4 * i + j) * TW:(4 * i + j + 1) * TW] for i in range(NSLOT)], axis=0)
        vm = np.zeros((NT, 32), f)
        vm[3 - j:3 - j + S // TW] = 1.0
        m = dict(shared)
        m["xT"] = xT
        m["xown"] = np.ascontiguousarray(xown)
        m["vmask"] = rep(vm.reshape(-1))
        m["memT"] = memTb[b]
        in_maps.append(m)
    if _sample:
        xs_all = np.asarray(x_sample, f)
        ck = np.asarray(cache_diff_k, f)[0]
        cvv = np.asarray(cache_diff_v, f)[0]
        cc_ = np.asarray(cache_conv, f)[0]
        cmk = np.asarray(cache_mem_k, f)[0]
        cmv = np.asarray(cache_mem_v, f)[0]
        for c in range(NCORES):
            sl = slice(c * NSTR, (c + 1) * NSTR)
            xs_c = xs_all[sl].reshape(NSTR * SS, D)
            m = in_maps[c]
            m["xs"] = np.ascontiguousarray(xs_c)
            m["xsT"] = np.ascontiguousarray(xs_c.T)
            m["ckT"] = np.ascontiguousarray(ck[sl].transpose(0, 2, 3, 1))
            m["cv"] = np.ascontiguousarray(cvv[sl].reshape(NSTR, PAST, 1024))
            m["cconv"] = np.ascontiguousarray(cc_[sl].reshape(NSTR, 2, 4, 128).transpose(3, 2, 0, 1))
            m["cmkT"] = np.ascontiguousarray(cmk[sl].transpose(0, 2, 3, 1))
            m["cmv"] = np.ascontiguousarray(cmv[sl].reshape(NSTR, NMEM, 512))
    res = run_bass_kernel_spmd(nc, in_maps, core_ids=list(range(NCORES)))
    R = res.results

    y = np.zeros((B, S, D), f)
    kp = np.zeros((1, B, S, H, 128), f)
    vp = np.zeros((1, B, S, H, 128), f)
    for c in range(NCORES):
        b, j = c // 4, c % 4
        for i in range(NSLOT):
            t = 4 * i + j
            y[b, t * TW:(t + 1) * TW] = R[c]["y"][i * TW:(i + 1) * TW]
            kp[0, b, t * TW:(t + 1) * TW] = R[c]["koT"][:, :, i * TW:(i + 1) * TW].transpose(2, 0, 1)
            vp[0, b, t * TW:(t + 1) * TW] = R[c]["vo"][i * TW:(i + 1) * TW].reshape(TW, H, 128)
    convp = np.zeros((1, B, 2, 512), f)
    mkp = np.zeros((1, B, NMEM, MH, 128), f)
    mvp = np.zeros((1, B, NMEM, MH, 128), f)
    for b in range(B):
        c = 4 * b + 3
        convp[0, b] = R[c]["convo"].transpose(2, 1, 0).reshape(2, 512)
        mkp[0, b] = R[c]["mkoT"].transpose(2, 0, 1)
        mvp[0, b] = R[c]["mvo"].reshape(NMEM, MH, 128)
    ys = np.zeros((32, SS, D), f)
    ks = np.zeros((1, 32, SS, H, 128), f)
    vs = np.zeros((1, 32, SS, H, 128), f)
    cs = np.zeros((1, 32, 2, 512), f)
    if _sample:
        for c in range(NCORES):
            sl = slice(c * NSTR, (c + 1) * NSTR)
            ys[sl] = R[c]["ys"].reshape(NSTR, SS, D)
            ks[0, sl] = R[c]["ksT"].transpose(2, 0, 1).reshape(NSTR, SS, H, 128)
            vs[0, sl] = R[c]["vs"].reshape(NSTR, SS, H, 128)
            cs[0, sl] = R[c]["convs"].transpose(1, 3, 2, 0).reshape(NSTR, 2, 512)
    return (y, ys, kp, vp, convp, mkp, mvp, ks, vs, cs)
```

```python
import math
from contextlib import ExitStack

import numpy as np
import concourse.bass as bass
import concourse.mybir as mybir
from concourse.bass_utils import run_bass_kernel_spmd

F32 = mybir.dt.float32
BF16 = mybir.dt.bfloat16
AF = mybir.ActivationFunctionType
ALU = mybir.AluOpType
AX = mybir.AxisListType

NCORES = 8
D = 2048
TW = 512
NT = 35
NSLOT = 8
NCH = 16
H = 8
MH = 4
NMEM = 256
EPS = 1e-5
ALPHA = 2.0 ** 0.25
LAM_INIT = 0.8 - 0.6 * math.exp(-0.0)
ACCW = 130
RK = 4
RP = 3
NEG = -30000.0
SS = 16
NSTR = 4
PAST = 1024

WG = [("w_in", 0), ("w_in", 512),
      ("w_in", 3072), ("w_in", 3584), ("w_in", 4096),
      ("w_in", 4608),
      ("w_in", 5120), ("w_in", 5632),
      ("w_in", 6144), ("w_in", 6656),
      ("w_out", 0), ("w_out", 512), ("w_out", 1024), ("w_out", 1536)]
G_Q0, G_Q1, G_H, G_B, G_C, G_MQ, G_GD0, G_GD1, G_GC, G_GM, G_O0 = range(11)


class Sem:
    def __init__(self, h):
        self.h = h
        self.v = 0


class Prog:
    ENG = ("pe", "act", "dve", "pool", "sp")

    def __init__(self, nc, stack):
        self.nc = nc
        self.stack = stack
        self.sems = []
        self.q = {e: [] for e in self.ENG}
        self.chain = {e: self.sem("ch_" + e) for e in self.ENG}
        self.barrier_toks = {e: [] for e in self.ENG}
        self.waited = {e: {} for e in self.ENG}
        self.dsems = {}

    def sem(self, name):
        s = Sem(self.stack.enter_context(self.nc.semaphore(name)))
        self.sems.append(s)
        return s

    def op(self, eng, fn, after=(), sig="chain", amt=1):
        waits = [t for t in after if t is not None]
        if self.barrier_toks[eng]:
            waits = list(self.barrier_toks[eng]) + waits
            self.barrier_toks[eng] = []
        w2 = []
        wd = self.waited[eng]
        for (s, v) in waits:
            if wd.get(id(s), 0) >= v:
                continue
            wd[id(s)] = v
            w2.append((s, v))
        if sig == "chain":
            sig = self.chain[eng]
        tok = None
        if sig is not None:
            sig.v += amt
            tok = (sig, sig.v)
        self.q[eng].append((w2, fn, sig, amt))
        return tok

    def dma(self, eng, out, in_, after=(), key=None):
        assert key is not None
        if key not in self.dsems:
            self.dsems[key] = self.sem("d_" + key)
        return self.op(eng, lambda e: e.dma_start(out=out, in_=in_), after, self.dsems[key], 16)

    def emit(self):
        nc = self.nc
        q = self.q

        def run(e, lst):
            for (waits, fn, sig, amt) in lst:
                for (s, v) in waits:
                    e.wait_ge(s.h, v)
                inst = fn(e)
                if sig is not None:
                    inst.then_inc(sig.h, amt)

        with nc.Block() as block:
            if q["pe"]:
                @block.tensor
                def _(e):
                    run(e, q["pe"])
            if q["act"]:
                @block.scalar
                def _(e):
                    run(e, q["act"])
            if q["dve"]:
                @block.vector
                def _(e):
                    run(e, q["dve"])
            if q["pool"]:
                @block.gpsimd
                def _(e):
                    run(e, q["pool"])
            if q["sp"]:
                @block.sync
                def _(e):
                    run(e, q["sp"])
        self.q = {e: [] for e in self.ENG}
        toks = [(s, s.v) for s in self.sems if s.v > 0]
        for e in self.ENG:
            self.barrier_toks[e] = list(toks)


def build(nslot=NSLOT, do_sample=True):
    nc = bass.Bass("TRN2", target_bir_lowering=False)
    top = ExitStack()
    with top:
        P = Prog(nc, top)

        def din(name, shape, dt=F32):
            return nc.dram_tensor(name, list(shape), dt, kind="ExternalInput").ap()

        def dout(name, shape, dt=F32):
            return nc.dram_tensor(name, list(shape), dt, kind="ExternalOutput").ap()

        def dscr(name, shape, dt):
            return nc.dram_tensor(name, list(shape), dt).ap()

        def sb(stack, name, shape, dt):
            return stack.enter_context(nc.sbuf_tensor("s_" + name, list(shape), dt))

        xT = din("xT", [D, NT * TW])
        xown = din("xown", [NSLOT * TW, D])
        vmask = din("vmask", [128, NT * 32])
        w_srcs = {"w_in": din("w_in", [D, 7168]), "w_out": din("w_out", [D, D])}
        w_mem = din("w_mem", [D, 1024])
        memT = din("memT", [D, NMEM])
        ident_d = din("ident", [128, 128])
        tab_d = din("tab", [128, 256])
        lam_d = din("lamv", [128, 256])
        gsub_d = din("gsub", [128, 512])
        lng_d = din("lng", [128, D])
        lnb_d = din("lnb", [128, D])
        cw_d = din("convw", [128, 12])
        bkt_d = din("bkt", [128, 256])
        dmask_d = din("dmask", [128, 128])

        y_o = dout("y", [NSLOT * TW, D])
        koT_o = dout("koT", [H, 128, NSLOT * TW])
        vo_o = dout("vo", [NSLOT * TW, 1024])
        convo_o = dout("convo", [128, 4, 2])
        mkoT_o = dout("mkoT", [MH, 128, NMEM])
        mvo_o = dout("mvo", [NMEM, 512])

        if do_sample:
            xsT_d = din("xsT", [D, NSTR * SS])
            xs_d = din("xs", [NSTR * SS, D])
            ckT_d = din("ckT", [NSTR, H, 128, PAST])
            cv_d = din("cv", [NSTR, PAST, 1024])
            cconv_d = din("cconv", [128, 4, NSTR, 2])
            cmkT_d = din("cmkT", [NSTR, MH, 128, NMEM])
            cmv_d = din("cmv", [NSTR, NMEM, 512])
            ys_o = dout("ys", [NSTR * SS, D])
            ksT_o = dout("ksT", [H, 128, NSTR * SS])
            vs_o = dout("vs", [NSTR * SS, 1024])
            convs_o = dout("convs", [128, NSTR, 4, 2])

        wsc = dscr("wsc", [len(WG) + 4, 128, NCH, 512], BF16)
        kts = dscr("kts", [NT, 128, H, 512], BF16)
        vxs = dscr("vxs", [NT, 128, H, 4 * 129], BF16)

        psS = top.enter_context(nc.psum_tensor("psS", [128, 4, 512], F32))
        psA = top.enter_context(nc.psum_tensor("psA", [128, 3, 512], F32))
        psT = top.enter_context(nc.psum_tensor("psT", [128, 512], F32))

        psTb = psT[:].bitcast(BF16)

        def bank(b):
            return psS[:, b, :] if b < 4 else psA[:, b - 4, :]

        ident = sb(top, "identb", [128, 128], BF16)
        tab = sb(top, "tab", [128, 256], F32)
        cfar = sb(top, "cfar", [128, 8], F32)
        biasT = sb(top, "biasT", [128, H, 2, 2, 128], BF16)
        gs4 = sb(top, "gs4", [128, 512], F32)
        cw = sb(top, "cw", [128, 12], F32)
        nlam = sb(top, "nlam", [128, 1], F32)
        epsb = sb(top, "epsb", [128, 1], F32)
        mkT = sb(top, "mkT", [128, MH, NMEM], BF16)
        mvx = sb(top, "mvx", [128, 2, MH, 129], BF16)

        SA = ExitStack()
        SA.__enter__()
        wkv = sb(SA, "wkv", [128, 4, NCH, 512], BF16)
        bkt = sb(SA, "bkt", [128, 256], F32)
        dmask = sb(SA, "dmask", [128, 128], F32)
        eqm = sb(SA, "eqm", [128, 128], F32)
        bacc = sb(SA, "bacc", [128, 128], F32)
        bhi = sb(SA, "bhi", [128, 128], F32)
        S0 = ExitStack()
        S0.__enter__()
        stg = [sb(S0, "stg%d" % i, [128, NCH, 512], F32) for i in range(2)]
        wbf = [sb(S0, "wbf%d" % i, [128, NCH, 512], BF16) for i in range(2)]
        identf = sb(S0, "identf", [128, 128], F32)
        lamv = sb(S0, "lamvs", [128, 256], F32)
        ltmp = sb(S0, "ltmp", [128, 64], F32)
        lsum = sb(S0, "lsum", [128, 4], F32)
        memb = sb(S0, "memb", [128, NCH, NMEM], BF16)
        mkf = sb(S0, "mkf", [128, MH, NMEM], F32)
        mvf = sb(S0, "mvf", [128, 2, 512], F32)

        P.dma("sp", identf[:], ident_d, key="c0")
        P.dma("sp", tab[:], tab_d, key="c0")
        P.dma("sp", lamv[:], lam_d, key="c0")
        P.dma("sp", gs4[:], gsub_d, key="c0")
        P.dma("sp", cw[:], cw_d, key="c0")
        P.dma("sp", bkt[:], bkt_d, key="c0")
        t_c0 = P.dma("sp", dmask[:], dmask_d, key="c0")
        t_idb = P.op("dve", lambda e: e.tensor_copy(out=ident[:], in_=identf[:]), [t_c0])
        P.op("dve", lambda e: e.memset(epsb[:], EPS), ())
        t_gs2 = P.op("dve", lambda e: e.tensor_scalar(out=gs4[:], in0=gs4[:], scalar1=1.0 - LAM_INIT,
                                                      scalar2=None, op0=ALU.mult), [t_c0])
        tl = t_c0
        for k in range(2):
            tl = P.op("dve", lambda e, k=k: e.tensor_tensor(out=ltmp[:], in0=lamv[:, 128 * k:128 * k + 64],
                                                            in1=lamv[:, 128 * k + 64:128 * k + 128], op=ALU.mult), [tl])
            tl = P.op("dve", lambda e, k=k: e.reduce_sum(out=lsum[:, k:k + 1], in_=ltmp[:], axis=AX.X), [tl])
        tl = P.op("act", lambda e: e.activation(out=lsum[:, 2:4], in_=lsum[:, 0:2], func=AF.Exp), [tl])
        tl = P.op("dve", lambda e: e.tensor_tensor(out=nlam[:], in0=lsum[:, 3:4], in1=lsum[:, 2:3], op=ALU.subtract), [tl])
        t_nlam = P.op("dve", lambda e: e.tensor_scalar(out=nlam[:], in0=nlam[:], scalar1=-LAM_INIT, scalar2=None,
                                                       op0=ALU.add), [tl])
        t_cf = P.op("dve", lambda e: e.tensor_copy(out=cfar[:], in_=tab[:, 120:128]), [t_c0])
        def wsrc(name, col):
            return w_srcs[name].rearrange("(c p) n -> p c n", p=128)[:, :, col:col + 512]

        jobs = [("kv", k, w_srcs["w_in"].rearrange("(c p) n -> p c n", p=128)[:, :, 1024 + 512 * k:1536 + 512 * k])
                for k in range(4)]
        jobs += [("mem", k, w_mem.rearrange("(c p) n -> p c n", p=128)[:, :, 512 * k:512 * k + 512]) for k in range(2)]
        stg_free = [None, None]
        wbf_free = [None, None]
        memw_tok = [None, None]
        for n, (kind, idx, src) in enumerate(jobs):
            bsel = n % 2
            t_ld = P.dma("sp", stg[bsel][:], src, [stg_free[bsel]], key="stg%d" % bsel)
            if kind == "kv":
                dst = wkv[:, idx]
            else:
                dst = wbf[bsel][:]
            if n % 2 == 0:
                t_c = P.op("dve", lambda e, dst=dst, bsel=bsel: e.tensor_copy(out=dst, in_=stg[bsel][:]),
                           [t_ld, wbf_free[bsel]])
            else:
                t_c = P.op("act", lambda e, dst=dst, bsel=bsel: e.activation(out=dst, in_=stg[bsel][:], func=AF.Copy),
                           [t_ld, wbf_free[bsel]])
            stg_free[bsel] = t_c
            if kind == "scr":
                wbf_free[bsel] = P.dma("act", wsc[idx], wbf[bsel][:], [t_c], key="wst%d" % bsel)
            elif kind == "kv":
                P.dma("act", wsc[len(WG) + idx], wkv[:, idx], [t_c], key="wstkv")
            elif kind == "mem":
                memw_tok[idx] = (t_c, bsel)

        memstg = stg[0][:, :, 0:NMEM]
        t_m = P.dma("sp", memstg, memT.rearrange("(c p) n -> p c n", p=128), [stg_free[0]], key="stg0")
        t_mb = P.op("dve", lambda e: e.tensor_copy(out=memb[:], in_=memstg), [t_m])
        bank_free = [None] * 7
        jb = 0
        wmk = wbf[memw_tok[0][1]]
        wmv = wbf[memw_tok[1][1]]
        for mh in range(MH):
            b = jb % 7
            jb += 1
            for c in range(NCH):
                t_pe = P.op("pe", lambda e, b=b, c=c, mh=mh: e.matmul(
                    bank(b)[:, 0:NMEM], lhsT=wmk[:, c, mh * 128:mh * 128 + 128], rhs=memb[:, c, :],
                    start=(c == 0), stop=(c == NCH - 1)),
                    [t_mb, memw_tok[0][0], bank_free[b], t_idb] if c == 0 else (), sig=("chain" if c == NCH - 1 else None))
            t1 = P.op("act", lambda e, b=b, mh=mh: e.activation(out=mkT[:, mh, :], in_=bank(b)[:, 0:NMEM], func=AF.Copy), [t_pe])
            t2 = P.op("dve", lambda e, b=b, mh=mh: e.tensor_copy(out=mkf[:, mh, :], in_=bank(b)[:, 0:NMEM]), [t_pe, t1])
            bank_free[b] = t2
        P.dma("sp", mkoT_o.rearrange("h p n -> p h n"), mkf[:], [t2], key="mko")
        for blk in range(2):
            b = jb % 7
            jb += 1
            for c in range(NCH):
                t_pe = P.op("pe", lambda e, b=b, c=c, blk=blk: e.matmul(
                    bank(b), lhsT=memb[:, c, blk * 128:blk * 128 + 128], rhs=wmv[:, c, :],
                    start=(c == 0), stop=(c == NCH - 1)),
                    [t_mb, memw_tok[1][0], bank_free[b]] if c == 0 else (), sig=("chain" if c == NCH - 1 else None))
            t1 = P.op("act", lambda e, b=b, blk=blk: e.activation(
                out=mvx[:, blk, :, 0:128], in_=bank(b).rearrange("p (h e) -> p h e", e=128), func=AF.Copy), [t_pe])
            t2 = P.op("dve", lambda e, b=b, blk=blk: e.tensor_copy(out=mvf[:, blk, :], in_=bank(b)), [t_pe, t1])
            bank_free[b] = t2
        for blk in range(2):
            t2 = P.op("dve", lambda e, blk=blk: e.memset(mvx[:, blk, :, 128:129], 1.0), [t2])
        P.dma("sp", mvo_o.rearrange("(b p) n -> p b n", p=128), mvf[:], [t2], key="mvo")
        P.emit()
        S0.__exit__(None, None, None)

        xstg = [sb(SA, "xstg%d" % i, [128, 4, TW], F32) for i in range(2)]
        xb = [sb(SA, "xb%d" % i, [128, NCH, TW], BF16) for i in range(2)]
        ktst = [sb(SA, "ktst%d" % i, [128, H, TW], BF16) for i in range(2)]
        vxst = [sb(SA, "vxst%d" % i, [128, H, 4, 129], BF16) for i in range(2)]
        vm = sb(SA, "vm", [128, NT * 32], F32)
        kof = [sb(SA, "kof%d" % i, [128, TW], F32) for i in range(2)]
        vof = [sb(SA, "vof%d" % i, [128, 512], F32) for i in range(2)]

        def bias_gen():
            tp = t_c0
            for h in range(H):
                for kind in range(2):
                    nb = range(32) if kind == 0 else range(1, 16)
                    first = True
                    for b in nb:
                        tp = P.op("dve", lambda e, b=b, kind=kind: e.tensor_single_scalar(
                            out=eqm[:], in_=bkt[:, 128 * kind:128 * kind + 128], scalar=float(b), op=ALU.is_equal), [tp])
                        if first:
                            tp = P.op("dve", lambda e, b=b, h=h: e.tensor_scalar(
                                out=bacc[:], in0=eqm[:], scalar1=tab[:, b * 8 + h:b * 8 + h + 1], scalar2=None,
                                op0=ALU.mult), [tp])
                            first = False
                        else:
                            tp = P.op("dve", lambda e, b=b, h=h: e.scalar_tensor_tensor(
                                out=bacc[:], in0=eqm[:], scalar=tab[:, b * 8 + h:b * 8 + h + 1], in1=bacc[:],
                                op0=ALU.mult, op1=ALU.add), [tp])
                        yield
                    tp = P.op("dve", lambda e, h=h: e.tensor_scalar(
                        out=bacc[:], in0=bacc[:], scalar1=tab[:, 120 + h:121 + h], scalar2=None, op0=ALU.subtract), [tp])
                    if kind == 0:
                        tp = P.op("dve", lambda e: e.tensor_tensor(out=bacc[:], in0=bacc[:], in1=dmask[:], op=ALU.add), [tp])
                    tp = P.op("dve", lambda e, h=h, kind=kind: e.tensor_copy(out=biasT[:, h, kind, 0, :], in_=bacc[:]), [tp])
                    tp = P.op("dve", lambda e, h=h, kind=kind: e.tensor_copy(out=bhi[:], in_=biasT[:, h, kind, 0, :]), [tp])
                    tp = P.op("dve", lambda e: e.tensor_tensor(out=bacc[:], in0=bacc[:], in1=bhi[:], op=ALU.subtract), [tp])
                    tp = P.op("dve", lambda e, h=h, kind=kind: e.tensor_copy(out=biasT[:, h, kind, 1, :], in_=bacc[:]), [tp])
                    yield
        bgen = bias_gen()
        wq_f = [sb(SA, "wqf%d" % i, [128, NCH, 128], F32) for i in range(2)]
        wq_b = [sb(SA, "wqb%d" % i, [128, NCH, 128], BF16) for i in range(2)]

        def wcast_gen():
            f_free = [None, None]
            b_free = [None, None]
            n = 0
            for gi in range(len(WG)):
                name, col = WG[gi]
                srcv = w_srcs[name].rearrange("(c p) n -> p c n", p=128)
                for qq in range(4):
                    k = n % 2
                    n += 1
                    t_ld = P.dma("sp", wq_f[k][:], srcv[:, :, col + qq * 128:col + qq * 128 + 128], [f_free[k]],
                                 key="wqf%d" % k)
                    t_c = P.op("act", lambda e, k=k: e.activation(out=wq_b[k][:], in_=wq_f[k][:], func=AF.Copy),
                               [t_ld, b_free[k]])
                    f_free[k] = t_c
                    b_free[k] = P.dma("act", wsc[gi, :, :, qq * 128:qq * 128 + 128], wq_b[k][:], [t_c], key="wqs%d" % k)
                    yield
        wgen = wcast_gen()
        t_vm = P.dma("sp", vm[:], vmask, key="vm")
        xTr = xT.rearrange("(c p) t -> p c t", p=128)
        xstg_free = [None] * 2
        xb_free = [None, None]
        kt_free = [None, None]
        vx_free = [None, None]
        kof_free = [None, None]
        vof_free = [None, None]
        nko = 0
        nvo = 0
        bank_free = [None] * 7
        jb = 0
        npiece = 0
        ntile = NT if nslot == NSLOT else 4 * (nslot - 1) + 4
        def load_x(T):
            nonlocal npiece
            xs_ = T % 2
            t_x = []
            for pc in range(4):
                st = npiece % 2
                npiece += 1
                t_ld = P.dma("sp", xstg[st][:], xTr[:, 4 * pc:4 * pc + 4, T * TW:(T + 1) * TW], [xstg_free[st]],
                             key="xstg%d" % st)
                t_c = P.op("dve", lambda e, st=st, pc=pc, xs_=xs_: e.tensor_copy(
                    out=xb[xs_][:, 4 * pc:4 * pc + 4, :], in_=xstg[st][:]), [t_ld, xb_free[xs_]])
                xstg_free[st] = t_c
                t_x.append(t_c)
            return t_x

        t_x_next = load_x(0)
        for T in range(ntile):
            xs_ = T % 2
            own = (T % 4 == 3) and (T // 4) < nslot
            slot = T // 4
            t_x = t_x_next
            if T + 1 < ntile:
                t_x_next = load_x(T + 1)
            t_kev = []
            for h in range(H):
                b = jb % 7
                jb += 1
                for c in range(NCH):
                    t_pe = P.op("pe", lambda e, b=b, c=c, h=h, xs_=xs_: e.matmul(
                        bank(b), lhsT=wkv[:, h // 4, c, (h % 4) * 128:(h % 4) * 128 + 128], rhs=xb[xs_][:, c, :],
                        start=(c == 0), stop=(c == NCH - 1)),
                        (t_x + [bank_free[b]]) if c == 0 else (), sig=("chain" if c == NCH - 1 else None))
                t1 = P.op("act", lambda e, b=b, h=h, xs_=xs_: e.activation(
                    out=ktst[xs_][:, h, :], in_=bank(b), func=AF.Copy), [t_pe, kt_free[xs_]])
                t_kev.append(t1)
                if own:
                    ks_ = nko % 2
                    nko += 1
                    t2 = P.op("dve", lambda e, b=b, ks_=ks_: e.tensor_copy(out=kof[ks_][:], in_=bank(b)),
                              [t_pe, t1, kof_free[ks_]])
                    bank_free[b] = t2
                    kof_free[ks_] = P.dma("act", koT_o[h, :, slot * TW:(slot + 1) * TW], kof[ks_][:], [t2],
                                          key="kof%d" % ks_)
                else:
                    bank_free[b] = t1
            kt_free[xs_] = P.dma("act", kts[T], ktst[xs_][:], t_kev, key="kst%d" % xs_)
            t_vev = []
            t_vm2 = P.op("pool", lambda e, xs_=xs_, T=T: e.tensor_copy(
                out=vxst[xs_][:, :, :, 128], in_=vm[:, T * 32:(T + 1) * 32].rearrange("p (h s) -> p h s", s=4)),
                [t_vm, vx_free[xs_]])
            t_vev.append(t_vm2)
            for s in range(4):
                for g in range(2):
                    b = jb % 7
                    jb += 1
                    for c in range(NCH):
                        t_pe = P.op("pe", lambda e, b=b, c=c, s=s, g=g, xs_=xs_: e.matmul(
                            bank(b), lhsT=xb[xs_][:, c, s * 128:s * 128 + 128], rhs=wkv[:, 2 + g, c, :],
                            start=(c == 0), stop=(c == NCH - 1)),
                            (t_x + [bank_free[b]]) if c == 0 else (), sig=("chain" if c == NCH - 1 else None))
                    t1 = P.op("dve", lambda e, b=b, s=s, g=g, xs_=xs_: e.tensor_copy(
                        out=vxst[xs_][:, 4 * g:4 * g + 4, s, 0:128], in_=bank(b).rearrange("p (h e) -> p h e", e=128)),
                        [t_pe, vx_free[xs_]])
                    t_vev.append(t1)
                    if own:
                        vs_ = nvo % 2
                        nvo += 1
                        t2 = P.op("act", lambda e, b=b, vs_=vs_: e.activation(
                            out=vof[vs_][:], in_=bank(b), func=AF.Copy), [t_pe, t1, vof_free[vs_]])
                        bank_free[b] = t2
                        vof_free[vs_] = P.dma("act", vo_o[slot * TW + s * 128:slot * TW + s * 128 + 128,
                                                         g * 512:g * 512 + 512], vof[vs_][:], [t2], key="vof%d" % vs_)
                    else:
                        bank_free[b] = t1
            for _ in range(30):
                if next(bgen, "done") == "done":
                    break
            for _ in range(2):
                if next(wgen, "done") == "done":
                    break
            xb_free[xs_] = t_pe
            vx_free[xs_] = P.dma("pool", vxs[T], vxst[xs_][:].rearrange("p h s e -> p h (s e)"), t_vev, key="vst%d" % xs_)
        for _ in bgen:
            pass
        for _ in wgen:
            pass
        P.emit()
        SA.__exit__(None, None, None)

        if do_sample:
            SS_ = ExitStack()
            SS_.__enter__()
            NTK = NSTR * SS
            wS = sb(SS_, "wS", [128, NCH, 512], BF16)
            xsf = sb(SS_, "xsf", [128, NCH, NTK], F32)
            xsb = sb(SS_, "xsb", [128, NCH, NTK], BF16)
            qTs = sb(SS_, "qTs", [128, H, NTK], BF16)
            kTs = sb(SS_, "kTs", [128, H, NTK], BF16)
            kTf = sb(SS_, "kTf", [128, H, NTK], F32)
            hTs = sb(SS_, "hTs", [128, 4, NTK], F32)
            CTs = sb(SS_, "CTs", [128, 4, NTK], F32)
            BTs = sb(SS_, "BTs", [128, 4, NTK], F32)
            gcTs = sb(SS_, "gcTs", [128, 4, NTK], F32)
            mqTs = sb(SS_, "mqTs", [128, MH, NTK], BF16)
            upad = sb(SS_, "upad", [128, 4, NSTR, SS + 2], F32)
            cvt = sb(SS_, "cvt", [128, NSTR, SS], F32)
            cvt2 = sb(SS_, "cvt2", [128, NSTR, SS], F32)
            zTcs = sb(SS_, "zTcs", [128, 4, NTK], BF16)
            convsb = sb(SS_, "convsb", [128, NSTR, 4, 2], F32)
            vfs = [sb(SS_, "vfs%d" % i, [SS, 512], F32) for i in range(2)]
            vnew = sb(SS_, "vnew", [SS, NSTR, H, 129], BF16)
            sgds = sb(SS_, "sgds", [SS, NSTR, 1024], F32)
            sgms = sb(SS_, "sgms", [SS, NSTR, 512], F32)
            ztoks = sb(SS_, "ztoks", [SS, NSTR, 1536], BF16)
            ckf = [sb(SS_, "ckf%d" % i, [128, 1024], F32) for i in range(2)]
            ckbr = [sb(SS_, "ckbr%d" % i, [128, 4, PAST], BF16) for i in range(3)]
            cvf = [sb(SS_, "cvf%d" % i, [128, 1024], F32) for i in range(2)]
            vxcr = [sb(SS_, "vxcr%d" % i, [128, 8, 4, 129], BF16) for i in range(3)]
            cmkf = sb(SS_, "cmkf", [128, MH, NMEM], F32)
            cmkb = sb(SS_, "cmkb", [128, MH, NMEM], BF16)
            cmvf = sb(SS_, "cmvf", [128, 2, 512], F32)
            cmvx = sb(SS_, "cmvx", [128, 2, MH, 129], BF16)
            pTs = sb(SS_, "pTs", [128, 2, 9, SS], BF16)
            pTm = sb(SS_, "pTm", [128, 2, SS], BF16)
            accS = sb(SS_, "accS", [SS, 2, ACCW], F32)
            oS2 = [sb(SS_, "oS2%d" % i, [SS, 128], F32) for i in range(2)]
            rstdS2 = [sb(SS_, "rstdS2%d" % i, [SS, 2], F32) for i in range(2)]
            acc_freeM = [None]
            tS = sb(SS_, "tS", [SS, 128], F32)
            sqS = sb(SS_, "sqS", [SS, 128], F32)
            recS = sb(SS_, "recS", [SS, 2], F32)
            nr2S = sb(SS_, "nr2S", [SS, 1], F32)
            ssqS = sb(SS_, "ssqS", [SS, 1], F32)
            rstdS = sb(SS_, "rstdS", [SS, 1], F32)
            zTs = sb(SS_, "zTs", [128, NCH, NTK], BF16)
            ysbS = sb(SS_, "ysbS", [NTK, D], F32)
            xrS = [sb(SS_, "xrS%d" % i, [NTK, 512], F32) for i in range(2)]
            lnS = [sb(SS_, "lnS%d" % i, [NTK, 512], F32) for i in range(2)]
            bnS = sb(SS_, "bnS", [NTK, 4, 6], F32)
            mvS = sb(SS_, "mvS", [NTK, 2], F32)
            lnrS = sb(SS_, "lnrS", [NTK, 1], F32)

            t_xs = P.dma("sp", xsf[:], xsT_d.rearrange("(c p) n -> p c n", p=128), key="xsf")
            t_xs = P.op("dve", lambda e: e.tensor_copy(out=xsb[:], in_=xsf[:]), [t_xs])
            t_cc = P.dma("sp", upad[:, :, :, 0:2], cconv_d, key="cconv")
            bank_free = [None] * 7
            jbs = [0]
            wS_free = [None]

            def loadS(gi):
                return P.dma("sp", wS[:], wsc[gi], [wS_free[0]], key="wS")

            def jobS(mm, parts, n_out, evac, deps):
                b = jbs[0] % 7
                jbs[0] += 1
                for c in range(NCH):
                    t_pe = P.op("pe", lambda e, b=b, c=c: mm(e, bank(b)[0:parts, 0:n_out], c, c == 0, c == NCH - 1),
                                (list(deps) + [bank_free[b]]) if c == 0 else (),
                                sig=("chain" if c == NCH - 1 else None))
                t = evac(bank(b)[0:parts, 0:n_out], t_pe)
                bank_free[b] = t
                return t_pe, t

            KV0 = len(WG)
            t_fm = []
            fm_groups = [(G_Q0, "q", 0), (G_Q1, "q", 4), (KV0, "k", 0), (KV0 + 1, "k", 4), (G_H, "h", 0), (G_C, "c", 0),
                         (G_B, "b", 0), (G_GC, "gc", 0), (G_MQ, "mq", 0)]
            for (gi, kind, h0) in fm_groups:
                t_w = loadS(gi)
                for sub in range(4):
                    def mm(e, out, c, st_, sp_, sub=sub):
                        return e.matmul(out, lhsT=wS[:, c, sub * 128:sub * 128 + 128], rhs=xsb[:, c, :], start=st_, stop=sp_)
                    if kind == "q":
                        def ev(bk, tpe, hh=h0 + sub):
                            return P.op("act", lambda e: e.mul(out=qTs[:, hh, :], in_=bk, mul=0.125), [tpe])
                    elif kind == "k":
                        def ev(bk, tpe, hh=h0 + sub):
                            t1 = P.op("act", lambda e: e.activation(out=kTs[:, hh, :], in_=bk, func=AF.Copy), [tpe])
                            return P.op("dve", lambda e: e.tensor_copy(out=kTf[:, hh, :], in_=bk), [tpe, t1])
                    elif kind == "mq":
                        def ev(bk, tpe, sub=sub):
                            return P.op("act", lambda e: e.mul(out=mqTs[:, sub, :], in_=bk, mul=128.0 ** -0.5), [tpe])
                    elif kind == "gc":
                        def ev(bk, tpe, sub=sub):
                            return P.op("act", lambda e: e.activation(out=gcTs[:, sub, :], in_=bk, func=AF.Silu), [tpe])
                    else:
                        tgt = {"h": hTs, "c": CTs, "b": BTs}[kind]

                        def ev(bk, tpe, sub=sub, tgt=tgt):
                            return P.op("dve", lambda e: e.tensor_copy(out=tgt[:, sub, :], in_=bk), [tpe])
                    last_pe, t_e = jobS(mm, 128, NTK, ev, [t_xs, t_w])
                    t_fm.append(t_e)
                wS_free[0] = last_pe
            P.dma("sp", ksT_o.rearrange("h p n -> p h n"), kTf[:], t_fm, key="ksT")
            t_tm = []
            nvf = 0
            vf_free = [None, None]
            for (gi, kind, g) in [(KV0 + 2, "v", 0), (KV0 + 3, "v", 1), (G_GD0, "gd", 0), (G_GD1, "gd", 1), (G_GM, "gm", 0)]:
                t_w = loadS(gi)
                for st in range(NSTR):
                    def mm(e, out, c, st_, sp_, st=st):
                        return e.matmul(out, lhsT=xsb[:, c, st * SS:(st + 1) * SS], rhs=wS[:, c, :], start=st_, stop=sp_)
                    if kind == "v":
                        vsel = nvf % 2
                        nvf += 1

                        def ev(bk, tpe, st=st, g=g, vsel=vsel):
                            t1 = P.op("act", lambda e: e.activation(
                                out=vnew[:, st, 4 * g:4 * g + 4, 0:128], in_=bk.rearrange("p (h e) -> p h e", e=128),
                                func=AF.Copy), [tpe])
                            t2 = P.op("dve", lambda e: e.tensor_copy(out=vfs[vsel][:], in_=bk), [tpe, t1, vf_free[vsel]])
                            vf_free[vsel] = P.dma("sp", vs_o[st * SS:(st + 1) * SS, g * 512:g * 512 + 512], vfs[vsel][:], [t2],
                                                  key="vfs%d" % vsel)
                            return t2
                    elif kind == "gd":
                        def ev(bk, tpe, st=st, g=g):
                            t1 = P.op("act", lambda e: e.activation(out=sgds[:, st, g * 512:g * 512 + 512], in_=bk,
                                                                    func=AF.Silu), [tpe])
                            return P.op("dve", lambda e: e.tensor_tensor(out=sgds[:, st, g * 512:g * 512 + 512],
                                                                         in0=sgds[:, st, g * 512:g * 512 + 512],
                                                                         in1=gs4[0:SS, :], op=ALU.mult), [t1])
                    else:
                        def ev(bk, tpe, st=st):
                            return P.op("act", lambda e: e.activation(out=sgms[:, st, :], in_=bk, func=AF.Silu), [tpe])
                    last_pe, t_e = jobS(mm, SS, 512, ev, [t_xs, t_w])
                    t_tm.append(t_e)
                wS_free[0] = last_pe
            t_on = P.op("dve", lambda e: e.memset(vnew[:, :, :, 128:129].rearrange("p s h e -> p (s h e)"), 1.0), t_tm)
            t_tm.append(t_on)
            t_wo = loadS(G_O0)
            t = None
            t_cvs = []
            for cc in range(4):
                t = P.op("pool", lambda e, cc=cc: e.tensor_tensor(
                    out=upad[:, cc, :, 2:SS + 2], in0=CTs[:, cc, :].rearrange("p (s t) -> p s t", t=SS),
                    in1=hTs[:, cc, :].rearrange("p (s t) -> p s t", t=SS), op=ALU.mult), t_fm + [t_cc, t])
                t_u = P.op("pool", lambda e, cc=cc: e.tensor_copy(out=convsb[:, :, cc, :], in_=upad[:, cc, :, SS:SS + 2]), [t])
                t_cvs.append(t_u)
                t = P.op("pool", lambda e, cc=cc: e.tensor_scalar(out=cvt[:], in0=upad[:, cc, :, 0:SS],
                                                                  scalar1=cw[:, cc * 3:cc * 3 + 1], scalar2=None,
                                                                  op0=ALU.mult), [t])
                for jx in (1, 2):
                    t = P.op("pool", lambda e, cc=cc, jx=jx: e.tensor_scalar(
                        out=cvt2[:], in0=upad[:, cc, :, jx:jx + SS], scalar1=cw[:, cc * 3 + jx:cc * 3 + jx + 1],
                        scalar2=None, op0=ALU.mult), [t])
                    t = P.op("pool", lambda e: e.tensor_tensor(out=cvt[:], in0=cvt[:], in1=cvt2[:], op=ALU.add), [t])
                t = P.op("pool", lambda e, cc=cc: e.tensor_tensor(
                    out=cvt[:], in0=cvt[:], in1=BTs[:, cc, :].rearrange("p (s t) -> p s t", t=SS), op=ALU.mult), [t])
                t = P.op("pool", lambda e, cc=cc: e.tensor_tensor(
                    out=zTcs[:, cc, :].rearrange("p (s t) -> p s t", t=SS), in0=cvt[:],
                    in1=gcTs[:, cc, :].rearrange("p (s t) -> p s t", t=SS), op=ALU.mult), [t])
                t_cvs.append(t)
            P.dma("pool", convs_o, convsb[:], t_cvs, key="convs")

            t_epS = [None]
            NU = NSTR * 2
            RU = 3
            ck_free = [None] * RU
            vx_free = [None] * RU
            cm_free = [None]
            nck = [0]
            ncv = [0]
            ckf_free = [None, None]
            cvf_free = [None, None]
            sb_free = {}
            acc_freeS = [None, None]
            ucache = {}
            t_ones = [P.op("dve", lambda e, r=r: e.memset(vxcr[r][:, :, :, 128:129].rearrange("p b h e -> p (b h e)"), 1.0), ())
                      for r in range(RU)]

            def load_unit(u):
                st, hh = divmod(u, 2)
                r = u % RU
                t_ck = []
                for h4 in range(4):
                    k_ = nck[0] % 2
                    nck[0] += 1
                    t_ld = P.dma("sp", ckf[k_][:], ckT_d[st, hh * 4 + h4], [ckf_free[k_]], key="ckf%d" % k_)
                    t_c = P.op("act", lambda e, k_=k_, h4=h4, r=r: e.activation(out=ckbr[r][:, h4, :], in_=ckf[k_][:],
                                                                             func=AF.Copy), [t_ld, ck_free[r]])
                    ckf_free[k_] = t_c
                    t_ck.append(t_c)
                t_cv_ = [t_ones[r]]
                for blk in range(8):
                    k_ = ncv[0] % 2
                    ncv[0] += 1
                    t_ld = P.dma("sp", cvf[k_][:, 0:512], cv_d[st, blk * 128:(blk + 1) * 128, hh * 512:(hh + 1) * 512],
                                 [cvf_free[k_]], key="cvf%d" % k_)
                    t_c = P.op("dve", lambda e, k_=k_, blk=blk, r=r: e.tensor_copy(
                        out=vxcr[r][:, blk, :, 0:128], in_=cvf[k_][:, 0:512].rearrange("p (h e) -> p h e", e=128)),
                        [t_ld, vx_free[r]])
                    cvf_free[k_] = t_c
                    t_cv_.append(t_c)
                ucache[u] = (t_ck, t_cv_)

            for u in range(min(2, NU)):
                load_unit(u)
            pend_final = [None]
            tp_prev = [None]
            for u in range(NU):
                st, hh = divmod(u, 2)
                r = u % RU
                if u + 2 < NU:
                    load_unit(u + 2)
                t_ck, t_cv_ = ucache[u]
                if hh == 0:
                    t_l1 = P.dma("sp", cmkf[:], cmkT_d[st].rearrange("h p n -> p h n"), [cm_free[0]], key="cmk")
                    t_l2 = P.dma("sp", cmvf[:], cmv_d[st].rearrange("(b p) n -> p b n", p=128), [cm_free[0]], key="cmv")
                    t_m1 = P.op("pool", lambda e: e.tensor_copy(out=cmkb[:], in_=cmkf[:]), [t_l1, cm_free[0]])
                    t_m2 = P.op("pool", lambda e: e.tensor_copy(
                        out=cmvx[:, :, :, 0:128], in_=cmvf[:].rearrange("p b (h e) -> p b h e", e=128)), [t_l2, cm_free[0]])
                    t_m3 = P.op("pool", lambda e: e.memset(cmvx[:, :, :, 128:129].rearrange("p b h e -> p (b h e)"), 1.0), [t_m2])
                    t_mem = [t_m1, t_m2, t_m3]
                qs = slice(st * SS, (st + 1) * SS)
                for h4 in range(4):
                    h = hh * 4 + h4
                    cnt = u * 4 + h4
                    sbk = cnt % 2
                    abk = cnt % 2
                    Sv = psS[:, sbk, 0:2 * 9 * SS].rearrange("p (m k q) -> p m k q", m=2, k=9)
                    for m in range(2):
                        for kb in range(8):
                            tq = P.op("pe", lambda e, m=m, kb=kb, h4=h4, h=h, Sv=Sv, qs=qs, r=r: e.matmul(
                                Sv[:, m, kb, :], lhsT=ckbr[r][64 * m:64 * m + 64, h4, kb * 128:kb * 128 + 128],
                                rhs=qTs[64 * m:64 * m + 64, h, qs], start=(m == 0 and kb == 0), stop=False,
                                skip_group_check=True),
                                (t_ck + t_fm + t_tm + [sb_free.get(sbk)]) if (m == 0 and kb == 0) else (), sig=None)
                        tq = P.op("pe", lambda e, m=m, h=h, Sv=Sv, qs=qs: e.matmul(
                            Sv[0:SS, m, 8, :], lhsT=kTs[64 * m:64 * m + 64, h, qs], rhs=qTs[64 * m:64 * m + 64, h, qs],
                            start=False, stop=False, skip_group_check=True), (), sig=None)
                        for part in range(2):
                            tq = P.op("pe", lambda e, m=m, h=h, part=part, Sv=Sv: e.matmul(
                                Sv[:, m, 7, :], lhsT=ident[:], rhs=biasT[:, h, 1, part, 0:SS], start=False, stop=False,
                                skip_group_check=True), (), sig=None)
                            tq = P.op("pe", lambda e, m=m, h=h, part=part, Sv=Sv: e.matmul(
                                Sv[0:SS, m, 8, :], lhsT=ident[0:SS, 0:SS], rhs=biasT[0:SS, h, 0, part, 0:SS], start=False,
                                stop=True, skip_group_check=True), (), sig=("chain" if (m == 1 and part == 1) else None))
                    te1 = P.op("act", lambda e, h=h, Sv=Sv: e.activation(
                        out=pTs[:, :, 0:8, :], in_=Sv[:, :, 0:8, :], func=AF.Exp, bias=cfar[:, h:h + 1], scale=1.0),
                        [tq, tp_prev[0]])
                    te2 = P.op("act", lambda e, h=h, Sv=Sv: e.activation(
                        out=pTs[0:SS, :, 8, :], in_=Sv[0:SS, :, 8, :], func=AF.Exp, bias=cfar[0:SS, h:h + 1], scale=1.0), [tq])
                    sb_free[sbk] = te2
                    for m in range(2):
                        for kb in range(8):
                            tp_ = P.op("pe", lambda e, m=m, kb=kb, h4=h4, r=r, abk=abk: e.matmul(
                                psA[0:SS, abk, m * ACCW:m * ACCW + 129], lhsT=pTs[:, m, kb, :], rhs=vxcr[r][:, kb, h4, :],
                                start=(m == 0 and kb == 0), stop=False, skip_group_check=True),
                                ([te1, te2, acc_freeS[abk]] + t_cv_) if (m == 0 and kb == 0) else (), sig=None)
                        tp_ = P.op("pe", lambda e, m=m, h=h, st=st, abk=abk: e.matmul(
                            psA[0:SS, abk, m * ACCW:m * ACCW + 129], lhsT=pTs[0:SS, m, 8, :], rhs=vnew[:, st, h, :],
                            start=False, stop=True, skip_group_check=True), (), sig=("chain" if m == 1 else None))
                    tp_prev[0] = tp_
                    if pend_final[0] is not None:
                        pend_final[0]()
                        pend_final[0] = None
                    t = P.op("dve", lambda e, abk=abk: e.tensor_copy(
                        out=accS[:], in_=psA[0:SS, abk, 0:2 * ACCW].rearrange("p (a w) -> p a w", w=ACCW)), [tp_, t_epS[0]])
                    acc_freeS[abk] = t
                    t = P.op("dve", lambda e: e.reciprocal(out=recS[:], in_=accS[:, :, 128]), [t])
                    t = P.op("dve", lambda e: e.tensor_scalar(out=nr2S[:], in0=recS[:, 1:2], scalar1=nlam[0:SS, 0:1],
                                                              scalar2=None, op0=ALU.mult), [t, t_nlam])
                    t = P.op("dve", lambda e: e.tensor_scalar(out=tS[:], in0=accS[:, 1, 0:128], scalar1=nr2S[:, 0:1],
                                                              scalar2=None, op0=ALU.mult), [t])
                    oSel = oS2[cnt % 2]
                    rSel = rstdS2[cnt % 2]
                    t = P.op("dve", lambda e, oSel=oSel: e.scalar_tensor_tensor(out=oSel[:], in0=accS[:, 0, 0:128], scalar=recS[:, 0:1],
                                                                                in1=tS[:], op0=ALU.mult, op1=ALU.add), [t])
                    t = P.op("dve", lambda e, oSel=oSel: e.tensor_tensor(out=sqS[:], in0=oSel[:], in1=oSel[:], op=ALU.mult), [t])
                    t = P.op("dve", lambda e, rSel=rSel: e.reduce_sum(out=rSel[:, 0:1], in_=sqS[:], axis=AX.X), [t])
                    t_epS[0] = t
                    t = P.op("act", lambda e, rSel=rSel: e.activation(out=rSel[:, 1:2], in_=rSel[:, 0:1], func=AF.Ln,
                                                                      bias=epsb[0:SS, 0:1], scale=1.0 / 128.0), [t])
                    t = P.op("act", lambda e, rSel=rSel: e.activation(out=rSel[:, 1:2], in_=rSel[:, 1:2], func=AF.Exp, scale=-0.5), [t])

                    def fin(st=st, h=h, oSel=oSel, rSel=rSel, t=t):
                        t_epS[0] = P.op("dve", lambda e: e.scalar_tensor_tensor(
                            out=ztoks[:, st, h * 128:h * 128 + 128], in0=oSel[:], scalar=rSel[:, 1:2],
                            in1=sgds[:, st, h * 128:h * 128 + 128], op0=ALU.mult, op1=ALU.mult), [t, t_epS[0]])
                    pend_final[0] = fin
                ck_free[r] = tq
                vx_free[r] = tp_
                if hh == 0:
                    continue
                if pend_final[0] is not None:
                    pend_final[0]()
                    pend_final[0] = None
                for mh in range(MH):
                    Sm = psS[:, 2, 0:2 * SS].rearrange("p (k q) -> p k q", k=2)
                    for blk in range(2):
                        tq = P.op("pe", lambda e, mh=mh, blk=blk, Sm=Sm, qs=qs: e.matmul(
                            Sm[:, blk, :], lhsT=cmkb[:, mh, blk * 128:blk * 128 + 128], rhs=mqTs[:, mh, qs],
                            start=(blk == 0), stop=(blk == 1), skip_group_check=True),
                            (t_mem + [sb_free.get(2)]) if blk == 0 else (), sig=("chain" if blk == 1 else None))
                    te = P.op("act", lambda e, Sm=Sm: e.activation(out=pTm[:], in_=Sm, func=AF.Exp), [tq, tp_prev[0]])
                    sb_free[2] = te
                    for blk in range(2):
                        tp_ = P.op("pe", lambda e, mh=mh, blk=blk: e.matmul(
                            psA[0:SS, 2, 0:129], lhsT=pTm[:, blk, :], rhs=cmvx[:, blk, mh, :],
                            start=(blk == 0), stop=(blk == 1), skip_group_check=True),
                            [te, acc_freeM[0]] if blk == 0 else (), sig=("chain" if blk == 1 else None))
                    tp_prev[0] = tp_
                    t = P.op("dve", lambda e: e.tensor_copy(out=accS[:, 0, :], in_=psA[0:SS, 2, 0:ACCW]), [tp_, t_epS[0]])
                    acc_freeM[0] = t
                    t = P.op("dve", lambda e: e.reciprocal(out=recS[:, 0:1], in_=accS[:, 0, 128:129]), [t])
                    t = P.op("dve", lambda e, st=st, mh=mh: e.scalar_tensor_tensor(
                        out=ztoks[:, st, 1024 + mh * 128:1024 + mh * 128 + 128], in0=accS[:, 0, 0:128], scalar=recS[:, 0:1],
                        in1=sgms[:, st, mh * 128:mh * 128 + 128], op0=ALU.mult, op1=ALU.mult), [t])
                    t_epS[0] = t
                cm_free[0] = tp_
            t_zs = []
            tpf = [None]
            for st in range(NSTR):
                for k in range(12):
                    tt = P.op("pe", lambda e, st=st, k=k: e.transpose(
                        psTb[:, k * SS:(k + 1) * SS], ztoks[:, st, k * 128:(k + 1) * 128], ident[0:SS, 0:SS]),
                        [t_epS[0], tpf[0]] if k == 0 else (), sig=("chain" if k == 11 else None))
                t1 = P.op("dve", lambda e, st=st: e.tensor_copy(
                    out=zTs[:, 0:8, st * SS:(st + 1) * SS], in_=psTb[:, 0:8 * SS].rearrange("p (k t) -> p k t", t=SS)), [tt])
                t2 = P.op("dve", lambda e, st=st: e.tensor_copy(
                    out=zTs[:, 12:16, st * SS:(st + 1) * SS], in_=psTb[:, 8 * SS:12 * SS].rearrange("p (k t) -> p k t", t=SS)), [tt, t1])
                tpf[0] = t2
                t_zs += [t1, t2]
            t_zs.append(P.op("pool", lambda e: e.tensor_copy(out=zTs[:, 8:12, :], in_=zTcs[:]), t_cvs))
            t_ys = []
            xr_free = [None, None]
            for g in range(4):
                t_w = t_wo if g == 0 else loadS(G_O0 + g)
                t_xr = P.dma("sp", xrS[g % 2][:], xs_d[:, g * 512:(g + 1) * 512], [xr_free[g % 2]], key="xrS%d" % (g % 2))

                def mm(e, out, c, st_, sp_):
                    return e.matmul(out, lhsT=zTs[:, c, :], rhs=wS[:, c, :], start=st_, stop=sp_)

                def ev(bk, tpe, g=g, t_xr=t_xr):
                    return P.op("dve", lambda e: e.scalar_tensor_tensor(
                        out=ysbS[:, g * 512:(g + 1) * 512], in0=xrS[g % 2][:], scalar=ALPHA, in1=bk, op0=ALU.mult,
                        op1=ALU.add), [tpe, t_xr])
                last_pe, t_e = jobS(mm, NTK, 512, ev, t_zs + [t_w])
                xr_free[g % 2] = t_e
                wS_free[0] = last_pe
                t_ys.append(t_e)
            t = None
            for g in range(4):
                t = P.op("dve", lambda e, g=g: e.bn_stats(out=bnS[:, g, :], in_=ysbS[:, g * 512:(g + 1) * 512]), t_ys + [t])
            t = P.op("dve", lambda e: e.bn_aggr(out=mvS[:], in_=bnS[:].rearrange("p a b -> p (a b)")), [t])
            t = P.op("act", lambda e: e.activation(out=lnrS[:], in_=mvS[:, 1:2], func=AF.Ln, bias=epsb[0:NTK, 0:1], scale=1.0), [t])
            t = P.op("act", lambda e: e.activation(out=lnrS[:], in_=lnrS[:], func=AF.Exp, scale=-0.5), [t])
            t = P.op("dve", lambda e: e.tensor_scalar(out=ysbS[:], in0=ysbS[:], scalar1=mvS[:, 0:1], scalar2=lnrS[:, 0:1],
                                                      op0=ALU.subtract, op1=ALU.mult), [t])
            ln_free = [None, None]
            for g in range(4):
                tg = P.dma("sp", lnS[0][:], lng_d[0:NTK, g * 512:(g + 1) * 512], [ln_free[0]], key="lnS0")
                tb = P.dma("sp", lnS[1][:], lnb_d[0:NTK, g * 512:(g + 1) * 512], [ln_free[1]], key="lnS1")
                t = P.op("dve", lambda e, g=g: e.tensor_tensor(out=ysbS[:, g * 512:(g + 1) * 512],
                                                               in0=ysbS[:, g * 512:(g + 1) * 512], in1=lnS[0][:], op=ALU.mult), [t, tg])
                ln_free[0] = t
                t = P.op("dve", lambda e, g=g: e.tensor_tensor(out=ysbS[:, g * 512:(g + 1) * 512],
                                                               in0=ysbS[:, g * 512:(g + 1) * 512], in1=lnS[1][:], op=ALU.add), [t, tb])
                ln_free[1] = t
            P.dma("sp", ys_o, ysbS[:], [t], key="ysS")
            P.emit()
            SS_.__exit__(None, None, None)

        SB = ExitStack()
        SB.__enter__()
        wring = [sb(SB, "wring%d" % i, [128, NCH, 512], BF16) for i in range(2)]
        xz = sb(SB, "xz", [128, NCH, TW + 2], BF16)
        stgB = sb(SB, "stgB", [128, 2, 2 * (TW + 2)], F32)
        qT = sb(SB, "qT", [128, H, TW], BF16)
        mqT = sb(SB, "mqT", [128, MH, TW], BF16)
        big = sb(SB, "big", [128, 8208], F32)
        sgd = sb(SB, "sgd", [128, 4, 1024], F32)
        sgm = sb(SB, "sgm", [128, 4, 512], F32)
        ztok = sb(SB, "ztok", [128, 4, 1536], BF16)
        zTc = sb(SB, "zTc", [128, 4, TW], BF16)
        kring = sb(SB, "kring", [128, RK, TW], BF16)
        vring = sb(SB, "vring", [128, RK, 4, 129], BF16)
        pring = sb(SB, "pring", [128, RP, 2, TW], BF16)
        accs = sb(SB, "accs", [128, 8, ACCW], F32)
        otmp = sb(SB, "otmp", [128, 4, 128], F32)
        ttmp = sb(SB, "ttmp", [128, 128], F32)
        sqtmp = sb(SB, "sqtmp", [128, 128], F32)
        rec = sb(SB, "rec", [128, 8], F32)
        nr2 = sb(SB, "nr2", [128, 4], F32)
        ssq = sb(SB, "ssq", [128, 4], F32)
        rstd = sb(SB, "rstd", [128, 4], F32)
        halo = sb(SB, "halo", [128, 8, 2], F32)
        ctmp = sb(SB, "ctmp", [128, TW], F32)
        ctmp2 = sb(SB, "ctmp2", [128, TW], F32)
        utmp = sb(SB, "utmp", [128, TW + 2], F32)
        xres = [sb(SB, "xres%d" % i, [128, 512], F32) for i in range(2)]
        bnst = sb(SB, "bnst", [128, 4, 6], F32)
        bnst4 = sb(SB, "bnst4", [128, 4, 4, 6], F32)
        mv4 = sb(SB, "mv4", [128, 4, 2], F32)
        lnr4 = sb(SB, "lnr4", [128, 4], F32)
        mv = sb(SB, "mv", [128, 2], F32)
        lnr = sb(SB, "lnr", [128, 1], F32)
        convu = sb(SB, "convu", [128, 4, 2], F32)

        hT = big[:, 0:4 * 514].rearrange("p (c t) -> p c t", t=514)
        CT = big[:, 2056:2 * 2056].rearrange("p (c t) -> p c t", t=514)
        BT = big[:, 4112:4112 + 2048].rearrange("p (c t) -> p c t", t=512)
        sgcT = big[:, 6160:6160 + 2048].rearrange("p (c t) -> p c t", t=512)
        ysb = big[:, 0:8192].rearrange("p (s n) -> p s n", n=2048)
        lngs = sb(SB, "lngs", [128, D], F32)
        lngb = lngs[:]
        t_lng = P.dma("sp", lngs[:], lng_d, key="lng")
        zT = sgd[:].rearrange("p s n -> p (s n)").bitcast(BF16).rearrange("p (c t) -> p c t", t=TW)

        lnb_sb = sb(SB, "lnb_sb", [128, D], F32)
        t_lnb = P.dma("sp", lnb_sb[:], lnb_d, key="lnb")

        sem_qk = P.sem("qk")
        sem_exp = P.sem("exp")
        sem_pv = P.sem("pv")

        bank_free = [None] * 7
        halo_free = [None]
        jbc = [0]
        wr_free = [None, None]
        wr_cnt = [0]

        def load_w(gi, after=()):
            r = wr_cnt[0] % 2
            wr_cnt[0] += 1
            t = P.dma("sp", wring[r][:], wsc[gi], [wr_free[r]] + list(after), key="w%d" % r)
            return r, t

        def job(mm, n_out, evac, deps):
            b = jbc[0] % 7
            jbc[0] += 1
            for c in range(NCH):
                t_pe = P.op("pe", lambda e, b=b, c=c: mm(e, bank(b)[:, 0:n_out], c, c == 0, c == NCH - 1),
                            (list(deps) + [bank_free[b]]) if c == 0 else (),
                            sig=("chain" if c == NCH - 1 else None))
            t = evac(bank(b)[:, 0:n_out], t_pe)
            bank_free[b] = t
            return t_pe, t

        xTr2 = xT.rearrange("(c p) t -> p c t", p=128)
        stgv = [stgB[:, k, :].rearrange("p (c t) -> p c t", t=TW + 2) for k in range(2)]
        prev_slot_done = []
        xz_free = []
        lng_used = []
        nblk = [0]
        ncidx = [0]
        kv_free = [None] * RK
        pv_tok = {}
        exp_tok = {}
        acc_free = [None]
        y_st_tok = [None]

        stg_free2 = [None, None]

        def load_xz(T):
            t_x = []
            for pc in range(8):
                k = pc % 2
                t_ld = P.dma("sp", stgv[k], xTr2[:, 2 * pc:2 * pc + 2, T * TW - 2:(T + 1) * TW],
                             [stg_free2[k]], key="xz%d" % k)
                t_c = P.op("dve", lambda e, k=k, pc=pc: e.tensor_copy(out=xz[:, 2 * pc:2 * pc + 2, :], in_=stgv[k]),
                           [t_ld] + xz_free)
                stg_free2[k] = t_c
                t_x.append(t_c)
            return t_x

        t_x_next = load_xz(3)
        for i in range(nslot):
            T = 4 * i + 3
            t_x = t_x_next
            nxt = load_w(G_Q0)
            order = [G_Q0, G_Q1, G_MQ, G_GD0, G_GD1, G_GM, G_H, G_C, G_B, G_GC]
            t_conv_in = {}
            t_q = []
            t_gate = []
            for oi, gi in enumerate(order):
                r, t_w = nxt
                last_pe = None
                if gi in (G_Q0, G_Q1, G_H, G_B, G_C, G_MQ, G_GC):
                    for sub in range(4):
                        def mm(e, out, c, st, sp_, r=r, sub=sub):
                            return e.matmul(out, lhsT=wring[r][:, c, sub * 128:sub * 128 + 128], rhs=xz[:, c, 2:TW + 2],
                                            start=st, stop=sp_)
                        if gi in (G_Q0, G_Q1):
                            hh = (gi - G_Q0) * 4 + sub

                            def ev(bk, tpe, hh=hh):
                                return P.op("act", lambda e: e.mul(out=qT[:, hh, :], in_=bk, mul=0.125), [tpe])
                        elif gi == G_MQ:
                            def ev(bk, tpe, sub=sub):
                                return P.op("act", lambda e: e.mul(out=mqT[:, sub, :], in_=bk, mul=128.0 ** -0.5), [tpe])
                        elif gi == G_H:
                            def ev(bk, tpe, sub=sub):
                                return P.op("dve", lambda e: e.tensor_copy(out=hT[:, sub, 2:TW + 2], in_=bk),
                                            [tpe] + prev_slot_done)
                        elif gi == G_C:
                            def ev(bk, tpe, sub=sub):
                                return P.op("dve", lambda e: e.tensor_copy(out=CT[:, sub, 2:TW + 2], in_=bk),
                                            [tpe] + prev_slot_done)
                        elif gi == G_B:
                            def ev(bk, tpe, sub=sub):
                                return P.op("dve", lambda e: e.tensor_copy(out=BT[:, sub, :], in_=bk),
                                            [tpe] + prev_slot_done)
                        else:
                            def ev(bk, tpe, sub=sub):
                                return P.op("act", lambda e: e.activation(out=sgcT[:, sub, :], in_=bk, func=AF.Silu),
                                            [tpe] + prev_slot_done)
                        last_pe, t_e = job(mm, TW, ev, t_x + [t_w])
                        if gi in (G_Q0, G_Q1, G_MQ):
                            t_q.append(t_e)
                        else:
                            t_conv_in[(gi, sub)] = t_e
                        if gi in (G_H, G_C):
                            hidx = (0 if gi == G_H else 4) + sub
                            for c in range(NCH):
                                last_pe = P.op("pe", lambda e, c=c, r=r, sub=sub, hidx=hidx: e.matmul(
                                    psT[:, 2 * hidx:2 * hidx + 2], lhsT=wring[r][:, c, sub * 128:sub * 128 + 128],
                                    rhs=xz[:, c, 0:2], start=(c == 0), stop=(c == NCH - 1)),
                                    [halo_free[0]] if c == 0 else (), sig=("chain" if c == NCH - 1 else None))
                            tgt = hT if gi == G_H else CT
                            halo_free[0] = P.op("dve", lambda e, tgt=tgt, sub=sub, hidx=hidx: e.tensor_copy(
                                out=tgt[:, sub, 0:2], in_=psT[:, 2 * hidx:2 * hidx + 2]), [last_pe] + prev_slot_done)
                            t_conv_in[(gi, sub, "halo")] = halo_free[0]
                else:
                    for s in range(4):
                        def mm(e, out, c, st, sp_, r=r, s=s):
                            return e.matmul(out, lhsT=xz[:, c, 2 + s * 128:2 + s * 128 + 128], rhs=wring[r][:, c, :],
                                            start=st, stop=sp_)
                        if gi in (G_GD0, G_GD1):
                            g = gi - G_GD0

                            def ev(bk, tpe, s=s, g=g):
                                t1 = P.op("act", lambda e: e.activation(out=sgd[:, s, g * 512:g * 512 + 512], in_=bk,
                                                                        func=AF.Silu), [tpe])
                                t2 = P.op("dve", lambda e: e.tensor_tensor(out=sgd[:, s, g * 512:g * 512 + 512],
                                                                            in0=sgd[:, s, g * 512:g * 512 + 512],
                                                                            in1=gs4[:], op=ALU.mult), [t1])
                                t_gate.append(t2)
                                return t1
                        else:
                            def ev(bk, tpe, s=s):
                                return P.op("act", lambda e: e.activation(out=sgm[:, s, :], in_=bk, func=AF.Silu),
                                            [tpe])
                        last_pe, t_e = job(mm, 512, ev, t_x + [t_w])
                        t_gate.append(t_e)
                wr_free[r] = last_pe
                if oi + 1 < len(order):
                    nxt = load_w(order[oi + 1])
            xz_free = [last_pe]
            w_o = [load_w(G_O0), load_w(G_O0 + 1)]
            t_cv = []

            def post_mem(i=i):
                nonlocal t_x_next
                for cc in range(4):
                    dep = [t_conv_in[(G_H, cc)], t_conv_in[(G_C, cc)], t_conv_in[(G_H, cc, "halo")],
                           t_conv_in[(G_C, cc, "halo")], t_conv_in[(G_B, cc)], t_conv_in[(G_GC, cc)]]
                    t = P.op("dve", lambda e, cc=cc: e.tensor_tensor(out=utmp[:], in0=CT[:, cc, :], in1=hT[:, cc, :],
                                                                     op=ALU.mult), dep + t_cv[-1:])
                    if i == nslot - 1:
                        t_u = P.op("dve", lambda e, cc=cc: e.tensor_copy(out=convu[:, cc, :], in_=utmp[:, TW:TW + 2]), [t])
                        t_cv.append(t_u)
                    t = P.op("dve", lambda e, cc=cc: e.tensor_scalar(out=ctmp[:], in0=utmp[:, 0:TW],
                                                                     scalar1=cw[:, cc * 3:cc * 3 + 1], scalar2=None,
                                                                     op0=ALU.mult), [t])
                    for jx in (1, 2):
                        t = P.op("dve", lambda e, cc=cc, jx=jx: e.scalar_tensor_tensor(
                            out=ctmp[:], in0=utmp[:, jx:jx + TW], scalar=cw[:, cc * 3 + jx:cc * 3 + jx + 1], in1=ctmp[:],
                            op0=ALU.mult, op1=ALU.add), [t])
                    t = P.op("dve", lambda e, cc=cc: e.tensor_tensor(out=ctmp[:], in0=ctmp[:], in1=BT[:, cc, :], op=ALU.mult), [t])
                    t = P.op("dve", lambda e, cc=cc: e.tensor_tensor(out=zTc[:, cc, :], in0=ctmp[:], in1=sgcT[:, cc, :],
                                                                     op=ALU.mult), [t])
                    t_cv.append(t)
                if i == nslot - 1:
                    t_cv.append(P.dma("pool", convo_o, convu[:], t_cv, key="convo"))
                if i + 1 < nslot:
                    t_x_next = load_xz(4 * (i + 1) + 3)

            t_q = t_q + t_gate + list(t_conv_in.values())
            blocks = []
            for h in range(H):
                for kc in range(T + 1):
                    for kbl in range(4):
                        blocks.append((h, kc, kbl))
            nb_total = len(blocks)
            t_ep = []
            kvslot = {}

            def issue_kv(h, kc):
                cidx = ncidx[0]
                ncidx[0] += 1
                sl = cidx % RK
                t1 = P.dma("sp", kring[:, sl, :], kts[kc, :, h, :], [kv_free[sl]], key="kv%d" % sl)
                t2 = P.dma("sp", vring[:, sl].rearrange("p s e -> p (s e)"), vxs[kc, :, h, :], [kv_free[sl]],
                           key="kv%d" % sl)
                kvslot[(h, kc)] = (sl, t2)

            chunk_list = [(h, kc) for h in range(H) for kc in range(T + 1)]
            kv_issued = [0]

            def ensure_kv(upto):
                while kv_issued[0] < min(upto, len(chunk_list)):
                    issue_kv(*chunk_list[kv_issued[0]])
                    kv_issued[0] += 1

            ensure_kv(RK - 1)

            def do_qk(bi):
                h, kc, kbl = blocks[bi]
                n = nblk[0] + bi
                buf = n % 2
                sl, t_kv = kvslot[(h, kc)]
                diag = (kc == T)
                a = kbl * 128 if diag else 0
                adds = []
                if diag:
                    adds.append((kbl, 0))
                    if kbl + 1 < 4:
                        adds.append((kbl + 1, 1))
                elif kc == T - 1 and kbl == 3:
                    adds.append((0, 1))
                deps = [t_kv] + t_q + (exp_tok.get(n - 2) and [exp_tok[n - 2]] or [])
                for m in range(2):
                    last = (m == 1 and not adds)
                    tq = P.op("pe", lambda e, m=m, buf=buf, sl=sl, kbl=kbl, a=a, h=h: e.matmul(
                        psS[:, buf * 2 + m, a:TW], lhsT=kring[64 * m:64 * m + 64, sl, kbl * 128:kbl * 128 + 128],
                        rhs=qT[64 * m:64 * m + 64, h, a:TW], start=True, stop=True, skip_group_check=True),
                        deps if m == 0 else (), sig=(sem_qk if last else None))
                for ai, (s_, kind) in enumerate(adds):
                    for m in range(2):
                        for part in range(2):
                            last = (ai == len(adds) - 1 and m == 1 and part == 1)
                            tq = P.op("pe", lambda e, m=m, buf=buf, s_=s_, kind=kind, part=part, h=h: e.matmul(
                                psS[:, buf * 2 + m, s_ * 128:s_ * 128 + 128], lhsT=ident[:],
                                rhs=biasT[:, h, kind, part, :], start=False, stop=True, skip_group_check=True),
                                (), sig=(sem_qk if last else None))
                return tq, a

            qk_info = {}
            deferred = []
            cur_bi = [0]

            def do_exp(bi):
                h, kc, kbl = blocks[bi]
                n = nblk[0] + bi
                tq, a = qk_info[bi]
                deps = [tq]
                if n - RP in pv_tok:
                    deps.append(pv_tok[n - RP])
                exp_tok[n] = P.op("act", lambda e, n=n, a=a, h=h: e.activation(
                    out=pring[:, n % RP, :, a:TW], in_=psS[:, (n % 2) * 2:(n % 2) * 2 + 2, a:TW], func=AF.Exp,
                    bias=cfar[:, h:h + 1], scale=1.0), deps, sig=sem_exp)

            def do_pv(bi):
                h, kc, kbl = blocks[bi]
                n = nblk[0] + bi
                sl, t_kv = kvslot[(h, kc)]
                diag = (kc == T)
                first = (kc == 0 and kbl == 0)
                lastb = (kc == T and kbl == 3)
                s0 = kbl if diag else 0
                ops = [(s, m) for s in range(s0, 4) for m in range(2)]
                for oi_, (s, m) in enumerate(ops):
                    a_ = s * 2 + m
                    deps = []
                    if oi_ == 0:
                        deps = [exp_tok[n]]
                        if first:
                            deps.append(acc_free[0])
                    tp_ = P.op("pe", lambda e, s=s, m=m, a_=a_, n=n, sl=sl, kbl=kbl, first=first, lastb=lastb: e.matmul(
                        psA[:, a_ // 3, (a_ % 3) * ACCW:(a_ % 3) * ACCW + 129],
                        lhsT=pring[:, n % RP, m, s * 128:s * 128 + 128], rhs=vring[:, sl, kbl, :],
                        start=(first and a_ % 3 == 0), stop=lastb, skip_group_check=True),
                        deps, sig=(sem_pv if oi_ == len(ops) - 1 else None))
                pv_tok[n] = tp_
                if kbl == 3:
                    kv_free[sl] = tp_
                    ensure_kv(kv_issued[0] + 1)
                if lastb:
                    epilogue(h, tp_)

            def epilogue(h, t_last):
                deps = [t_last] + t_gate
                tcs = []
                for bk_ in range(3):
                    na = 3 if bk_ < 2 else 2
                    tcs.append(P.op("dve", lambda e, bk_=bk_, na=na: e.tensor_copy(
                        out=accs[:, bk_ * 3:bk_ * 3 + na, :],
                        in_=psA[:, bk_, 0:na * ACCW].rearrange("p (a w) -> p a w", w=ACCW)), deps + t_ep[-1:]))
                acc_free[0] = tcs[-1]
                t = P.op("dve", lambda e: e.reciprocal(out=rec[:], in_=accs[:, :, 128]), tcs)
                t = P.op("dve", lambda e: e.tensor_scalar(out=nr2[:], in0=rec[:].rearrange("p (s m) -> p s m", m=2)[:, :, 1],
                                                          scalar1=nlam[:, 0:1], scalar2=None, op0=ALU.mult), [t, t_nlam])
                for s in range(4):
                    t = P.op("dve", lambda e, s=s: e.tensor_scalar(out=ttmp[:], in0=accs[:, 2 * s + 1, 0:128],
                                                                   scalar1=nr2[:, s:s + 1], scalar2=None, op0=ALU.mult), [t])
                    t = P.op("dve", lambda e, s=s: e.scalar_tensor_tensor(
                        out=otmp[:, s, :], in0=accs[:, 2 * s, 0:128], scalar=rec[:, 2 * s:2 * s + 1], in1=ttmp[:],
                        op0=ALU.mult, op1=ALU.add), [t])
                    t = P.op("dve", lambda e, s=s: e.tensor_tensor(out=sqtmp[:], in0=otmp[:, s, :], in1=otmp[:, s, :],
                                                                   op=ALU.mult), [t])
                    t = P.op("dve", lambda e, s=s: e.reduce_sum(out=ssq[:, s:s + 1], in_=sqtmp[:], axis=AX.X), [t])
                t_ssq = t

                def part2(h=h, t_ssq=t_ssq):
                    t = P.op("act", lambda e: e.activation(out=rstd[:], in_=ssq[:], func=AF.Ln, bias=epsb[:, 0:1],
                                                           scale=1.0 / 128.0), [t_ssq])
                    t = P.op("act", lambda e: e.activation(out=rstd[:], in_=rstd[:], func=AF.Exp, scale=-0.5), [t])
                    for s in range(4):
                        t = P.op("dve", lambda e, s=s, h=h: e.scalar_tensor_tensor(
                            out=ztok[:, s, h * 128:h * 128 + 128], in0=otmp[:, s, :], scalar=rstd[:, s:s + 1],
                            in1=sgd[:, s, h * 128:h * 128 + 128], op0=ALU.mult, op1=ALU.mult), [t])
                    t_ep.append(t)
                deferred.append([cur_bi[0] + 6, part2])

            for mh in range(MH):
                for blk in range(2):
                    n = nblk[0]
                    nblk[0] += 1
                    buf = n % 2
                    deps = t_q + ([exp_tok[n - 2]] if (n - 2) in exp_tok else [])
                    tq = P.op("pe", lambda e, buf=buf, mh=mh, blk=blk: e.matmul(
                        psS[:, buf * 2, :], lhsT=mkT[:, mh, blk * 128:blk * 128 + 128], rhs=mqT[:, mh, :],
                        start=True, stop=True, skip_group_check=True), deps, sig=sem_qk)
                    deps = [tq]
                    if n - RP in pv_tok:
                        deps.append(pv_tok[n - RP])
                    exp_tok[n] = P.op("act", lambda e, n=n, buf=buf: e.activation(
                        out=pring[:, n % RP, 0, :], in_=psS[:, buf * 2, :], func=AF.Exp), deps, sig=sem_exp)
                    for s in range(4):
                        deps = []
                        if s == 0:
                            deps = [exp_tok[n]]
                            if blk == 0:
                                deps.append(acc_free[0])
                        tp_ = P.op("pe", lambda e, s=s, n=n, mh=mh, blk=blk: e.matmul(
                            psA[:, s // 3, (s % 3) * ACCW:(s % 3) * ACCW + 129],
                            lhsT=pring[:, n % RP, 0, s * 128:s * 128 + 128], rhs=mvx[:, blk, mh, :],
                            start=(blk == 0 and s % 3 == 0), stop=(blk == 1), skip_group_check=True),
                            deps, sig=(sem_pv if s == 3 else None))
                    pv_tok[n] = tp_
                tcs = [P.op("dve", lambda e: e.tensor_copy(
                    out=accs[:, 0:3, :], in_=psA[:, 0, 0:3 * ACCW].rearrange("p (a w) -> p a w", w=ACCW)),
                    [tp_] + t_gate + t_ep[-1:]),
                    P.op("dve", lambda e: e.tensor_copy(
                        out=accs[:, 3:4, :], in_=psA[:, 1, 0:ACCW].rearrange("p (a w) -> p a w", w=ACCW)), [tp_])]
                acc_free[0] = tcs[-1]
                t = P.op("dve", lambda e: e.reciprocal(out=rec[:, 0:4], in_=accs[:, 0:4, 128]), tcs)
                for s in range(4):
                    t = P.op("dve", lambda e, s=s, mh=mh: e.scalar_tensor_tensor(
                        out=ztok[:, s, 1024 + mh * 128:1024 + mh * 128 + 128], in0=accs[:, s, 0:128],
                        scalar=rec[:, s:s + 1], in1=sgm[:, s, mh * 128:mh * 128 + 128], op0=ALU.mult, op1=ALU.mult), [t])
                t_ep.append(t)

            post_mem()
            for b0 in range(min(2, nb_total)):
                qk_info[b0] = do_qk(b0)
                do_exp(b0)
            for bi in range(nb_total):
                cur_bi[0] = bi
                if bi + 2 < nb_total:
                    qk_info[bi + 2] = do_qk(bi + 2)
                    do_exp(bi + 2)
                while deferred and deferred[0][0] <= bi:
                    deferred.pop(0)[1]()
                do_pv(bi)
            while deferred:
                deferred.pop(0)[1]()
            nblk[0] += nb_total

            t_zt = []
            t_tp_free = [None]
            for s in range(4):
                for grp in range(3):
                    for k4 in range(4):
                        cidx_ = grp * 4 + k4
                        tt = P.op("pe", lambda e, s=s, cidx_=cidx_, k4=k4: e.transpose(
                            psTb[:, k4 * 128:k4 * 128 + 128], ztok[:, s, cidx_ * 128:cidx_ * 128 + 128], ident[:]),
                            (t_ep[-1:] + [t_tp_free[0]]) if k4 == 0 else (), sig=("chain" if k4 == 3 else None))
                    dst0 = grp * 4 if grp < 2 else 12
                    tcp = P.op("dve", lambda e, s=s, dst0=dst0: e.tensor_copy(
                        out=zT[:, dst0:dst0 + 4, s * 128:s * 128 + 128],
                        in_=psTb[:, 0:512].rearrange("p (k t) -> p k t", t=128)), [tt])
                    t_tp_free[0] = tcp
                    t_zt.append(tcp)
            t_zt.append(P.op("pool", lambda e: e.tensor_copy(out=zT[:, 8:12, 0:TW], in_=zTc[:]), t_cv + t_ep[-1:]))

            t_y = {}
            xr_free = [None, None]
            xcnt = 0
            for g in range(4):
                if g < 2:
                    r, t_w = w_o[g]
                else:
                    r, t_w = load_w(G_O0 + g)
                for s in range(4):
                    xsel = xcnt % 2
                    xcnt += 1
                    t_xr = P.dma("sp", xres[xsel][:], xown[i * TW + s * 128:i * TW + s * 128 + 128, g * 512:g * 512 + 512],
                                 [xr_free[xsel]], key="xr%d" % xsel)

                    def mm(e, out, c, st, sp_, r=r, s=s):
                        return e.matmul(out, lhsT=zT[:, c, s * 128:s * 128 + 128], rhs=wring[r][:, c, :], start=st, stop=sp_)

                    def ev(bk, tpe, s=s, g=g, xsel=xsel, t_xr=t_xr):
                        return P.op("dve", lambda e: e.scalar_tensor_tensor(
                            out=ysb[:, s, g * 512:g * 512 + 512], in0=xres[xsel][:], scalar=ALPHA, in1=bk,
                            op0=ALU.mult, op1=ALU.add), [tpe, t_xr, y_st_tok[0]] + t_cv)
                    last_pe, t_e = job(mm, 512, ev, t_zt + [t_w])
                    xr_free[xsel] = t_e
                    t_y[(s, g)] = t_e
                wr_free[r] = last_pe
            t_done = []
            t = None
            for s in range(4):
                for g in range(4):
                    t = P.op("dve", lambda e, s=s, g=g: e.bn_stats(out=bnst4[:, s, g, :], in_=ysb[:, s, g * 512:g * 512 + 512]),
                             [t_y[(s, g)], t])
                t = P.op("dve", lambda e, s=s: e.bn_aggr(out=mv4[:, s, :], in_=bnst4[:, s].rearrange("p a b -> p (a b)")), [t])
            t_r = P.op("act", lambda e: e.activation(out=lnr4[:], in_=mv4[:, :, 1], func=AF.Ln, bias=epsb[:, 0:1], scale=1.0), [t])
            t_r = P.op("act", lambda e: e.activation(out=lnr4[:], in_=lnr4[:], func=AF.Exp, scale=-0.5), [t_r])
            for s in range(4):
                t = P.op("dve", lambda e, s=s: e.scalar_tensor_tensor(out=ysb[:, s, :], in0=ysb[:, s, :], scalar=mv4[:, s, 0:1],
                                                                      in1=lngb, op0=ALU.subtract, op1=ALU.mult), [t, t_lng])
                t = P.op("dve", lambda e, s=s: e.scalar_tensor_tensor(out=ysb[:, s, :], in0=ysb[:, s, :], scalar=lnr4[:, s:s + 1],
                                                                      in1=lnb_sb[:], op0=ALU.mult, op1=ALU.add), [t, t_lnb, t_r])
                t_o = P.dma("pool", y_o[i * TW + s * 128:i * TW + s * 128 + 128, :], ysb[:, s, :], [t], key="yout")
                t_done.append(t_o)
            y_st_tok[0] = t_done[-1]
            prev_slot_done = t_done + [t]
        P.emit()
        P.op("dve", lambda e: e.memset(lnr[:], 0.0), ())
        P.emit()
        SB.__exit__(None, None, None)
    return nc


_CACHE = {}


def _host_consts():
    half, max_exact = 16, 8

    def bucket(rel):
        ret = np.where(rel > 0, half, 0)
        n = np.abs(rel)
        nf = np.maximum(n, 1).astype(np.float32)
        large = max_exact + (np.log(nf / max_exact) / math.log(128 / max_exact) * (half - max_exact)).astype(np.int32)
        large = np.minimum(large, half - 1)
        return ret + np.where(n < max_exact, n, large)

    k = np.arange(128)[:, None]
    q = np.arange(128)[None, :]
    bd = bucket(k - q)
    bp = bucket(k - q - 128)
    bkt = np.concatenate([bd, bp], axis=1).astype(np.float32)
    dmask = np.where((k // 64) <= (q // 64), 0.0, NEG).astype(np.float32)
    return bkt, dmask


def kernel(x_prompt, x_sample, cache_diff_k, cache_diff_v, cache_conv, cache_mem_k, cache_mem_v,
           mem_prompt, rel_bias_table, w_in, w_mem_kv, conv_w, lambda_q1, lambda_k1, lambda_q2,
           lambda_k2, subln_g, w_out, ln_g, ln_b, _nslot=NSLOT, _sample=True):
    f = np.float32
    x_prompt = np.asarray(x_prompt, f)
    B, S, _ = x_prompt.shape
    key = (_nslot, _sample)
    if key not in _CACHE:
        _CACHE[key] = build(_nslot, _sample)
    nc = _CACHE[key]

    bkt, dmask = _host_consts()
    rep = lambda v, n=128: np.ascontiguousarray(np.broadcast_to(np.asarray(v, f).reshape(1, -1), (n, np.asarray(v).size)))
    shared = {
        "w_in": np.ascontiguousarray(np.asarray(w_in, f)[0]),
        "w_out": np.ascontiguousarray(np.asarray(w_out, f)[0]),
        "w_mem": np.ascontiguousarray(np.asarray(w_mem_kv, f)[0]),
        "ident": np.eye(128, dtype=f),
        "tab": rep(np.asarray(rel_bias_table, f).reshape(-1)),
        "lamv": rep(np.concatenate([np.asarray(a, f).reshape(-1) for a in
                                    (lambda_q1, lambda_k1, lambda_q2, lambda_k2)])),
        "gsub": rep(np.tile(np.asarray(subln_g, f).reshape(-1), 4)),
        "lng": rep(np.asarray(ln_g, f).reshape(-1)),
        "lnb": rep(np.asarray(ln_b, f).reshape(-1)),
        "convw": np.ascontiguousarray(np.asarray(conv_w, f)[0].reshape(3, 4, 128).transpose(2, 1, 0).reshape(128, 12)),
        "bkt": bkt,
        "dmask": dmask,
    }
    in_maps = []
    xTb = [np.ascontiguousarray(x_prompt[b].T) for b in range(B)]
    memTb = [np.ascontiguousarray(np.asarray(mem_prompt, f)[b].T) for b in range(B)]
    for c in range(NCORES):
        b, j = c // 4, c % 4
        xT = np.zeros((D, NT * TW), f)
        xT[:, (3 - j) * TW:(3 - j) * TW + S] = xTb[b]
        xown = np.concatenate([x_prompt[b, (4 * i + j) * TW:(4 * i + j + 1) * TW] for i in range(NSLOT)], axis=0)
        vm = np.zeros((NT, 32), f)
        vm[3 - j:3 - j + S // TW] = 1.0
        m = dict(shared)
        m["xT"] = xT
        m["xown"] = np.ascontiguousarray(xown)
        m["vmask"] = rep(vm.reshape(-1))
        m["memT"] = memTb[b]
        in_maps.append(m)
    if _sample:
        xs_all = np.asarray(x_sample, f)
        ck = np.asarray(cache_diff_k, f)[0]
        cvv = np.asarray(cache_diff_v, f)[0]
        cc_ = np.asarray(cache_conv, f)[0]
        cmk = np.asarray(cache_mem_k, f)[0]
        cmv = np.asarray(cache_mem_v, f)[0]
        for c in range(NCORES):
            sl = slice(c * NSTR, (c + 1) * NSTR)
            xs_c = xs_all[sl].reshape(NSTR * SS, D)
            m = in_maps[c]
            m["xs"] = np.ascontiguousarray(xs_c)
            m["xsT"] = np.ascontiguousarray(xs_c.T)
            m["ckT"] = np.ascontiguousarray(ck[sl].transpose(0, 2, 3, 1))
            m["cv"] = np.ascontiguousarray(cvv[sl].reshape(NSTR, PAST, 1024))
            m["cconv"] = np.ascontiguousarray(cc_[sl].reshape(NSTR, 2, 4, 128).transpose(3, 2, 0, 1))
            m["cmkT"] = np.ascontiguousarray(cmk[sl].transpose(0, 2, 3, 1))
            m["cmv"] = np.ascontiguousarray(cmv[sl].reshape(NSTR, NMEM, 512))
    res = run_bass_kernel_spmd(nc, in_maps, core_ids=list(range(NCORES)))
    R = res.results

    y = np.zeros((B, S, D), f)
    kp = np.zeros((1, B, S, H, 128), f)
    vp = np.zeros((1, B, S, H, 128), f)
    for c in range(NCORES):
        b, j = c // 4, c % 4
        for i in range(NSLOT):
            t = 4 * i + j
            y[b, t * TW:(t + 1) * TW] = R[c]["y"][i * TW:(i + 1) * TW]
            kp[0, b, t * TW:(t + 1) * TW] = R[c]["koT"][:, :, i * TW:(i + 1) * TW].transpose(2, 0, 1)
            vp[0, b, t * TW:(t + 1) * TW] = R[c]["vo"][i * TW:(i + 1) * TW].reshape(TW, H, 128)
    convp = np.zeros((1, B, 2, 512), f)
    mkp = np.zeros((1, B, NMEM, MH, 128), f)
    mvp = np.zeros((1, B, NMEM, MH, 128), f)
    for b in range(B):
        c = 4 * b + 3
        convp[0, b] = R[c]["convo"].transpose(2, 1, 0).reshape(2, 512)
        mkp[0, b] = R[c]["mkoT"].transpose(2, 0, 1)
        mvp[0, b] = R[c]["mvo"].reshape(NMEM, MH, 128)
    ys = np.zeros((32, SS, D), f)
    ks = np.zeros((1, 32, SS, H, 128), f)
    vs = np.zeros((1, 32, SS, H, 128), f)
    cs = np.zeros((1, 32, 2, 512), f)
    if _sample:
        for c in range(NCORES):
            sl = slice(c * NSTR, (c + 1) * NSTR)
            ys[sl] = R[c]["ys"].reshape(NSTR, SS, D)
            ks[0, sl] = R[c]["ksT"].transpose(2, 0, 1).reshape(NSTR, SS, H, 128)
            vs[0, sl] = R[c]["vs"].reshape(NSTR, SS, H, 128)
            cs[0, sl] = R[c]["convs"].transpose(1, 3, 2, 0).reshape(NSTR, 2, 512)
    return (y, ys, kp, vp, convp, mkp, mvp, ks, vs, cs)
```

```python
import math
from contextlib import ExitStack

import numpy as np
import concourse.bass as bass
import concourse.mybir as mybir
from concourse.bass_utils import run_bass_kernel_spmd

F32 = mybir.dt.float32
BF16 = mybir.dt.bfloat16
AF = mybir.ActivationFunctionType
ALU = mybir.AluOpType
AX = mybir.AxisListType

NCORES = 8
D = 2048
TW = 512
NT = 35
NSLOT = 8
NCH = 16
H = 8
MH = 4
NMEM = 256
EPS = 1e-5
ALPHA = 2.0 ** 0.25
LAM_INIT = 0.8 - 0.6 * math.exp(-0.0)
ACCW = 130
RK = 4
RP = 3
NEG = -30000.0
SS = 16
NSTR = 4
PAST = 1024

WG = [("w_in", 0), ("w_in", 512),
      ("w_in", 3072), ("w_in", 3584), ("w_in", 4096),
      ("w_in", 4608),
      ("w_in", 5120), ("w_in", 5632),
      ("w_in", 6144), ("w_in", 6656),
      ("w_out", 0), ("w_out", 512), ("w_out", 1024), ("w_out", 1536)]
G_Q0, G_Q1, G_H, G_B, G_C, G_MQ, G_GD0, G_GD1, G_GC, G_GM, G_O0 = range(11)


class Sem:
    def __init__(self, h):
        self.h = h
        self.v = 0


class Prog:
    ENG = ("pe", "act", "dve", "pool", "sp")

    def __init__(self, nc, stack):
        self.nc = nc
        self.stack = stack
        self.sems = []
        self.q = {e: [] for e in self.ENG}
        self.chain = {e: self.sem("ch_" + e) for e in self.ENG}
        self.barrier_toks = {e: [] for e in self.ENG}
        self.waited = {e: {} for e in self.ENG}
        self.dsems = {}

    def sem(self, name):
        s = Sem(self.stack.enter_context(self.nc.semaphore(name)))
        self.sems.append(s)
        return s

    def op(self, eng, fn, after=(), sig="chain", amt=1):
        waits = [t for t in after if t is not None]
        if self.barrier_toks[eng]:
            waits = list(self.barrier_toks[eng]) + waits
            self.barrier_toks[eng] = []
        w2 = []
        wd = self.waited[eng]
        for (s, v) in waits:
            if wd.get(id(s), 0) >= v:
                continue
            wd[id(s)] = v
            w2.append((s, v))
        if sig == "chain":
            sig = self.chain[eng]
        tok = None
        if sig is not None:
            sig.v += amt
            tok = (sig, sig.v)
        self.q[eng].append((w2, fn, sig, amt))
        return tok

    def dma(self, eng, out, in_, after=(), key=None):
        assert key is not None
        if key not in self.dsems:
            self.dsems[key] = self.sem("d_" + key)
        return self.op(eng, lambda e: e.dma_start(out=out, in_=in_), after, self.dsems[key], 16)

    def emit(self):
        nc = self.nc
        q = self.q

        def run(e, lst):
            for (waits, fn, sig, amt) in lst:
                for (s, v) in waits:
                    e.wait_ge(s.h, v)
                inst = fn(e)
                if sig is not None:
                    inst.then_inc(sig.h, amt)

        with nc.Block() as block:
            if q["pe"]:
                @block.tensor
                def _(e):
                    run(e, q["pe"])
            if q["act"]:
                @block.scalar
                def _(e):
                    run(e, q["act"])
            if q["dve"]:
                @block.vector
                def _(e):
                    run(e, q["dve"])
            if q["pool"]:
                @block.gpsimd
                def _(e):
                    run(e, q["pool"])
            if q["sp"]:
                @block.sync
                def _(e):
                    run(e, q["sp"])
        self.q = {e: [] for e in self.ENG}
        toks = [(s, s.v) for s in self.sems if s.v > 0]
        for e in self.ENG:
            self.barrier_toks[e] = list(toks)


def build(nslot=NSLOT, do_sample=True):
    nc = bass.Bass("TRN2", target_bir_lowering=False)
    top = ExitStack()
    with top:
        P = Prog(nc, top)

        def din(name, shape, dt=F32):
            return nc.dram_tensor(name, list(shape), dt, kind="ExternalInput").ap()

        def dout(name, shape, dt=F32):
            return nc.dram_tensor(name, list(shape), dt, kind="ExternalOutput").ap()

        def dscr(name, shape, dt):
            return nc.dram_tensor(name, list(shape), dt).ap()

        def sb(stack, name, shape, dt):
            return stack.enter_context(nc.sbuf_tensor("s_" + name, list(shape), dt))

        xT = din("xT", [D, NT * TW])
        xown = din("xown", [NSLOT * TW, D])
        vmask = din("vmask", [128, NT * 32])
        w_srcs = {"w_in": din("w_in", [D, 7168]), "w_out": din("w_out", [D, D])}
        w_mem = din("w_mem", [D, 1024])
        memT = din("memT", [D, NMEM])
        ident_d = din("ident", [128, 128])
        tab_d = din("tab", [128, 256])
        lam_d = din("lamv", [128, 256])
        gsub_d = din("gsub", [128, 512])
        lng_d = din("lng", [128, D])
        lnb_d = din("lnb", [128, D])
        cw_d = din("convw", [128, 12])
        bkt_d = din("bkt", [128, 256])
        dmask_d = din("dmask", [128, 128])

        y_o = dout("y", [NSLOT * TW, D])
        koT_o = dout("koT", [H, 128, NSLOT * TW])
        vo_o = dout("vo", [NSLOT * TW, 1024])
        convo_o = dout("convo", [128, 4, 2])
        mkoT_o = dout("mkoT", [MH, 128, NMEM])
        mvo_o = dout("mvo", [NMEM, 512])

        if do_sample:
            xsT_d = din("xsT", [D, NSTR * SS])
            xs_d = din("xs", [NSTR * SS, D])
            ckT_d = din("ckT", [NSTR, H, 128, PAST])
            cv_d = din("cv", [NSTR, PAST, 1024])
            cconv_d = din("cconv", [128, 4, NSTR, 2])
            cmkT_d = din("cmkT", [NSTR, MH, 128, NMEM])
            cmv_d = din("cmv", [NSTR, NMEM, 512])
            ys_o = dout("ys", [NSTR * SS, D])
            ksT_o = dout("ksT", [H, 128, NSTR * SS])
            vs_o = dout("vs", [NSTR * SS, 1024])
            convs_o = dout("convs", [128, NSTR, 4, 2])

        wsc = dscr("wsc", [len(WG) + 4, 128, NCH, 512], BF16)
        kts = dscr("kts", [NT, 128, H, 512], BF16)
        vxs = dscr("vxs", [NT, 128, H, 4 * 129], BF16)

        psS = top.enter_context(nc.psum_tensor("psS", [128, 4, 512], F32))
        psA = top.enter_context(nc.psum_tensor("psA", [128, 3, 512], F32))
        psT = top.enter_context(nc.psum_tensor("psT", [128, 512], F32))

        psTb = psT[:].bitcast(BF16)

        def bank(b):
            return psS[:, b, :] if b < 4 else psA[:, b - 4, :]

        ident = sb(top, "identb", [128, 128], BF16)
        tab = sb(top, "tab", [128, 256], F32)
        cfar = sb(top, "cfar", [128, 8], F32)
        biasT = sb(top, "biasT", [128, H, 2, 2, 128], BF16)
        gs4 = sb(top, "gs4", [128, 512], F32)
        cw = sb(top, "cw", [128, 12], F32)
        nlam = sb(top, "nlam", [128, 1], F32)
        epsb = sb(top, "epsb", [128, 1], F32)
        mkT = sb(top, "mkT", [128, MH, NMEM], BF16)
        mvx = sb(top, "mvx", [128, 2, MH, 129], BF16)

        SA = ExitStack()
        SA.__enter__()
        wkv = sb(SA, "wkv", [128, 4, NCH, 512], BF16)
        bkt = sb(SA, "bkt", [128, 256], F32)
        dmask = sb(SA, "dmask", [128, 128], F32)
        eqm = sb(SA, "eqm", [128, 128], F32)
        bacc = sb(SA, "bacc", [128, 128], F32)
        bhi = sb(SA, "bhi", [128, 128], F32)
        S0 = ExitStack()
        S0.__enter__()
        stg = [sb(S0, "stg%d" % i, [128, NCH, 512], F32) for i in range(2)]
        wbf = [sb(S0, "wbf%d" % i, [128, NCH, 512], BF16) for i in range(2)]
        identf = sb(S0, "identf", [128, 128], F32)
        lamv = sb(S0, "lamvs", [128, 256], F32)
        ltmp = sb(S0, "ltmp", [128, 64], F32)
        lsum = sb(S0, "lsum", [128, 4], F32)
        memb = sb(S0, "memb", [128, NCH, NMEM], BF16)
        mkf = sb(S0, "mkf", [128, MH, NMEM], F32)
        mvf = sb(S0, "mvf", [128, 2, 512], F32)

        P.dma("sp", identf[:], ident_d, key="c0")
        P.dma("sp", tab[:], tab_d, key="c0")
        P.dma("sp", lamv[:], lam_d, key="c0")
        P.dma("sp", gs4[:], gsub_d, key="c0")
        P.dma("sp", cw[:], cw_d, key="c0")
        P.dma("sp", bkt[:], bkt_d, key="c0")
        t_c0 = P.dma("sp", dmask[:], dmask_d, key="c0")
        t_idb = P.op("dve", lambda e: e.tensor_copy(out=ident[:], in_=identf[:]), [t_c0])
        P.op("dve", lambda e: e.memset(epsb[:], EPS), ())
        t_gs2 = P.op("dve", lambda e: e.tensor_scalar(out=gs4[:], in0=gs4[:], scalar1=1.0 - LAM_INIT,
                                                      scalar2=None, op0=ALU.mult), [t_c0])
        tl = t_c0
        for k in range(2):
            tl = P.op("dve", lambda e, k=k: e.tensor_tensor(out=ltmp[:], in0=lamv[:, 128 * k:128 * k + 64],
                                                            in1=lamv[:, 128 * k + 64:128 * k + 128], op=ALU.mult), [tl])
            tl = P.op("dve", lambda e, k=k: e.reduce_sum(out=lsum[:, k:k + 1], in_=ltmp[:], axis=AX.X), [tl])
        tl = P.op("act", lambda e: e.activation(out=lsum[:, 2:4], in_=lsum[:, 0:2], func=AF.Exp), [tl])
        tl = P.op("dve", lambda e: e.tensor_tensor(out=nlam[:], in0=lsum[:, 3:4], in1=lsum[:, 2:3], op=ALU.subtract), [tl])
        t_nlam = P.op("dve", lambda e: e.tensor_scalar(out=nlam[:], in0=nlam[:], scalar1=-LAM_INIT, scalar2=None,
                                                       op0=ALU.add), [tl])
        t_cf = P.op("dve", lambda e: e.tensor_copy(out=cfar[:], in_=tab[:, 120:128]), [t_c0])
        def wsrc(name, col):
            return w_srcs[name].rearrange("(c p) n -> p c n", p=128)[:, :, col:col + 512]

        jobs = [("kv", k, w_srcs["w_in"].rearrange("(c p) n -> p c n", p=128)[:, :, 1024 + 512 * k:1536 + 512 * k])
                for k in range(4)]
        jobs += [("mem", k, w_mem.rearrange("(c p) n -> p c n", p=128)[:, :, 512 * k:512 * k + 512]) for k in range(2)]
        stg_free = [None, None]
        wbf_free = [None, None]
        memw_tok = [None, None]
        for n, (kind, idx, src) in enumerate(jobs):
            bsel = n % 2
            t_ld = P.dma("sp", stg[bsel][:], src, [stg_free[bsel]], key="stg%d" % bsel)
            if kind == "kv":
                dst = wkv[:, idx]
            else:
                dst = wbf[bsel][:]
            if n % 2 == 0:
                t_c = P.op("dve", lambda e, dst=dst, bsel=bsel: e.tensor_copy(out=dst, in_=stg[bsel][:]),
                           [t_ld, wbf_free[bsel]])
            else:
                t_c = P.op("act", lambda e, dst=dst, bsel=bsel: e.activation(out=dst, in_=stg[bsel][:], func=AF.Copy),
                           [t_ld, wbf_free[bsel]])
            stg_free[bsel] = t_c
            if kind == "scr":
                wbf_free[bsel] = P.dma("act", wsc[idx], wbf[bsel][:], [t_c], key="wst%d" % bsel)
            elif kind == "kv":
                P.dma("act", wsc[len(WG) + idx], wkv[:, idx], [t_c], key="wstkv")
            elif kind == "mem":
                memw_tok[idx] = (t_c, bsel)

        memstg = stg[0][:, :, 0:NMEM]
        t_m = P.dma("sp", memstg, memT.rearrange("(c p) n -> p c n", p=128), [stg_free[0]], key="stg0")
        t_mb = P.op("dve", lambda e: e.tensor_copy(out=memb[:], in_=memstg), [t_m])
        bank_free = [None] * 7
        jb = 0
        wmk = wbf[memw_tok[0][1]]
        wmv = wbf[memw_tok[1][1]]
        for mh in range(MH):
            b = jb % 7
            jb += 1
            for c in range(NCH):
                t_pe = P.op("pe", lambda e, b=b, c=c, mh=mh: e.matmul(
                    bank(b)[:, 0:NMEM], lhsT=wmk[:, c, mh * 128:mh * 128 + 128], rhs=memb[:, c, :],
                    start=(c == 0), stop=(c == NCH - 1)),
                    [t_mb, memw_tok[0][0], bank_free[b], t_idb] if c == 0 else (), sig=("chain" if c == NCH - 1 else None))
            t1 = P.op("act", lambda e, b=b, mh=mh: e.activation(out=mkT[:, mh, :], in_=bank(b)[:, 0:NMEM], func=AF.Copy), [t_pe])
            t2 = P.op("dve", lambda e, b=b, mh=mh: e.tensor_copy(out=mkf[:, mh, :], in_=bank(b)[:, 0:NMEM]), [t_pe, t1])
            bank_free[b] = t2
        P.dma("sp", mkoT_o.rearrange("h p n -> p h n"), mkf[:], [t2], key="mko")
        for blk in range(2):
            b = jb % 7
            jb += 1
            for c in range(NCH):
                t_pe = P.op("pe", lambda e, b=b, c=c, blk=blk: e.matmul(
                    bank(b), lhsT=memb[:, c, blk * 128:blk * 128 + 128], rhs=wmv[:, c, :],
                    start=(c == 0), stop=(c == NCH - 1)),
                    [t_mb, memw_tok[1][0], bank_free[b]] if c == 0 else (), sig=("chain" if c == NCH - 1 else None))
            t1 = P.op("act", lambda e, b=b, blk=blk: e.activation(
                out=mvx[:, blk, :, 0:128], in_=bank(b).rearrange("p (h e) -> p h e", e=128), func=AF.Copy), [t_pe])
            t2 = P.op("dve", lambda e, b=b, blk=blk: e.tensor_copy(out=mvf[:, blk, :], in_=bank(b)), [t_pe, t1])
            bank_free[b] = t2
        for blk in range(2):
            t2 = P.op("dve", lambda e, blk=blk: e.memset(mvx[:, blk, :, 128:129], 1.0), [t2])
        P.dma("sp", mvo_o.rearrange("(b p) n -> p b n", p=128), mvf[:], [t2], key="mvo")
        P.emit()
        S0.__exit__(None, None, None)

        xstg = [sb(SA, "xstg%d" % i, [128, 4, TW], F32) for i in range(2)]
        xb = [sb(SA, "xb%d" % i, [128, NCH, TW], BF16) for i in range(2)]
        ktst = [sb(SA, "ktst%d" % i, [128, H, TW], BF16) for i in range(2)]
        vxst = [sb(SA, "vxst%d" % i, [128, H, 4, 129], BF16) for i in range(2)]
        vm = sb(SA, "vm", [128, NT * 32], F32)
        kof = [sb(SA, "kof%d" % i, [128, TW], F32) for i in range(2)]
        vof = [sb(SA, "vof%d" % i, [128, 512], F32) for i in range(2)]

        def bias_gen():
            tp = t_c0
            for h in range(H):
                for kind in range(2):
                    nb = range(32) if kind == 0 else range(1, 16)
                    first = True
                    for b in nb:
                        tp = P.op("dve", lambda e, b=b, kind=kind: e.tensor_single_scalar(
                            out=eqm[:], in_=bkt[:, 128 * kind:128 * kind + 128], scalar=float(b), op=ALU.is_equal), [tp])
                        if first:
                            tp = P.op("dve", lambda e, b=b, h=h: e.tensor_scalar(
                                out=bacc[:], in0=eqm[:], scalar1=tab[:, b * 8 + h:b * 8 + h + 1], scalar2=None,
                                op0=ALU.mult), [tp])
                            first = False
                        else:
                            tp = P.op("dve", lambda e, b=b, h=h: e.scalar_tensor_tensor(
                                out=bacc[:], in0=eqm[:], scalar=tab[:, b * 8 + h:b * 8 + h + 1], in1=bacc[:],
                                op0=ALU.mult, op1=ALU.add), [tp])
                        yield
                    tp = P.op("dve", lambda e, h=h: e.tensor_scalar(
                        out=bacc[:], in0=bacc[:], scalar1=tab[:, 120 + h:121 + h], scalar2=None, op0=ALU.subtract), [tp])
                    if kind == 0:
                        tp = P.op("dve", lambda e: e.tensor_tensor(out=bacc[:], in0=bacc[:], in1=dmask[:], op=ALU.add), [tp])
                    tp = P.op("dve", lambda e, h=h, kind=kind: e.tensor_copy(out=biasT[:, h, kind, 0, :], in_=bacc[:]), [tp])
                    tp = P.op("dve", lambda e, h=h, kind=kind: e.tensor_copy(out=bhi[:], in_=biasT[:, h, kind, 0, :]), [tp])
                    tp = P.op("dve", lambda e: e.tensor_tensor(out=bacc[:], in0=bacc[:], in1=bhi[:], op=ALU.subtract), [tp])
                    tp = P.op("dve", lambda e, h=h, kind=kind: e.tensor_copy(out=biasT[:, h, kind, 1, :], in_=bacc[:]), [tp])
                    yield
        bgen = bias_gen()
        wq_f = [sb(SA, "wqf%d" % i, [128, NCH, 128], F32) for i in range(2)]
        wq_b = [sb(SA, "wqb%d" % i, [128, NCH, 128], BF16) for i in range(2)]

        def wcast_gen():
            f_free = [None, None]
            b_free = [None, None]
            n = 0
            for gi in range(len(WG)):
                name, col = WG[gi]
                srcv = w_srcs[name].rearrange("(c p) n -> p c n", p=128)
                for qq in range(4):
                    k = n % 2
                    n += 1
                    t_ld = P.dma("sp", wq_f[k][:], srcv[:, :, col + qq * 128:col + qq * 128 + 128], [f_free[k]],
                                 key="wqf%d" % k)
                    t_c = P.op("act", lambda e, k=k: e.activation(out=wq_b[k][:], in_=wq_f[k][:], func=AF.Copy),
                               [t_ld, b_free[k]])
                    f_free[k] = t_c
                    b_free[k] = P.dma("act", wsc[gi, :, :, qq * 128:qq * 128 + 128], wq_b[k][:], [t_c], key="wqs%d" % k)
                    yield
        wgen = wcast_gen()
        t_vm = P.dma("sp", vm[:], vmask, key="vm")
        xTr = xT.rearrange("(c p) t -> p c t", p=128)
        xstg_free = [None] * 2
        xb_free = [None, None]
        kt_free = [None, None]
        vx_free = [None, None]
        kof_free = [None, None]
        vof_free = [None, None]
        nko = 0
        nvo = 0
        bank_free = [None] * 7
        jb = 0
        npiece = 0
        ntile = NT if nslot == NSLOT else 4 * (nslot - 1) + 4
        def load_x(T):
            nonlocal npiece
            xs_ = T % 2
            t_x = []
            for pc in range(4):
                st = npiece % 2
                npiece += 1
                t_ld = P.dma("sp", xstg[st][:], xTr[:, 4 * pc:4 * pc + 4, T * TW:(T + 1) * TW], [xstg_free[st]],
                             key="xstg%d" % st)
                t_c = P.op("dve", lambda e, st=st, pc=pc, xs_=xs_: e.tensor_copy(
                    out=xb[xs_][:, 4 * pc:4 * pc + 4, :], in_=xstg[st][:]), [t_ld, xb_free[xs_]])
                xstg_free[st] = t_c
                t_x.append(t_c)
            return t_x

        t_x_next = load_x(0)
        for T in range(ntile):
            xs_ = T % 2
            own = (T % 4 == 3) and (T // 4) < nslot
            slot = T // 4
            t_x = t_x_next
            if T + 1 < ntile:
                t_x_next = load_x(T + 1)
            t_kev = []
            for h in range(H):
                b = jb % 7
                jb += 1
                for c in range(NCH):
                    t_pe = P.op("pe", lambda e, b=b, c=c, h=h, xs_=xs_: e.matmul(
                        bank(b), lhsT=wkv[:, h // 4, c, (h % 4) * 128:(h % 4) * 128 + 128], rhs=xb[xs_][:, c, :],
                        start=(c == 0), stop=(c == NCH - 1)),
                        (t_x + [bank_free[b]]) if c == 0 else (), sig=("chain" if c == NCH - 1 else None))
                t1 = P.op("act", lambda e, b=b, h=h, xs_=xs_: e.activation(
                    out=ktst[xs_][:, h, :], in_=bank(b), func=AF.Copy), [t_pe, kt_free[xs_]])
                t_kev.append(t1)
                if own:
                    ks_ = nko % 2
                    nko += 1
                    t2 = P.op("dve", lambda e, b=b, ks_=ks_: e.tensor_copy(out=kof[ks_][:], in_=bank(b)),
                              [t_pe, t1, kof_free[ks_]])
                    bank_free[b] = t2
                    kof_free[ks_] = P.dma("act", koT_o[h, :, slot * TW:(slot + 1) * TW], kof[ks_][:], [t2],
                                          key="kof%d" % ks_)
                else:
                    bank_free[b] = t1
            kt_free[xs_] = P.dma("act", kts[T], ktst[xs_][:], t_kev, key="kst%d" % xs_)
            t_vev = []
            t_vm2 = P.op("pool", lambda e, xs_=xs_, T=T: e.tensor_copy(
                out=vxst[xs_][:, :, :, 128], in_=vm[:, T * 32:(T + 1) * 32].rearrange("p (h s) -> p h s", s=4)),
                [t_vm, vx_free[xs_]])
            t_vev.append(t_vm2)
            for s in range(4):
                for g in range(2):
                    b = jb % 7
                    jb += 1
                    for c in range(NCH):
                        t_pe = P.op("pe", lambda e, b=b, c=c, s=s, g=g, xs_=xs_: e.matmul(
                            bank(b), lhsT=xb[xs_][:, c, s * 128:s * 128 + 128], rhs=wkv[:, 2 + g, c, :],
                            start=(c == 0), stop=(c == NCH - 1)),
                            (t_x + [bank_free[b]]) if c == 0 else (), sig=("chain" if c == NCH - 1 else None))
                    t1 = P.op("dve", lambda e, b=b, s=s, g=g, xs_=xs_: e.tensor_copy(
                        out=vxst[xs_][:, 4 * g:4 * g + 4, s, 0:128], in_=bank(b).rearrange("p (h e) -> p h e", e=128)),
                        [t_pe, vx_free[xs_]])
                    t_vev.append(t1)
                    if own:
                        vs_ = nvo % 2
                        nvo += 1
                        t2 = P.op("act", lambda e, b=b, vs_=vs_: e.activation(
                            out=vof[vs_][:], in_=bank(b), func=AF.Copy), [t_pe, t1, vof_free[vs_]])
                        bank_free[b] = t2
                        vof_free[vs_] = P.dma("act", vo_o[slot * TW + s * 128:slot * TW + s * 128 + 128,
                                                         g * 512:g * 512 + 512], vof[vs_][:], [t2], key="vof%d" % vs_)
                    else:
                        bank_free[b] = t1
            for _ in range(30):
                if next(bgen, "done") == "done":
                    break
            for _ in range(2):
                if next(wgen, "done") == "done":
                    break
            xb_free[xs_] = t_pe
            vx_free[xs_] = P.dma("pool", vxs[T], vxst[xs_][:].rearrange("p h s e -> p h (s e)"), t_vev, key="vst%d" % xs_)
        for _ in bgen:
            pass
        for _ in wgen:
            pass
        P.emit()
        SA.__exit__(None, None, None)

        if do_sample:
            SS_ = ExitStack()
            SS_.__enter__()
            NTK = NSTR * SS
            wS = sb(SS_, "wS", [128, NCH, 512], BF16)
            xsf = sb(SS_, "xsf", [128, NCH, NTK], F32)
            xsb = sb(SS_, "xsb", [128, NCH, NTK], BF16)
            qTs = sb(SS_, "qTs", [128, H, NTK], BF16)
            kTs = sb(SS_, "kTs", [128, H, NTK], BF16)
            kTf = sb(SS_, "kTf", [128, H, NTK], F32)
            hTs = sb(SS_, "hTs", [128, 4, NTK], F32)
            CTs = sb(SS_, "CTs", [128, 4, NTK], F32)
            BTs = sb(SS_, "BTs", [128, 4, NTK], F32)
            gcTs = sb(SS_, "gcTs", [128, 4, NTK], F32)
            mqTs = sb(SS_, "mqTs", [128, MH, NTK], BF16)
            upad = sb(SS_, "upad", [128, 4, NSTR, SS + 2], F32)
            cvt = sb(SS_, "cvt", [128, NSTR, SS], F32)
            cvt2 = sb(SS_, "cvt2", [128, NSTR, SS], F32)
            zTcs = sb(SS_, "zTcs", [128, 4, NTK], BF16)
            convsb = sb(SS_, "convsb", [128, NSTR, 4, 2], F32)
            vfs = [sb(SS_, "vfs%d" % i, [SS, 512], F32) for i in range(2)]
            vnew = sb(SS_, "vnew", [SS, NSTR, H, 129], BF16)
            sgds = sb(SS_, "sgds", [SS, NSTR, 1024], F32)
            sgms = sb(SS_, "sgms", [SS, NSTR, 512], F32)
            ztoks = sb(SS_, "ztoks", [SS, NSTR, 1536], BF16)
            ckf = [sb(SS_, "ckf%d" % i, [128, 1024], F32) for i in range(2)]
            ckbr = [sb(SS_, "ckbr%d" % i, [128, 4, PAST], BF16) for i in range(3)]
            cvf = [sb(SS_, "cvf%d" % i, [128, 1024], F32) for i in range(2)]
            vxcr = [sb(SS_, "vxcr%d" % i, [128, 8, 4, 129], BF16) for i in range(3)]
            cmkf = sb(SS_, "cmkf", [128, MH, NMEM], F32)
            cmkb = sb(SS_, "cmkb", [128, MH, NMEM], BF16)
            cmvf = sb(SS_, "cmvf", [128, 2, 512], F32)
            cmvx = sb(SS_, "cmvx", [128, 2, MH, 129], BF16)
            pTs = sb(SS_, "pTs", [128, 2, 9, SS], BF16)
            pTm = sb(SS_, "pTm", [128, 2, SS], BF16)
            accS = sb(SS_, "accS", [SS, 2, ACCW], F32)
            oS2 = [sb(SS_, "oS2%d" % i, [SS, 128], F32) for i in range(2)]
            rstdS2 = [sb(SS_, "rstdS2%d" % i, [SS, 2], F32) for i in range(2)]
            acc_freeM = [None]
            tS = sb(SS_, "tS", [SS, 128], F32)
            sqS = sb(SS_, "sqS", [SS, 128], F32)
            recS = sb(SS_, "recS", [SS, 2], F32)
            nr2S = sb(SS_, "nr2S", [SS, 1], F32)
            ssqS = sb(SS_, "ssqS", [SS, 1], F32)
            rstdS = sb(SS_, "rstdS", [SS, 1], F32)
            zTs = sb(SS_, "zTs", [128, NCH, NTK], BF16)
            ysbS = sb(SS_, "ysbS", [NTK, D], F32)
            xrS = [sb(SS_, "xrS%d" % i, [NTK, 512], F32) for i in range(2)]
            lnS = [sb(SS_, "lnS%d" % i, [NTK, 512], F32) for i in range(2)]
            bnS = sb(SS_, "bnS", [NTK, 4, 6], F32)
            mvS = sb(SS_, "mvS", [NTK, 2], F32)
            lnrS = sb(SS_, "lnrS", [NTK, 1], F32)

            t_xs = P.dma("sp", xsf[:], xsT_d.rearrange("(c p) n -> p c n", p=128), key="xsf")
            t_xs = P.op("dve", lambda e: e.tensor_copy(out=xsb[:], in_=xsf[:]), [t_xs])
            t_cc = P.dma("sp", upad[:, :, :, 0:2], cconv_d, key="cconv")
            bank_free = [None] * 7
            jbs = [0]
            wS_free = [None]

            def loadS(gi):
                return P.dma("sp", wS[:], wsc[gi], [wS_free[0]], key="wS")

            def jobS(mm, parts, n_out, evac, deps):
                b = jbs[0] % 7
                jbs[0] += 1
                for c in range(NCH):
                    t_pe = P.op("pe", lambda e, b=b, c=c: mm(e, bank(b)[0:parts, 0:n_out], c, c == 0, c == NCH - 1),
                                (list(deps) + [bank_free[b]]) if c == 0 else (),
                                sig=("chain" if c == NCH - 1 else None))
                t = evac(bank(b)[0:parts, 0:n_out], t_pe)
                bank_free[b] = t
                return t_pe, t

            KV0 = len(WG)
            t_fm = []
            fm_groups = [(G_Q0, "q", 0), (G_Q1, "q", 4), (KV0, "k", 0), (KV0 + 1, "k", 4), (G_H, "h", 0), (G_C, "c", 0),
                         (G_B, "b", 0), (G_GC, "gc", 0), (G_MQ, "mq", 0)]
            for (gi, kind, h0) in fm_groups:
                t_w = loadS(gi)
                for sub in range(4):
                    def mm(e, out, c, st_, sp_, sub=sub):
                        return e.matmul(out, lhsT=wS[:, c, sub * 128:sub * 128 + 128], rhs=xsb[:, c, :], start=st_, stop=sp_)
                    if kind == "q":
                        def ev(bk, tpe, hh=h0 + sub):
                            return P.op("act", lambda e: e.mul(out=qTs[:, hh, :], in_=bk, mul=0.125), [tpe])
                    elif kind == "k":
                        def ev(bk, tpe, hh=h0 + sub):
                            t1 = P.op("act", lambda e: e.activation(out=kTs[:, hh, :], in_=bk, func=AF.Copy), [tpe])
                            return P.op("dve", lambda e: e.tensor_copy(out=kTf[:, hh, :], in_=bk), [tpe, t1])
                    elif kind == "mq":
                        def ev(bk, tpe, sub=sub):
                            return P.op("act", lambda e: e.mul(out=mqTs[:, sub, :], in_=bk, mul=128.0 ** -0.5), [tpe])
                    elif kind == "gc":
                        def ev(bk, tpe, sub=sub):
                            return P.op("act", lambda e: e.activation(out=gcTs[:, sub, :], in_=bk, func=AF.Silu), [tpe])
                    else:
                        tgt = {"h": hTs, "c": CTs, "b": BTs}[kind]

                        def ev(bk, tpe, sub=sub, tgt=tgt):
                            return P.op("dve", lambda e: e.tensor_copy(out=tgt[:, sub, :], in_=bk), [tpe])
                    last_pe, t_e = jobS(mm, 128, NTK, ev, [t_xs, t_w])
                    t_fm.append(t_e)
                wS_free[0] = last_pe
            P.dma("sp", ksT_o.rearrange("h p n -> p h n"), kTf[:], t_fm, key="ksT")
            t_tm = []
            nvf = 0
            vf_free = [None, None]
            for (gi, kind, g) in [(KV0 + 2, "v", 0), (KV0 + 3, "v", 1), (G_GD0, "gd", 0), (G_GD1, "gd", 1), (G_GM, "gm", 0)]:
                t_w = loadS(gi)
                for st in range(NSTR):
                    def mm(e, out, c, st_, sp_, st=st):
                        return e.matmul(out, lhsT=xsb[:, c, st * SS:(st + 1) * SS], rhs=wS[:, c, :], start=st_, stop=sp_)
                    if kind == "v":
                        vsel = nvf % 2
                        nvf += 1

                        def ev(bk, tpe, st=st, g=g, vsel=vsel):
                            t1 = P.op("act", lambda e: e.activation(
                                out=vnew[:, st, 4 * g:4 * g + 4, 0:128], in_=bk.rearrange("p (h e) -> p h e", e=128),
                                func=AF.Copy), [tpe])
                            t2 = P.op("dve", lambda e: e.tensor_copy(out=vfs[vsel][:], in_=bk), [tpe, t1, vf_free[vsel]])
                            vf_free[vsel] = P.dma("sp", vs_o[st * SS:(st + 1) * SS, g * 512:g * 512 + 512], vfs[vsel][:], [t2],
                                                  key="vfs%d" % vsel)
                            return t2
                    elif kind == "gd":
                        def ev(bk, tpe, st=st, g=g):
                            t1 = P.op("act", lambda e: e.activation(out=sgds[:, st, g * 512:g * 512 + 512], in_=bk,
                                                                    func=AF.Silu), [tpe])
                            return P.op("dve", lambda e: e.tensor_tensor(out=sgds[:, st, g * 512:g * 512 + 512],
                                                                         in0=sgds[:, st, g * 512:g * 512 + 512],
                                                                         in1=gs4[0:SS, :], op=ALU.mult), [t1])
                    else:
                        def ev(bk, tpe, st=st):
                            return P.op("act", lambda e: e.activation(out=sgms[:, st, :], in_=bk, func=AF.Silu), [tpe])
                    last_pe, t_e = jobS(mm, SS, 512, ev, [t_xs, t_w])
                    t_tm.append(t_e)
                wS_free[0] = last_pe
            t_on = P.op("dve", lambda e: e.memset(vnew[:, :, :, 128:129].rearrange("p s h e -> p (s h e)"), 1.0), t_tm)
            t_tm.append(t_on)
            t_wo = loadS(G_O0)
            t = None
            t_cvs = []
            for cc in range(4):
                t = P.op("pool", lambda e, cc=cc: e.tensor_tensor(
                    out=upad[:, cc, :, 2:SS + 2], in0=CTs[:, cc, :].rearrange("p (s t) -> p s t", t=SS),
                    in1=hTs[:, cc, :].rearrange("p (s t) -> p s t", t=SS), op=ALU.mult), t_fm + [t_cc, t])
                t_u = P.op("pool", lambda e, cc=cc: e.tensor_copy(out=convsb[:, :, cc, :], in_=upad[:, cc, :, SS:SS + 2]), [t])
                t_cvs.append(t_u)
                t = P.op("pool", lambda e, cc=cc: e.tensor_scalar(out=cvt[:], in0=upad[:, cc, :, 0:SS],
                                                                  scalar1=cw[:, cc * 3:cc * 3 + 1], scalar2=None,
                                                                  op0=ALU.mult), [t])
                for jx in (1, 2):
                    t = P.op("pool", lambda e, cc=cc, jx=jx: e.tensor_scalar(
                        out=cvt2[:], in0=upad[:, cc, :, jx:jx + SS], scalar1=cw[:, cc * 3 + jx:cc * 3 + jx + 1],
                        scalar2=None, op0=ALU.mult), [t])
                    t = P.op("pool", lambda e: e.tensor_tensor(out=cvt[:], in0=cvt[:], in1=cvt2[:], op=ALU.add), [t])
                t = P.op("pool", lambda e, cc=cc: e.tensor_tensor(
                    out=cvt[:], in0=cvt[:], in1=BTs[:, cc, :].rearrange("p (s t) -> p s t", t=SS), op=ALU.mult), [t])
                t = P.op("pool", lambda e, cc=cc: e.tensor_tensor(
                    out=zTcs[:, cc, :].rearrange("p (s t) -> p s t", t=SS), in0=cvt[:],
                    in1=gcTs[:, cc, :].rearrange("p (s t) -> p s t", t=SS), op=ALU.mult), [t])
                t_cvs.append(t)
            P.dma("pool", convs_o, convsb[:], t_cvs, key="convs")

            t_epS = [None]
            NU = NSTR * 2
            RU = 3
            ck_free = [None] * RU
            vx_free = [None] * RU
            cm_free = [None]
            nck = [0]
            ncv = [0]
            ckf_free = [None, None]
            cvf_free = [None, None]
            sb_free = {}
            acc_freeS = [None, None]
            ucache = {}
            t_ones = [P.op("dve", lambda e, r=r: e.memset(vxcr[r][:, :, :, 128:129].rearrange("p b h e -> p (b h e)"), 1.0), ())
                      for r in range(RU)]

            def load_unit(u):
                st, hh = divmod(u, 2)
                r = u % RU
                t_ck = []
                for h4 in range(4):
                    k_ = nck[0] % 2
                    nck[0] += 1
                    t_ld = P.dma("sp", ckf[k_][:], ckT_d[st, hh * 4 + h4], [ckf_free[k_]], key="ckf%d" % k_)
                    t_c = P.op("act", lambda e, k_=k_, h4=h4, r=r: e.activation(out=ckbr[r][:, h4, :], in_=ckf[k_][:],
                                                                             func=AF.Copy), [t_ld, ck_free[r]])
                    ckf_free[k_] = t_c
                    t_ck.append(t_c)
                t_cv_ = [t_ones[r]]
                for blk in range(8):
                    k_ = ncv[0] % 2
                    ncv[0] += 1
                    t_ld = P.dma("sp", cvf[k_][:, 0:512], cv_d[st, blk * 128:(blk + 1) * 128, hh * 512:(hh + 1) * 512],
                                 [cvf_free[k_]], key="cvf%d" % k_)
                    t_c = P.op("dve", lambda e, k_=k_, blk=blk, r=r: e.tensor_copy(
                        out=vxcr[r][:, blk, :, 0:128], in_=cvf[k_][:, 0:512].rearrange("p (h e) -> p h e", e=128)),
                        [t_ld, vx_free[r]])
                    cvf_free[k_] = t_c
                    t_cv_.append(t_c)
                ucache[u] = (t_ck, t_cv_)

            for u in range(min(2, NU)):
                load_unit(u)
            pend_final = [None]
            tp_prev = [None]
            for u in range(NU):
                st, hh = divmod(u, 2)
                r = u % RU
                if u + 2 < NU:
                    load_unit(u + 2)
                t_ck, t_cv_ = ucache[u]
                if hh == 0:
                    t_l1 = P.dma("sp", cmkf[:], cmkT_d[st].rearrange("h p n -> p h n"), [cm_free[0]], key="cmk")
                    t_l2 = P.dma("sp", cmvf[:], cmv_d[st].rearrange("(b p) n -> p b n", p=128), [cm_free[0]], key="cmv")
                    t_m1 = P.op("pool", lambda e: e.tensor_copy(out=cmkb[:], in_=cmkf[:]), [t_l1, cm_free[0]])
                    t_m2 = P.op("pool", lambda e: e.tensor_copy(
                        out=cmvx[:, :, :, 0:128], in_=cmvf[:].rearrange("p b (h e) -> p b h e", e=128)), [t_l2, cm_free[0]])
                    t_m3 = P.op("pool", lambda e: e.memset(cmvx[:, :, :, 128:129].rearrange("p b h e -> p (b h e)"), 1.0), [t_m2])
                    t_mem = [t_m1, t_m2, t_m3]
                qs = slice(st * SS, (st + 1) * SS)
                for h4 in range(4):
                    h = hh * 4 + h4
                    cnt = u * 4 + h4
                    sbk = cnt % 2
                    abk = cnt % 2
                    Sv = psS[:, sbk, 0:2 * 9 * SS].rearrange("p (m k q) -> p m k q", m=2, k=9)
                    for m in range(2):
                        for kb in range(8):
                            tq = P.op("pe", lambda e, m=m, kb=kb, h4=h4, h=h, Sv=Sv, qs=qs, r=r: e.matmul(
                                Sv[:, m, kb, :], lhsT=ckbr[r][64 * m:64 * m + 64, h4, kb * 128:kb * 128 + 128],
                                rhs=qTs[64 * m:64 * m + 64, h, qs], start=(m == 0 and kb == 0), stop=False,
                                skip_group_check=True),
                                (t_ck + t_fm + t_tm + [sb_free.get(sbk)]) if (m == 0 and kb == 0) else (), sig=None)
                        tq = P.op("pe", lambda e, m=m, h=h, Sv=Sv, qs=qs: e.matmul(
                            Sv[0:SS, m, 8, :], lhsT=kTs[64 * m:64 * m + 64, h, qs], rhs=qTs[64 * m:64 * m + 64, h, qs],
                            start=False, stop=False, skip_group_check=True), (), sig=None)
                        for part in range(2):
                            tq = P.op("pe", lambda e, m=m, h=h, part=part, Sv=Sv: e.matmul(
                                Sv[:, m, 7, :], lhsT=ident[:], rhs=biasT[:, h, 1, part, 0:SS], start=False, stop=False,
                                skip_group_check=True), (), sig=None)
                            tq = P.op("pe", lambda e, m=m, h=h, part=part, Sv=Sv: e.matmul(
                                Sv[0:SS, m, 8, :], lhsT=ident[0:SS, 0:SS], rhs=biasT[0:SS, h, 0, part, 0:SS], start=False,
                                stop=True, skip_group_check=True), (), sig=("chain" if (m == 1 and part == 1) else None))
                    te1 = P.op("act", lambda e, h=h, Sv=Sv: e.activation(
                        out=pTs[:, :, 0:8, :], in_=Sv[:, :, 0:8, :], func=AF.Exp, bias=cfar[:, h:h + 1], scale=1.0),
                        [tq, tp_prev[0]])
                    te2 = P.op("act", lambda e, h=h, Sv=Sv: e.activation(
                        out=pTs[0:SS, :, 8, :], in_=Sv[0:SS, :, 8, :], func=AF.Exp, bias=cfar[0:SS, h:h + 1], scale=1.0), [tq])
                    sb_free[sbk] = te2
                    for m in range(2):
                        for kb in range(8):
                            tp_ = P.op("pe", lambda e, m=m, kb=kb, h4=h4, r=r, abk=abk: e.matmul(
                                psA[0:SS, abk, m * ACCW:m * ACCW + 129], lhsT=pTs[:, m, kb, :], rhs=vxcr[r][:, kb, h4, :],
                                start=(m == 0 and kb == 0), stop=False, skip_group_check=True),
                                ([te1, te2, acc_freeS[abk]] + t_cv_) if (m == 0 and kb == 0) else (), sig=None)
                        tp_ = P.op("pe", lambda e, m=m, h=h, st=st, abk=abk: e.matmul(
                            psA[0:SS, abk, m * ACCW:m * ACCW + 129], lhsT=pTs[0:SS, m, 8, :], rhs=vnew[:, st, h, :],
                            start=False, stop=True, skip_group_check=True), (), sig=("chain" if m == 1 else None))
                    tp_prev[0] = tp_
                    if pend_final[0] is not None:
                        pend_final[0]()
                        pend_final[0] = None
                    t = P.op("dve", lambda e, abk=abk: e.tensor_copy(
                        out=accS[:], in_=psA[0:SS, abk, 0:2 * ACCW].rearrange("p (a w) -> p a w", w=ACCW)), [tp_, t_epS[0]])
                    acc_freeS[abk] = t
                    t = P.op("dve", lambda e: e.reciprocal(out=recS[:], in_=accS[:, :, 128]), [t])
                    t = P.op("dve", lambda e: e.tensor_scalar(out=nr2S[:], in0=recS[:, 1:2], scalar1=nlam[0:SS, 0:1],
                                                              scalar2=None, op0=ALU.mult), [t, t_nlam])
                    t = P.op("dve", lambda e: e.tensor_scalar(out=tS[:], in0=accS[:, 1, 0:128], scalar1=nr2S[:, 0:1],
                                                              scalar2=None, op0=ALU.mult), [t])
                    oSel = oS2[cnt % 2]
                    rSel = rstdS2[cnt % 2]
                    t = P.op("dve", lambda e, oSel=oSel: e.scalar_tensor_tensor(out=oSel[:], in0=accS[:, 0, 0:128], scalar=recS[:, 0:1],
                                                                                in1=tS[:], op0=ALU.mult, op1=ALU.add), [t])
                    t = P.op("dve", lambda e, oSel=oSel: e.tensor_tensor(out=sqS[:], in0=oSel[:], in1=oSel[:], op=ALU.mult), [t])
                    t = P.op("dve", lambda e, rSel=rSel: e.reduce_sum(out=rSel[:, 0:1], in_=sqS[:], axis=AX.X), [t])
                    t_epS[0] = t
                    t = P.op("act", lambda e, rSel=rSel: e.activation(out=rSel[:, 1:2], in_=rSel[:, 0:1], func=AF.Ln,
                                                                      bias=epsb[0:SS, 0:1], scale=1.0 / 128.0), [t])
                    t = P.op("act", lambda e, rSel=rSel: e.activation(out=rSel[:, 1:2], in_=rSel[:, 1:2], func=AF.Exp, scale=-0.5), [t])

                    def fin(st=st, h=h, oSel=oSel, rSel=rSel, t=t):
                        t_epS[0] = P.op("dve", lambda e: e.scalar_tensor_tensor(
                            out=ztoks[:, st, h * 128:h * 128 + 128], in0=oSel[:], scalar=rSel[:, 1:2],
                            in1=sgds[:, st, h * 128:h * 128 + 128], op0=ALU.mult, op1=ALU.mult), [t, t_epS[0]])
                    pend_final[0] = fin
                ck_free[r] = tq
                vx_free[r] = tp_
                if hh == 0:
                    continue
                if pend_final[0] is not None:
                    pend_final[0]()
                    pend_final[0] = None
                for mh in range(MH):
                    Sm = psS[:, 2, 0:2 * SS].rearrange("p (k q) -> p k q", k=2)
                    for blk in range(2):
                        tq = P.op("pe", lambda e, mh=mh, blk=blk, Sm=Sm, qs=qs: e.matmul(
                            Sm[:, blk, :], lhsT=cmkb[:, mh, blk * 128:blk * 128 + 128], rhs=mqTs[:, mh, qs],
                            start=(blk == 0), stop=(blk == 1), skip_group_check=True),
                            (t_mem + [sb_free.get(2)]) if blk == 0 else (), sig=("chain" if blk == 1 else None))
                    te = P.op("act", lambda e, Sm=Sm: e.activation(out=pTm[:], in_=Sm, func=AF.Exp), [tq, tp_prev[0]])
                    sb_free[2] = te
                    for blk in range(2):
                        tp_ = P.op("pe", lambda e, mh=mh, blk=blk: e.matmul(
                            psA[0:SS, 2, 0:129], lhsT=pTm[:, blk, :], rhs=cmvx[:, blk, mh, :],
                            start=(blk == 0), stop=(blk == 1), skip_group_check=True),
                            [te, acc_freeM[0]] if blk == 0 else (), sig=("chain" if blk == 1 else None))
                    tp_prev[0] = tp_
                    t = P.op("dve", lambda e: e.tensor_copy(out=accS[:, 0, :], in_=psA[0:SS, 2, 0:ACCW]), [tp_, t_epS[0]])
                    acc_freeM[0] = t
                    t = P.op("dve", lambda e: e.reciprocal(out=recS[:, 0:1], in_=accS[:, 0, 128:129]), [t])
                    t = P.op("dve", lambda e, st=st, mh=mh: e.scalar_tensor_tensor(
                        out=ztoks[:, st, 1024 + mh * 128:1024 + mh * 128 + 128], in0=accS[:, 0, 0:128], scalar=recS[:, 0:1],
                        in1=sgms[:, st, mh * 128:mh * 128 + 128], op0=ALU.mult, op1=ALU.mult), [t])
                    t_epS[0] = t
                cm_free[0] = tp_
            t_zs = []
            tpf = [None]
            for st in range(NSTR):
                for k in range(12):
                    tt = P.op("pe", lambda e, st=st, k=k: e.transpose(
                        psTb[:, k * SS:(k + 1) * SS], ztoks[:, st, k * 128:(k + 1) * 128], ident[0:SS, 0:SS]),
                        [t_epS[0], tpf[0]] if k == 0 else (), sig=("chain" if k == 11 else None))
                t1 = P.op("dve", lambda e, st=st: e.tensor_copy(
                    out=zTs[:, 0:8, st * SS:(st + 1) * SS], in_=psTb[:, 0:8 * SS].rearrange("p (k t) -> p k t", t=SS)), [tt])
                t2 = P.op("dve", lambda e, st=st: e.tensor_copy(
                    out=zTs[:, 12:16, st * SS:(st + 1) * SS], in_=psTb[:, 8 * SS:12 * SS].rearrange("p (k t) -> p k t", t=SS)), [tt, t1])
                tpf[0] = t2
                t_zs += [t1, t2]
            t_zs.append(P.op("pool", lambda e: e.tensor_copy(out=zTs[:, 8:12, :], in_=zTcs[:]), t_cvs))
            t_ys = []
            xr_free = [None, None]
            for g in range(4):
                t_w = t_wo if g == 0 else loadS(G_O0 + g)
                t_xr = P.dma("sp", xrS[g % 2][:], xs_d[:, g * 512:(g + 1) * 512], [xr_free[g % 2]], key="xrS%d" % (g % 2))

                def mm(e, out, c, st_, sp_):
                    return e.matmul(out, lhsT=zTs[:, c, :], rhs=wS[:, c, :], start=st_, stop=sp_)

                def ev(bk, tpe, g=g, t_xr=t_xr):
                    return P.op("dve", lambda e: e.scalar_tensor_tensor(
                        out=ysbS[:, g * 512:(g + 1) * 512], in0=xrS[g % 2][:], scalar=ALPHA, in1=bk, op0=ALU.mult,
                        op1=ALU.add), [tpe, t_xr])
                last_pe, t_e = jobS(mm, NTK, 512, ev, t_zs + [t_w])
                xr_free[g % 2] = t_e
                wS_free[0] = last_pe
                t_ys.append(t_e)
            t = None
            for g in range(4):
                t = P.op("dve", lambda e, g=g: e.bn_stats(out=bnS[:, g, :], in_=ysbS[:, g * 512:(g + 1) * 512]), t_ys + [t])
            t = P.op("dve", lambda e: e.bn_aggr(out=mvS[:], in_=bnS[:].rearrange("p a b -> p (a b)")), [t])
            t = P.op("act", lambda e: e.activation(out=lnrS[:], in_=mvS[:, 1:2], func=AF.Ln, bias=epsb[0:NTK, 0:1], scale=1.0), [t])
            t = P.op("act", lambda e: e.activation(out=lnrS[:], in_=lnrS[:], func=AF.Exp, scale=-0.5), [t])
            t = P.op("dve", lambda e: e.tensor_scalar(out=ysbS[:], in0=ysbS[:], scalar1=mvS[:, 0:1], scalar2=lnrS[:, 0:1],
                                                      op0=ALU.subtract, op1=ALU.mult), [t])
            ln_free = [None, None]
            for g in range(4):
                tg = P.dma("sp", lnS[0][:], lng_d[0:NTK, g * 512:(g + 1) * 512], [ln_free[0]], key="lnS0")
                tb = P.dma("sp", lnS[1][:], lnb_d[0:NTK, g * 512:(g + 1) * 512], [ln_free[1]], key="lnS1")
                t = P.op("dve", lambda e, g=g: e.tensor_tensor(out=ysbS[:, g * 512:(g + 1) * 512],
                                                               in0=ysbS[:, g * 512:(g + 1) * 512], in1=lnS[0][:], op=ALU.mult), [t, tg])
                ln_free[0] = t
                t = P.op("dve", lambda e, g=g: e.tensor_tensor(out=ysbS[:, g * 512:(g + 1) * 512],
                                                               in0=ysbS[:, g * 512:(g + 1) * 512], in1=lnS[1][:], op=ALU.add), [t, tb])
                ln_free[1] = t
            P.dma("sp", ys_o, ysbS[:], [t], key="ysS")
            P.emit()
            SS_.__exit__(None, None, None)

        SB = ExitStack()
        SB.__enter__()
        wring = [sb(SB, "wring%d" % i, [128, NCH, 512], BF16) for i in range(2)]
        xz = sb(SB, "xz", [128, NCH, TW + 2], BF16)
        stgB = sb(SB, "stgB", [128, 2, 2 * (TW + 2)], F32)
        qT = sb(SB, "qT", [128, H, TW], BF16)
        mqT = sb(SB, "mqT", [128, MH, TW], BF16)
        big = sb(SB, "big", [128, 8208], F32)
        sgd = sb(SB, "sgd", [128, 4, 1024], F32)
        sgm = sb(SB, "sgm", [128, 4, 512], F32)
        ztok = sb(SB, "ztok", [128, 4, 1536], BF16)
        zTc = sb(SB, "zTc", [128, 4, TW], BF16)
        kring = sb(SB, "kring", [128, RK, TW], BF16)
        vring = sb(SB, "vring", [128, RK, 4, 129], BF16)
        pring = sb(SB, "pring", [128, RP, 2, TW], BF16)
        accs = sb(SB, "accs", [128, 8, ACCW], F32)
        otmp = sb(SB, "otmp", [128, 4, 128], F32)
        ttmp = sb(SB, "ttmp", [128, 128], F32)
        sqtmp = sb(SB, "sqtmp", [128, 128], F32)
        rec = sb(SB, "rec", [128, 8], F32)
        nr2 = sb(SB, "nr2", [128, 4], F32)
        ssq = sb(SB, "ssq", [128, 4], F32)
        rstd = sb(SB, "rstd", [128, 4], F32)
        halo = sb(SB, "halo", [128, 8, 2], F32)
        ctmp = sb(SB, "ctmp", [128, TW], F32)
        ctmp2 = sb(SB, "ctmp2", [128, TW], F32)
        utmp = sb(SB, "utmp", [128, TW + 2], F32)
        xres = [sb(SB, "xres%d" % i, [128, 512], F32) for i in range(2)]
        bnst = sb(SB, "bnst", [128, 4, 6], F32)
        bnst4 = sb(SB, "bnst4", [128, 4, 4, 6], F32)
        mv4 = sb(SB, "mv4", [128, 4, 2], F32)
        lnr4 = sb(SB, "lnr4", [128, 4], F32)
        mv = sb(SB, "mv", [128, 2], F32)
        lnr = sb(SB, "lnr", [128, 1], F32)
        convu = sb(SB, "convu", [128, 4, 2], F32)

        hT = big[:, 0:4 * 514].rearrange("p (c t) -> p c t", t=514)
        CT = big[:, 2056:2 * 2056].rearrange("p (c t) -> p c t", t=514)
        BT = big[:, 4112:4112 + 2048].rearrange("p (c t) -> p c t", t=512)
        sgcT = big[:, 6160:6160 + 2048].rearrange("p (c t) -> p c t", t=512)
        ysb = big[:, 0:8192].rearrange("p (s n) -> p s n", n=2048)
        lngs = sb(SB, "lngs", [128, D], F32)
        lngb = lngs[:]
        t_lng = P.dma("sp", lngs[:], lng_d, key="lng")
        zT = sgd[:].rearrange("p s n -> p (s n)").bitcast(BF16).rearrange("p (c t) -> p c t", t=TW)

        lnb_sb = sb(SB, "lnb_sb", [128, D], F32)
        t_lnb = P.dma("sp", lnb_sb[:], lnb_d, key="lnb")

        sem_qk = P.sem("qk")
        sem_exp = P.sem("exp")
        sem_pv = P.sem("pv")

        bank_free = [None] * 7
        halo_free = [None]
        jbc = [0]
        wr_free = [None, None]
        wr_cnt = [0]

        def load_w(gi, after=()):
            r = wr_cnt[0] % 2
            wr_cnt[0] += 1
            t = P.dma("sp", wring[r][:], wsc[gi], [wr_free[r]] + list(after), key="w%d" % r)
            return r, t

        def job(mm, n_out, evac, deps):
            b = jbc[0] % 7
            jbc[0] += 1
            for c in range(NCH):
                t_pe = P.op("pe", lambda e, b=b, c=c: mm(e, bank(b)[:, 0:n_out], c, c == 0, c == NCH - 1),
                            (list(deps) + [bank_free[b]]) if c == 0 else (),
                            sig=("chain" if c == NCH - 1 else None))
            t = evac(bank(b)[:, 0:n_out], t_pe)
            bank_free[b] = t
            return t_pe, t

        xTr2 = xT.rearrange("(c p) t -> p c t", p=128)
        stgv = [stgB[:, k, :].rearrange("p (c t) -> p c t", t=TW + 2) for k in range(2)]
        prev_slot_done = []
        xz_free = []
        lng_used = []
        nblk = [0]
        ncidx = [0]
        kv_free = [None] * RK
        pv_tok = {}
        exp_tok = {}
        acc_free = [None]
        y_st_tok = [None]

        stg_free2 = [None, None]

        def load_xz(T):
            t_x = []
            for pc in range(8):
                k = pc % 2
                t_ld = P.dma("pool", stgv[k], xTr2[:, 2 * pc:2 * pc + 2, T * TW - 2:(T + 1) * TW],
                             [stg_free2[k]], key="xz%d" % k)
                t_c = P.op("dve", lambda e, k=k, pc=pc: e.tensor_copy(out=xz[:, 2 * pc:2 * pc + 2, :], in_=stgv[k]),
                           [t_ld] + xz_free)
                stg_free2[k] = t_c
                t_x.append(t_c)
            return t_x

        t_x_next = load_xz(3)
        for i in range(nslot):
            T = 4 * i + 3
            t_x = t_x_next
            nxt = load_w(G_Q0)
            order = [G_Q0, G_Q1, G_MQ, G_GD0, G_GD1, G_GM, G_H, G_C, G_B, G_GC]
            t_conv_in = {}
            t_q = []
            t_gate = []
            for oi, gi in enumerate(order):
                r, t_w = nxt
                last_pe = None
                if gi in (G_Q0, G_Q1, G_H, G_B, G_C, G_MQ, G_GC):
                    for sub in range(4):
                        def mm(e, out, c, st, sp_, r=r, sub=sub):
                            return e.matmul(out, lhsT=wring[r][:, c, sub * 128:sub * 128 + 128], rhs=xz[:, c, 2:TW + 2],
                                            start=st, stop=sp_)
                        if gi in (G_Q0, G_Q1):
                            hh = (gi - G_Q0) * 4 + sub

                            def ev(bk, tpe, hh=hh):
                                return P.op("act", lambda e: e.mul(out=qT[:, hh, :], in_=bk, mul=0.125), [tpe])
                        elif gi == G_MQ:
                            def ev(bk, tpe, sub=sub):
                                return P.op("act", lambda e: e.mul(out=mqT[:, sub, :], in_=bk, mul=128.0 ** -0.5), [tpe])
                        elif gi == G_H:
                            def ev(bk, tpe, sub=sub):
                                return P.op("dve", lambda e: e.tensor_copy(out=hT[:, sub, 2:TW + 2], in_=bk),
                                            [tpe] + prev_slot_done)
                        elif gi == G_C:
                            def ev(bk, tpe, sub=sub):
                                return P.op("dve", lambda e: e.tensor_copy(out=CT[:, sub, 2:TW + 2], in_=bk),
                                            [tpe] + prev_slot_done)
                        elif gi == G_B:
                            def ev(bk, tpe, sub=sub):
                                return P.op("dve", lambda e: e.tensor_copy(out=BT[:, sub, :], in_=bk),
                                            [tpe] + prev_slot_done)
                        else:
                            def ev(bk, tpe, sub=sub):
                                return P.op("act", lambda e: e.activation(out=sgcT[:, sub, :], in_=bk, func=AF.Silu),
                                            [tpe] + prev_slot_done)
                        last_pe, t_e = job(mm, TW, ev, t_x + [t_w])
                        if gi in (G_Q0, G_Q1, G_MQ):
                            t_q.append(t_e)
                        else:
                            t_conv_in[(gi, sub)] = t_e
                        if gi in (G_H, G_C):
                            hidx = (0 if gi == G_H else 4) + sub
                            for c in range(NCH):
                                last_pe = P.op("pe", lambda e, c=c, r=r, sub=sub, hidx=hidx: e.matmul(
                                    psT[:, 2 * hidx:2 * hidx + 2], lhsT=wring[r][:, c, sub * 128:sub * 128 + 128],
                                    rhs=xz[:, c, 0:2], start=(c == 0), stop=(c == NCH - 1)),
                                    [halo_free[0]] if c == 0 else (), sig=("chain" if c == NCH - 1 else None))
                            tgt = hT if gi == G_H else CT
                            halo_free[0] = P.op("dve", lambda e, tgt=tgt, sub=sub, hidx=hidx: e.tensor_copy(
                                out=tgt[:, sub, 0:2], in_=psT[:, 2 * hidx:2 * hidx + 2]), [last_pe] + prev_slot_done)
                            t_conv_in[(gi, sub, "halo")] = halo_free[0]
                else:
                    for s in range(4):
                        def mm(e, out, c, st, sp_, r=r, s=s):
                            return e.matmul(out, lhsT=xz[:, c, 2 + s * 128:2 + s * 128 + 128], rhs=wring[r][:, c, :],
                                            start=st, stop=sp_)
                        if gi in (G_GD0, G_GD1):
                            g = gi - G_GD0

                            def ev(bk, tpe, s=s, g=g):
                                t1 = P.op("act", lambda e: e.activation(out=sgd[:, s, g * 512:g * 512 + 512], in_=bk,
                                                                        func=AF.Silu), [tpe])
                                t2 = P.op("dve", lambda e: e.tensor_tensor(out=sgd[:, s, g * 512:g * 512 + 512],
                                                                            in0=sgd[:, s, g * 512:g * 512 + 512],
                                                                            in1=gs4[:], op=ALU.mult), [t1])
                                t_gate.append(t2)
                                return t1
                        else:
                            def ev(bk, tpe, s=s):
                                return P.op("act", lambda e: e.activation(out=sgm[:, s, :], in_=bk, func=AF.Silu),
                                            [tpe])
                        last_pe, t_e = job(mm, 512, ev, t_x + [t_w])
                        t_gate.append(t_e)
                wr_free[r] = last_pe
                if oi + 1 < len(order):
                    nxt = load_w(order[oi + 1])
            xz_free = [last_pe]
            w_o = [load_w(G_O0), load_w(G_O0 + 1)]
            t_cv = []

            def post_mem(i=i):
                nonlocal t_x_next
                for cc in range(4):
                    dep = [t_conv_in[(G_H, cc)], t_conv_in[(G_C, cc)], t_conv_in[(G_H, cc, "halo")],
                           t_conv_in[(G_C, cc, "halo")], t_conv_in[(G_B, cc)], t_conv_in[(G_GC, cc)]]
                    t = P.op("dve", lambda e, cc=cc: e.tensor_tensor(out=utmp[:], in0=CT[:, cc, :], in1=hT[:, cc, :],
                                                                     op=ALU.mult), dep + t_cv[-1:])
                    if i == nslot - 1:
                        t_u = P.op("dve", lambda e, cc=cc: e.tensor_copy(out=convu[:, cc, :], in_=utmp[:, TW:TW + 2]), [t])
                        t_cv.append(t_u)
                    t = P.op("dve", lambda e, cc=cc: e.tensor_scalar(out=ctmp[:], in0=utmp[:, 0:TW],
                                                                     scalar1=cw[:, cc * 3:cc * 3 + 1], scalar2=None,
                                                                     op0=ALU.mult), [t])
                    for jx in (1, 2):
                        t = P.op("dve", lambda e, cc=cc, jx=jx: e.scalar_tensor_tensor(
                            out=ctmp[:], in0=utmp[:, jx:jx + TW], scalar=cw[:, cc * 3 + jx:cc * 3 + jx + 1], in1=ctmp[:],
                            op0=ALU.mult, op1=ALU.add), [t])
                    t = P.op("dve", lambda e, cc=cc: e.tensor_tensor(out=ctmp[:], in0=ctmp[:], in1=BT[:, cc, :], op=ALU.mult), [t])
                    t = P.op("dve", lambda e, cc=cc: e.tensor_tensor(out=zTc[:, cc, :], in0=ctmp[:], in1=sgcT[:, cc, :],
                                                                     op=ALU.mult), [t])
                    t_cv.append(t)
                if i == nslot - 1:
                    t_cv.append(P.dma("pool", convo_o, convu[:], t_cv, key="convo"))
                if i + 1 < nslot:
                    t_x_next = load_xz(4 * (i + 1) + 3)

            t_q = t_q + t_gate + list(t_conv_in.values())
            blocks = []
            for h in range(H):
                for kc in range(T + 1):
                    for kbl in range(4):
                        blocks.append((h, kc, kbl))
            nb_total = len(blocks)
            t_ep = []
            kvslot = {}

            def issue_kv(h, kc):
                cidx = ncidx[0]
                ncidx[0] += 1
                sl = cidx % RK
                t1 = P.dma("sp", kring[:, sl, :], kts[kc, :, h, :], [kv_free[sl]], key="kv%d" % sl)
                t2 = P.dma("sp", vring[:, sl].rearrange("p s e -> p (s e)"), vxs[kc, :, h, :], [kv_free[sl]],
                           key="kv%d" % sl)
                kvslot[(h, kc)] = (sl, t2)

            chunk_list = [(h, kc) for h in range(H) for kc in range(T + 1)]
            kv_issued = [0]

            def ensure_kv(upto):
                while kv_issued[0] < min(upto, len(chunk_list)):
                    issue_kv(*chunk_list[kv_issued[0]])
                    kv_issued[0] += 1

            ensure_kv(RK - 1)

            def do_qk(bi):
                h, kc, kbl = blocks[bi]
                n = nblk[0] + bi
                buf = n % 2
                sl, t_kv = kvslot[(h, kc)]
                diag = (kc == T)
                a = kbl * 128 if diag else 0
                adds = []
                if diag:
                    adds.append((kbl, 0))
                    if kbl + 1 < 4:
                        adds.append((kbl + 1, 1))
                elif kc == T - 1 and kbl == 3:
                    adds.append((0, 1))
                deps = [t_kv] + t_q + (exp_tok.get(n - 2) and [exp_tok[n - 2]] or [])
                for m in range(2):
                    last = (m == 1 and not adds)
                    tq = P.op("pe", lambda e, m=m, buf=buf, sl=sl, kbl=kbl, a=a, h=h: e.matmul(
                        psS[:, buf * 2 + m, a:TW], lhsT=kring[64 * m:64 * m + 64, sl, kbl * 128:kbl * 128 + 128],
                        rhs=qT[64 * m:64 * m + 64, h, a:TW], start=True, stop=True, skip_group_check=True),
                        deps if m == 0 else (), sig=(sem_qk if last else None))
                for ai, (s_, kind) in enumerate(adds):
                    for m in range(2):
                        for part in range(2):
                            last = (ai == len(adds) - 1 and m == 1 and part == 1)
                            tq = P.op("pe", lambda e, m=m, buf=buf, s_=s_, kind=kind, part=part, h=h: e.matmul(
                                psS[:, buf * 2 + m, s_ * 128:s_ * 128 + 128], lhsT=ident[:],
                                rhs=biasT[:, h, kind, part, :], start=False, stop=True, skip_group_check=True),
                                (), sig=(sem_qk if last else None))
                return tq, a

            qk_info = {}
            deferred = []
            cur_bi = [0]

            def do_exp(bi):
                h, kc, kbl = blocks[bi]
                n = nblk[0] + bi
                tq, a = qk_info[bi]
                deps = [tq]
                if n - RP in pv_tok:
                    deps.append(pv_tok[n - RP])
                exp_tok[n] = P.op("act", lambda e, n=n, a=a, h=h: e.activation(
                    out=pring[:, n % RP, :, a:TW], in_=psS[:, (n % 2) * 2:(n % 2) * 2 + 2, a:TW], func=AF.Exp,
                    bias=cfar[:, h:h + 1], scale=1.0), deps, sig=sem_exp)

            def do_pv(bi):
                h, kc, kbl = blocks[bi]
                n = nblk[0] + bi
                sl, t_kv = kvslot[(h, kc)]
                diag = (kc == T)
                first = (kc == 0 and kbl == 0)
                lastb = (kc == T and kbl == 3)
                s0 = kbl if diag else 0
                ops = [(s, m) for s in range(s0, 4) for m in range(2)]
                for oi_, (s, m) in enumerate(ops):
                    a_ = s * 2 + m
                    deps = []
                    if oi_ == 0:
                        deps = [exp_tok[n]]
                        if first:
                            deps.append(acc_free[0])
                    tp_ = P.op("pe", lambda e, s=s, m=m, a_=a_, n=n, sl=sl, kbl=kbl, first=first, lastb=lastb: e.matmul(
                        psA[:, a_ // 3, (a_ % 3) * ACCW:(a_ % 3) * ACCW + 129],
                        lhsT=pring[:, n % RP, m, s * 128:s * 128 + 128], rhs=vring[:, sl, kbl, :],
                        start=(first and a_ % 3 == 0), stop=lastb, skip_group_check=True),
                        deps, sig=(sem_pv if oi_ == len(ops) - 1 else None))
                pv_tok[n] = tp_
                if kbl == 3:
                    kv_free[sl] = tp_
                    ensure_kv(kv_issued[0] + 1)
                if lastb:
                    epilogue(h, tp_)

            def epilogue(h, t_last):
                deps = [t_last] + t_gate
                tcs = []
                for bk_ in range(3):
                    na = 3 if bk_ < 2 else 2
                    tcs.append(P.op("dve", lambda e, bk_=bk_, na=na: e.tensor_copy(
                        out=accs[:, bk_ * 3:bk_ * 3 + na, :],
                        in_=psA[:, bk_, 0:na * ACCW].rearrange("p (a w) -> p a w", w=ACCW)), deps + t_ep[-1:]))
                acc_free[0] = tcs[-1]
                t = P.op("dve", lambda e: e.reciprocal(out=rec[:], in_=accs[:, :, 128]), tcs)
                t = P.op("dve", lambda e: e.tensor_scalar(out=nr2[:], in0=rec[:].rearrange("p (s m) -> p s m", m=2)[:, :, 1],
                                                          scalar1=nlam[:, 0:1], scalar2=None, op0=ALU.mult), [t, t_nlam])
                for s in range(4):
                    t = P.op("dve", lambda e, s=s: e.tensor_scalar(out=ttmp[:], in0=accs[:, 2 * s + 1, 0:128],
                                                                   scalar1=nr2[:, s:s + 1], scalar2=None, op0=ALU.mult), [t])
                    t = P.op("dve", lambda e, s=s: e.scalar_tensor_tensor(
                        out=otmp[:, s, :], in0=accs[:, 2 * s, 0:128], scalar=rec[:, 2 * s:2 * s + 1], in1=ttmp[:],
                        op0=ALU.mult, op1=ALU.add), [t])
                    t = P.op("dve", lambda e, s=s: e.tensor_tensor(out=sqtmp[:], in0=otmp[:, s, :], in1=otmp[:, s, :],
                                                                   op=ALU.mult), [t])
                    t = P.op("dve", lambda e, s=s: e.reduce_sum(out=ssq[:, s:s + 1], in_=sqtmp[:], axis=AX.X), [t])
                t_ssq = t

                def part2(h=h, t_ssq=t_ssq):
                    t = P.op("act", lambda e: e.activation(out=rstd[:], in_=ssq[:], func=AF.Ln, bias=epsb[:, 0:1],
                                                           scale=1.0 / 128.0), [t_ssq])
                    t = P.op("act", lambda e: e.activation(out=rstd[:], in_=rstd[:], func=AF.Exp, scale=-0.5), [t])
                    for s in range(4):
                        t = P.op("dve", lambda e, s=s, h=h: e.scalar_tensor_tensor(
                            out=ztok[:, s, h * 128:h * 128 + 128], in0=otmp[:, s, :], scalar=rstd[:, s:s + 1],
                            in1=sgd[:, s, h * 128:h * 128 + 128], op0=ALU.mult, op1=ALU.mult), [t])
                    t_ep.append(t)
                deferred.append([cur_bi[0] + 6, part2])

            for mh in range(MH):
                for blk in range(2):
                    n = nblk[0]
                    nblk[0] += 1
                    buf = n % 2
                    deps = t_q + ([exp_tok[n - 2]] if (n - 2) in exp_tok else [])
                    tq = P.op("pe", lambda e, buf=buf, mh=mh, blk=blk: e.matmul(
                        psS[:, buf * 2, :], lhsT=mkT[:, mh, blk * 128:blk * 128 + 128], rhs=mqT[:, mh, :],
                        start=True, stop=True, skip_group_check=True), deps, sig=sem_qk)
                    deps = [tq]
                    if n - RP in pv_tok:
                        deps.append(pv_tok[n - RP])
                    exp_tok[n] = P.op("act", lambda e, n=n, buf=buf: e.activation(
                        out=pring[:, n % RP, 0, :], in_=psS[:, buf * 2, :], func=AF.Exp), deps, sig=sem_exp)
                    for s in range(4):
                        deps = []
                        if s == 0:
                            deps = [exp_tok[n]]
                            if blk == 0:
                                deps.append(acc_free[0])
                        tp_ = P.op("pe", lambda e, s=s, n=n, mh=mh, blk=blk: e.matmul(
                            psA[:, s // 3, (s % 3) * ACCW:(s % 3) * ACCW + 129],
                            lhsT=pring[:, n % RP, 0, s * 128:s * 128 + 128], rhs=mvx[:, blk, mh, :],
                            start=(blk == 0 and s % 3 == 0), stop=(blk == 1), skip_group_check=True),
                            deps, sig=(sem_pv if s == 3 else None))
                    pv_tok[n] = tp_
                tcs = [P.op("dve", lambda e: e.tensor_copy(
                    out=accs[:, 0:3, :], in_=psA[:, 0, 0:3 * ACCW].rearrange("p (a w) -> p a w", w=ACCW)),
                    [tp_] + t_gate + t_ep[-1:]),
                    P.op("dve", lambda e: e.tensor_copy(
                        out=accs[:, 3:4, :], in_=psA[:, 1, 0:ACCW].rearrange("p (a w) -> p a w", w=ACCW)), [tp_])]
                acc_free[0] = tcs[-1]
                t = P.op("dve", lambda e: e.reciprocal(out=rec[:, 0:4], in_=accs[:, 0:4, 128]), tcs)
                for s in range(4):
                    t = P.op("dve", lambda e, s=s, mh=mh: e.scalar_tensor_tensor(
                        out=ztok[:, s, 1024 + mh * 128:1024 + mh * 128 + 128], in0=accs[:, s, 0:128],
                        scalar=rec[:, s:s + 1], in1=sgm[:, s, mh * 128:mh * 128 + 128], op0=ALU.mult, op1=ALU.mult), [t])
                t_ep.append(t)

            post_mem()
            for b0 in range(min(2, nb_total)):
                qk_info[b0] = do_qk(b0)
                do_exp(b0)
            for bi in range(nb_total):
                cur_bi[0] = bi
                if bi + 2 < nb_total:
                    qk_info[bi + 2] = do_qk(bi + 2)
                    do_exp(bi + 2)
                while deferred and deferred[0][0] <= bi:
                    deferred.pop(0)[1]()
                do_pv(bi)
            while deferred:
                deferred.pop(0)[1]()
            nblk[0] += nb_total

            t_zt = []
            t_tp_free = [None]
            for s in range(4):
                for grp in range(3):
                    for k4 in range(4):
                        cidx_ = grp * 4 + k4
                        tt = P.op("pe", lambda e, s=s, cidx_=cidx_, k4=k4: e.transpose(
                            psTb[:, k4 * 128:k4 * 128 + 128], ztok[:, s, cidx_ * 128:cidx_ * 128 + 128], ident[:]),
                            (t_ep[-1:] + [t_tp_free[0]]) if k4 == 0 else (), sig=("chain" if k4 == 3 else None))
                    dst0 = grp * 4 if grp < 2 else 12
                    tcp = P.op("dve", lambda e, s=s, dst0=dst0: e.tensor_copy(
                        out=zT[:, dst0:dst0 + 4, s * 128:s * 128 + 128],
                        in_=psTb[:, 0:512].rearrange("p (k t) -> p k t", t=128)), [tt])
                    t_tp_free[0] = tcp
                    t_zt.append(tcp)
            t_zt.append(P.op("pool", lambda e: e.tensor_copy(out=zT[:, 8:12, 0:TW], in_=zTc[:]), t_cv + t_ep[-1:]))

            t_y = {}
            xr_free = [None, None]
            xcnt = 0
            for g in range(4):
                if g < 2:
                    r, t_w = w_o[g]
                else:
                    r, t_w = load_w(G_O0 + g)
                for s in range(4):
                    xsel = xcnt % 2
                    xcnt += 1
                    t_xr = P.dma("sp", xres[xsel][:], xown[i * TW + s * 128:i * TW + s * 128 + 128, g * 512:g * 512 + 512],
                                 [xr_free[xsel]], key="xr%d" % xsel)

                    def mm(e, out, c, st, sp_, r=r, s=s):
                        return e.matmul(out, lhsT=zT[:, c, s * 128:s * 128 + 128], rhs=wring[r][:, c, :], start=st, stop=sp_)

                    def ev(bk, tpe, s=s, g=g, xsel=xsel, t_xr=t_xr):
                        return P.op("dve", lambda e: e.scalar_tensor_tensor(
                            out=ysb[:, s, g * 512:g * 512 + 512], in0=xres[xsel][:], scalar=ALPHA, in1=bk,
                            op0=ALU.mult, op1=ALU.add), [tpe, t_xr, y_st_tok[0]] + t_cv)
                    last_pe, t_e = job(mm, 512, ev, t_zt + [t_w])
                    xr_free[xsel] = t_e
                    t_y[(s, g)] = t_e
                wr_free[r] = last_pe
            t_done = []
            t = None
            for s in range(4):
                for g in range(4):
                    t = P.op("dve", lambda e, s=s, g=g: e.bn_stats(out=bnst4[:, s, g, :], in_=ysb[:, s, g * 512:g * 512 + 512]),
                             [t_y[(s, g)], t])
                t = P.op("dve", lambda e, s=s: e.bn_aggr(out=mv4[:, s, :], in_=bnst4[:, s].rearrange("p a b -> p (a b)")), [t])
            t_r = P.op("act", lambda e: e.activation(out=lnr4[:], in_=mv4[:, :, 1], func=AF.Ln, bias=epsb[:, 0:1], scale=1.0), [t])
            t_r = P.op("act", lambda e: e.activation(out=lnr4[:], in_=lnr4[:], func=AF.Exp, scale=-0.5), [t_r])
            for s in range(4):
                t = P.op("dve", lambda e, s=s: e.scalar_tensor_tensor(out=ysb[:, s, :], in0=ysb[:, s, :], scalar=mv4[:, s, 0:1],
                                                                      in1=lngb, op0=ALU.subtract, op1=ALU.mult), [t, t_lng])
                t = P.op("dve", lambda e, s=s: e.scalar_tensor_tensor(out=ysb[:, s, :], in0=ysb[:, s, :], scalar=lnr4[:, s:s + 1],
                                                                      in1=lnb_sb[:], op0=ALU.mult, op1=ALU.add), [t, t_lnb, t_r])
                t_o = P.dma("pool", y_o[i * TW + s * 128:i * TW + s * 128 + 128, :], ysb[:, s, :], [t], key="yout")
                t_done.append(t_o)
            y_st_tok[0] = t_done[-1]
            prev_slot_done = t_done + [t]
        P.emit()
        P.op("dve", lambda e: e.memset(lnr[:], 0.0), ())
        P.emit()
        SB.__exit__(None, None, None)
    return nc


_CACHE = {}


def _host_consts():
    half, max_exact = 16, 8

    def bucket(rel):
        ret = np.where(rel > 0, half, 0)
        n = np.abs(rel)
        nf = np.maximum(n, 1).astype(np.float32)
        large = max_exact + (np.log(nf / max_exact) / math.log(128 / max_exact) * (half - max_exact)).astype(np.int32)
        large = np.minimum(large, half - 1)
        return ret + np.where(n < max_exact, n, large)

    k = np.arange(128)[:, None]
    q = np.arange(128)[None, :]
    bd = bucket(k - q)
    bp = bucket(k - q - 128)
    bkt = np.concatenate([bd, bp], axis=1).astype(np.float32)
    dmask = np.where((k // 64) <= (q // 64), 0.0, NEG).astype(np.float32)
    return bkt, dmask


def kernel(x_prompt, x_sample, cache_diff_k, cache_diff_v, cache_conv, cache_mem_k, cache_mem_v,
           mem_prompt, rel_bias_table, w_in, w_mem_kv, conv_w, lambda_q1, lambda_k1, lambda_q2,
           lambda_k2, subln_g, w_out, ln_g, ln_b, _nslot=NSLOT, _sample=True):
    f = np.float32
    x_prompt = np.asarray(x_prompt, f)
    B, S, _ = x_prompt.shape
    key = (_nslot, _sample)
    if key not in _CACHE:
        _CACHE[key] = build(_nslot, _sample)
    nc = _CACHE[key]

    bkt, dmask = _host_consts()
    rep = lambda v, n=128: np.ascontiguousarray(np.broadcast_to(np.asarray(v, f).reshape(1, -1), (n, np.asarray(v).size)))
    shared = {
        "w_in": np.ascontiguousarray(np.asarray(w_in, f)[0]),
        "w_out": np.ascontiguousarray(np.asarray(w_out, f)[0]),
        "w_mem": np.ascontiguousarray(np.asarray(w_mem_kv, f)[0]),
        "ident": np.eye(128, dtype=f),
        "tab": rep(np.asarray(rel_bias_table, f).reshape(-1)),
        "lamv": rep(np.concatenate([np.asarray(a, f).reshape(-1) for a in
                                    (lambda_q1, lambda_k1, lambda_q2, lambda_k2)])),
        "gsub": rep(np.tile(np.asarray(subln_g, f).reshape(-1), 4)),
        "lng": rep(np.asarray(ln_g, f).reshape(-1)),
        "lnb": rep(np.asarray(ln_b, f).reshape(-1)),
        "convw": np.ascontiguousarray(np.asarray(conv_w, f)[0].reshape(3, 4, 128).transpose(2, 1, 0).reshape(128, 12)),
        "bkt": bkt,
        "dmask": dmask,
    }
    in_maps = []
    xTb = [np.ascontiguousarray(x_prompt[b].T) for b in range(B)]
    memTb = [np.ascontiguousarray(np.asarray(mem_prompt, f)[b].T) for b in range(B)]
    for c in range(NCORES):
        b, j = c // 4, c % 4
        xT = np.zeros((D, NT * TW), f)
        xT[:, (3 - j) * TW:(3 - j) * TW + S] = xTb[b]
        xown = np.concatenate([x_prompt[b, (4 * i + j) * TW:(4 * i + j + 1) * TW] for i in range(NSLOT)], axis=0)
        vm = np.zeros((NT, 32), f)
        vm[3 - j:3 - j + S // TW] = 1.0
        m = dict(shared)
        m["xT"] = xT
        m["xown"] = np.ascontiguousarray(xown)
        m["vmask"] = rep(vm.reshape(-1))
        m["memT"] = memTb[b]
        in_maps.append(m)
    if _sample:
        xs_all = np.asarray(x_sample, f)
        ck = np.asarray(cache_diff_k, f)[0]
        cvv = np.asarray(cache_diff_v, f)[0]
        cc_ = np.asarray(cache_conv, f)[0]
        cmk = np.asarray(cache_mem_k, f)[0]
        cmv = np.asarray(cache_mem_v, f)[0]
        for c in range(NCORES):
            sl = slice(c * NSTR, (c + 1) * NSTR)
            xs_c = xs_all[sl].reshape(NSTR * SS, D)
            m = in_maps[c]
            m["xs"] = np.ascontiguousarray(xs_c)
            m["xsT"] = np.ascontiguousarray(xs_c.T)
            m["ckT"] = np.ascontiguousarray(ck[sl].transpose(0, 2, 3, 1))
            m["cv"] = np.ascontiguousarray(cvv[sl].reshape(NSTR, PAST, 1024))
            m["cconv"] = np.ascontiguousarray(cc_[sl].reshape(NSTR, 2, 4, 128).transpose(3, 2, 0, 1))
            m["cmkT"] = np.ascontiguousarray(cmk[sl].transpose(0, 2, 3, 1))
            m["cmv"] = np.ascontiguousarray(cmv[sl].reshape(NSTR, NMEM, 512))
    res = run_bass_kernel_spmd(nc, in_maps, core_ids=list(range(NCORES)))
    R = res.results

    y = np.zeros((B, S, D), f)
    kp = np.zeros((1, B, S, H, 128), f)
    vp = np.zeros((1, B, S, H, 128), f)
    for c in range(NCORES):
        b, j = c // 4, c % 4
        for i in range(NSLOT):
            t = 4 * i + j
            y[b, t * TW:(t + 1) * TW] = R[c]["y"][i * TW:(i + 1) * TW]
            kp[0, b, t * TW:(t + 1) * TW] = R[c]["koT"][:, :, i * TW:(i + 1) * TW].transpose(2, 0, 1)
            vp[0, b, t * TW:(t + 1) * TW] = R[c]["vo"][i * TW:(i + 1) * TW].reshape(TW, H, 128)
    convp = np.zeros((1, B, 2, 512), f)
    mkp = np.zeros((1, B, NMEM, MH, 128), f)
    mvp = np.zeros((1, B, NMEM, MH, 128), f)
    for b in range(B):
        c = 4 * b + 3
        convp[0, b] = R[c]["convo"].transpose(2, 1, 0).reshape(2, 512)
        mkp[0, b] = R[c]["mkoT"].transpose(2, 0, 1)
        mvp[0, b] = R[c]["mvo"].reshape(NMEM, MH, 128)
    ys = np.zeros((32, SS, D), f)
    ks = np.zeros((1, 32, SS, H, 128), f)
    vs = np.zeros((1, 32, SS, H, 128), f)
    cs = np.zeros((1, 32, 2, 512), f)
    if _sample:
        for c in range(NCORES):
            sl = slice(c * NSTR, (c + 1) * NSTR)
            ys[sl] = R[c]["ys"].reshape(NSTR, SS, D)
            ks[0, sl] = R[c]["ksT"].transpose(2, 0, 1).reshape(NSTR, SS, H, 128)
            vs[0, sl] = R[c]["vs"].reshape(NSTR, SS, H, 128)
            cs[0, sl] = R[c]["convs"].transpose(1, 3, 2, 0).reshape(NSTR, 2, 512)
    return (y, ys, kp, vp, convp, mkp, mvp, ks, vs, cs)
```

```python
import math
from contextlib import ExitStack

import numpy as np
import concourse.bass as bass
import concourse.mybir as mybir
from concourse.bass_utils import run_bass_kernel_spmd

F32 = mybir.dt.float32
BF16 = mybir.dt.bfloat16
AF = mybir.ActivationFunctionType
ALU = mybir.AluOpType
AX = mybir.AxisListType

NCORES = 8
D = 2048
TW = 512
NT = 35
NSLOT = 8
NCH = 16
H = 8
MH = 4
NMEM = 256
EPS = 1e-5
ALPHA = 2.0 ** 0.25
LAM_INIT = 0.8 - 0.6 * math.exp(-0.0)
ACCW = 130
RK = 4
RP = 3
NEG = -30000.0
SS = 16
NSTR = 4
PAST = 1024

WG = [("w_in", 0), ("w_in", 512),
      ("w_in", 3072), ("w_in", 3584), ("w_in", 4096),
      ("w_in", 4608),
      ("w_in", 5120), ("w_in", 5632),
      ("w_in", 6144), ("w_in", 6656),
      ("w_out", 0), ("w_out", 512), ("w_out", 1024), ("w_out", 1536)]
G_Q0, G_Q1, G_H, G_B, G_C, G_MQ, G_GD0, G_GD1, G_GC, G_GM, G_O0 = range(11)


class Sem:
    def __init__(self, h):
        self.h = h
        self.v = 0


class Prog:
    ENG = ("pe", "act", "dve", "pool", "sp")

    def __init__(self, nc, stack):
        self.nc = nc
        self.stack = stack
        self.sems = []
        self.q = {e: [] for e in self.ENG}
        self.chain = {e: self.sem("ch_" + e) for e in self.ENG}
        self.barrier_toks = {e: [] for e in self.ENG}
        self.waited = {e: {} for e in self.ENG}
        self.dsems = {}

    def sem(self, name):
        s = Sem(self.stack.enter_context(self.nc.semaphore(name)))
        self.sems.append(s)
        return s

    def op(self, eng, fn, after=(), sig="chain", amt=1):
        waits = [t for t in after if t is not None]
        if self.barrier_toks[eng]:
            waits = list(self.barrier_toks[eng]) + waits
            self.barrier_toks[eng] = []
        w2 = []
        wd = self.waited[eng]
        for (s, v) in waits:
            if wd.get(id(s), 0) >= v:
                continue
            wd[id(s)] = v
            w2.append((s, v))
        if sig == "chain":
            sig = self.chain[eng]
        tok = None
        if sig is not None:
            sig.v += amt
            tok = (sig, sig.v)
        self.q[eng].append((w2, fn, sig, amt))
        return tok

    def dma(self, eng, out, in_, after=(), key=None):
        assert key is not None
        if key not in self.dsems:
            self.dsems[key] = self.sem("d_" + key)
        return self.op(eng, lambda e: e.dma_start(out=out, in_=in_), after, self.dsems[key], 16)

    def emit(self):
        nc = self.nc
        q = self.q

        def run(e, lst):
            for (waits, fn, sig, amt) in lst:
                for (s, v) in waits:
                    e.wait_ge(s.h, v)
                inst = fn(e)
                if sig is not None:
                    inst.then_inc(sig.h, amt)

        with nc.Block() as block:
            if q["pe"]:
                @block.tensor
                def _(e):
                    run(e, q["pe"])
            if q["act"]:
                @block.scalar
                def _(e):
                    run(e, q["act"])
            if q["dve"]:
                @block.vector
                def _(e):
                    run(e, q["dve"])
            if q["pool"]:
                @block.gpsimd
                def _(e):
                    run(e, q["pool"])
            if q["sp"]:
                @block.sync
                def _(e):
                    run(e, q["sp"])
        self.q = {e: [] for e in self.ENG}
        toks = [(s, s.v) for s in self.sems if s.v > 0]
        for e in self.ENG:
            self.barrier_toks[e] = list(toks)


def build(nslot=NSLOT, do_sample=True):
    nc = bass.Bass("TRN2", target_bir_lowering=False)
    top = ExitStack()
    with top:
        P = Prog(nc, top)

        def din(name, shape, dt=F32):
            return nc.dram_tensor(name, list(shape), dt, kind="ExternalInput").ap()

        def dout(name, shape, dt=F32):
            return nc.dram_tensor(name, list(shape), dt, kind="ExternalOutput").ap()

        def dscr(name, shape, dt):
            return nc.dram_tensor(name, list(shape), dt).ap()

        def sb(stack, name, shape, dt):
            return stack.enter_context(nc.sbuf_tensor("s_" + name, list(shape), dt))

        xT = din("xT", [D, NT * TW])
        xown = din("xown", [NSLOT * TW, D])
        vmask = din("vmask", [128, NT * 32])
        w_srcs = {"w_in": din("w_in", [D, 7168]), "w_out": din("w_out", [D, D])}
        w_mem = din("w_mem", [D, 1024])
        memT = din("memT", [D, NMEM])
        ident_d = din("ident", [128, 128])
        tab_d = din("tab", [128, 256])
        lam_d = din("lamv", [128, 256])
        gsub_d = din("gsub", [128, 512])
        lng_d = din("lng", [128, D])
        lnb_d = din("lnb", [128, D])
        cw_d = din("convw", [128, 12])
        bkt_d = din("bkt", [128, 256])
        dmask_d = din("dmask", [128, 128])

        y_o = dout("y", [NSLOT * TW, D])
        koT_o = dout("koT", [H, 128, NSLOT * TW])
        vo_o = dout("vo", [NSLOT * TW, 1024])
        convo_o = dout("convo", [128, 4, 2])
        mkoT_o = dout("mkoT", [MH, 128, NMEM])
        mvo_o = dout("mvo", [NMEM, 512])

        if do_sample:
            xsT_d = din("xsT", [D, NSTR * SS])
            xs_d = din("xs", [NSTR * SS, D])
            ckT_d = din("ckT", [NSTR, H, 128, PAST])
            cv_d = din("cv", [NSTR, PAST, 1024])
            cconv_d = din("cconv", [128, 4, NSTR, 2])
            cmkT_d = din("cmkT", [NSTR, MH, 128, NMEM])
            cmv_d = din("cmv", [NSTR, NMEM, 512])
            ys_o = dout("ys", [NSTR * SS, D])
            ksT_o = dout("ksT", [H, 128, NSTR * SS])
            vs_o = dout("vs", [NSTR * SS, 1024])
            convs_o = dout("convs", [128, NSTR, 4, 2])

        wsc = dscr("wsc", [len(WG) + 4, 128, NCH, 512], BF16)
        kts = dscr("kts", [NT, 128, H, 512], BF16)
        vxs = dscr("vxs", [NT, 128, H, 4 * 129], BF16)

        psS = top.enter_context(nc.psum_tensor("psS", [128, 4, 512], F32))
        psA = top.enter_context(nc.psum_tensor("psA", [128, 3, 512], F32))
        psT = top.enter_context(nc.psum_tensor("psT", [128, 512], F32))

        psTb = psT[:].bitcast(BF16)

        def bank(b):
            return psS[:, b, :] if b < 4 else psA[:, b - 4, :]

        ident = sb(top, "identb", [128, 128], BF16)
        tab = sb(top, "tab", [128, 256], F32)
        cfar = sb(top, "cfar", [128, 8], F32)
        biasT = sb(top, "biasT", [128, H, 2, 2, 128], BF16)
        gs4 = sb(top, "gs4", [128, 512], F32)
        cw = sb(top, "cw", [128, 12], F32)
        nlam = sb(top, "nlam", [128, 1], F32)
        epsb = sb(top, "epsb", [128, 1], F32)
        mkT = sb(top, "mkT", [128, MH, NMEM], BF16)
        mvx = sb(top, "mvx", [128, 2, MH, 129], BF16)

        SA = ExitStack()
        SA.__enter__()
        wkv = sb(SA, "wkv", [128, 4, NCH, 512], BF16)
        bkt = sb(SA, "bkt", [128, 256], F32)
        dmask = sb(SA, "dmask", [128, 128], F32)
        eqm = sb(SA, "eqm", [128, 128], F32)
        bacc = sb(SA, "bacc", [128, 128], F32)
        bhi = sb(SA, "bhi", [128, 128], F32)
        S0 = ExitStack()
        S0.__enter__()
        stg = [sb(S0, "stg%d" % i, [128, NCH, 512], F32) for i in range(2)]
        wbf = [sb(S0, "wbf%d" % i, [128, NCH, 512], BF16) for i in range(2)]
        identf = sb(S0, "identf", [128, 128], F32)
        lamv = sb(S0, "lamvs", [128, 256], F32)
        ltmp = sb(S0, "ltmp", [128, 64], F32)
        lsum = sb(S0, "lsum", [128, 4], F32)
        memb = sb(S0, "memb", [128, NCH, NMEM], BF16)
        mkf = sb(S0, "mkf", [128, MH, NMEM], F32)
        mvf = sb(S0, "mvf", [128, 2, 512], F32)

        P.dma("sp", identf[:], ident_d, key="c0")
        P.dma("sp", tab[:], tab_d, key="c0")
        P.dma("sp", lamv[:], lam_d, key="c0")
        P.dma("sp", gs4[:], gsub_d, key="c0")
        P.dma("sp", cw[:], cw_d, key="c0")
        P.dma("sp", bkt[:], bkt_d, key="c0")
        t_c0 = P.dma("sp", dmask[:], dmask_d, key="c0")
        t_idb = P.op("dve", lambda e: e.tensor_copy(out=ident[:], in_=identf[:]), [t_c0])
        P.op("dve", lambda e: e.memset(epsb[:], EPS), ())
        t_gs2 = P.op("dve", lambda e: e.tensor_scalar(out=gs4[:], in0=gs4[:], scalar1=1.0 - LAM_INIT,
                                                      scalar2=None, op0=ALU.mult), [t_c0])
        tl = t_c0
        for k in range(2):
            tl = P.op("dve", lambda e, k=k: e.tensor_tensor(out=ltmp[:], in0=lamv[:, 128 * k:128 * k + 64],
                                                            in1=lamv[:, 128 * k + 64:128 * k + 128], op=ALU.mult), [tl])
            tl = P.op("dve", lambda e, k=k: e.reduce_sum(out=lsum[:, k:k + 1], in_=ltmp[:], axis=AX.X), [tl])
        tl = P.op("act", lambda e: e.activation(out=lsum[:, 2:4], in_=lsum[:, 0:2], func=AF.Exp), [tl])
        tl = P.op("dve", lambda e: e.tensor_tensor(out=nlam[:], in0=lsum[:, 3:4], in1=lsum[:, 2:3], op=ALU.subtract), [tl])
        t_nlam = P.op("dve", lambda e: e.tensor_scalar(out=nlam[:], in0=nlam[:], scalar1=-LAM_INIT, scalar2=None,
                                                       op0=ALU.add), [tl])
        t_cf = P.op("dve", lambda e: e.tensor_copy(out=cfar[:], in_=tab[:, 120:128]), [t_c0])
        def wsrc(name, col):
            return w_srcs[name].rearrange("(c p) n -> p c n", p=128)[:, :, col:col + 512]

        jobs = [("kv", k, w_srcs["w_in"].rearrange("(c p) n -> p c n", p=128)[:, :, 1024 + 512 * k:1536 + 512 * k])
                for k in range(4)]
        jobs += [("mem", k, w_mem.rearrange("(c p) n -> p c n", p=128)[:, :, 512 * k:512 * k + 512]) for k in range(2)]
        stg_free = [None, None]
        wbf_free = [None, None]
        memw_tok = [None, None]
        for n, (kind, idx, src) in enumerate(jobs):
            bsel = n % 2
            t_ld = P.dma("sp", stg[bsel][:], src, [stg_free[bsel]], key="stg%d" % bsel)
            if kind == "kv":
                dst = wkv[:, idx]
            else:
                dst = wbf[bsel][:]
            if n % 2 == 0:
                t_c = P.op("dve", lambda e, dst=dst, bsel=bsel: e.tensor_copy(out=dst, in_=stg[bsel][:]),
                           [t_ld, wbf_free[bsel]])
            else:
                t_c = P.op("act", lambda e, dst=dst, bsel=bsel: e.activation(out=dst, in_=stg[bsel][:], func=AF.Copy),
                           [t_ld, wbf_free[bsel]])
            stg_free[bsel] = t_c
            if kind == "scr":
                wbf_free[bsel] = P.dma("act", wsc[idx], wbf[bsel][:], [t_c], key="wst%d" % bsel)
            elif kind == "kv":
                P.dma("act", wsc[len(WG) + idx], wkv[:, idx], [t_c], key="wstkv")
            elif kind == "mem":
                memw_tok[idx] = (t_c, bsel)

        memstg = stg[0][:, :, 0:NMEM]
        t_m = P.dma("sp", memstg, memT.rearrange("(c p) n -> p c n", p=128), [stg_free[0]], key="stg0")
        t_mb = P.op("dve", lambda e: e.tensor_copy(out=memb[:], in_=memstg), [t_m])
        bank_free = [None] * 7
        jb = 0
        wmk = wbf[memw_tok[0][1]]
        wmv = wbf[memw_tok[1][1]]
        for mh in range(MH):
            b = jb % 7
            jb += 1
            for c in range(NCH):
                t_pe = P.op("pe", lambda e, b=b, c=c, mh=mh: e.matmul(
                    bank(b)[:, 0:NMEM], lhsT=wmk[:, c, mh * 128:mh * 128 + 128], rhs=memb[:, c, :],
                    start=(c == 0), stop=(c == NCH - 1)),
                    [t_mb, memw_tok[0][0], bank_free[b], t_idb] if c == 0 else (), sig=("chain" if c == NCH - 1 else None))
            t1 = P.op("act", lambda e, b=b, mh=mh: e.activation(out=mkT[:, mh, :], in_=bank(b)[:, 0:NMEM], func=AF.Copy), [t_pe])
            t2 = P.op("dve", lambda e, b=b, mh=mh: e.tensor_copy(out=mkf[:, mh, :], in_=bank(b)[:, 0:NMEM]), [t_pe, t1])
            bank_free[b] = t2
        P.dma("sp", mkoT_o.rearrange("h p n -> p h n"), mkf[:], [t2], key="mko")
        for blk in range(2):
            b = jb % 7
            jb += 1
            for c in range(NCH):
                t_pe = P.op("pe", lambda e, b=b, c=c, blk=blk: e.matmul(
                    bank(b), lhsT=memb[:, c, blk * 128:blk * 128 + 128], rhs=wmv[:, c, :],
                    start=(c == 0), stop=(c == NCH - 1)),
                    [t_mb, memw_tok[1][0], bank_free[b]] if c == 0 else (), sig=("chain" if c == NCH - 1 else None))
            t1 = P.op("act", lambda e, b=b, blk=blk: e.activation(
                out=mvx[:, blk, :, 0:128], in_=bank(b).rearrange("p (h e) -> p h e", e=128), func=AF.Copy), [t_pe])
            t2 = P.op("dve", lambda e, b=b, blk=blk: e.tensor_copy(out=mvf[:, blk, :], in_=bank(b)), [t_pe, t1])
            bank_free[b] = t2
        for blk in range(2):
            t2 = P.op("dve", lambda e, blk=blk: e.memset(mvx[:, blk, :, 128:129], 1.0), [t2])
        P.dma("sp", mvo_o.rearrange("(b p) n -> p b n", p=128), mvf[:], [t2], key="mvo")
        P.emit()
        S0.__exit__(None, None, None)

        xstg = [sb(SA, "xstg%d" % i, [128, 4, TW], F32) for i in range(2)]
        xb = [sb(SA, "xb%d" % i, [128, NCH, TW], BF16) for i in range(2)]
        ktst = [sb(SA, "ktst%d" % i, [128, H, TW], BF16) for i in range(2)]
        vxst = [sb(SA, "vxst%d" % i, [128, H, 4, 129], BF16) for i in range(2)]
        vm = sb(SA, "vm", [128, NT * 32], F32)
        kof = [sb(SA, "kof%d" % i, [128, TW], F32) for i in range(2)]
        vof = [sb(SA, "vof%d" % i, [128, 512], F32) for i in range(2)]

        def bias_gen():
            tp = t_c0
            for h in range(H):
                for kind in range(2):
                    nb = range(32) if kind == 0 else range(1, 16)
                    first = True
                    for b in nb:
                        tp = P.op("dve", lambda e, b=b, kind=kind: e.tensor_single_scalar(
                            out=eqm[:], in_=bkt[:, 128 * kind:128 * kind + 128], scalar=float(b), op=ALU.is_equal), [tp])
                        if first:
                            tp = P.op("dve", lambda e, b=b, h=h: e.tensor_scalar(
                                out=bacc[:], in0=eqm[:], scalar1=tab[:, b * 8 + h:b * 8 + h + 1], scalar2=None,
                                op0=ALU.mult), [tp])
                            first = False
                        else:
                            tp = P.op("dve", lambda e, b=b, h=h: e.scalar_tensor_tensor(
                                out=bacc[:], in0=eqm[:], scalar=tab[:, b * 8 + h:b * 8 + h + 1], in1=bacc[:],
                                op0=ALU.mult, op1=ALU.add), [tp])
                        yield
                    tp = P.op("dve", lambda e, h=h: e.tensor_scalar(
                        out=bacc[:], in0=bacc[:], scalar1=tab[:, 120 + h:121 + h], scalar2=None, op0=ALU.subtract), [tp])
                    if kind == 0:
                        tp = P.op("dve", lambda e: e.tensor_tensor(out=bacc[:], in0=bacc[:], in1=dmask[:], op=ALU.add), [tp])
                    tp = P.op("dve", lambda e, h=h, kind=kind: e.tensor_copy(out=biasT[:, h, kind, 0, :], in_=bacc[:]), [tp])
                    tp = P.op("dve", lambda e, h=h, kind=kind: e.tensor_copy(out=bhi[:], in_=biasT[:, h, kind, 0, :]), [tp])
                    tp = P.op("dve", lambda e: e.tensor_tensor(out=bacc[:], in0=bacc[:], in1=bhi[:], op=ALU.subtract), [tp])
                    tp = P.op("dve", lambda e, h=h, kind=kind: e.tensor_copy(out=biasT[:, h, kind, 1, :], in_=bacc[:]), [tp])
                    yield
        bgen = bias_gen()
        wq_f = [sb(SA, "wqf%d" % i, [128, NCH, 128], F32) for i in range(2)]
        wq_b = [sb(SA, "wqb%d" % i, [128, NCH, 128], BF16) for i in range(2)]

        def wcast_gen():
            f_free = [None, None]
            b_free = [None, None]
            n = 0
            for gi in range(len(WG)):
                name, col = WG[gi]
                srcv = w_srcs[name].rearrange("(c p) n -> p c n", p=128)
                for qq in range(4):
                    k = n % 2
                    n += 1
                    t_ld = P.dma("sp", wq_f[k][:], srcv[:, :, col + qq * 128:col + qq * 128 + 128], [f_free[k]],
                                 key="wqf%d" % k)
                    t_c = P.op("act", lambda e, k=k: e.activation(out=wq_b[k][:], in_=wq_f[k][:], func=AF.Copy),
                               [t_ld, b_free[k]])
                    f_free[k] = t_c
                    b_free[k] = P.dma("act", wsc[gi, :, :, qq * 128:qq * 128 + 128], wq_b[k][:], [t_c], key="wqs%d" % k)
                    yield
        wgen = wcast_gen()
        t_vm = P.dma("sp", vm[:], vmask, key="vm")
        xTr = xT.rearrange("(c p) t -> p c t", p=128)
        xstg_free = [None] * 2
        xb_free = [None, None]
        kt_free = [None, None]
        vx_free = [None, None]
        kof_free = [None, None]
        vof_free = [None, None]
        nko = 0
        nvo = 0
        bank_free = [None] * 7
        jb = 0
        npiece = 0
        ntile = NT if nslot == NSLOT else 4 * (nslot - 1) + 4
        def load_x(T):
            nonlocal npiece
            xs_ = T % 2
            t_x = []
            for pc in range(4):
                st = npiece % 2
                npiece += 1
                t_ld = P.dma("sp", xstg[st][:], xTr[:, 4 * pc:4 * pc + 4, T * TW:(T + 1) * TW], [xstg_free[st]],
                             key="xstg%d" % st)
                t_c = P.op("dve", lambda e, st=st, pc=pc, xs_=xs_: e.tensor_copy(
                    out=xb[xs_][:, 4 * pc:4 * pc + 4, :], in_=xstg[st][:]), [t_ld, xb_free[xs_]])
                xstg_free[st] = t_c
                t_x.append(t_c)
            return t_x

        t_x_next = load_x(0)
        for T in range(ntile):
            xs_ = T % 2
            own = (T % 4 == 3) and (T // 4) < nslot
            slot = T // 4
            t_x = t_x_next
            if T + 1 < ntile:
                t_x_next = load_x(T + 1)
            t_kev = []
            for h in range(H):
                b = jb % 7
                jb += 1
                for c in range(NCH):
                    t_pe = P.op("pe", lambda e, b=b, c=c, h=h, xs_=xs_: e.matmul(
                        bank(b), lhsT=wkv[:, h // 4, c, (h % 4) * 128:(h % 4) * 128 + 128], rhs=xb[xs_][:, c, :],
                        start=(c == 0), stop=(c == NCH - 1)),
                        (t_x + [bank_free[b]]) if c == 0 else (), sig=("chain" if c == NCH - 1 else None))
                t1 = P.op("act", lambda e, b=b, h=h, xs_=xs_: e.activation(
                    out=ktst[xs_][:, h, :], in_=bank(b), func=AF.Copy), [t_pe, kt_free[xs_]])
                t_kev.append(t1)
                if own:
                    ks_ = nko % 2
                    nko += 1
                    t2 = P.op("dve", lambda e, b=b, ks_=ks_: e.tensor_copy(out=kof[ks_][:], in_=bank(b)),
                              [t_pe, t1, kof_free[ks_]])
                    bank_free[b] = t2
                    kof_free[ks_] = P.dma("act", koT_o[h, :, slot * TW:(slot + 1) * TW], kof[ks_][:], [t2],
                                          key="kof%d" % ks_)
                else:
                    bank_free[b] = t1
            kt_free[xs_] = P.dma("act", kts[T], ktst[xs_][:], t_kev, key="kst%d" % xs_)
            t_vev = []
            t_vm2 = P.op("pool", lambda e, xs_=xs_, T=T: e.tensor_copy(
                out=vxst[xs_][:, :, :, 128], in_=vm[:, T * 32:(T + 1) * 32].rearrange("p (h s) -> p h s", s=4)),
                [t_vm, vx_free[xs_]])
            t_vev.append(t_vm2)
            for s in range(4):
                for g in range(2):
                    b = jb % 7
                    jb += 1
                    for c in range(NCH):
                        t_pe = P.op("pe", lambda e, b=b, c=c, s=s, g=g, xs_=xs_: e.matmul(
                            bank(b), lhsT=xb[xs_][:, c, s * 128:s * 128 + 128], rhs=wkv[:, 2 + g, c, :],
                            start=(c == 0), stop=(c == NCH - 1)),
                            (t_x + [bank_free[b]]) if c == 0 else (), sig=("chain" if c == NCH - 1 else None))
                    t1 = P.op("dve", lambda e, b=b, s=s, g=g, xs_=xs_: e.tensor_copy(
                        out=vxst[xs_][:, 4 * g:4 * g + 4, s, 0:128], in_=bank(b).rearrange("p (h e) -> p h e", e=128)),
                        [t_pe, vx_free[xs_]])
                    t_vev.append(t1)
                    if own:
                        vs_ = nvo % 2
                        nvo += 1
                        t2 = P.op("act", lambda e, b=b, vs_=vs_: e.activation(
                            out=vof[vs_][:], in_=bank(b), func=AF.Copy), [t_pe, t1, vof_free[vs_]])
                        bank_free[b] = t2
                        vof_free[vs_] = P.dma("act", vo_o[slot * TW + s * 128:slot * TW + s * 128 + 128,
                                                         g * 512:g * 512 + 512], vof[vs_][:], [t2], key="vof%d" % vs_)
                    else:
                        bank_free[b] = t1
            for _ in range(30):
                if next(bgen, "done") == "done":
                    break
            for _ in range(2):
                if next(wgen, "done") == "done":
                    break
            xb_free[xs_] = t_pe
            vx_free[xs_] = P.dma("pool", vxs[T], vxst[xs_][:].rearrange("p h s e -> p h (s e)"), t_vev, key="vst%d" % xs_)
        for _ in bgen:
            pass
        for _ in wgen:
            pass
        P.emit()
        SA.__exit__(None, None, None)

        if do_sample:
            SS_ = ExitStack()
            SS_.__enter__()
            NTK = NSTR * SS
            wS = sb(SS_, "wS", [128, NCH, 512], BF16)
            xsf = sb(SS_, "xsf", [128, NCH, NTK], F32)
            xsb = sb(SS_, "xsb", [128, NCH, NTK], BF16)
            qTs = sb(SS_, "qTs", [128, H, NTK], BF16)
            kTs = sb(SS_, "kTs", [128, H, NTK], BF16)
            kTf = sb(SS_, "kTf", [128, H, NTK], F32)
            hTs = sb(SS_, "hTs", [128, 4, NTK], F32)
            CTs = sb(SS_, "CTs", [128, 4, NTK], F32)
            BTs = sb(SS_, "BTs", [128, 4, NTK], F32)
            gcTs = sb(SS_, "gcTs", [128, 4, NTK], F32)
            mqTs = sb(SS_, "mqTs", [128, MH, NTK], BF16)
            upad = sb(SS_, "upad", [128, 4, NSTR, SS + 2], F32)
            cvt = sb(SS_, "cvt", [128, NSTR, SS], F32)
            cvt2 = sb(SS_, "cvt2", [128, NSTR, SS], F32)
            zTcs = sb(SS_, "zTcs", [128, 4, NTK], BF16)
            convsb = sb(SS_, "convsb", [128, NSTR, 4, 2], F32)
            vfs = [sb(SS_, "vfs%d" % i, [SS, 512], F32) for i in range(2)]
            vnew = sb(SS_, "vnew", [SS, NSTR, H, 129], BF16)
            sgds = sb(SS_, "sgds", [SS, NSTR, 1024], F32)
            sgms = sb(SS_, "sgms", [SS, NSTR, 512], F32)
            ztoks = sb(SS_, "ztoks", [SS, NSTR, 1536], BF16)
            ckf = [sb(SS_, "ckf%d" % i, [128, 1024], F32) for i in range(2)]
            ckbr = [sb(SS_, "ckbr%d" % i, [128, 4, PAST], BF16) for i in range(3)]
            cvf = [sb(SS_, "cvf%d" % i, [128, 1024], F32) for i in range(2)]
            vxcr = [sb(SS_, "vxcr%d" % i, [128, 8, 4, 129], BF16) for i in range(3)]
            cmkf = sb(SS_, "cmkf", [128, MH, NMEM], F32)
            cmkb = sb(SS_, "cmkb", [128, MH, NMEM], BF16)
            cmvf = sb(SS_, "cmvf", [128, 2, 512], F32)
            cmvx = sb(SS_, "cmvx", [128, 2, MH, 129], BF16)
            pTs = sb(SS_, "pTs", [128, 2, 9, SS], BF16)
            pTm = sb(SS_, "pTm", [128, 2, SS], BF16)
            accS = sb(SS_, "accS", [SS, 2, ACCW], F32)
            oS2 = [sb(SS_, "oS2%d" % i, [SS, 128], F32) for i in range(2)]
            rstdS2 = [sb(SS_, "rstdS2%d" % i, [SS, 2], F32) for i in range(2)]
            acc_freeM = [None]
            tS = sb(SS_, "tS", [SS, 128], F32)
            sqS = sb(SS_, "sqS", [SS, 128], F32)
            recS = sb(SS_, "recS", [SS, 2], F32)
            nr2S = sb(SS_, "nr2S", [SS, 1], F32)
            ssqS = sb(SS_, "ssqS", [SS, 1], F32)
            rstdS = sb(SS_, "rstdS", [SS, 1], F32)
            zTs = sb(SS_, "zTs", [128, NCH, NTK], BF16)
            ysbS = sb(SS_, "ysbS", [NTK, D], F32)
            xrS = [sb(SS_, "xrS%d" % i, [NTK, 512], F32) for i in range(2)]
            lnS = [sb(SS_, "lnS%d" % i, [NTK, 512], F32) for i in range(2)]
            bnS = sb(SS_, "bnS", [NTK, 4, 6], F32)
            mvS = sb(SS_, "mvS", [NTK, 2], F32)
            lnrS = sb(SS_, "lnrS", [NTK, 1], F32)

            t_xs = P.dma("sp", xsf[:], xsT_d.rearrange("(c p) n -> p c n", p=128), key="xsf")
            t_xs = P.op("dve", lambda e: e.tensor_copy(out=xsb[:], in_=xsf[:]), [t_xs])
            t_cc = P.dma("sp", upad[:, :, :, 0:2], cconv_d, key="cconv")
            bank_free = [None] * 7
            jbs = [0]
            wS_free = [None]

            def loadS(gi):
                return P.dma("sp", wS[:], wsc[gi], [wS_free[0]], key="wS")

            def jobS(mm, parts, n_out, evac, deps):
                b = jbs[0] % 7
                jbs[0] += 1
                for c in range(NCH):
                    t_pe = P.op("pe", lambda e, b=b, c=c: mm(e, bank(b)[0:parts, 0:n_out], c, c == 0, c == NCH - 1),
                                (list(deps) + [bank_free[b]]) if c == 0 else (),
                                sig=("chain" if c == NCH - 1 else None))
                t = evac(bank(b)[0:parts, 0:n_out], t_pe)
                bank_free[b] = t
                return t_pe, t

            KV0 = len(WG)
            t_fm = []
            fm_groups = [(G_Q0, "q", 0), (G_Q1, "q", 4), (KV0, "k", 0), (KV0 + 1, "k", 4), (G_H, "h", 0), (G_C, "c", 0),
                         (G_B, "b", 0), (G_GC, "gc", 0), (G_MQ, "mq", 0)]
            for (gi, kind, h0) in fm_groups:
                t_w = loadS(gi)
                for sub in range(4):
                    def mm(e, out, c, st_, sp_, sub=sub):
                        return e.matmul(out, lhsT=wS[:, c, sub * 128:sub * 128 + 128], rhs=xsb[:, c, :], start=st_, stop=sp_)
                    if kind == "q":
                        def ev(bk, tpe, hh=h0 + sub):
                            return P.op("act", lambda e: e.mul(out=qTs[:, hh, :], in_=bk, mul=0.125), [tpe])
                    elif kind == "k":
                        def ev(bk, tpe, hh=h0 + sub):
                            t1 = P.op("act", lambda e: e.activation(out=kTs[:, hh, :], in_=bk, func=AF.Copy), [tpe])
                            return P.op("dve", lambda e: e.tensor_copy(out=kTf[:, hh, :], in_=bk), [tpe, t1])
                    elif kind == "mq":
                        def ev(bk, tpe, sub=sub):
                            return P.op("act", lambda e: e.mul(out=mqTs[:, sub, :], in_=bk, mul=128.0 ** -0.5), [tpe])
                    elif kind == "gc":
                        def ev(bk, tpe, sub=sub):
                            return P.op("act", lambda e: e.activation(out=gcTs[:, sub, :], in_=bk, func=AF.Silu), [tpe])
                    else:
                        tgt = {"h": hTs, "c": CTs, "b": BTs}[kind]

                        def ev(bk, tpe, sub=sub, tgt=tgt):
                            return P.op("dve", lambda e: e.tensor_copy(out=tgt[:, sub, :], in_=bk), [tpe])
                    last_pe, t_e = jobS(mm, 128, NTK, ev, [t_xs, t_w])
                    t_fm.append(t_e)
                wS_free[0] = last_pe
            P.dma("sp", ksT_o.rearrange("h p n -> p h n"), kTf[:], t_fm, key="ksT")
            t_tm = []
            nvf = 0
            vf_free = [None, None]
            for (gi, kind, g) in [(KV0 + 2, "v", 0), (KV0 + 3, "v", 1), (G_GD0, "gd", 0), (G_GD1, "gd", 1), (G_GM, "gm", 0)]:
                t_w = loadS(gi)
                for st in range(NSTR):
                    def mm(e, out, c, st_, sp_, st=st):
                        return e.matmul(out, lhsT=xsb[:, c, st * SS:(st + 1) * SS], rhs=wS[:, c, :], start=st_, stop=sp_)
                    if kind == "v":
                        vsel = nvf % 2
                        nvf += 1

                        def ev(bk, tpe, st=st, g=g, vsel=vsel):
                            t1 = P.op("act", lambda e: e.activation(
                                out=vnew[:, st, 4 * g:4 * g + 4, 0:128], in_=bk.rearrange("p (h e) -> p h e", e=128),
                                func=AF.Copy), [tpe])
                            t2 = P.op("dve", lambda e: e.tensor_copy(out=vfs[vsel][:], in_=bk), [tpe, t1, vf_free[vsel]])
                            vf_free[vsel] = P.dma("sp", vs_o[st * SS:(st + 1) * SS, g * 512:g * 512 + 512], vfs[vsel][:], [t2],
                                                  key="vfs%d" % vsel)
                            return t2
                    elif kind == "gd":
                        def ev(bk, tpe, st=st, g=g):
                            t1 = P.op("act", lambda e: e.activation(out=sgds[:, st, g * 512:g * 512 + 512], in_=bk,
                                                                    func=AF.Silu), [tpe])
                            return P.op("dve", lambda e: e.tensor_tensor(out=sgds[:, st, g * 512:g * 512 + 512],
                                                                         in0=sgds[:, st, g * 512:g * 512 + 512],
                                                                         in1=gs4[0:SS, :], op=ALU.mult), [t1])
                    else:
                        def ev(bk, tpe, st=st):
                            return P.op("act", lambda e: e.activation(out=sgms[:, st, :], in_=bk, func=AF.Silu), [tpe])
                    last_pe, t_e = jobS(mm, SS, 512, ev, [t_xs, t_w])
                    t_tm.append(t_e)
                wS_free[0] = last_pe
            t_on = P.op("dve", lambda e: e.memset(vnew[:, :, :, 128:129].rearrange("p s h e -> p (s h e)"), 1.0), t_tm)
            t_tm.append(t_on)
            t_wo = loadS(G_O0)
            t = None
            t_cvs = []
            for cc in range(4):
                t = P.op("pool", lambda e, cc=cc: e.tensor_tensor(
                    out=upad[:, cc, :, 2:SS + 2], in0=CTs[:, cc, :].rearrange("p (s t) -> p s t", t=SS),
                    in1=hTs[:, cc, :].rearrange("p (s t) -> p s t", t=SS), op=ALU.mult), t_fm + [t_cc, t])
                t_u = P.op("pool", lambda e, cc=cc: e.tensor_copy(out=convsb[:, :, cc, :], in_=upad[:, cc, :, SS:SS + 2]), [t])
                t_cvs.append(t_u)
                t = P.op("pool", lambda e, cc=cc: e.tensor_scalar(out=cvt[:], in0=upad[:, cc, :, 0:SS],
                                                                  scalar1=cw[:, cc * 3:cc * 3 + 1], scalar2=None,
                                                                  op0=ALU.mult), [t])
                for jx in (1, 2):
                    t = P.op("pool", lambda e, cc=cc, jx=jx: e.tensor_scalar(
                        out=cvt2[:], in0=upad[:, cc, :, jx:jx + SS], scalar1=cw[:, cc * 3 + jx:cc * 3 + jx + 1],
                        scalar2=None, op0=ALU.mult), [t])
                    t = P.op("pool", lambda e: e.tensor_tensor(out=cvt[:], in0=cvt[:], in1=cvt2[:], op=ALU.add), [t])
                t = P.op("pool", lambda e, cc=cc: e.tensor_tensor(
                    out=cvt[:], in0=cvt[:], in1=BTs[:, cc, :].rearrange("p (s t) -> p s t", t=SS), op=ALU.mult), [t])
                t = P.op("pool", lambda e, cc=cc: e.tensor_tensor(
                    out=zTcs[:, cc, :].rearrange("p (s t) -> p s t", t=SS), in0=cvt[:],
                    in1=gcTs[:, cc, :].rearrange("p (s t) -> p s t", t=SS), op=ALU.mult), [t])
                t_cvs.append(t)
            P.dma("pool", convs_o, convsb[:], t_cvs, key="convs")

            t_epS = [None]
            NU = NSTR * 2
            RU = 3
            ck_free = [None] * RU
            vx_free = [None] * RU
            cm_free = [None]
            nck = [0]
            ncv = [0]
            ckf_free = [None, None]
            cvf_free = [None, None]
            sb_free = {}
            acc_freeS = [None, None]
            ucache = {}
            t_ones = [P.op("dve", lambda e, r=r: e.memset(vxcr[r][:, :, :, 128:129].rearrange("p b h e -> p (b h e)"), 1.0), ())
                      for r in range(RU)]

            def load_unit(u):
                st, hh = divmod(u, 2)
                r = u % RU
                t_ck = []
                for h4 in range(4):
                    k_ = nck[0] % 2
                    nck[0] += 1
                    t_ld = P.dma("sp", ckf[k_][:], ckT_d[st, hh * 4 + h4], [ckf_free[k_]], key="ckf%d" % k_)
                    t_c = P.op("act", lambda e, k_=k_, h4=h4, r=r: e.activation(out=ckbr[r][:, h4, :], in_=ckf[k_][:],
                                                                             func=AF.Copy), [t_ld, ck_free[r]])
                    ckf_free[k_] = t_c
                    t_ck.append(t_c)
                t_cv_ = [t_ones[r]]
                for blk in range(8):
                    k_ = ncv[0] % 2
                    ncv[0] += 1
                    t_ld = P.dma("sp", cvf[k_][:, 0:512], cv_d[st, blk * 128:(blk + 1) * 128, hh * 512:(hh + 1) * 512],
                                 [cvf_free[k_]], key="cvf%d" % k_)
                    t_c = P.op("dve", lambda e, k_=k_, blk=blk, r=r: e.tensor_copy(
                        out=vxcr[r][:, blk, :, 0:128], in_=cvf[k_][:, 0:512].rearrange("p (h e) -> p h e", e=128)),
                        [t_ld, vx_free[r]])
                    cvf_free[k_] = t_c
                    t_cv_.append(t_c)
                ucache[u] = (t_ck, t_cv_)

            for u in range(min(2, NU)):
                load_unit(u)
            pend_final = [None]
            tp_prev = [None]
            for u in range(NU):
                st, hh = divmod(u, 2)
                r = u % RU
                if u + 2 < NU:
                    load_unit(u + 2)
                t_ck, t_cv_ = ucache[u]
                if hh == 0:
                    t_l1 = P.dma("sp", cmkf[:], cmkT_d[st].rearrange("h p n -> p h n"), [cm_free[0]], key="cmk")
                    t_l2 = P.dma("sp", cmvf[:], cmv_d[st].rearrange("(b p) n -> p b n", p=128), [cm_free[0]], key="cmv")
                    t_m1 = P.op("pool", lambda e: e.tensor_copy(out=cmkb[:], in_=cmkf[:]), [t_l1, cm_free[0]])
                    t_m2 = P.op("pool", lambda e: e.tensor_copy(
                        out=cmvx[:, :, :, 0:128], in_=cmvf[:].rearrange("p b (h e) -> p b h e", e=128)), [t_l2, cm_free[0]])
                    t_m3 = P.op("pool", lambda e: e.memset(cmvx[:, :, :, 128:129].rearrange("p b h e -> p (b h e)"), 1.0), [t_m2])
                    t_mem = [t_m1, t_m2, t_m3]
                qs = slice(st * SS, (st + 1) * SS)
                for h4 in range(4):
                    h = hh * 4 + h4
                    cnt = u * 4 + h4
                    sbk = cnt % 2
                    abk = cnt % 2
                    Sv = psS[:, sbk, 0:2 * 9 * SS].rearrange("p (m k q) -> p m k q", m=2, k=9)
                    for m in range(2):
                        for kb in range(8):
                            tq = P.op("pe", lambda e, m=m, kb=kb, h4=h4, h=h, Sv=Sv, qs=qs, r=r: e.matmul(
                                Sv[:, m, kb, :], lhsT=ckbr[r][64 * m:64 * m + 64, h4, kb * 128:kb * 128 + 128],
                                rhs=qTs[64 * m:64 * m + 64, h, qs], start=(m == 0 and kb == 0), stop=False,
                                skip_group_check=True),
                                (t_ck + t_fm + t_tm + [sb_free.get(sbk)]) if (m == 0 and kb == 0) else (), sig=None)
                        tq = P.op("pe", lambda e, m=m, h=h, Sv=Sv, qs=qs: e.matmul(
                            Sv[0:SS, m, 8, :], lhsT=kTs[64 * m:64 * m + 64, h, qs], rhs=qTs[64 * m:64 * m + 64, h, qs],
                            start=False, stop=False, skip_group_check=True), (), sig=None)
                        for part in range(2):
                            tq = P.op("pe", lambda e, m=m, h=h, part=part, Sv=Sv: e.matmul(
                                Sv[:, m, 7, :], lhsT=ident[:], rhs=biasT[:, h, 1, part, 0:SS], start=False, stop=False,
                                skip_group_check=True), (), sig=None)
                            tq = P.op("pe", lambda e, m=m, h=h, part=part, Sv=Sv: e.matmul(
                                Sv[0:SS, m, 8, :], lhsT=ident[0:SS, 0:SS], rhs=biasT[0:SS, h, 0, part, 0:SS], start=False,
                                stop=True, skip_group_check=True), (), sig=("chain" if (m == 1 and part == 1) else None))
                    te1 = P.op("act", lambda e, h=h, Sv=Sv: e.activation(
                        out=pTs[:, :, 0:8, :], in_=Sv[:, :, 0:8, :], func=AF.Exp, bias=cfar[:, h:h + 1], scale=1.0),
                        [tq, tp_prev[0]])
                    te2 = P.op("act", lambda e, h=h, Sv=Sv: e.activation(
                        out=pTs[0:SS, :, 8, :], in_=Sv[0:SS, :, 8, :], func=AF.Exp, bias=cfar[0:SS, h:h + 1], scale=1.0), [tq])
                    sb_free[sbk] = te2
                    for m in range(2):
                        for kb in range(8):
                            tp_ = P.op("pe", lambda e, m=m, kb=kb, h4=h4, r=r, abk=abk: e.matmul(
                                psA[0:SS, abk, m * ACCW:m * ACCW + 129], lhsT=pTs[:, m, kb, :], rhs=vxcr[r][:, kb, h4, :],
                                start=(m == 0 and kb == 0), stop=False, skip_group_check=True),
                                ([te1, te2, acc_freeS[abk]] + t_cv_) if (m == 0 and kb == 0) else (), sig=None)
                        tp_ = P.op("pe", lambda e, m=m, h=h, st=st, abk=abk: e.matmul(
                            psA[0:SS, abk, m * ACCW:m * ACCW + 129], lhsT=pTs[0:SS, m, 8, :], rhs=vnew[:, st, h, :],
                            start=False, stop=True, skip_group_check=True), (), sig=("chain" if m == 1 else None))
                    tp_prev[0] = tp_
                    if pend_final[0] is not None:
                        pend_final[0]()
                        pend_final[0] = None
                    t = P.op("dve", lambda e, abk=abk: e.tensor_copy(
                        out=accS[:], in_=psA[0:SS, abk, 0:2 * ACCW].rearrange("p (a w) -> p a w", w=ACCW)), [tp_, t_epS[0]])
                    acc_freeS[abk] = t
                    t = P.op("dve", lambda e: e.reciprocal(out=recS[:], in_=accS[:, :, 128]), [t])
                    t = P.op("dve", lambda e: e.tensor_scalar(out=nr2S[:], in0=recS[:, 1:2], scalar1=nlam[0:SS, 0:1],
                                                              scalar2=None, op0=ALU.mult), [t, t_nlam])
                    t = P.op("dve", lambda e: e.tensor_scalar(out=tS[:], in0=accS[:, 1, 0:128], scalar1=nr2S[:, 0:1],
                                                              scalar2=None, op0=ALU.mult), [t])
                    oSel = oS2[cnt % 2]
                    rSel = rstdS2[cnt % 2]
                    t = P.op("dve", lambda e, oSel=oSel: e.scalar_tensor_tensor(out=oSel[:], in0=accS[:, 0, 0:128], scalar=recS[:, 0:1],
                                                                                in1=tS[:], op0=ALU.mult, op1=ALU.add), [t])
                    t = P.op("dve", lambda e, oSel=oSel: e.tensor_tensor(out=sqS[:], in0=oSel[:], in1=oSel[:], op=ALU.mult), [t])
                    t = P.op("dve", lambda e, rSel=rSel: e.reduce_sum(out=rSel[:, 0:1], in_=sqS[:], axis=AX.X), [t])
                    t_epS[0] = t
                    t = P.op("act", lambda e, rSel=rSel: e.activation(out=rSel[:, 1:2], in_=rSel[:, 0:1], func=AF.Ln,
                                                                      bias=epsb[0:SS, 0:1], scale=1.0 / 128.0), [t])
                    t = P.op("act", lambda e, rSel=rSel: e.activation(out=rSel[:, 1:2], in_=rSel[:, 1:2], func=AF.Exp, scale=-0.5), [t])

                    def fin(st=st, h=h, oSel=oSel, rSel=rSel, t=t):
                        t_epS[0] = P.op("dve", lambda e: e.scalar_tensor_tensor(
                            out=ztoks[:, st, h * 128:h * 128 + 128], in0=oSel[:], scalar=rSel[:, 1:2],
                            in1=sgds[:, st, h * 128:h * 128 + 128], op0=ALU.mult, op1=ALU.mult), [t, t_epS[0]])
                    pend_final[0] = fin
                ck_free[r] = tq
                vx_free[r] = tp_
                if hh == 0:
                    continue
                if pend_final[0] is not None:
                    pend_final[0]()
                    pend_final[0] = None
                for mh in range(MH):
                    Sm = psS[:, 2, 0:2 * SS].rearrange("p (k q) -> p k q", k=2)
                    for blk in range(2):
                        tq = P.op("pe", lambda e, mh=mh, blk=blk, Sm=Sm, qs=qs: e.matmul(
                            Sm[:, blk, :], lhsT=cmkb[:, mh, blk * 128:blk * 128 + 128], rhs=mqTs[:, mh, qs],
                            start=(blk == 0), stop=(blk == 1), skip_group_check=True),
                            (t_mem + [sb_free.get(2)]) if blk == 0 else (), sig=("chain" if blk == 1 else None))
                    te = P.op("act", lambda e, Sm=Sm: e.activation(out=pTm[:], in_=Sm, func=AF.Exp), [tq, tp_prev[0]])
                    sb_free[2] = te
                    for blk in range(2):
                        tp_ = P.op("pe", lambda e, mh=mh, blk=blk: e.matmul(
                            psA[0:SS, 2, 0:129], lhsT=pTm[:, blk, :], rhs=cmvx[:, blk, mh, :],
                            start=(blk == 0), stop=(blk == 1), skip_group_check=True),
                            [te, acc_freeM[0]] if blk == 0 else (), sig=("chain" if blk == 1 else None))
                    tp_prev[0] = tp_
                    t = P.op("dve", lambda e: e.tensor_copy(out=accS[:, 0, :], in_=psA[0:SS, 2, 0:ACCW]), [tp_, t_epS[0]])
                    acc_freeM[0] = t
                    t = P.op("dve", lambda e: e.reciprocal(out=recS[:, 0:1], in_=accS[:, 0, 128:129]), [t])
                    t = P.op("dve", lambda e, st=st, mh=mh: e.scalar_tensor_tensor(
                        out=ztoks[:, st, 1024 + mh * 128:1024 + mh * 128 + 128], in0=accS[:, 0, 0:128], scalar=recS[:, 0:1],
                        in1=sgms[:, st, mh * 128:mh * 128 + 128], op0=ALU.mult, op1=ALU.mult), [t])
                    t_epS[0] = t
                cm_free[0] = tp_
            t_zs = []
            tpf = [None]
            for st in range(NSTR):
                for k in range(12):
                    tt = P.op("pe", lambda e, st=st, k=k: e.transpose(
                        psTb[:, k * SS:(k + 1) * SS], ztoks[:, st, k * 128:(k + 1) * 128], ident[0:SS, 0:SS]),
                        [t_epS[0], tpf[0]] if k == 0 else (), sig=("chain" if k == 11 else None))
                t1 = P.op("dve", lambda e, st=st: e.tensor_copy(
                    out=zTs[:, 0:8, st * SS:(st + 1) * SS], in_=psTb[:, 0:8 * SS].rearrange("p (k t) -> p k t", t=SS)), [tt])
                t2 = P.op("dve", lambda e, st=st: e.tensor_copy(
                    out=zTs[:, 12:16, st * SS:(st + 1) * SS], in_=psTb[:, 8 * SS:12 * SS].rearrange("p (k t) -> p k t", t=SS)), [tt, t1])
                tpf[0] = t2
                t_zs += [t1, t2]
            t_zs.append(P.op("pool", lambda e: e.tensor_copy(out=zTs[:, 8:12, :], in_=zTcs[:]), t_cvs))
            t_ys = []
            xr_free = [None, None]
            for g in range(4):
                t_w = t_wo if g == 0 else loadS(G_O0 + g)
                t_xr = P.dma("sp", xrS[g % 2][:], xs_d[:, g * 512:(g + 1) * 512], [xr_free[g % 2]], key="xrS%d" % (g % 2))

                def mm(e, out, c, st_, sp_):
                    return e.matmul(out, lhsT=zTs[:, c, :], rhs=wS[:, c, :], start=st_, stop=sp_)

                def ev(bk, tpe, g=g, t_xr=t_xr):
                    return P.op("dve", lambda e: e.scalar_tensor_tensor(
                        out=ysbS[:, g * 512:(g + 1) * 512], in0=xrS[g % 2][:], scalar=ALPHA, in1=bk, op0=ALU.mult,
                        op1=ALU.add), [tpe, t_xr])
                last_pe, t_e = jobS(mm, NTK, 512, ev, t_zs + [t_w])
                xr_free[g % 2] = t_e
                wS_free[0] = last_pe
                t_ys.append(t_e)
            t = None
            for g in range(4):
                t = P.op("dve", lambda e, g=g: e.bn_stats(out=bnS[:, g, :], in_=ysbS[:, g * 512:(g + 1) * 512]), t_ys + [t])
            t = P.op("dve", lambda e: e.bn_aggr(out=mvS[:], in_=bnS[:].rearrange("p a b -> p (a b)")), [t])
            t = P.op("act", lambda e: e.activation(out=lnrS[:], in_=mvS[:, 1:2], func=AF.Ln, bias=epsb[0:NTK, 0:1], scale=1.0), [t])
            t = P.op("act", lambda e: e.activation(out=lnrS[:], in_=lnrS[:], func=AF.Exp, scale=-0.5), [t])
            t = P.op("dve", lambda e: e.tensor_scalar(out=ysbS[:], in0=ysbS[:], scalar1=mvS[:, 0:1], scalar2=lnrS[:, 0:1],
                                                      op0=ALU.subtract, op1=ALU.mult), [t])
            ln_free = [None, None]
            for g in range(4):
                tg = P.dma("sp", lnS[0][:], lng_d[0:NTK, g * 512:(g + 1) * 512], [ln_free[0]], key="lnS0")
                tb = P.dma("sp", lnS[1][:], lnb_d[0:NTK, g * 512:(g + 1) * 512], [ln_free[1]], key="lnS1")
                t = P.op("dve", lambda e, g=g: e.tensor_tensor(out=ysbS[:, g * 512:(g + 1) * 512],
                                                               in0=ysbS[:, g * 512:(g + 1) * 512], in1=lnS[0][:], op=ALU.mult), [t, tg])
                ln_free[0] = t
                t = P.op("dve", lambda e, g=g: e.tensor_tensor(out=ysbS[:, g * 512:(g + 1) * 512],
                                                               in0=ysbS[:, g * 512:(g + 1) * 512], in1=lnS[1][:], op=ALU.add), [t, tb])
                ln_free[1] = t
            P.dma("sp", ys_o, ysbS[:], [t], key="ysS")
            P.emit()
            SS_.__exit__(None, None, None)

        SB = ExitStack()
        SB.__enter__()
        wring = [sb(SB, "wring%d" % i, [128, NCH, 512], BF16) for i in range(2)]
        xz = sb(SB, "xz", [128, NCH, TW + 2], BF16)
        stgB = sb(SB, "stgB", [128, 2, 2 * (TW + 2)], F32)
        qT = sb(SB, "qT", [128, H, TW], BF16)
        mqT = sb(SB, "mqT", [128, MH, TW], BF16)
        big = sb(SB, "big", [128, 8208], F32)
        sgd = sb(SB, "sgd", [128, 4, 1024], F32)
        sgm = sb(SB, "sgm", [128, 4, 512], F32)
        ztok = sb(SB, "ztok", [128, 4, 1536], BF16)
        zTc = sb(SB, "zTc", [128, 4, TW], BF16)
        kring = sb(SB, "kring", [128, RK, TW], BF16)
        vring = sb(SB, "vring", [128, RK, 4, 129], BF16)
        pring = sb(SB, "pring", [128, RP, 2, TW], BF16)
        accs = sb(SB, "accs", [128, 8, ACCW], F32)
        otmp = sb(SB, "otmp", [128, 4, 128], F32)
        ttmp = sb(SB, "ttmp", [128, 128], F32)
        sqtmp = sb(SB, "sqtmp", [128, 128], F32)
        rec = sb(SB, "rec", [128, 8], F32)
        nr2 = sb(SB, "nr2", [128, 4], F32)
        ssq = sb(SB, "ssq", [128, 4], F32)
        rstd = sb(SB, "rstd", [128, 4], F32)
        halo = sb(SB, "halo", [128, 8, 2], F32)
        ctmp = sb(SB, "ctmp", [128, TW], F32)
        ctmp2 = sb(SB, "ctmp2", [128, TW], F32)
        utmp = sb(SB, "utmp", [128, TW + 2], F32)
        xres = [sb(SB, "xres%d" % i, [128, 512], F32) for i in range(2)]
        bnst = sb(SB, "bnst", [128, 4, 6], F32)
        bnst4 = sb(SB, "bnst4", [128, 4, 4, 6], F32)
        mv4 = sb(SB, "mv4", [128, 4, 2], F32)
        lnr4 = sb(SB, "lnr4", [128, 4], F32)
        mv = sb(SB, "mv", [128, 2], F32)
        lnr = sb(SB, "lnr", [128, 1], F32)
        convu = sb(SB, "convu", [128, 4, 2], F32)

        hT = big[:, 0:4 * 514].rearrange("p (c t) -> p c t", t=514)
        CT = big[:, 2056:2 * 2056].rearrange("p (c t) -> p c t", t=514)
        BT = big[:, 4112:4112 + 2048].rearrange("p (c t) -> p c t", t=512)
        sgcT = big[:, 6160:6160 + 2048].rearrange("p (c t) -> p c t", t=512)
        ysb = big[:, 0:8192].rearrange("p (s n) -> p s n", n=2048)
        lngs = sb(SB, "lngs", [128, D], F32)
        lngb = lngs[:]
        t_lng = P.dma("sp", lngs[:], lng_d, key="lng")
        zT = sgd[:].rearrange("p s n -> p (s n)").bitcast(BF16).rearrange("p (c t) -> p c t", t=TW)

        lnb_sb = sb(SB, "lnb_sb", [128, D], F32)
        t_lnb = P.dma("sp", lnb_sb[:], lnb_d, key="lnb")

        sem_qk = P.sem("qk")
        sem_exp = P.sem("exp")
        sem_pv = P.sem("pv")

        bank_free = [None] * 7
        halo_free = [None]
        jbc = [0]
        wr_free = [None, None]
        wr_cnt = [0]

        def load_w(gi, after=()):
            r = wr_cnt[0] % 2
            wr_cnt[0] += 1
            t = P.dma("sp", wring[r][:], wsc[gi], [wr_free[r]] + list(after), key="w%d" % r)
            return r, t

        def job(mm, n_out, evac, deps):
            b = jbc[0] % 7
            jbc[0] += 1
            for c in range(NCH):
                t_pe = P.op("pe", lambda e, b=b, c=c: mm(e, bank(b)[:, 0:n_out], c, c == 0, c == NCH - 1),
                            (list(deps) + [bank_free[b]]) if c == 0 else (),
                            sig=("chain" if c == NCH - 1 else None))
            t = evac(bank(b)[:, 0:n_out], t_pe)
            bank_free[b] = t
            return t_pe, t

        xTr2 = xT.rearrange("(c p) t -> p c t", p=128)
        stgv = [stgB[:, k, :].rearrange("p (c t) -> p c t", t=TW + 2) for k in range(2)]
        prev_slot_done = []
        xz_free = []
        lng_used = []
        nblk = [0]
        ncidx = [0]
        kv_free = [None] * RK
        pv_tok = {}
        exp_tok = {}
        acc_free = [None]
        y_st_tok = [None]

        stg_free2 = [None, None]

        def load_xz(T):
            t_x = []
            for pc in range(8):
                k = pc % 2
                t_ld = P.dma("pool", stgv[k], xTr2[:, 2 * pc:2 * pc + 2, T * TW - 2:(T + 1) * TW],
                             [stg_free2[k]], key="xz%d" % k)
                t_c = P.op("dve", lambda e, k=k, pc=pc: e.tensor_copy(out=xz[:, 2 * pc:2 * pc + 2, :], in_=stgv[k]),
                           [t_ld] + xz_free)
                stg_free2[k] = t_c
                t_x.append(t_c)
            return t_x

        t_x_next = load_xz(3)
        for i in range(nslot):
            T = 4 * i + 3
            t_x = t_x_next
            nxt = load_w(G_Q0)
            order = [G_Q0, G_Q1, G_MQ, G_GD0, G_GD1, G_GM, G_H, G_C, G_B, G_GC]
            t_conv_in = {}
            t_q = []
            t_gate = []
            for oi, gi in enumerate(order):
                r, t_w = nxt
                last_pe = None
                if gi in (G_Q0, G_Q1, G_H, G_B, G_C, G_MQ, G_GC):
                    for sub in range(4):
                        def mm(e, out, c, st, sp_, r=r, sub=sub):
                            return e.matmul(out, lhsT=wring[r][:, c, sub * 128:sub * 128 + 128], rhs=xz[:, c, 2:TW + 2],
                                            start=st, stop=sp_)
                        if gi in (G_Q0, G_Q1):
                            hh = (gi - G_Q0) * 4 + sub

                            def ev(bk, tpe, hh=hh):
                                return P.op("act", lambda e: e.mul(out=qT[:, hh, :], in_=bk, mul=0.125), [tpe])
                        elif gi == G_MQ:
                            def ev(bk, tpe, sub=sub):
                                return P.op("act", lambda e: e.mul(out=mqT[:, sub, :], in_=bk, mul=128.0 ** -0.5), [tpe])
                        elif gi == G_H:
                            def ev(bk, tpe, sub=sub):
                                return P.op("dve", lambda e: e.tensor_copy(out=hT[:, sub, 2:TW + 2], in_=bk),
                                            [tpe] + prev_slot_done)
                        elif gi == G_C:
                            def ev(bk, tpe, sub=sub):
                                return P.op("dve", lambda e: e.tensor_copy(out=CT[:, sub, 2:TW + 2], in_=bk),
                                            [tpe] + prev_slot_done)
                        elif gi == G_B:
                            def ev(bk, tpe, sub=sub):
                                return P.op("dve", lambda e: e.tensor_copy(out=BT[:, sub, :], in_=bk),
                                            [tpe] + prev_slot_done)
                        else:
                            def ev(bk, tpe, sub=sub):
                                return P.op("act", lambda e: e.activation(out=sgcT[:, sub, :], in_=bk, func=AF.Silu),
                                            [tpe] + prev_slot_done)
                        last_pe, t_e = job(mm, TW, ev, t_x + [t_w])
                        if gi in (G_Q0, G_Q1, G_MQ):
                            t_q.append(t_e)
                        else:
                            t_conv_in[(gi, sub)] = t_e
                        if gi in (G_H, G_C):
                            hidx = (0 if gi == G_H else 4) + sub
                            for c in range(NCH):
                                last_pe = P.op("pe", lambda e, c=c, r=r, sub=sub, hidx=hidx: e.matmul(
                                    psT[:, 2 * hidx:2 * hidx + 2], lhsT=wring[r][:, c, sub * 128:sub * 128 + 128],
                                    rhs=xz[:, c, 0:2], start=(c == 0), stop=(c == NCH - 1)),
                                    [halo_free[0]] if c == 0 else (), sig=("chain" if c == NCH - 1 else None))
                            tgt = hT if gi == G_H else CT
                            halo_free[0] = P.op("dve", lambda e, tgt=tgt, sub=sub, hidx=hidx: e.tensor_copy(
                                out=tgt[:, sub, 0:2], in_=psT[:, 2 * hidx:2 * hidx + 2]), [last_pe] + prev_slot_done)
                            t_conv_in[(gi, sub, "halo")] = halo_free[0]
                else:
                    for s in range(4):
                        def mm(e, out, c, st, sp_, r=r, s=s):
                            return e.matmul(out, lhsT=xz[:, c, 2 + s * 128:2 + s * 128 + 128], rhs=wring[r][:, c, :],
                                            start=st, stop=sp_)
                        if gi in (G_GD0, G_GD1):
                            g = gi - G_GD0

                            def ev(bk, tpe, s=s, g=g):
                                t1 = P.op("act", lambda e: e.activation(out=sgd[:, s, g * 512:g * 512 + 512], in_=bk,
                                                                        func=AF.Silu), [tpe])
                                t2 = P.op("dve", lambda e: e.tensor_tensor(out=sgd[:, s, g * 512:g * 512 + 512],
                                                                            in0=sgd[:, s, g * 512:g * 512 + 512],
                                                                            in1=gs4[:], op=ALU.mult), [t1])
                                t_gate.append(t2)
                                return t1
                        else:
                            def ev(bk, tpe, s=s):
                                return P.op("act", lambda e: e.activation(out=sgm[:, s, :], in_=bk, func=AF.Silu),
                                            [tpe])
                        last_pe, t_e = job(mm, 512, ev, t_x + [t_w])
                        t_gate.append(t_e)
                wr_free[r] = last_pe
                if oi + 1 < len(order):
                    nxt = load_w(order[oi + 1])
            xz_free = [last_pe]
            w_o = [load_w(G_O0), load_w(G_O0 + 1)]
            t_cv = []

            def post_mem(i=i):
                nonlocal t_x_next
                for cc in range(4):
                    dep = [t_conv_in[(G_H, cc)], t_conv_in[(G_C, cc)], t_conv_in[(G_H, cc, "halo")],
                           t_conv_in[(G_C, cc, "halo")], t_conv_in[(G_B, cc)], t_conv_in[(G_GC, cc)]]
                    t = P.op("dve", lambda e, cc=cc: e.tensor_tensor(out=utmp[:], in0=CT[:, cc, :], in1=hT[:, cc, :],
                                                                     op=ALU.mult), dep + t_cv[-1:])
                    if i == nslot - 1:
                        t_u = P.op("dve", lambda e, cc=cc: e.tensor_copy(out=convu[:, cc, :], in_=utmp[:, TW:TW + 2]), [t])
                        t_cv.append(t_u)
                    t = P.op("dve", lambda e, cc=cc: e.tensor_scalar(out=ctmp[:], in0=utmp[:, 0:TW],
                                                                     scalar1=cw[:, cc * 3:cc * 3 + 1], scalar2=None,
                                                                     op0=ALU.mult), [t])
                    for jx in (1, 2):
                        t = P.op("dve", lambda e, cc=cc, jx=jx: e.scalar_tensor_tensor(
                            out=ctmp[:], in0=utmp[:, jx:jx + TW], scalar=cw[:, cc * 3 + jx:cc * 3 + jx + 1], in1=ctmp[:],
                            op0=ALU.mult, op1=ALU.add), [t])
                    t = P.op("dve", lambda e, cc=cc: e.tensor_tensor(out=ctmp[:], in0=ctmp[:], in1=BT[:, cc, :], op=ALU.mult), [t])
                    t = P.op("dve", lambda e, cc=cc: e.tensor_tensor(out=zTc[:, cc, :], in0=ctmp[:], in1=sgcT[:, cc, :],
                                                                     op=ALU.mult), [t])
                    t_cv.append(t)
                if i == nslot - 1:
                    t_cv.append(P.dma("pool", convo_o, convu[:], t_cv, key="convo"))
                if i + 1 < nslot:
                    t_x_next = load_xz(4 * (i + 1) + 3)

            t_q = t_q + t_gate + list(t_conv_in.values())
            blocks = []
            for h in range(H):
                for kc in range(T + 1):
                    for kbl in range(4):
                        blocks.append((h, kc, kbl))
            nb_total = len(blocks)
            t_ep = []
            kvslot = {}

            def issue_kv(h, kc):
                cidx = ncidx[0]
                ncidx[0] += 1
                sl = cidx % RK
                t1 = P.dma("sp", kring[:, sl, :], kts[kc, :, h, :], [kv_free[sl]], key="kv%d" % sl)
                t2 = P.dma("sp", vring[:, sl].rearrange("p s e -> p (s e)"), vxs[kc, :, h, :], [kv_free[sl]],
                           key="kv%d" % sl)
                kvslot[(h, kc)] = (sl, t2)

            chunk_list = [(h, kc) for h in range(H) for kc in range(T + 1)]
            kv_issued = [0]

            def ensure_kv(upto):
                while kv_issued[0] < min(upto, len(chunk_list)):
                    issue_kv(*chunk_list[kv_issued[0]])
                    kv_issued[0] += 1

            ensure_kv(RK - 1)

            def do_qk(bi):
                h, kc, kbl = blocks[bi]
                n = nblk[0] + bi
                buf = n % 2
                sl, t_kv = kvslot[(h, kc)]
                diag = (kc == T)
                a = kbl * 128 if diag else 0
                adds = []
                if diag:
                    adds.append((kbl, 0))
                    if kbl + 1 < 4:
                        adds.append((kbl + 1, 1))
                elif kc == T - 1 and kbl == 3:
                    adds.append((0, 1))
                deps = [t_kv] + t_q + (exp_tok.get(n - 2) and [exp_tok[n - 2]] or [])
                for m in range(2):
                    last = (m == 1 and not adds)
                    tq = P.op("pe", lambda e, m=m, buf=buf, sl=sl, kbl=kbl, a=a, h=h: e.matmul(
                        psS[:, buf * 2 + m, a:TW], lhsT=kring[64 * m:64 * m + 64, sl, kbl * 128:kbl * 128 + 128],
                        rhs=qT[64 * m:64 * m + 64, h, a:TW], start=True, stop=True, skip_group_check=True),
                        deps if m == 0 else (), sig=(sem_qk if last else None))
                for ai, (s_, kind) in enumerate(adds):
                    for m in range(2):
                        for part in range(2):
                            last = (ai == len(adds) - 1 and m == 1 and part == 1)
                            tq = P.op("pe", lambda e, m=m, buf=buf, s_=s_, kind=kind, part=part, h=h: e.matmul(
                                psS[:, buf * 2 + m, s_ * 128:s_ * 128 + 128], lhsT=ident[:],
                                rhs=biasT[:, h, kind, part, :], start=False, stop=True, skip_group_check=True),
                                (), sig=(sem_qk if last else None))
                return tq, a

            qk_info = {}
            deferred = []
            cur_bi = [0]

            def do_exp(bi):
                h, kc, kbl = blocks[bi]
                n = nblk[0] + bi
                tq, a = qk_info[bi]
                deps = [tq]
                if n - RP in pv_tok:
                    deps.append(pv_tok[n - RP])
                exp_tok[n] = P.op("act", lambda e, n=n, a=a, h=h: e.activation(
                    out=pring[:, n % RP, :, a:TW], in_=psS[:, (n % 2) * 2:(n % 2) * 2 + 2, a:TW], func=AF.Exp,
                    bias=cfar[:, h:h + 1], scale=1.0), deps, sig=sem_exp)

            def do_pv(bi):
                h, kc, kbl = blocks[bi]
                n = nblk[0] + bi
                sl, t_kv = kvslot[(h, kc)]
                diag = (kc == T)
                first = (kc == 0 and kbl == 0)
                lastb = (kc == T and kbl == 3)
                s0 = kbl if diag else 0
                ops = [(s, m) for s in range(s0, 4) for m in range(2)]
                for oi_, (s, m) in enumerate(ops):
                    a_ = s * 2 + m
                    deps = []
                    if oi_ == 0:
                        deps = [exp_tok[n]]
                        if first:
                            deps.append(acc_free[0])
                    tp_ = P.op("pe", lambda e, s=s, m=m, a_=a_, n=n, sl=sl, kbl=kbl, first=first, lastb=lastb: e.matmul(
                        psA[:, a_ // 3, (a_ % 3) * ACCW:(a_ % 3) * ACCW + 129],
                        lhsT=pring[:, n % RP, m, s * 128:s * 128 + 128], rhs=vring[:, sl, kbl, :],
                        start=(first and a_ % 3 == 0), stop=lastb, skip_group_check=True),
                        deps, sig=(sem_pv if oi_ == len(ops) - 1 else None))
                pv_tok[n] = tp_
                if kbl == 3:
                    kv_free[sl] = tp_
                    ensure_kv(kv_issued[0] + 1)
                if lastb:
                    epilogue(h, tp_)

            def epilogue(h, t_last):
                deps = [t_last] + t_gate
                tcs = []
                for bk_ in range(3):
                    na = 3 if bk_ < 2 else 2
                    tcs.append(P.op("dve", lambda e, bk_=bk_, na=na: e.tensor_copy(
                        out=accs[:, bk_ * 3:bk_ * 3 + na, :],
                        in_=psA[:, bk_, 0:na * ACCW].rearrange("p (a w) -> p a w", w=ACCW)), deps + t_ep[-1:]))
                acc_free[0] = tcs[-1]
                t = P.op("dve", lambda e: e.reciprocal(out=rec[:], in_=accs[:, :, 128]), tcs)
                t = P.op("dve", lambda e: e.tensor_scalar(out=nr2[:], in0=rec[:].rearrange("p (s m) -> p s m", m=2)[:, :, 1],
                                                          scalar1=nlam[:, 0:1], scalar2=None, op0=ALU.mult), [t, t_nlam])
                for s in range(4):
                    t = P.op("dve", lambda e, s=s: e.tensor_scalar(out=ttmp[:], in0=accs[:, 2 * s + 1, 0:128],
                                                                   scalar1=nr2[:, s:s + 1], scalar2=None, op0=ALU.mult), [t])
                    t = P.op("dve", lambda e, s=s: e.scalar_tensor_tensor(
                        out=otmp[:, s, :], in0=accs[:, 2 * s, 0:128], scalar=rec[:, 2 * s:2 * s + 1], in1=ttmp[:],
                        op0=ALU.mult, op1=ALU.add), [t])
                    t = P.op("dve", lambda e, s=s: e.tensor_tensor(out=sqtmp[:], in0=otmp[:, s, :], in1=otmp[:, s, :],
                                                                   op=ALU.mult), [t])
                    t = P.op("dve", lambda e, s=s: e.reduce_sum(out=ssq[:, s:s + 1], in_=sqtmp[:], axis=AX.X), [t])
                t_ssq = t

                def part2(h=h, t_ssq=t_ssq):
                    t = P.op("act", lambda e: e.activation(out=rstd[:], in_=ssq[:], func=AF.Ln, bias=epsb[:, 0:1],
                                                           scale=1.0 / 128.0), [t_ssq])
                    t = P.op("act", lambda e: e.activation(out=rstd[:], in_=rstd[:], func=AF.Exp, scale=-0.5), [t])
                    for s in range(4):
                        t = P.op("dve", lambda e, s=s, h=h: e.scalar_tensor_tensor(
                            out=ztok[:, s, h * 128:h * 128 + 128], in0=otmp[:, s, :], scalar=rstd[:, s:s + 1],
                            in1=sgd[:, s, h * 128:h * 128 + 128], op0=ALU.mult, op1=ALU.mult), [t])
                    t_ep.append(t)
                deferred.append([cur_bi[0] + 6, part2])

            for mh in range(MH):
                for blk in range(2):
                    n = nblk[0]
                    nblk[0] += 1
                    buf = n % 2
                    deps = t_q + ([exp_tok[n - 2]] if (n - 2) in exp_tok else [])
                    tq = P.op("pe", lambda e, buf=buf, mh=mh, blk=blk: e.matmul(
                        psS[:, buf * 2, :], lhsT=mkT[:, mh, blk * 128:blk * 128 + 128], rhs=mqT[:, mh, :],
                        start=True, stop=True, skip_group_check=True), deps, sig=sem_qk)
                    deps = [tq]
                    if n - RP in pv_tok:
                        deps.append(pv_tok[n - RP])
                    exp_tok[n] = P.op("act", lambda e, n=n, buf=buf: e.activation(
                        out=pring[:, n % RP, 0, :], in_=psS[:, buf * 2, :], func=AF.Exp), deps, sig=sem_exp)
                    for s in range(4):
                        deps = []
                        if s == 0:
                            deps = [exp_tok[n]]
                            if blk == 0:
                                deps.append(acc_free[0])
                        tp_ = P.op("pe", lambda e, s=s, n=n, mh=mh, blk=blk: e.matmul(
                            psA[:, s // 3, (s % 3) * ACCW:(s % 3) * ACCW + 129],
                            lhsT=pring[:, n % RP, 0, s * 128:s * 128 + 128], rhs=mvx[:, blk, mh, :],
                            start=(blk == 0 and s % 3 == 0), stop=(blk == 1), skip_group_check=True),
                            deps, sig=(sem_pv if s == 3 else None))
                    pv_tok[n] = tp_
                tcs = [P.op("dve", lambda e: e.tensor_copy(
                    out=accs[:, 0:3, :], in_=psA[:, 0, 0:3 * ACCW].rearrange("p (a w) -> p a w", w=ACCW)),
                    [tp_] + t_gate + t_ep[-1:]),
                    P.op("dve", lambda e: e.tensor_copy(
                        out=accs[:, 3:4, :], in_=psA[:, 1, 0:ACCW].rearrange("p (a w) -> p a w", w=ACCW)), [tp_])]
                acc_free[0] = tcs[-1]
                t = P.op("dve", lambda e: e.reciprocal(out=rec[:, 0:4], in_=accs[:, 0:4, 128]), tcs)
                for s in range(4):
                    t = P.op("dve", lambda e, s=s, mh=mh: e.scalar_tensor_tensor(
                        out=ztok[:, s, 1024 + mh * 128:1024 + mh * 128 + 128], in0=accs[:, s, 0:128],
                        scalar=rec[:, s:s + 1], in1=sgm[:, s, mh * 128:mh * 128 + 128], op0=ALU.mult, op1=ALU.mult), [t])
                t_ep.append(t)

            post_mem()
            for b0 in range(min(2, nb_total)):
                qk_info[b0] = do_qk(b0)
                do_exp(b0)
            for bi in range(nb_total):
                cur_bi[0] = bi
                if bi + 2 < nb_total:
                    qk_info[bi + 2] = do_qk(bi + 2)
                    do_exp(bi + 2)
                while deferred and deferred[0][0] <= bi:
                    deferred.pop(0)[1]()
                do_pv(bi)
            while deferred:
                deferred.pop(0)[1]()
            nblk[0] += nb_total

            t_zt = []
            t_tp_free = [None]
            for s in range(4):
                for grp in range(3):
                    for k4 in range(4):
                        cidx_ = grp * 4 + k4
                        tt = P.op("pe", lambda e, s=s, cidx_=cidx_, k4=k4: e.transpose(
                            psTb[:, k4 * 128:k4 * 128 + 128], ztok[:, s, cidx_ * 128:cidx_ * 128 + 128], ident[:]),
                            (t_ep[-1:] + [t_tp_free[0]]) if k4 == 0 else (), sig=("chain" if k4 == 3 else None))
                    dst0 = grp * 4 if grp < 2 else 12
                    tcp = P.op("dve", lambda e, s=s, dst0=dst0: e.tensor_copy(
                        out=zT[:, dst0:dst0 + 4, s * 128:s * 128 + 128],
                        in_=psTb[:, 0:512].rearrange("p (k t) -> p k t", t=128)), [tt])
                    t_tp_free[0] = tcp
                    t_zt.append(tcp)
            t_zt.append(P.op("pool", lambda e: e.tensor_copy(out=zT[:, 8:12, 0:TW], in_=zTc[:]), t_cv + t_ep[-1:]))

            t_y = {}
            xr_free = [None, None]
            xcnt = 0
            for g in range(4):
                if g < 2:
                    r, t_w = w_o[g]
                else:
                    r, t_w = load_w(G_O0 + g)
                for s in range(4):
                    xsel = xcnt % 2
                    xcnt += 1
                    t_xr = P.dma("pool", xres[xsel][:], xown[i * TW + s * 128:i * TW + s * 128 + 128, g * 512:g * 512 + 512],
                                 [xr_free[xsel]], key="xr%d" % xsel)

                    def mm(e, out, c, st, sp_, r=r, s=s):
                        return e.matmul(out, lhsT=zT[:, c, s * 128:s * 128 + 128], rhs=wring[r][:, c, :], start=st, stop=sp_)

                    def ev(bk, tpe, s=s, g=g, xsel=xsel, t_xr=t_xr):
                        return P.op("dve", lambda e: e.scalar_tensor_tensor(
                            out=ysb[:, s, g * 512:g * 512 + 512], in0=xres[xsel][:], scalar=ALPHA, in1=bk,
                            op0=ALU.mult, op1=ALU.add), [tpe, t_xr, y_st_tok[0]] + t_cv)
                    last_pe, t_e = job(mm, 512, ev, t_zt + [t_w])
                    xr_free[xsel] = t_e
                    t_y[(s, g)] = t_e
                wr_free[r] = last_pe
            t_done = []
            t = None
            for s in range(4):
                for g in range(4):
                    t = P.op("dve", lambda e, s=s, g=g: e.bn_stats(out=bnst4[:, s, g, :], in_=ysb[:, s, g * 512:g * 512 + 512]),
                             [t_y[(s, g)], t])
                t = P.op("dve", lambda e, s=s: e.bn_aggr(out=mv4[:, s, :], in_=bnst4[:, s].rearrange("p a b -> p (a b)")), [t])
            t_r = P.op("act", lambda e: e.activation(out=lnr4[:], in_=mv4[:, :, 1], func=AF.Ln, bias=epsb[:, 0:1], scale=1.0), [t])
            t_r = P.op("act", lambda e: e.activation(out=lnr4[:], in_=lnr4[:], func=AF.Exp, scale=-0.5), [t_r])
            for s in range(4):
                t = P.op("dve", lambda e, s=s: e.scalar_tensor_tensor(out=ysb[:, s, :], in0=ysb[:, s, :], scalar=mv4[:, s, 0:1],
                                                                      in1=lngb, op0=ALU.subtract, op1=ALU.mult), [t, t_lng])
                t = P.op("dve", lambda e, s=s: e.scalar_tensor_tensor(out=ysb[:, s, :], in0=ysb[:, s, :], scalar=lnr4[:, s:s + 1],
                                                                      in1=lnb_sb[:], op0=ALU.mult, op1=ALU.add), [t, t_lnb, t_r])
                t_o = P.dma("pool", y_o[i * TW + s * 128:i * TW + s * 128 + 128, :], ysb[:, s, :], [t], key="yout")
                t_done.append(t_o)
            y_st_tok[0] = t_done[-1]
            prev_slot_done = t_done + [t]
        P.emit()
        P.op("dve", lambda e: e.memset(lnr[:], 0.0), ())
        P.emit()
        SB.__exit__(None, None, None)
    return nc


_CACHE = {}


def _host_consts():
    half, max_exact = 16, 8

    def bucket(rel):
        ret = np.where(rel > 0, half, 0)
        n = np.abs(rel)
        nf = np.maximum(n, 1).astype(np.float32)
        large = max_exact + (np.log(nf / max_exact) / math.log(128 / max_exact) * (half - max_exact)).astype(np.int32)
        large = np.minimum(large, half - 1)
        return ret + np.where(n < max_exact, n, large)

    k = np.arange(128)[:, None]
    q = np.arange(128)[None, :]
    bd = bucket(k - q)
    bp = bucket(k - q - 128)
    bkt = np.concatenate([bd, bp], axis=1).astype(np.float32)
    dmask = np.where((k // 64) <= (q // 64), 0.0, NEG).astype(np.float32)
    return bkt, dmask


def kernel(x_prompt, x_sample, cache_diff_k, cache_diff_v, cache_conv, cache_mem_k, cache_mem_v,
           mem_prompt, rel_bias_table, w_in, w_mem_kv, conv_w, lambda_q1, lambda_k1, lambda_q2,
           lambda_k2, subln_g, w_out, ln_g, ln_b, _nslot=NSLOT, _sample=True):
    f = np.float32
    x_prompt = np.asarray(x_prompt, f)
    B, S, _ = x_prompt.shape
    key = (_nslot, _sample)
    if key not in _CACHE:
        _CACHE[key] = build(_nslot, _sample)
    nc = _CACHE[key]

    bkt, dmask = _host_consts()
    rep = lambda v, n=128: np.ascontiguousarray(np.broadcast_to(np.asarray(v, f).reshape(1, -1), (n, np.asarray(v).size)))
    shared = {
        "w_in": np.ascontiguousarray(np.asarray(w_in, f)[0]),
        "w_out": np.ascontiguousarray(np.asarray(w_out, f)[0]),
        "w_mem": np.ascontiguousarray(np.asarray(w_mem_kv, f)[0]),
        "ident": np.eye(128, dtype=f),
        "tab": rep(np.asarray(rel_bias_table, f).reshape(-1)),
        "lamv": rep(np.concatenate([np.asarray(a, f).reshape(-1) for a in
                                    (lambda_q1, lambda_k1, lambda_q2, lambda_k2)])),
        "gsub": rep(np.tile(np.asarray(subln_g, f).reshape(-1), 4)),
        "lng": rep(np.asarray(ln_g, f).reshape(-1)),
        "lnb": rep(np.asarray(ln_b, f).reshape(-1)),
        "convw": np.ascontiguousarray(np.asarray(conv_w, f)[0].reshape(3, 4, 128).transpose(2, 1, 0).reshape(128, 12)),
        "bkt": bkt,
        "dmask": dmask,
    }
    in_maps = []
    xTb = [np.ascontiguousarray(x_prompt[b].T) for b in range(B)]
    memTb = [np.ascontiguousarray(np.asarray(mem_prompt, f)[b].T) for b in range(B)]
    for c in range(NCORES):
        b, j = c // 4, c % 4
        xT = np.zeros((D, NT * TW), f)
        xT[:, (3 - j) * TW:(3 - j) * TW + S] = xTb[b]
        xown = np.concatenate([x_prompt[b, (4 * i + j) * TW:(4 * i + j + 1) * TW] for i in range(NSLOT)], axis=0)
        vm = np.zeros((NT, 32), f)
        vm[3 - j:3 - j + S // TW] = 1.0
        m = dict(shared)
        m["xT"] = xT
        m["xown"] = np.ascontiguousarray(xown)
        m["vmask"] = rep(vm.reshape(-1))
        m["memT"] = memTb[b]
        in_maps.append(m)
    if _sample:
        xs_all = np.asarray(x_sample, f)
        ck = np.asarray(cache_diff_k, f)[0]
        cvv = np.asarray(cache_diff_v, f)[0]
        cc_ = np.asarray(cache_conv, f)[0]
        cmk = np.asarray(cache_mem_k, f)[0]
        cmv = np.asarray(cache_mem_v, f)[0]
        for c in range(NCORES):
            sl = slice(c * NSTR, (c + 1) * NSTR)
            xs_c = xs_all[sl].reshape(NSTR * SS, D)
            m = in_maps[c]
            m["xs"] = np.ascontiguousarray(xs_c)
            m["xsT"] = np.ascontiguousarray(xs_c.T)
            m["ckT"] = np.ascontiguousarray(ck[sl].transpose(0, 2, 3, 1))
            m["cv"] = np.ascontiguousarray(cvv[sl].reshape(NSTR, PAST, 1024))
            m["cconv"] = np.ascontiguousarray(cc_[sl].reshape(NSTR, 2, 4, 128).transpose(3, 2, 0, 1))
            m["cmkT"] = np.ascontiguousarray(cmk[sl].transpose(0, 2, 3, 1))
            m["cmv"] = np.ascontiguousarray(cmv[sl].reshape(NSTR, NMEM, 512))
    res = run_bass_kernel_spmd(nc, in_maps, core_ids=list(range(NCORES)))
    R = res.results

    y = np.zeros((B, S, D), f)
    kp = np.zeros((1, B, S, H, 128), f)
    vp = np.zeros((1, B, S, H, 128), f)
    for c in range(NCORES):
        b, j = c // 4, c % 4
        for i in range(NSLOT):
            t = 4 * i + j
            y[b, t * TW:(t + 1) * TW] = R[c]["y"][i * TW:(i + 1) * TW]
            kp[0, b, t * TW:(t + 1) * TW] = R[c]["koT"][:, :, i * TW:(i + 1) * TW].transpose(2, 0, 1)
            vp[0, b, t * TW:(t + 1) * TW] = R[c]["vo"][i * TW:(i + 1) * TW].reshape(TW, H, 128)
    convp = np.zeros((1, B, 2, 512), f)
    mkp = np.zeros((1, B, NMEM, MH, 128), f)
    mvp = np.zeros((1, B, NMEM, MH, 128), f)
    for b in range(B):
        c = 4 * b + 3
        convp[0, b] = R[c]["convo"].transpose(2, 1, 0).reshape(2, 512)
        mkp[0, b] = R[c]["mkoT"].transpose(2, 0, 1)
        mvp[0, b] = R[c]["mvo"].reshape(NMEM, MH, 128)
    ys = np.zeros((32, SS, D), f)
    ks = np.zeros((1, 32, SS, H, 128), f)
    vs = np.zeros((1, 32, SS, H, 128), f)
    cs = np.zeros((1, 32, 2, 512), f)
    if _sample:
        for c in range(NCORES):
            sl = slice(c * NSTR, (c + 1) * NSTR)
            ys[sl] = R[c]["ys"].reshape(NSTR, SS, D)
            ks[0, sl] = R[c]["ksT"].transpose(2, 0, 1).reshape(NSTR, SS, H, 128)
            vs[0, sl] = R[c]["vs"].reshape(NSTR, SS, H, 128)
            cs[0, sl] = R[c]["convs"].transpose(1, 3, 2, 0).reshape(NSTR, 2, 512)
    return (y, ys, kp, vp, convp, mkp, mvp, ks, vs, cs)
```

```python
import math
from contextlib import ExitStack

import numpy as np
import concourse.bass as bass
import concourse.mybir as mybir
from concourse.bass_utils import run_bass_kernel_spmd

F32 = mybir.dt.float32
BF16 = mybir.dt.bfloat16
AF = mybir.ActivationFunctionType
ALU = mybir.AluOpType
AX = mybir.AxisListType

NCORES = 8
D = 2048
TW = 512
NT = 35
NSLOT = 8
NCH = 16
H = 8
MH = 4
NMEM = 256
EPS = 1e-5
ALPHA = 2.0 ** 0.25
LAM_INIT = 0.8 - 0.6 * math.exp(-0.0)
ACCW = 130
RK = 4
RP = 3
NEG = -30000.0
SS = 16
NSTR = 4
PAST = 1024

WG = [("w_in", 0), ("w_in", 512),
      ("w_in", 3072), ("w_in", 3584), ("w_in", 4096),
      ("w_in", 4608),
      ("w_in", 5120), ("w_in", 5632),
      ("w_in", 6144), ("w_in", 6656),
      ("w_out", 0), ("w_out", 512), ("w_out", 1024), ("w_out", 1536)]
G_Q0, G_Q1, G_H, G_B, G_C, G_MQ, G_GD0, G_GD1, G_GC, G_GM, G_O0 = range(11)


class Sem:
    def __init__(self, h):
        self.h = h
        self.v = 0


class Prog:
    ENG = ("pe", "act", "dve", "pool", "sp")

    def __init__(self, nc, stack):
        self.nc = nc
        self.stack = stack
        self.sems = []
        self.q = {e: [] for e in self.ENG}
        self.chain = {e: self.sem("ch_" + e) for e in self.ENG}
        self.barrier_toks = {e: [] for e in self.ENG}
        self.waited = {e: {} for e in self.ENG}
        self.dsems = {}

    def sem(self, name):
        s = Sem(self.stack.enter_context(self.nc.semaphore(name)))
        self.sems.append(s)
        return s

    def op(self, eng, fn, after=(), sig="chain", amt=1):
        waits = [t for t in after if t is not None]
        if self.barrier_toks[eng]:
            waits = list(self.barrier_toks[eng]) + waits
            self.barrier_toks[eng] = []
        w2 = []
        wd = self.waited[eng]
        for (s, v) in waits:
            if wd.get(id(s), 0) >= v:
                continue
            wd[id(s)] = v
            w2.append((s, v))
        if sig == "chain":
            sig = self.chain[eng]
        tok = None
        if sig is not None:
            sig.v += amt
            tok = (sig, sig.v)
        self.q[eng].append((w2, fn, sig, amt))
        return tok

    def dma(self, eng, out, in_, after=(), key=None):
        assert key is not None
        if key not in self.dsems:
            self.dsems[key] = self.sem("d_" + key)
        return self.op(eng, lambda e: e.dma_start(out=out, in_=in_), after, self.dsems[key], 16)

    def emit(self):
        nc = self.nc
        q = self.q

        def run(e, lst):
            for (waits, fn, sig, amt) in lst:
                for (s, v) in waits:
                    e.wait_ge(s.h, v)
                inst = fn(e)
                if sig is not None:
                    inst.then_inc(sig.h, amt)

        with nc.Block() as block:
            if q["pe"]:
                @block.tensor
                def _(e):
                    run(e, q["pe"])
            if q["act"]:
                @block.scalar
                def _(e):
                    run(e, q["act"])
            if q["dve"]:
                @block.vector
                def _(e):
                    run(e, q["dve"])
            if q["pool"]:
                @block.gpsimd
                def _(e):
                    run(e, q["pool"])
            if q["sp"]:
                @block.sync
                def _(e):
                    run(e, q["sp"])
        self.q = {e: [] for e in self.ENG}
        toks = [(s, s.v) for s in self.sems if s.v > 0]
        for e in self.ENG:
            self.barrier_toks[e] = list(toks)


def build(nslot=NSLOT, do_sample=True):
    nc = bass.Bass("TRN2", target_bir_lowering=False)
    top = ExitStack()
    with top:
        P = Prog(nc, top)

        def din(name, shape, dt=F32):
            return nc.dram_tensor(name, list(shape), dt, kind="ExternalInput").ap()

        def dout(name, shape, dt=F32):
            return nc.dram_tensor(name, list(shape), dt, kind="ExternalOutput").ap()

        def dscr(name, shape, dt):
            return nc.dram_tensor(name, list(shape), dt).ap()

        def sb(stack, name, shape, dt):
            return stack.enter_context(nc.sbuf_tensor("s_" + name, list(shape), dt))

        xT = din("xT", [D, NT * TW])
        xown = din("xown", [NSLOT * TW, D])
        vmask = din("vmask", [128, NT * 32])
        w_srcs = {"w_in": din("w_in", [D, 7168]), "w_out": din("w_out", [D, D])}
        w_mem = din("w_mem", [D, 1024])
        memT = din("memT", [D, NMEM])
        ident_d = din("ident", [128, 128])
        tab_d = din("tab", [128, 256])
        lam_d = din("lamv", [128, 256])
        gsub_d = din("gsub", [128, 512])
        lng_d = din("lng", [128, D])
        lnb_d = din("lnb", [128, D])
        cw_d = din("convw", [128, 12])
        bkt_d = din("bkt", [128, 256])
        dmask_d = din("dmask", [128, 128])

        y_o = dout("y", [NSLOT * TW, D])
        koT_o = dout("koT", [H, 128, NSLOT * TW])
        vo_o = dout("vo", [NSLOT * TW, 1024])
        convo_o = dout("convo", [128, 4, 2])
        mkoT_o = dout("mkoT", [MH, 128, NMEM])
        mvo_o = dout("mvo", [NMEM, 512])

        if do_sample:
            xsT_d = din("xsT", [D, NSTR * SS])
            xs_d = din("xs", [NSTR * SS, D])
            ckT_d = din("ckT", [NSTR, H, 128, PAST])
            cv_d = din("cv", [NSTR, PAST, 1024])
            cconv_d = din("cconv", [128, 4, NSTR, 2])
            cmkT_d = din("cmkT", [NSTR, MH, 128, NMEM])
            cmv_d = din("cmv", [NSTR, NMEM, 512])
            ys_o = dout("ys", [NSTR * SS, D])
            ksT_o = dout("ksT", [H, 128, NSTR * SS])
            vs_o = dout("vs", [NSTR * SS, 1024])
            convs_o = dout("convs", [128, NSTR, 4, 2])

        wsc = dscr("wsc", [len(WG) + 4, 128, NCH, 512], BF16)
        kts = dscr("kts", [NT, 128, H, 512], BF16)
        vxs = dscr("vxs", [NT, 128, H, 4 * 129], BF16)

        psS = top.enter_context(nc.psum_tensor("psS", [128, 4, 512], F32))
        psA = top.enter_context(nc.psum_tensor("psA", [128, 3, 512], F32))
        psT = top.enter_context(nc.psum_tensor("psT", [128, 512], F32))

        psTb = psT[:].bitcast(BF16)

        def bank(b):
            return psS[:, b, :] if b < 4 else psA[:, b - 4, :]

        ident = sb(top, "identb", [128, 128], BF16)
        tab = sb(top, "tab", [128, 256], F32)
        cfar = sb(top, "cfar", [128, 8], F32)
        biasT = sb(top, "biasT", [128, H, 2, 2, 128], BF16)
        gs4 = sb(top, "gs4", [128, 512], F32)
        cw = sb(top, "cw", [128, 12], F32)
        nlam = sb(top, "nlam", [128, 1], F32)
        epsb = sb(top, "epsb", [128, 1], F32)
        mkT = sb(top, "mkT", [128, MH, NMEM], BF16)
        mvx = sb(top, "mvx", [128, 2, MH, 129], BF16)

        SA = ExitStack()
        SA.__enter__()
        wkv = sb(SA, "wkv", [128, 4, NCH, 512], BF16)
        bkt = sb(SA, "bkt", [128, 256], F32)
        dmask = sb(SA, "dmask", [128, 128], F32)
        eqm = sb(SA, "eqm", [128, 128], F32)
        bacc = sb(SA, "bacc", [128, 128], F32)
        bhi = sb(SA, "bhi", [128, 128], F32)
        S0 = ExitStack()
        S0.__enter__()
        stg = [sb(S0, "stg%d" % i, [128, NCH, 512], F32) for i in range(2)]
        wbf = [sb(S0, "wbf%d" % i, [128, NCH, 512], BF16) for i in range(2)]
        identf = sb(S0, "identf", [128, 128], F32)
        lamv = sb(S0, "lamvs", [128, 256], F32)
        ltmp = sb(S0, "ltmp", [128, 64], F32)
        lsum = sb(S0, "lsum", [128, 4], F32)
        memb = sb(S0, "memb", [128, NCH, NMEM], BF16)
        mkf = sb(S0, "mkf", [128, MH, NMEM], F32)
        mvf = sb(S0, "mvf", [128, 2, 512], F32)

        P.dma("sp", identf[:], ident_d, key="c0")
        P.dma("sp", tab[:], tab_d, key="c0")
        P.dma("sp", lamv[:], lam_d, key="c0")
        P.dma("sp", gs4[:], gsub_d, key="c0")
        P.dma("sp", cw[:], cw_d, key="c0")
        P.dma("sp", bkt[:], bkt_d, key="c0")
        t_c0 = P.dma("sp", dmask[:], dmask_d, key="c0")
        t_idb = P.op("dve", lambda e: e.tensor_copy(out=ident[:], in_=identf[:]), [t_c0])
        P.op("dve", lambda e: e.memset(epsb[:], EPS), ())
        t_gs2 = P.op("dve", lambda e: e.tensor_scalar(out=gs4[:], in0=gs4[:], scalar1=1.0 - LAM_INIT,
                                                      scalar2=None, op0=ALU.mult), [t_c0])
        tl = t_c0
        for k in range(2):
            tl = P.op("dve", lambda e, k=k: e.tensor_tensor(out=ltmp[:], in0=lamv[:, 128 * k:128 * k + 64],
                                                            in1=lamv[:, 128 * k + 64:128 * k + 128], op=ALU.mult), [tl])
            tl = P.op("dve", lambda e, k=k: e.reduce_sum(out=lsum[:, k:k + 1], in_=ltmp[:], axis=AX.X), [tl])
        tl = P.op("act", lambda e: e.activation(out=lsum[:, 2:4], in_=lsum[:, 0:2], func=AF.Exp), [tl])
        tl = P.op("dve", lambda e: e.tensor_tensor(out=nlam[:], in0=lsum[:, 3:4], in1=lsum[:, 2:3], op=ALU.subtract), [tl])
        t_nlam = P.op("dve", lambda e: e.tensor_scalar(out=nlam[:], in0=nlam[:], scalar1=-LAM_INIT, scalar2=None,
                                                       op0=ALU.add), [tl])
        t_cf = P.op("dve", lambda e: e.tensor_copy(out=cfar[:], in_=tab[:, 120:128]), [t_c0])
        def wsrc(name, col):
            return w_srcs[name].rearrange("(c p) n -> p c n", p=128)[:, :, col:col + 512]

        jobs = [("kv", k, w_srcs["w_in"].rearrange("(c p) n -> p c n", p=128)[:, :, 1024 + 512 * k:1536 + 512 * k])
                for k in range(4)]
        jobs += [("mem", k, w_mem.rearrange("(c p) n -> p c n", p=128)[:, :, 512 * k:512 * k + 512]) for k in range(2)]
        stg_free = [None, None]
        wbf_free = [None, None]
        memw_tok = [None, None]
        for n, (kind, idx, src) in enumerate(jobs):
            bsel = n % 2
            t_ld = P.dma("sp", stg[bsel][:], src, [stg_free[bsel]], key="stg%d" % bsel)
            if kind == "kv":
                dst = wkv[:, idx]
            else:
                dst = wbf[bsel][:]
            if n % 2 == 0:
                t_c = P.op("dve", lambda e, dst=dst, bsel=bsel: e.tensor_copy(out=dst, in_=stg[bsel][:]),
                           [t_ld, wbf_free[bsel]])
            else:
                t_c = P.op("act", lambda e, dst=dst, bsel=bsel: e.activation(out=dst, in_=stg[bsel][:], func=AF.Copy),
                           [t_ld, wbf_free[bsel]])
            stg_free[bsel] = t_c
            if kind == "scr":
                wbf_free[bsel] = P.dma("act", wsc[idx], wbf[bsel][:], [t_c], key="wst%d" % bsel)
            elif kind == "kv":
                P.dma("act", wsc[len(WG) + idx], wkv[:, idx], [t_c], key="wstkv")
            elif kind == "mem":
                memw_tok[idx] = (t_c, bsel)

        memstg = stg[0][:, :, 0:NMEM]
        t_m = P.dma("sp", memstg, memT.rearrange("(c p) n -> p c n", p=128), [stg_free[0]], key="stg0")
        t_mb = P.op("dve", lambda e: e.tensor_copy(out=memb[:], in_=memstg), [t_m])
        bank_free = [None] * 7
        jb = 0
        wmk = wbf[memw_tok[0][1]]
        wmv = wbf[memw_tok[1][1]]
        for mh in range(MH):
            b = jb % 7
            jb += 1
            for c in range(NCH):
                t_pe = P.op("pe", lambda e, b=b, c=c, mh=mh: e.matmul(
                    bank(b)[:, 0:NMEM], lhsT=wmk[:, c, mh * 128:mh * 128 + 128], rhs=memb[:, c, :],
                    start=(c == 0), stop=(c == NCH - 1)),
                    [t_mb, memw_tok[0][0], bank_free[b], t_idb] if c == 0 else (), sig=("chain" if c == NCH - 1 else None))
            t1 = P.op("act", lambda e, b=b, mh=mh: e.activation(out=mkT[:, mh, :], in_=bank(b)[:, 0:NMEM], func=AF.Copy), [t_pe])
            t2 = P.op("dve", lambda e, b=b, mh=mh: e.tensor_copy(out=mkf[:, mh, :], in_=bank(b)[:, 0:NMEM]), [t_pe, t1])
            bank_free[b] = t2
        P.dma("sp", mkoT_o.rearrange("h p n -> p h n"), mkf[:], [t2], key="mko")
        for blk in range(2):
            b = jb % 7
            jb += 1
            for c in range(NCH):
                t_pe = P.op("pe", lambda e, b=b, c=c, blk=blk: e.matmul(
                    bank(b), lhsT=memb[:, c, blk * 128:blk * 128 + 128], rhs=wmv[:, c, :],
                    start=(c == 0), stop=(c == NCH - 1)),
                    [t_mb, memw_tok[1][0], bank_free[b]] if c == 0 else (), sig=("chain" if c == NCH - 1 else None))
            t1 = P.op("act", lambda e, b=b, blk=blk: e.activation(
                out=mvx[:, blk, :, 0:128], in_=bank(b).rearrange("p (h e) -> p h e", e=128), func=AF.Copy), [t_pe])
            t2 = P.op("dve", lambda e, b=b, blk=blk: e.tensor_copy(out=mvf[:, blk, :], in_=bank(b)), [t_pe, t1])
            bank_free[b] = t2
        for blk in range(2):
            t2 = P.op("dve", lambda e, blk=blk: e.memset(mvx[:, blk, :, 128:129], 1.0), [t2])
        P.dma("sp", mvo_o.rearrange("(b p) n -> p b n", p=128), mvf[:], [t2], key="mvo")
        P.emit()
        S0.__exit__(None, None, None)

        xstg = [sb(SA, "xstg%d" % i, [128, 4, TW], F32) for i in range(2)]
        xb = [sb(SA, "xb%d" % i, [128, NCH, TW], BF16) for i in range(2)]
        ktst = [sb(SA, "ktst%d" % i, [128, H, TW], BF16) for i in range(2)]
        vxst = [sb(SA, "vxst%d" % i, [128, H, 4, 129], BF16) for i in range(2)]
        vm = sb(SA, "vm", [128, NT * 32], F32)
        kof = [sb(SA, "kof%d" % i, [128, TW], F32) for i in range(2)]
        vof = [sb(SA, "vof%d" % i, [128, 512], F32) for i in range(2)]

        def bias_gen():
            tp = t_c0
            for h in range(H):
                for kind in range(2):
                    nb = range(32) if kind == 0 else range(1, 16)
                    first = True
                    for b in nb:
                        tp = P.op("dve", lambda e, b=b, kind=kind: e.tensor_single_scalar(
                            out=eqm[:], in_=bkt[:, 128 * kind:128 * kind + 128], scalar=float(b), op=ALU.is_equal), [tp])
                        if first:
                            tp = P.op("dve", lambda e, b=b, h=h: e.tensor_scalar(
                                out=bacc[:], in0=eqm[:], scalar1=tab[:, b * 8 + h:b * 8 + h + 1], scalar2=None,
                                op0=ALU.mult), [tp])
                            first = False
                        else:
                            tp = P.op("dve", lambda e, b=b, h=h: e.scalar_tensor_tensor(
                                out=bacc[:], in0=eqm[:], scalar=tab[:, b * 8 + h:b * 8 + h + 1], in1=bacc[:],
                                op0=ALU.mult, op1=ALU.add), [tp])
                        yield
                    tp = P.op("dve", lambda e, h=h: e.tensor_scalar(
                        out=bacc[:], in0=bacc[:], scalar1=tab[:, 120 + h:121 + h], scalar2=None, op0=ALU.subtract), [tp])
                    if kind == 0:
                        tp = P.op("dve", lambda e: e.tensor_tensor(out=bacc[:], in0=bacc[:], in1=dmask[:], op=ALU.add), [tp])
                    tp = P.op("dve", lambda e, h=h, kind=kind: e.tensor_copy(out=biasT[:, h, kind, 0, :], in_=bacc[:]), [tp])
                    tp = P.op("dve", lambda e, h=h, kind=kind: e.tensor_copy(out=bhi[:], in_=biasT[:, h, kind, 0, :]), [tp])
                    tp = P.op("dve", lambda e: e.tensor_tensor(out=bacc[:], in0=bacc[:], in1=bhi[:], op=ALU.subtract), [tp])
                    tp = P.op("dve", lambda e, h=h, kind=kind: e.tensor_copy(out=biasT[:, h, kind, 1, :], in_=bacc[:]), [tp])
                    yield
        bgen = bias_gen()
        wq_f = [sb(SA, "wqf%d" % i, [128, NCH, 128], F32) for i in range(2)]
        wq_b = [sb(SA, "wqb%d" % i, [128, NCH, 128], BF16) for i in range(2)]

        def wcast_gen():
            f_free = [None, None]
            b_free = [None, None]
            n = 0
            for gi in range(len(WG)):
                name, col = WG[gi]
                srcv = w_srcs[name].rearrange("(c p) n -> p c n", p=128)
                for qq in range(4):
                    k = n % 2
                    n += 1
                    t_ld = P.dma("sp", wq_f[k][:], srcv[:, :, col + qq * 128:col + qq * 128 + 128], [f_free[k]],
                                 key="wqf%d" % k)
                    t_c = P.op("act", lambda e, k=k: e.activation(out=wq_b[k][:], in_=wq_f[k][:], func=AF.Copy),
                               [t_ld, b_free[k]])
                    f_free[k] = t_c
                    b_free[k] = P.dma("act", wsc[gi, :, :, qq * 128:qq * 128 + 128], wq_b[k][:], [t_c], key="wqs%d" % k)
                    yield
        wgen = wcast_gen()
        t_vm = P.dma("sp", vm[:], vmask, key="vm")
        xTr = xT.rearrange("(c p) t -> p c t", p=128)
        xstg_free = [None] * 2
        xb_free = [None, None]
        kt_free = [None, None]
        vx_free = [None, None]
        kof_free = [None, None]
        vof_free = [None, None]
        nko = 0
        nvo = 0
        bank_free = [None] * 7
        jb = 0
        npiece = 0
        ntile = 4 * (nslot - 1) + 4
        def load_x(T):
            nonlocal npiece
            xs_ = T % 2
            t_x = []
            for pc in range(4):
                st = npiece % 2
                npiece += 1
                t_ld = P.dma("sp", xstg[st][:], xTr[:, 4 * pc:4 * pc + 4, T * TW:(T + 1) * TW], [xstg_free[st]],
                             key="xstg%d" % st)
                t_c = P.op("dve", lambda e, st=st, pc=pc, xs_=xs_: e.tensor_copy(
                    out=xb[xs_][:, 4 * pc:4 * pc + 4, :], in_=xstg[st][:]), [t_ld, xb_free[xs_]])
                xstg_free[st] = t_c
                t_x.append(t_c)
            return t_x

        t_x_next = load_x(0)
        for T in range(ntile):
            xs_ = T % 2
            own = (T % 4 == 3) and (T // 4) < nslot
            slot = T // 4
            t_x = t_x_next
            if T + 1 < ntile:
                t_x_next = load_x(T + 1)
            t_kev = []
            for h in range(H):
                b = jb % 7
                jb += 1
                for c in range(NCH):
                    t_pe = P.op("pe", lambda e, b=b, c=c, h=h, xs_=xs_: e.matmul(
                        bank(b), lhsT=wkv[:, h // 4, c, (h % 4) * 128:(h % 4) * 128 + 128], rhs=xb[xs_][:, c, :],
                        start=(c == 0), stop=(c == NCH - 1)),
                        (t_x + [bank_free[b]]) if c == 0 else (), sig=("chain" if c == NCH - 1 else None))
                t1 = P.op("act", lambda e, b=b, h=h, xs_=xs_: e.activation(
                    out=ktst[xs_][:, h, :], in_=bank(b), func=AF.Copy), [t_pe, kt_free[xs_]])
                t_kev.append(t1)
                if own:
                    ks_ = nko % 2
                    nko += 1
                    t2 = P.op("dve", lambda e, b=b, ks_=ks_: e.tensor_copy(out=kof[ks_][:], in_=bank(b)),
                              [t_pe, t1, kof_free[ks_]])
                    bank_free[b] = t2
                    kof_free[ks_] = P.dma("act", koT_o[h, :, slot * TW:(slot + 1) * TW], kof[ks_][:], [t2],
                                          key="kof%d" % ks_)
                else:
                    bank_free[b] = t1
            kt_free[xs_] = P.dma("act", kts[T], ktst[xs_][:], t_kev, key="kst%d" % xs_)
            t_vev = []
            t_vm2 = P.op("pool", lambda e, xs_=xs_, T=T: e.tensor_copy(
                out=vxst[xs_][:, :, :, 128], in_=vm[:, T * 32:(T + 1) * 32].rearrange("p (h s) -> p h s", s=4)),
                [t_vm, vx_free[xs_]])
            t_vev.append(t_vm2)
            for s in range(4):
                for g in range(2):
                    b = jb % 7
                    jb += 1
                    for c in range(NCH):
                        t_pe = P.op("pe", lambda e, b=b, c=c, s=s, g=g, xs_=xs_: e.matmul(
                            bank(b), lhsT=xb[xs_][:, c, s * 128:s * 128 + 128], rhs=wkv[:, 2 + g, c, :],
                            start=(c == 0), stop=(c == NCH - 1)),
                            (t_x + [bank_free[b]]) if c == 0 else (), sig=("chain" if c == NCH - 1 else None))
                    t1 = P.op("dve", lambda e, b=b, s=s, g=g, xs_=xs_: e.tensor_copy(
                        out=vxst[xs_][:, 4 * g:4 * g + 4, s, 0:128], in_=bank(b).rearrange("p (h e) -> p h e", e=128)),
                        [t_pe, vx_free[xs_]])
                    t_vev.append(t1)
                    if own:
                        vs_ = nvo % 2
                        nvo += 1
                        t2 = P.op("act", lambda e, b=b, vs_=vs_: e.activation(
                            out=vof[vs_][:], in_=bank(b), func=AF.Copy), [t_pe, t1, vof_free[vs_]])
                        bank_free[b] = t2
                        vof_free[vs_] = P.dma("act", vo_o[slot * TW + s * 128:slot * TW + s * 128 + 128,
                                                         g * 512:g * 512 + 512], vof[vs_][:], [t2], key="vof%d" % vs_)
                    else:
                        bank_free[b] = t1
            for _ in range(30):
                if next(bgen, "done") == "done":
                    break
            for _ in range(2):
                if next(wgen, "done") == "done":
                    break
            xb_free[xs_] = t_pe
            vx_free[xs_] = P.dma("pool", vxs[T], vxst[xs_][:].rearrange("p h s e -> p h (s e)"), t_vev, key="vst%d" % xs_)
        for _ in bgen:
            pass
        for _ in wgen:
            pass
        P.emit()
        SA.__exit__(None, None, None)

        if do_sample:
            SS_ = ExitStack()
            SS_.__enter__()
            NTK = NSTR * SS
            wS = sb(SS_, "wS", [128, NCH, 512], BF16)
            xsf = sb(SS_, "xsf", [128, NCH, NTK], F32)
            xsb = sb(SS_, "xsb", [128, NCH, NTK], BF16)
            qTs = sb(SS_, "qTs", [128, H, NTK], BF16)
            kTs = sb(SS_, "kTs", [128, H, NTK], BF16)
            kTf = sb(SS_, "kTf", [128, H, NTK], F32)
            hTs = sb(SS_, "hTs", [128, 4, NTK], F32)
            CTs = sb(SS_, "CTs", [128, 4, NTK], F32)
            BTs = sb(SS_, "BTs", [128, 4, NTK], F32)
            gcTs = sb(SS_, "gcTs", [128, 4, NTK], F32)
            mqTs = sb(SS_, "mqTs", [128, MH, NTK], BF16)
            upad = sb(SS_, "upad", [128, 4, NSTR, SS + 2], F32)
            cvt = sb(SS_, "cvt", [128, NSTR, SS], F32)
            cvt2 = sb(SS_, "cvt2", [128, NSTR, SS], F32)
            zTcs = sb(SS_, "zTcs", [128, 4, NTK], BF16)
            convsb = sb(SS_, "convsb", [128, NSTR, 4, 2], F32)
            vfs = [sb(SS_, "vfs%d" % i, [SS, 512], F32) for i in range(2)]
            vnew = sb(SS_, "vnew", [SS, NSTR, H, 129], BF16)
            sgds = sb(SS_, "sgds", [SS, NSTR, 1024], F32)
            sgms = sb(SS_, "sgms", [SS, NSTR, 512], F32)
            ztoks = sb(SS_, "ztoks", [SS, NSTR, 1536], BF16)
            ckf = [sb(SS_, "ckf%d" % i, [128, 1024], F32) for i in range(2)]
            ckbr = [sb(SS_, "ckbr%d" % i, [128, 4, PAST], BF16) for i in range(3)]
            cvf = [sb(SS_, "cvf%d" % i, [128, 1024], F32) for i in range(2)]
            vxcr = [sb(SS_, "vxcr%d" % i, [128, 8, 4, 129], BF16) for i in range(3)]
            cmkf = sb(SS_, "cmkf", [128, MH, NMEM], F32)
            cmkb = sb(SS_, "cmkb", [128, MH, NMEM], BF16)
            cmvf = sb(SS_, "cmvf", [128, 2, 512], F32)
            cmvx = sb(SS_, "cmvx", [128, 2, MH, 129], BF16)
            pTs = sb(SS_, "pTs", [128, 2, 9, SS], BF16)
            pTm = sb(SS_, "pTm", [128, 2, SS], BF16)
            accS = sb(SS_, "accS", [SS, 2, ACCW], F32)
            oS2 = [sb(SS_, "oS2%d" % i, [SS, 128], F32) for i in range(2)]
            rstdS2 = [sb(SS_, "rstdS2%d" % i, [SS, 2], F32) for i in range(2)]
            acc_freeM = [None]
            tS = sb(SS_, "tS", [SS, 128], F32)
            sqS = sb(SS_, "sqS", [SS, 128], F32)
            recS = sb(SS_, "recS", [SS, 2], F32)
            nr2S = sb(SS_, "nr2S", [SS, 1], F32)
            ssqS = sb(SS_, "ssqS", [SS, 1], F32)
            rstdS = sb(SS_, "rstdS", [SS, 1], F32)
            zTs = sb(SS_, "zTs", [128, NCH, NTK], BF16)
            ysbS = sb(SS_, "ysbS", [NTK, D], F32)
            xrS = [sb(SS_, "xrS%d" % i, [NTK, 512], F32) for i in range(2)]
            lnS = [sb(SS_, "lnS%d" % i, [NTK, 512], F32) for i in range(2)]
            bnS = sb(SS_, "bnS", [NTK, 4, 6], F32)
            mvS = sb(SS_, "mvS", [NTK, 2], F32)
            lnrS = sb(SS_, "lnrS", [NTK, 1], F32)

            t_xs = P.dma("sp", xsf[:], xsT_d.rearrange("(c p) n -> p c n", p=128), key="xsf")
            t_xs = P.op("dve", lambda e: e.tensor_copy(out=xsb[:], in_=xsf[:]), [t_xs])
            t_cc = P.dma("sp", upad[:, :, :, 0:2], cconv_d, key="cconv")
            bank_free = [None] * 7
            jbs = [0]
            wS_free = [None]

            def loadS(gi):
                return P.dma("sp", wS[:], wsc[gi], [wS_free[0]], key="wS")

            def jobS(mm, parts, n_out, evac, deps):
                b = jbs[0] % 7
                jbs[0] += 1
                for c in range(NCH):
                    t_pe = P.op("pe", lambda e, b=b, c=c: mm(e, bank(b)[0:parts, 0:n_out], c, c == 0, c == NCH - 1),
                                (list(deps) + [bank_free[b]]) if c == 0 else (),
                                sig=("chain" if c == NCH - 1 else None))
                t = evac(bank(b)[0:parts, 0:n_out], t_pe)
                bank_free[b] = t
                return t_pe, t

            KV0 = len(WG)
            t_fm = []
            fm_groups = [(G_Q0, "q", 0), (G_Q1, "q", 4), (KV0, "k", 0), (KV0 + 1, "k", 4), (G_H, "h", 0), (G_C, "c", 0),
                         (G_B, "b", 0), (G_GC, "gc", 0), (G_MQ, "mq", 0)]
            for (gi, kind, h0) in fm_groups:
                t_w = loadS(gi)
                for sub in range(4):
                    def mm(e, out, c, st_, sp_, sub=sub):
                        return e.matmul(out, lhsT=wS[:, c, sub * 128:sub * 128 + 128], rhs=xsb[:, c, :], start=st_, stop=sp_)
                    if kind == "q":
                        def ev(bk, tpe, hh=h0 + sub):
                            return P.op("act", lambda e: e.mul(out=qTs[:, hh, :], in_=bk, mul=0.125), [tpe])
                    elif kind == "k":
                        def ev(bk, tpe, hh=h0 + sub):
                            t1 = P.op("act", lambda e: e.activation(out=kTs[:, hh, :], in_=bk, func=AF.Copy), [tpe])
                            return P.op("dve", lambda e: e.tensor_copy(out=kTf[:, hh, :], in_=bk), [tpe, t1])
                    elif kind == "mq":
                        def ev(bk, tpe, sub=sub):
                            return P.op("act", lambda e: e.mul(out=mqTs[:, sub, :], in_=bk, mul=128.0 ** -0.5), [tpe])
                    elif kind == "gc":
                        def ev(bk, tpe, sub=sub):
                            return P.op("act", lambda e: e.activation(out=gcTs[:, sub, :], in_=bk, func=AF.Silu), [tpe])
                    else:
                        tgt = {"h": hTs, "c": CTs, "b": BTs}[kind]

                        def ev(bk, tpe, sub=sub, tgt=tgt):
                            return P.op("dve", lambda e: e.tensor_copy(out=tgt[:, sub, :], in_=bk), [tpe])
                    last_pe, t_e = jobS(mm, 128, NTK, ev, [t_xs, t_w])
                    t_fm.append(t_e)
                wS_free[0] = last_pe
            P.dma("sp", ksT_o.rearrange("h p n -> p h n"), kTf[:], t_fm, key="ksT")
            t_tm = []
            nvf = 0
            vf_free = [None, None]
            for (gi, kind, g) in [(KV0 + 2, "v", 0), (KV0 + 3, "v", 1), (G_GD0, "gd", 0), (G_GD1, "gd", 1), (G_GM, "gm", 0)]:
                t_w = loadS(gi)
                for st in range(NSTR):
                    def mm(e, out, c, st_, sp_, st=st):
                        return e.matmul(out, lhsT=xsb[:, c, st * SS:(st + 1) * SS], rhs=wS[:, c, :], start=st_, stop=sp_)
                    if kind == "v":
                        vsel = nvf % 2
                        nvf += 1

                        def ev(bk, tpe, st=st, g=g, vsel=vsel):
                            t1 = P.op("act", lambda e: e.activation(
                                out=vnew[:, st, 4 * g:4 * g + 4, 0:128], in_=bk.rearrange("p (h e) -> p h e", e=128),
                                func=AF.Copy), [tpe])
                            t2 = P.op("dve", lambda e: e.tensor_copy(out=vfs[vsel][:], in_=bk), [tpe, t1, vf_free[vsel]])
                            vf_free[vsel] = P.dma("sp", vs_o[st * SS:(st + 1) * SS, g * 512:g * 512 + 512], vfs[vsel][:], [t2],
                                                  key="vfs%d" % vsel)
                            return t2
                    elif kind == "gd":
                        def ev(bk, tpe, st=st, g=g):
                            t1 = P.op("act", lambda e: e.activation(out=sgds[:, st, g * 512:g * 512 + 512], in_=bk,
                                                                    func=AF.Silu), [tpe])
                            return P.op("dve", lambda e: e.tensor_tensor(out=sgds[:, st, g * 512:g * 512 + 512],
                                                                         in0=sgds[:, st, g * 512:g * 512 + 512],
                                                                         in1=gs4[0:SS, :], op=ALU.mult), [t1])
                    else:
                        def ev(bk, tpe, st=st):
                            return P.op("act", lambda e: e.activation(out=sgms[:, st, :], in_=bk, func=AF.Silu), [tpe])
                    last_pe, t_e = jobS(mm, SS, 512, ev, [t_xs, t_w])
                    t_tm.append(t_e)
                wS_free[0] = last_pe
            t_on = P.op("dve", lambda e: e.memset(vnew[:, :, :, 128:129].rearrange("p s h e -> p (s h e)"), 1.0), t_tm)
            t_tm.append(t_on)
            t_wo = loadS(G_O0)
            t = None
            t_cvs = []
            for cc in range(4):
                t = P.op("pool", lambda e, cc=cc: e.tensor_tensor(
                    out=upad[:, cc, :, 2:SS + 2], in0=CTs[:, cc, :].rearrange("p (s t) -> p s t", t=SS),
                    in1=hTs[:, cc, :].rearrange("p (s t) -> p s t", t=SS), op=ALU.mult), t_fm + [t_cc, t])
                t_u = P.op("pool", lambda e, cc=cc: e.tensor_copy(out=convsb[:, :, cc, :], in_=upad[:, cc, :, SS:SS + 2]), [t])
                t_cvs.append(t_u)
                t = P.op("pool", lambda e, cc=cc: e.tensor_scalar(out=cvt[:], in0=upad[:, cc, :, 0:SS],
                                                                  scalar1=cw[:, cc * 3:cc * 3 + 1], scalar2=None,
                                                                  op0=ALU.mult), [t])
                for jx in (1, 2):
                    t = P.op("pool", lambda e, cc=cc, jx=jx: e.tensor_scalar(
                        out=cvt2[:], in0=upad[:, cc, :, jx:jx + SS], scalar1=cw[:, cc * 3 + jx:cc * 3 + jx + 1],
                        scalar2=None, op0=ALU.mult), [t])
                    t = P.op("pool", lambda e: e.tensor_tensor(out=cvt[:], in0=cvt[:], in1=cvt2[:], op=ALU.add), [t])
                t = P.op("pool", lambda e, cc=cc: e.tensor_tensor(
                    out=cvt[:], in0=cvt[:], in1=BTs[:, cc, :].rearrange("p (s t) -> p s t", t=SS), op=ALU.mult), [t])
                t = P.op("pool", lambda e, cc=cc: e.tensor_tensor(
                    out=zTcs[:, cc, :].rearrange("p (s t) -> p s t", t=SS), in0=cvt[:],
                    in1=gcTs[:, cc, :].rearrange("p (s t) -> p s t", t=SS), op=ALU.mult), [t])
                t_cvs.append(t)
            P.dma("pool", convs_o, convsb[:], t_cvs, key="convs")

            t_epS = [None]
            NU = NSTR * 2
            RU = 3
            ck_free = [None] * RU
            vx_free = [None] * RU
            cm_free = [None]
            nck = [0]
            ncv = [0]
            ckf_free = [None, None]
            cvf_free = [None, None]
            sb_free = {}
            acc_freeS = [None, None]
            ucache = {}
            t_ones = [P.op("dve", lambda e, r=r: e.memset(vxcr[r][:, :, :, 128:129].rearrange("p b h e -> p (b h e)"), 1.0), ())
                      for r in range(RU)]

            def load_unit(u):
                st, hh = divmod(u, 2)
                r = u % RU
                t_ck = []
                for h4 in range(4):
                    k_ = nck[0] % 2
                    nck[0] += 1
                    t_ld = P.dma("sp", ckf[k_][:], ckT_d[st, hh * 4 + h4], [ckf_free[k_]], key="ckf%d" % k_)
                    t_c = P.op("act", lambda e, k_=k_, h4=h4, r=r: e.activation(out=ckbr[r][:, h4, :], in_=ckf[k_][:],
                                                                             func=AF.Copy), [t_ld, ck_free[r]])
                    ckf_free[k_] = t_c
                    t_ck.append(t_c)
                t_cv_ = [t_ones[r]]
                for blk in range(8):
                    k_ = ncv[0] % 2
                    ncv[0] += 1
                    t_ld = P.dma("sp", cvf[k_][:, 0:512], cv_d[st, blk * 128:(blk + 1) * 128, hh * 512:(hh + 1) * 512],
                                 [cvf_free[k_]], key="cvf%d" % k_)
                    t_c = P.op("dve", lambda e, k_=k_, blk=blk, r=r: e.tensor_copy(
                        out=vxcr[r][:, blk, :, 0:128], in_=cvf[k_][:, 0:512].rearrange("p (h e) -> p h e", e=128)),
                        [t_ld, vx_free[r]])
                    cvf_free[k_] = t_c
                    t_cv_.append(t_c)
                ucache[u] = (t_ck, t_cv_)

            for u in range(min(2, NU)):
                load_unit(u)
            pend_final = [None]
            tp_prev = [None]
            for u in range(NU):
                st, hh = divmod(u, 2)
                r = u % RU
                if u + 2 < NU:
                    load_unit(u + 2)
                t_ck, t_cv_ = ucache[u]
                if hh == 0:
                    t_l1 = P.dma("sp", cmkf[:], cmkT_d[st].rearrange("h p n -> p h n"), [cm_free[0]], key="cmk")
                    t_l2 = P.dma("sp", cmvf[:], cmv_d[st].rearrange("(b p) n -> p b n", p=128), [cm_free[0]], key="cmv")
                    t_m1 = P.op("pool", lambda e: e.tensor_copy(out=cmkb[:], in_=cmkf[:]), [t_l1, cm_free[0]])
                    t_m2 = P.op("pool", lambda e: e.tensor_copy(
                        out=cmvx[:, :, :, 0:128], in_=cmvf[:].rearrange("p b (h e) -> p b h e", e=128)), [t_l2, cm_free[0]])
                    t_m3 = P.op("pool", lambda e: e.memset(cmvx[:, :, :, 128:129].rearrange("p b h e -> p (b h e)"), 1.0), [t_m2])
                    t_mem = [t_m1, t_m2, t_m3]
                qs = slice(st * SS, (st + 1) * SS)
                for h4 in range(4):
                    h = hh * 4 + h4
                    cnt = u * 4 + h4
                    sbk = cnt % 2
                    abk = cnt % 2
                    Sv = psS[:, sbk, 0:2 * 9 * SS].rearrange("p (m k q) -> p m k q", m=2, k=9)
                    for m in range(2):
                        for kb in range(8):
                            tq = P.op("pe", lambda e, m=m, kb=kb, h4=h4, h=h, Sv=Sv, qs=qs, r=r: e.matmul(
                                Sv[:, m, kb, :], lhsT=ckbr[r][64 * m:64 * m + 64, h4, kb * 128:kb * 128 + 128],
                                rhs=qTs[64 * m:64 * m + 64, h, qs], start=(m == 0 and kb == 0), stop=False,
                                skip_group_check=True),
                                (t_ck + t_fm + t_tm + [sb_free.get(sbk)]) if (m == 0 and kb == 0) else (), sig=None)
                        tq = P.op("pe", lambda e, m=m, h=h, Sv=Sv, qs=qs: e.matmul(
                            Sv[0:SS, m, 8, :], lhsT=kTs[64 * m:64 * m + 64, h, qs], rhs=qTs[64 * m:64 * m + 64, h, qs],
                            start=False, stop=False, skip_group_check=True), (), sig=None)
                        for part in range(2):
                            tq = P.op("pe", lambda e, m=m, h=h, part=part, Sv=Sv: e.matmul(
                                Sv[:, m, 7, :], lhsT=ident[:], rhs=biasT[:, h, 1, part, 0:SS], start=False, stop=False,
                                skip_group_check=True), (), sig=None)
                            tq = P.op("pe", lambda e, m=m, h=h, part=part, Sv=Sv: e.matmul(
                                Sv[0:SS, m, 8, :], lhsT=ident[0:SS, 0:SS], rhs=biasT[0:SS, h, 0, part, 0:SS], start=False,
                                stop=True, skip_group_check=True), (), sig=("chain" if (m == 1 and part == 1) else None))
                    te1 = P.op("act", lambda e, h=h, Sv=Sv: e.activation(
                        out=pTs[:, :, 0:8, :], in_=Sv[:, :, 0:8, :], func=AF.Exp, bias=cfar[:, h:h + 1], scale=1.0),
                        [tq, tp_prev[0]])
                    te2 = P.op("act", lambda e, h=h, Sv=Sv: e.activation(
                        out=pTs[0:SS, :, 8, :], in_=Sv[0:SS, :, 8, :], func=AF.Exp, bias=cfar[0:SS, h:h + 1], scale=1.0), [tq])
                    sb_free[sbk] = te2
                    for m in range(2):
                        for kb in range(8):
                            tp_ = P.op("pe", lambda e, m=m, kb=kb, h4=h4, r=r, abk=abk: e.matmul(
                                psA[0:SS, abk, m * ACCW:m * ACCW + 129], lhsT=pTs[:, m, kb, :], rhs=vxcr[r][:, kb, h4, :],
                                start=(m == 0 and kb == 0), stop=False, skip_group_check=True),
                                ([te1, te2, acc_freeS[abk]] + t_cv_) if (m == 0 and kb == 0) else (), sig=None)
                        tp_ = P.op("pe", lambda e, m=m, h=h, st=st, abk=abk: e.matmul(
                            psA[0:SS, abk, m * ACCW:m * ACCW + 129], lhsT=pTs[0:SS, m, 8, :], rhs=vnew[:, st, h, :],
                            start=False, stop=True, skip_group_check=True), (), sig=("chain" if m == 1 else None))
                    tp_prev[0] = tp_
                    if pend_final[0] is not None:
                        pend_final[0]()
                        pend_final[0] = None
                    t = P.op("dve", lambda e, abk=abk: e.tensor_copy(
                        out=accS[:], in_=psA[0:SS, abk, 0:2 * ACCW].rearrange("p (a w) -> p a w", w=ACCW)), [tp_, t_epS[0]])
                    acc_freeS[abk] = t
                    t = P.op("dve", lambda e: e.reciprocal(out=recS[:], in_=accS[:, :, 128]), [t])
                    t = P.op("dve", lambda e: e.tensor_scalar(out=nr2S[:], in0=recS[:, 1:2], scalar1=nlam[0:SS, 0:1],
                                                              scalar2=None, op0=ALU.mult), [t, t_nlam])
                    t = P.op("dve", lambda e: e.tensor_scalar(out=tS[:], in0=accS[:, 1, 0:128], scalar1=nr2S[:, 0:1],
                                                              scalar2=None, op0=ALU.mult), [t])
                    oSel = oS2[cnt % 2]
                    rSel = rstdS2[cnt % 2]
                    t = P.op("dve", lambda e, oSel=oSel: e.scalar_tensor_tensor(out=oSel[:], in0=accS[:, 0, 0:128], scalar=recS[:, 0:1],
                                                                                in1=tS[:], op0=ALU.mult, op1=ALU.add), [t])
                    t = P.op("dve", lambda e, oSel=oSel: e.tensor_tensor(out=sqS[:], in0=oSel[:], in1=oSel[:], op=ALU.mult), [t])
                    t = P.op("dve", lambda e, rSel=rSel: e.reduce_sum(out=rSel[:, 0:1], in_=sqS[:], axis=AX.X), [t])
                    t_epS[0] = t
                    t = P.op("act", lambda e, rSel=rSel: e.activation(out=rSel[:, 1:2], in_=rSel[:, 0:1], func=AF.Ln,
                                                                      bias=epsb[0:SS, 0:1], scale=1.0 / 128.0), [t])
                    t = P.op("act", lambda e, rSel=rSel: e.activation(out=rSel[:, 1:2], in_=rSel[:, 1:2], func=AF.Exp, scale=-0.5), [t])

                    def fin(st=st, h=h, oSel=oSel, rSel=rSel, t=t):
                        t_epS[0] = P.op("dve", lambda e: e.scalar_tensor_tensor(
                            out=ztoks[:, st, h * 128:h * 128 + 128], in0=oSel[:], scalar=rSel[:, 1:2],
                            in1=sgds[:, st, h * 128:h * 128 + 128], op0=ALU.mult, op1=ALU.mult), [t, t_epS[0]])
                    pend_final[0] = fin
                ck_free[r] = tq
                vx_free[r] = tp_
                if hh == 0:
                    continue
                if pend_final[0] is not None:
                    pend_final[0]()
                    pend_final[0] = None
                for mh in range(MH):
                    Sm = psS[:, 2, 0:2 * SS].rearrange("p (k q) -> p k q", k=2)
                    for blk in range(2):
                        tq = P.op("pe", lambda e, mh=mh, blk=blk, Sm=Sm, qs=qs: e.matmul(
                            Sm[:, blk, :], lhsT=cmkb[:, mh, blk * 128:blk * 128 + 128], rhs=mqTs[:, mh, qs],
                            start=(blk == 0), stop=(blk == 1), skip_group_check=True),
                            (t_mem + [sb_free.get(2)]) if blk == 0 else (), sig=("chain" if blk == 1 else None))
                    te = P.op("act", lambda e, Sm=Sm: e.activation(out=pTm[:], in_=Sm, func=AF.Exp), [tq, tp_prev[0]])
                    sb_free[2] = te
                    for blk in range(2):
                        tp_ = P.op("pe", lambda e, mh=mh, blk=blk: e.matmul(
                            psA[0:SS, 2, 0:129], lhsT=pTm[:, blk, :], rhs=cmvx[:, blk, mh, :],
                            start=(blk == 0), stop=(blk == 1), skip_group_check=True),
                            [te, acc_freeM[0]] if blk == 0 else (), sig=("chain" if blk == 1 else None))
                    tp_prev[0] = tp_
                    t = P.op("dve", lambda e: e.tensor_copy(out=accS[:, 0, :], in_=psA[0:SS, 2, 0:ACCW]), [tp_, t_epS[0]])
                    acc_freeM[0] = t
                    t = P.op("dve", lambda e: e.reciprocal(out=recS[:, 0:1], in_=accS[:, 0, 128:129]), [t])
                    t = P.op("dve", lambda e, st=st, mh=mh: e.scalar_tensor_tensor(
                        out=ztoks[:, st, 1024 + mh * 128:1024 + mh * 128 + 128], in0=accS[:, 0, 0:128], scalar=recS[:, 0:1],
                        in1=sgms[:, st, mh * 128:mh * 128 + 128], op0=ALU.mult, op1=ALU.mult), [t])
                    t_epS[0] = t
                cm_free[0] = tp_
            t_zs = []
            tpf = [None]
            for st in range(NSTR):
                for k in range(12):
                    tt = P.op("pe", lambda e, st=st, k=k: e.transpose(
                        psTb[:, k * SS:(k + 1) * SS], ztoks[:, st, k * 128:(k + 1) * 128], ident[0:SS, 0:SS]),
                        [t_epS[0], tpf[0]] if k == 0 else (), sig=("chain" if k == 11 else None))
                t1 = P.op("dve", lambda e, st=st: e.tensor_copy(
                    out=zTs[:, 0:8, st * SS:(st + 1) * SS], in_=psTb[:, 0:8 * SS].rearrange("p (k t) -> p k t", t=SS)), [tt])
                t2 = P.op("dve", lambda e, st=st: e.tensor_copy(
                    out=zTs[:, 12:16, st * SS:(st + 1) * SS], in_=psTb[:, 8 * SS:12 * SS].rearrange("p (k t) -> p k t", t=SS)), [tt, t1])
                tpf[0] = t2
                t_zs += [t1, t2]
            t_zs.append(P.op("pool", lambda e: e.tensor_copy(out=zTs[:, 8:12, :], in_=zTcs[:]), t_cvs))
            t_ys = []
            xr_free = [None, None]
            for g in range(4):
                t_w = t_wo if g == 0 else loadS(G_O0 + g)
                t_xr = P.dma("sp", xrS[g % 2][:], xs_d[:, g * 512:(g + 1) * 512], [xr_free[g % 2]], key="xrS%d" % (g % 2))

                def mm(e, out, c, st_, sp_):
                    return e.matmul(out, lhsT=zTs[:, c, :], rhs=wS[:, c, :], start=st_, stop=sp_)

                def ev(bk, tpe, g=g, t_xr=t_xr):
                    return P.op("dve", lambda e: e.scalar_tensor_tensor(
                        out=ysbS[:, g * 512:(g + 1) * 512], in0=xrS[g % 2][:], scalar=ALPHA, in1=bk, op0=ALU.mult,
                        op1=ALU.add), [tpe, t_xr])
                last_pe, t_e = jobS(mm, NTK, 512, ev, t_zs + [t_w])
                xr_free[g % 2] = t_e
                wS_free[0] = last_pe
                t_ys.append(t_e)
            t = None
            for g in range(4):
                t = P.op("dve", lambda e, g=g: e.bn_stats(out=bnS[:, g, :], in_=ysbS[:, g * 512:(g + 1) * 512]), t_ys + [t])
            t = P.op("dve", lambda e: e.bn_aggr(out=mvS[:], in_=bnS[:].rearrange("p a b -> p (a b)")), [t])
            t = P.op("act", lambda e: e.activation(out=lnrS[:], in_=mvS[:, 1:2], func=AF.Ln, bias=epsb[0:NTK, 0:1], scale=1.0), [t])
            t = P.op("act", lambda e: e.activation(out=lnrS[:], in_=lnrS[:], func=AF.Exp, scale=-0.5), [t])
            t = P.op("dve", lambda e: e.tensor_scalar(out=ysbS[:], in0=ysbS[:], scalar1=mvS[:, 0:1], scalar2=lnrS[:, 0:1],
                                                      op0=ALU.subtract, op1=ALU.mult), [t])
            ln_free = [None, None]
            for g in range(4):
                tg = P.dma("sp", lnS[0][:], lng_d[0:NTK, g * 512:(g + 1) * 512], [ln_free[0]], key="lnS0")
                tb = P.dma("sp", lnS[1][:], lnb_d[0:NTK, g * 512:(g + 1) * 512], [ln_free[1]], key="lnS1")
                t = P.op("dve", lambda e, g=g: e.tensor_tensor(out=ysbS[:, g * 512:(g + 1) * 512],
                                                               in0=ysbS[:, g * 512:(g + 1) * 512], in1=lnS[0][:], op=ALU.mult), [t, tg])
                ln_free[0] = t
                t = P.op("dve", lambda e, g=g: e.tensor_tensor(out=ysbS[:, g * 512:(g + 1) * 512],
                                                               in0=ysbS[:, g * 512:(g + 1) * 512], in1=lnS[1][:], op=ALU.add), [t, tb])
                ln_free[1] = t
            P.dma("sp", ys_o, ysbS[:], [t], key="ysS")
            P.emit()
            SS_.__exit__(None, None, None)

        SB = ExitStack()
        SB.__enter__()
        wring = [sb(SB, "wring%d" % i, [128, NCH, 512], BF16) for i in range(2)]
        xz = sb(SB, "xz", [128, NCH, TW + 2], BF16)
        stgB = sb(SB, "stgB", [128, 2, 2 * (TW + 2)], F32)
        qT = sb(SB, "qT", [128, H, TW], BF16)
        mqT = sb(SB, "mqT", [128, MH, TW], BF16)
        big = sb(SB, "big", [128, 8208], F32)
        sgd = sb(SB, "sgd", [128, 4, 1024], F32)
        sgm = sb(SB, "sgm", [128, 4, 512], F32)
        ztok = sb(SB, "ztok", [128, 4, 1536], BF16)
        zTc = sb(SB, "zTc", [128, 4, TW], BF16)
        kring = sb(SB, "kring", [128, RK, TW], BF16)
        vring = sb(SB, "vring", [128, RK, 4, 129], BF16)
        pring = sb(SB, "pring", [128, RP, 2, TW], BF16)
        accs = sb(SB, "accs", [128, 8, ACCW], F32)
        otmp = sb(SB, "otmp", [128, 4, 128], F32)
        ttmp = sb(SB, "ttmp", [128, 128], F32)
        sqtmp = sb(SB, "sqtmp", [128, 128], F32)
        rec = sb(SB, "rec", [128, 8], F32)
        nr2 = sb(SB, "nr2", [128, 4], F32)
        ssq = sb(SB, "ssq", [128, 4], F32)
        rstd = sb(SB, "rstd", [128, 4], F32)
        halo = sb(SB, "halo", [128, 8, 2], F32)
        ctmp = sb(SB, "ctmp", [128, TW], F32)
        ctmp2 = sb(SB, "ctmp2", [128, TW], F32)
        utmp = sb(SB, "utmp", [128, TW + 2], F32)
        xres = [sb(SB, "xres%d" % i, [128, 512], F32) for i in range(2)]
        bnst = sb(SB, "bnst", [128, 4, 6], F32)
        bnst4 = sb(SB, "bnst4", [128, 4, 4, 6], F32)
        mv4 = sb(SB, "mv4", [128, 4, 2], F32)
        lnr4 = sb(SB, "lnr4", [128, 4], F32)
        mv = sb(SB, "mv", [128, 2], F32)
        lnr = sb(SB, "lnr", [128, 1], F32)
        convu = sb(SB, "convu", [128, 4, 2], F32)

        hT = big[:, 0:4 * 514].rearrange("p (c t) -> p c t", t=514)
        CT = big[:, 2056:2 * 2056].rearrange("p (c t) -> p c t", t=514)
        BT = big[:, 4112:4112 + 2048].rearrange("p (c t) -> p c t", t=512)
        sgcT = big[:, 6160:6160 + 2048].rearrange("p (c t) -> p c t", t=512)
        ysb = big[:, 0:8192].rearrange("p (s n) -> p s n", n=2048)
        lngs = sb(SB, "lngs", [128, D], F32)
        lngb = lngs[:]
        t_lng = P.dma("sp", lngs[:], lng_d, key="lng")
        zT = sgd[:].rearrange("p s n -> p (s n)").bitcast(BF16).rearrange("p (c t) -> p c t", t=TW)

        lnb_sb = sb(SB, "lnb_sb", [128, D], F32)
        t_lnb = P.dma("sp", lnb_sb[:], lnb_d, key="lnb")

        sem_qk = P.sem("qk")
        sem_exp = P.sem("exp")
        sem_pv = P.sem("pv")

        bank_free = [None] * 7
        halo_free = [None]
        jbc = [0]
        wr_free = [None, None]
        wr_cnt = [0]

        def load_w(gi, after=()):
            r = wr_cnt[0] % 2
            wr_cnt[0] += 1
            t = P.dma("sp", wring[r][:], wsc[gi], [wr_free[r]] + list(after), key="w%d" % r)
            return r, t

        def job(mm, n_out, evac, deps):
            b = jbc[0] % 7
            jbc[0] += 1
            for c in range(NCH):
                t_pe = P.op("pe", lambda e, b=b, c=c: mm(e, bank(b)[:, 0:n_out], c, c == 0, c == NCH - 1),
                            (list(deps) + [bank_free[b]]) if c == 0 else (),
                            sig=("chain" if c == NCH - 1 else None))
            t = evac(bank(b)[:, 0:n_out], t_pe)
            bank_free[b] = t
            return t_pe, t

        xTr2 = xT.rearrange("(c p) t -> p c t", p=128)
        stgv = [stgB[:, k, :].rearrange("p (c t) -> p c t", t=TW + 2) for k in range(2)]
        prev_slot_done = []
        xz_free = []
        lng_used = []
        nblk = [0]
        ncidx = [0]
        kv_free = [None] * RK
        pv_tok = {}
        exp_tok = {}
        acc_free = [None]
        y_st_tok = [None]

        stg_free2 = [None, None]

        def load_xz(T):
            t_x = []
            for pc in range(8):
                k = pc % 2
                t_ld = P.dma("pool", stgv[k], xTr2[:, 2 * pc:2 * pc + 2, T * TW - 2:(T + 1) * TW],
                             [stg_free2[k]], key="xz%d" % k)
                t_c = P.op("dve", lambda e, k=k, pc=pc: e.tensor_copy(out=xz[:, 2 * pc:2 * pc + 2, :], in_=stgv[k]),
                           [t_ld] + xz_free)
                stg_free2[k] = t_c
                t_x.append(t_c)
            return t_x

        t_x_next = load_xz(3)
        for i in range(nslot):
            T = 4 * i + 3
            t_x = t_x_next
            nxt = load_w(G_Q0)
            order = [G_Q0, G_Q1, G_MQ, G_GD0, G_GD1, G_GM, G_H, G_C, G_B, G_GC]
            t_conv_in = {}
            t_q = []
            t_gate = []
            for oi, gi in enumerate(order):
                r, t_w = nxt
                last_pe = None
                if gi in (G_Q0, G_Q1, G_H, G_B, G_C, G_MQ, G_GC):
                    for sub in range(4):
                        def mm(e, out, c, st, sp_, r=r, sub=sub):
                            return e.matmul(out, lhsT=wring[r][:, c, sub * 128:sub * 128 + 128], rhs=xz[:, c, 2:TW + 2],
                                            start=st, stop=sp_)
                        if gi in (G_Q0, G_Q1):
                            hh = (gi - G_Q0) * 4 + sub

                            def ev(bk, tpe, hh=hh):
                                return P.op("act", lambda e: e.mul(out=qT[:, hh, :], in_=bk, mul=0.125), [tpe])
                        elif gi == G_MQ:
                            def ev(bk, tpe, sub=sub):
                                return P.op("act", lambda e: e.mul(out=mqT[:, sub, :], in_=bk, mul=128.0 ** -0.5), [tpe])
                        elif gi == G_H:
                            def ev(bk, tpe, sub=sub):
                                return P.op("dve", lambda e: e.tensor_copy(out=hT[:, sub, 2:TW + 2], in_=bk),
                                            [tpe] + prev_slot_done)
                        elif gi == G_C:
                            def ev(bk, tpe, sub=sub):
                                return P.op("dve", lambda e: e.tensor_copy(out=CT[:, sub, 2:TW + 2], in_=bk),
                                            [tpe] + prev_slot_done)
                        elif gi == G_B:
                            def ev(bk, tpe, sub=sub):
                                return P.op("dve", lambda e: e.tensor_copy(out=BT[:, sub, :], in_=bk),
                                            [tpe] + prev_slot_done)
                        else:
                            def ev(bk, tpe, sub=sub):
                                return P.op("act", lambda e: e.activation(out=sgcT[:, sub, :], in_=bk, func=AF.Silu),
                                            [tpe] + prev_slot_done)
                        last_pe, t_e = job(mm, TW, ev, t_x + [t_w])
                        if gi in (G_Q0, G_Q1, G_MQ):
                            t_q.append(t_e)
                        else:
                            t_conv_in[(gi, sub)] = t_e
                        if gi in (G_H, G_C):
                            hidx = (0 if gi == G_H else 4) + sub
                            for c in range(NCH):
                                last_pe = P.op("pe", lambda e, c=c, r=r, sub=sub, hidx=hidx: e.matmul(
                                    psT[:, 2 * hidx:2 * hidx + 2], lhsT=wring[r][:, c, sub * 128:sub * 128 + 128],
                                    rhs=xz[:, c, 0:2], start=(c == 0), stop=(c == NCH - 1)),
                                    [halo_free[0]] if c == 0 else (), sig=("chain" if c == NCH - 1 else None))
                            tgt = hT if gi == G_H else CT
                            halo_free[0] = P.op("dve", lambda e, tgt=tgt, sub=sub, hidx=hidx: e.tensor_copy(
                                out=tgt[:, sub, 0:2], in_=psT[:, 2 * hidx:2 * hidx + 2]), [last_pe] + prev_slot_done)
                            t_conv_in[(gi, sub, "halo")] = halo_free[0]
                else:
                    for s in range(4):
                        def mm(e, out, c, st, sp_, r=r, s=s):
                            return e.matmul(out, lhsT=xz[:, c, 2 + s * 128:2 + s * 128 + 128], rhs=wring[r][:, c, :],
                                            start=st, stop=sp_)
                        if gi in (G_GD0, G_GD1):
                            g = gi - G_GD0

                            def ev(bk, tpe, s=s, g=g):
                                t1 = P.op("act", lambda e: e.activation(out=sgd[:, s, g * 512:g * 512 + 512], in_=bk,
                                                                        func=AF.Silu), [tpe])
                                t2 = P.op("dve", lambda e: e.tensor_tensor(out=sgd[:, s, g * 512:g * 512 + 512],
                                                                            in0=sgd[:, s, g * 512:g * 512 + 512],
                                                                            in1=gs4[:], op=ALU.mult), [t1])
                                t_gate.append(t2)
                                return t1
                        else:
                            def ev(bk, tpe, s=s):
                                return P.op("act", lambda e: e.activation(out=sgm[:, s, :], in_=bk, func=AF.Silu),
                                            [tpe])
                        last_pe, t_e = job(mm, 512, ev, t_x + [t_w])
                        t_gate.append(t_e)
                wr_free[r] = last_pe
                if oi + 1 < len(order):
                    nxt = load_w(order[oi + 1])
            xz_free = [last_pe]
            w_o = [load_w(G_O0), load_w(G_O0 + 1)]
            t_cv = []

            def post_mem(i=i):
                nonlocal t_x_next
                for cc in range(4):
                    dep = [t_conv_in[(G_H, cc)], t_conv_in[(G_C, cc)], t_conv_in[(G_H, cc, "halo")],
                           t_conv_in[(G_C, cc, "halo")], t_conv_in[(G_B, cc)], t_conv_in[(G_GC, cc)]]
                    t = P.op("dve", lambda e, cc=cc: e.tensor_tensor(out=utmp[:], in0=CT[:, cc, :], in1=hT[:, cc, :],
                                                                     op=ALU.mult), dep + t_cv[-1:])
                    if i == nslot - 1:
                        t_u = P.op("dve", lambda e, cc=cc: e.tensor_copy(out=convu[:, cc, :], in_=utmp[:, TW:TW + 2]), [t])
                        t_cv.append(t_u)
                    t = P.op("dve", lambda e, cc=cc: e.tensor_scalar(out=ctmp[:], in0=utmp[:, 0:TW],
                                                                     scalar1=cw[:, cc * 3:cc * 3 + 1], scalar2=None,
                                                                     op0=ALU.mult), [t])
                    for jx in (1, 2):
                        t = P.op("dve", lambda e, cc=cc, jx=jx: e.scalar_tensor_tensor(
                            out=ctmp[:], in0=utmp[:, jx:jx + TW], scalar=cw[:, cc * 3 + jx:cc * 3 + jx + 1], in1=ctmp[:],
                            op0=ALU.mult, op1=ALU.add), [t])
                    t = P.op("dve", lambda e, cc=cc: e.tensor_tensor(out=ctmp[:], in0=ctmp[:], in1=BT[:, cc, :], op=ALU.mult), [t])
                    t = P.op("dve", lambda e, cc=cc: e.tensor_tensor(out=zTc[:, cc, :], in0=ctmp[:], in1=sgcT[:, cc, :],
                                                                     op=ALU.mult), [t])
                    t_cv.append(t)
                if i == nslot - 1:
                    t_cv.append(P.dma("pool", convo_o, convu[:], t_cv, key="convo"))
                if i + 1 < nslot:
                    t_x_next = load_xz(4 * (i + 1) + 3)

            t_q = t_q + t_gate + list(t_conv_in.values())
            blocks = []
            for h in range(H):
                for kc in range(T + 1):
                    for kbl in range(4):
                        blocks.append((h, kc, kbl))
            nb_total = len(blocks)
            t_ep = []
            kvslot = {}

            def issue_kv(h, kc):
                cidx = ncidx[0]
                ncidx[0] += 1
                sl = cidx % RK
                t1 = P.dma("sp", kring[:, sl, :], kts[kc, :, h, :], [kv_free[sl]], key="kv%d" % sl)
                t2 = P.dma("sp", vring[:, sl].rearrange("p s e -> p (s e)"), vxs[kc, :, h, :], [kv_free[sl]],
                           key="kv%d" % sl)
                kvslot[(h, kc)] = (sl, t2)

            chunk_list = [(h, kc) for h in range(H) for kc in range(T + 1)]
            kv_issued = [0]

            def ensure_kv(upto):
                while kv_issued[0] < min(upto, len(chunk_list)):
                    issue_kv(*chunk_list[kv_issued[0]])
                    kv_issued[0] += 1

            ensure_kv(RK - 1)

            def do_qk(bi):
                h, kc, kbl = blocks[bi]
                n = nblk[0] + bi
                buf = n % 2
                sl, t_kv = kvslot[(h, kc)]
                diag = (kc == T)
                a = kbl * 128 if diag else 0
                adds = []
                if diag:
                    adds.append((kbl, 0))
                    if kbl + 1 < 4:
                        adds.append((kbl + 1, 1))
                elif kc == T - 1 and kbl == 3:
                    adds.append((0, 1))
                deps = [t_kv] + t_q + (exp_tok.get(n - 2) and [exp_tok[n - 2]] or [])
                for m in range(2):
                    last = (m == 1 and not adds)
                    tq = P.op("pe", lambda e, m=m, buf=buf, sl=sl, kbl=kbl, a=a, h=h: e.matmul(
                        psS[:, buf * 2 + m, a:TW], lhsT=kring[64 * m:64 * m + 64, sl, kbl * 128:kbl * 128 + 128],
                        rhs=qT[64 * m:64 * m + 64, h, a:TW], start=True, stop=True, skip_group_check=True),
                        deps if m == 0 else (), sig=(sem_qk if last else None))
                for ai, (s_, kind) in enumerate(adds):
                    for m in range(2):
                        for part in range(2):
                            last = (ai == len(adds) - 1 and m == 1 and part == 1)
                            tq = P.op("pe", lambda e, m=m, buf=buf, s_=s_, kind=kind, part=part, h=h: e.matmul(
                                psS[:, buf * 2 + m, s_ * 128:s_ * 128 + 128], lhsT=ident[:],
                                rhs=biasT[:, h, kind, part, :], start=False, stop=True, skip_group_check=True),
                                (), sig=(sem_qk if last else None))
                return tq, a

            qk_info = {}
            deferred = []
            cur_bi = [0]

            def do_exp(bi):
                h, kc, kbl = blocks[bi]
                n = nblk[0] + bi
                tq, a = qk_info[bi]
                deps = [tq]
                if n - RP in pv_tok:
                    deps.append(pv_tok[n - RP])
                exp_tok[n] = P.op("act", lambda e, n=n, a=a, h=h: e.activation(
                    out=pring[:, n % RP, :, a:TW], in_=psS[:, (n % 2) * 2:(n % 2) * 2 + 2, a:TW], func=AF.Exp,
                    bias=cfar[:, h:h + 1], scale=1.0), deps, sig=sem_exp)

            def do_pv(bi):
                h, kc, kbl = blocks[bi]
                n = nblk[0] + bi
                sl, t_kv = kvslot[(h, kc)]
                diag = (kc == T)
                first = (kc == 0 and kbl == 0)
                lastb = (kc == T and kbl == 3)
                s0 = kbl if diag else 0
                ops = [(s, m) for s in range(s0, 4) for m in range(2)]
                for oi_, (s, m) in enumerate(ops):
                    a_ = s * 2 + m
                    deps = []
                    if oi_ == 0:
                        deps = [exp_tok[n]]
                        if first:
                            deps.append(acc_free[0])
                    tp_ = P.op("pe", lambda e, s=s, m=m, a_=a_, n=n, sl=sl, kbl=kbl, first=first, lastb=lastb: e.matmul(
                        psA[:, a_ // 3, (a_ % 3) * ACCW:(a_ % 3) * ACCW + 129],
                        lhsT=pring[:, n % RP, m, s * 128:s * 128 + 128], rhs=vring[:, sl, kbl, :],
                        start=(first and a_ % 3 == 0), stop=lastb, skip_group_check=True),
                        deps, sig=(sem_pv if oi_ == len(ops) - 1 else None))
                pv_tok[n] = tp_
                if kbl == 3:
                    kv_free[sl] = tp_
                    ensure_kv(kv_issued[0] + 1)
                if lastb:
                    epilogue(h, tp_)

            def epilogue(h, t_last):
                deps = [t_last] + t_gate
                tcs = []
                for bk_ in range(3):
                    na = 3 if bk_ < 2 else 2
                    tcs.append(P.op("dve", lambda e, bk_=bk_, na=na: e.tensor_copy(
                        out=accs[:, bk_ * 3:bk_ * 3 + na, :],
                        in_=psA[:, bk_, 0:na * ACCW].rearrange("p (a w) -> p a w", w=ACCW)), deps + t_ep[-1:]))
                acc_free[0] = tcs[-1]
                t = P.op("dve", lambda e: e.reciprocal(out=rec[:], in_=accs[:, :, 128]), tcs)
                t = P.op("dve", lambda e: e.tensor_scalar(out=nr2[:], in0=rec[:].rearrange("p (s m) -> p s m", m=2)[:, :, 1],
                                                          scalar1=nlam[:, 0:1], scalar2=None, op0=ALU.mult), [t, t_nlam])
                for s in range(4):
                    t = P.op("dve", lambda e, s=s: e.tensor_scalar(out=ttmp[:], in0=accs[:, 2 * s + 1, 0:128],
                                                                   scalar1=nr2[:, s:s + 1], scalar2=None, op0=ALU.mult), [t])
                    t = P.op("dve", lambda e, s=s: e.scalar_tensor_tensor(
                        out=otmp[:, s, :], in0=accs[:, 2 * s, 0:128], scalar=rec[:, 2 * s:2 * s + 1], in1=ttmp[:],
                        op0=ALU.mult, op1=ALU.add), [t])
                    t = P.op("dve", lambda e, s=s: e.tensor_tensor(out=sqtmp[:], in0=otmp[:, s, :], in1=otmp[:, s, :],
                                                                   op=ALU.mult), [t])
                    t = P.op("dve", lambda e, s=s: e.reduce_sum(out=ssq[:, s:s + 1], in_=sqtmp[:], axis=AX.X), [t])
                t_ssq = t

                def part2(h=h, t_ssq=t_ssq):
                    t = P.op("act", lambda e: e.activation(out=rstd[:], in_=ssq[:], func=AF.Ln, bias=epsb[:, 0:1],
                                                           scale=1.0 / 128.0), [t_ssq])
                    t = P.op("act", lambda e: e.activation(out=rstd[:], in_=rstd[:], func=AF.Exp, scale=-0.5), [t])
                    for s in range(4):
                        t = P.op("dve", lambda e, s=s, h=h: e.scalar_tensor_tensor(
                            out=ztok[:, s, h * 128:h * 128 + 128], in0=otmp[:, s, :], scalar=rstd[:, s:s + 1],
                            in1=sgd[:, s, h * 128:h * 128 + 128], op0=ALU.mult, op1=ALU.mult), [t])
                    t_ep.append(t)
                deferred.append([cur_bi[0] + 6, part2])

            for mh in range(MH):
                for blk in range(2):
                    n = nblk[0]
                    nblk[0] += 1
                    buf = n % 2
                    deps = t_q + ([exp_tok[n - 2]] if (n - 2) in exp_tok else [])
                    tq = P.op("pe", lambda e, buf=buf, mh=mh, blk=blk: e.matmul(
                        psS[:, buf * 2, :], lhsT=mkT[:, mh, blk * 128:blk * 128 + 128], rhs=mqT[:, mh, :],
                        start=True, stop=True, skip_group_check=True), deps, sig=sem_qk)
                    deps = [tq]
                    if n - RP in pv_tok:
                        deps.append(pv_tok[n - RP])
                    exp_tok[n] = P.op("act", lambda e, n=n, buf=buf: e.activation(
                        out=pring[:, n % RP, 0, :], in_=psS[:, buf * 2, :], func=AF.Exp), deps, sig=sem_exp)
                    for s in range(4):
                        deps = []
                        if s == 0:
                            deps = [exp_tok[n]]
                            if blk == 0:
                                deps.append(acc_free[0])
                        tp_ = P.op("pe", lambda e, s=s, n=n, mh=mh, blk=blk: e.matmul(
                            psA[:, s // 3, (s % 3) * ACCW:(s % 3) * ACCW + 129],
                            lhsT=pring[:, n % RP, 0, s * 128:s * 128 + 128], rhs=mvx[:, blk, mh, :],
                            start=(blk == 0 and s % 3 == 0), stop=(blk == 1), skip_group_check=True),
                            deps, sig=(sem_pv if s == 3 else None))
                    pv_tok[n] = tp_
                tcs = [P.op("dve", lambda e: e.tensor_copy(
                    out=accs[:, 0:3, :], in_=psA[:, 0, 0:3 * ACCW].rearrange("p (a w) -> p a w", w=ACCW)),
                    [tp_] + t_gate + t_ep[-1:]),
                    P.op("dve", lambda e: e.tensor_copy(
                        out=accs[:, 3:4, :], in_=psA[:, 1, 0:ACCW].rearrange("p (a w) -> p a w", w=ACCW)), [tp_])]
                acc_free[0] = tcs[-1]
                t = P.op("dve", lambda e: e.reciprocal(out=rec[:, 0:4], in_=accs[:, 0:4, 128]), tcs)
                for s in range(4):
                    t = P.op("dve", lambda e, s=s, mh=mh: e.scalar_tensor_tensor(
                        out=ztok[:, s, 1024 + mh * 128:1024 + mh * 128 + 128], in0=accs[:, s, 0:128],
                        scalar=rec[:, s:s + 1], in1=sgm[:, s, mh * 128:mh * 128 + 128], op0=ALU.mult, op1=ALU.mult), [t])
                t_ep.append(t)

            post_mem()
            for b0 in range(min(2, nb_total)):
                qk_info[b0] = do_qk(b0)
                do_exp(b0)
            for bi in range(nb_total):
                cur_bi[0] = bi
                if bi + 2 < nb_total:
                    qk_info[bi + 2] = do_qk(bi + 2)
                    do_exp(bi + 2)
                while deferred and deferred[0][0] <= bi:
                    deferred.pop(0)[1]()
                do_pv(bi)
            while deferred:
                deferred.pop(0)[1]()
            nblk[0] += nb_total

            t_zt = []
            t_tp_free = [None]
            for s in range(4):
                for grp in range(3):
                    for k4 in range(4):
                        cidx_ = grp * 4 + k4
                        tt = P.op("pe", lambda e, s=s, cidx_=cidx_, k4=k4: e.transpose(
                            psTb[:, k4 * 128:k4 * 128 + 128], ztok[:, s, cidx_ * 128:cidx_ * 128 + 128], ident[:]),
                            (t_ep[-1:] + [t_tp_free[0]]) if k4 == 0 else (), sig=("chain" if k4 == 3 else None))
                    dst0 = grp * 4 if grp < 2 else 12
                    tcp = P.op("dve", lambda e, s=s, dst0=dst0: e.tensor_copy(
                        out=zT[:, dst0:dst0 + 4, s * 128:s * 128 + 128],
                        in_=psTb[:, 0:512].rearrange("p (k t) -> p k t", t=128)), [tt])
                    t_tp_free[0] = tcp
                    t_zt.append(tcp)
            t_zt.append(P.op("pool", lambda e: e.tensor_copy(out=zT[:, 8:12, 0:TW], in_=zTc[:]), t_cv + t_ep[-1:]))

            t_y = {}
            xr_free = [None, None]
            xcnt = 0
            for g in range(4):
                if g < 2:
                    r, t_w = w_o[g]
                else:
                    r, t_w = load_w(G_O0 + g)
                for s in range(4):
                    xsel = xcnt % 2
                    xcnt += 1
                    t_xr = P.dma("pool", xres[xsel][:], xown[i * TW + s * 128:i * TW + s * 128 + 128, g * 512:g * 512 + 512],
                                 [xr_free[xsel]], key="xr%d" % xsel)

                    def mm(e, out, c, st, sp_, r=r, s=s):
                        return e.matmul(out, lhsT=zT[:, c, s * 128:s * 128 + 128], rhs=wring[r][:, c, :], start=st, stop=sp_)

                    def ev(bk, tpe, s=s, g=g, xsel=xsel, t_xr=t_xr):
                        return P.op("dve", lambda e: e.scalar_tensor_tensor(
                            out=ysb[:, s, g * 512:g * 512 + 512], in0=xres[xsel][:], scalar=ALPHA, in1=bk,
                            op0=ALU.mult, op1=ALU.add), [tpe, t_xr, y_st_tok[0]] + t_cv)
                    last_pe, t_e = job(mm, 512, ev, t_zt + [t_w])
                    xr_free[xsel] = t_e
                    t_y[(s, g)] = t_e
                wr_free[r] = last_pe
            t_done = []
            t = None
            for s in range(4):
                for g in range(4):
                    t = P.op("dve", lambda e, s=s, g=g: e.bn_stats(out=bnst4[:, s, g, :], in_=ysb[:, s, g * 512:g * 512 + 512]),
                             [t_y[(s, g)], t])
                t = P.op("dve", lambda e, s=s: e.bn_aggr(out=mv4[:, s, :], in_=bnst4[:, s].rearrange("p a b -> p (a b)")), [t])
            t_r = P.op("act", lambda e: e.activation(out=lnr4[:], in_=mv4[:, :, 1], func=AF.Ln, bias=epsb[:, 0:1], scale=1.0), [t])
            t_r = P.op("act", lambda e: e.activation(out=lnr4[:], in_=lnr4[:], func=AF.Exp, scale=-0.5), [t_r])
            for s in range(4):
                t = P.op("dve", lambda e, s=s: e.scalar_tensor_tensor(out=ysb[:, s, :], in0=ysb[:, s, :], scalar=mv4[:, s, 0:1],
                                                                      in1=lngb, op0=ALU.subtract, op1=ALU.mult), [t, t_lng])
                t = P.op("dve", lambda e, s=s: e.scalar_tensor_tensor(out=ysb[:, s, :], in0=ysb[:, s, :], scalar=lnr4[:, s:s + 1],
                                                                      in1=lnb_sb[:], op0=ALU.mult, op1=ALU.add), [t, t_lnb, t_r])
                t_o = P.dma("pool", y_o[i * TW + s * 128:i * TW + s * 128 + 128, :], ysb[:, s, :], [t], key="yout")
                t_done.append(t_o)
            y_st_tok[0] = t_done[-1]
            prev_slot_done = t_done + [t]
        P.emit()
        P.op("dve", lambda e: e.memset(lnr[:], 0.0), ())
        P.emit()
        SB.__exit__(None, None, None)
    return nc


_CACHE = {}


def _host_consts():
    half, max_exact = 16, 8

    def bucket(rel):
        ret = np.where(rel > 0, half, 0)
        n = np.abs(rel)
        nf = np.maximum(n, 1).astype(np.float32)
        large = max_exact + (np.log(nf / max_exact) / math.log(128 / max_exact) * (half - max_exact)).astype(np.int32)
        large = np.minimum(large, half - 1)
        return ret + np.where(n < max_exact, n, large)

    k = np.arange(128)[:, None]
    q = np.arange(128)[None, :]
    bd = bucket(k - q)
    bp = bucket(k - q - 128)
    bkt = np.concatenate([bd, bp], axis=1).astype(np.float32)
    dmask = np.where((k // 64) <= (q // 64), 0.0, NEG).astype(np.float32)
    return bkt, dmask


def kernel(x_prompt, x_sample, cache_diff_k, cache_diff_v, cache_conv, cache_mem_k, cache_mem_v,
           mem_prompt, rel_bias_table, w_in, w_mem_kv, conv_w, lambda_q1, lambda_k1, lambda_q2,
           lambda_k2, subln_g, w_out, ln_g, ln_b, _nslot=NSLOT, _sample=True):
    f = np.float32
    x_prompt = np.asarray(x_prompt, f)
    B, S, _ = x_prompt.shape
    key = (_nslot, _sample)
    if key not in _CACHE:
        _CACHE[key] = build(_nslot, _sample)
    nc = _CACHE[key]

    bkt, dmask = _host_consts()
    rep = lambda v, n=128: np.ascontiguousarray(np.broadcast_to(np.asarray(v, f).reshape(1, -1), (n, np.asarray(v).size)))
    shared = {
        "w_in": np.ascontiguousarray(np.asarray(w_in, f)[0]),
        "w_out": np.ascontiguousarray(np.asarray(w_out, f)[0]),
        "w_mem": np.ascontiguousarray(np.asarray(w_mem_kv, f)[0]),
        "ident": np.eye(128, dtype=f),
        "tab": rep(np.asarray(rel_bias_table, f).reshape(-1)),
        "lamv": rep(np.concatenate([np.asarray(a, f).reshape(-1) for a in
                                    (lambda_q1, lambda_k1, lambda_q2, lambda_k2)])),
        "gsub": rep(np.tile(np.asarray(subln_g, f).reshape(-1), 4)),
        "lng": rep(np.asarray(ln_g, f).reshape(-1)),
        "lnb": rep(np.asarray(ln_b, f).reshape(-1)),
        "convw": np.ascontiguousarray(np.asarray(conv_w, f)[0].reshape(3, 4, 128).transpose(2, 1, 0).reshape(128, 12)),
        "bkt": bkt,
        "dmask": dmask,
    }
    in_maps = []
    xTb = [np.ascontiguousarray(x_prompt[b].T) for b in range(B)]
    memTb = [np.ascontiguousarray(np.asarray(mem_prompt, f)[b].T) for b in range(B)]
    for c in range(NCORES):
        b, j = c // 4, c % 4
        xT = np.zeros((D, NT * TW), f)
        xT[:, (3 - j) * TW:(3 - j) * TW + S] = xTb[b]
        xown = np.concatenate([x_prompt[b, (4 * i + j) * TW:(4 * i + j + 1) * TW] for i in range(NSLOT)], axis=0)
        vm = np.zeros((NT, 32), f)
        vm[3 - j:3 - j + S // TW] = 1.0
        m = dict(shared)
        m["xT"] = xT
        m["xown"] = np.ascontiguousarray(xown)
        m["vmask"] = rep(vm.reshape(-1))
        m["memT"] = memTb[b]
        in_maps.append(m)
    if _sample:
        xs_all = np.asarray(x_sample, f)
        ck = np.asarray(cache_diff_k, f)[0]
        cvv = np.asarray(cache_diff_v, f)[0]
        cc_ = np.asarray(cache_conv, f)[0]
        cmk = np.asarray(cache_mem_k, f)[0]
        cmv = np.asarray(cache_mem_v, f)[0]
        for c in range(NCORES):
            sl = slice(c * NSTR, (c + 1) * NSTR)
            xs_c = xs_all[sl].reshape(NSTR * SS, D)
            m = in_maps[c]
            m["xs"] = np.ascontiguousarray(xs_c)
            m["xsT"] = np.ascontiguousarray(xs_c.T)
            m["ckT"] = np.ascontiguousarray(ck[sl].transpose(0, 2, 3, 1))
            m["cv"] = np.ascontiguousarray(cvv[sl].reshape(NSTR, PAST, 1024))
            m["cconv"] = np.ascontiguousarray(cc_[sl].reshape(NSTR, 2, 4, 128).transpose(3, 2, 0, 1))
            m["cmkT"] = np.ascontiguousarray(cmk[sl].transpose(0, 2, 3, 1))
            m["cmv"] = np.ascontiguousarray(cmv[sl].reshape(NSTR, NMEM, 512))
    res = run_bass_kernel_spmd(nc, in_maps, core_ids=list(range(NCORES)))
    R = res.results

    y = np.zeros((B, S, D), f)
    kp = np.zeros((1, B, S, H, 128), f)
    vp = np.zeros((1, B, S, H, 128), f)
    for c in range(NCORES):
        b, j = c // 4, c % 4
        for i in range(NSLOT):
            t = 4 * i + j
            y[b, t * TW:(t + 1) * TW] = R[c]["y"][i * TW:(i + 1) * TW]
            kp[0, b, t * TW:(t + 1) * TW] = R[c]["koT"][:, :, i * TW:(i + 1) * TW].transpose(2, 0, 1)
            vp[0, b, t * TW:(t + 1) * TW] = R[c]["vo"][i * TW:(i + 1) * TW].reshape(TW, H, 128)
    convp = np.zeros((1, B, 2, 512), f)
    mkp = np.zeros((1, B, NMEM, MH, 128), f)
    mvp = np.zeros((1, B, NMEM, MH, 128), f)
    for b in range(B):
        c = 4 * b + 3
        convp[0, b] = R[c]["convo"].transpose(2, 1, 0).reshape(2, 512)
        mkp[0, b] = R[c]["mkoT"].transpose(2, 0, 1)
        mvp[0, b] = R[c]["mvo"].reshape(NMEM, MH, 128)
    ys = np.zeros((32, SS, D), f)
    ks = np.zeros((1, 32, SS, H, 128), f)
    vs = np.zeros((1, 32, SS, H, 128), f)
    cs = np.zeros((1, 32, 2, 512), f)
    if _sample:
        for c in range(NCORES):
            sl = slice(c * NSTR, (c + 1) * NSTR)
            ys[sl] = R[c]["ys"].reshape(NSTR, SS, D)
            ks[0, sl] = R[c]["ksT"].transpose(2, 0, 1).reshape(NSTR, SS, H, 128)
            vs[0, sl] = R[c]["vs"].reshape(NSTR, SS, H, 128)
            cs[0, sl] = R[c]["convs"].transpose(1, 3, 2, 0).reshape(NSTR, 2, 512)
    return (y, ys, kp, vp, convp, mkp, mvp, ks, vs, cs)
```
